# Optimizing a Trainium2 kernel written in Bass

```python
import jax, jax.numpy as jnp
from jax import lax
import numpy as np

D_MODEL = 1024
BATCH = 4
SEQ = 4096
DEPTH = 1
DEC_BATCH = 1
DEC_SEQ = 16384
PAST_LEN = 128

N_HEADS = 8
N_KV_HEADS = 2
HEAD_DIM = 64
Q_PER_KV = N_HEADS // N_KV_HEADS
ATTN_WIDTH = N_HEADS * HEAD_DIM
KV_WIDTH = N_KV_HEADS * HEAD_DIM
GMLP_GROUPS = 8
GMLP_GROUP_DIM = 64
GMLP_WIDTH = GMLP_GROUPS * GMLP_GROUP_DIM
CHUNK = 128
Q_BLOCK = 128
GRID_W = 64
ROPE_THETA = 10000.0
ROPE_AXIS_FREQS = HEAD_DIM // 4
D_FF = 2816
PLE_DIM = 256
EPS = 1e-6
IN_SPLITS = (ATTN_WIDTH, KV_WIDTH, KV_WIDTH, GMLP_WIDTH, GMLP_WIDTH, D_MODEL, D_MODEL)
IN_WIDTH = 3840
IN_OFFSETS = (512, 640, 768, 1280, 1792, 2816)

kernel_name = 'hybrid_gqa_gmlp_macaron_encoder'


def rms_norm(x, g):
    xf = x.astype(jnp.float32)
    y = xf * lax.rsqrt(jnp.mean(xf * xf, axis=-1, keepdims=True) + EPS)
    return (y * g.astype(jnp.float32)).astype(x.dtype)


def swiglu(x, w_gu, w_down):
    a, b = jnp.split(x @ w_gu, 2, axis=-1)
    return (jax.nn.silu(a) * b) @ w_down


def axial_rope_tables(n_tok, dtype):
    rows = n_tok // GRID_W
    row = jnp.repeat(jnp.arange(rows, dtype=jnp.float32), GRID_W)
    col = jnp.tile(jnp.arange(GRID_W, dtype=jnp.float32), rows)
    inv = jnp.power(jnp.float32(ROPE_THETA), -jnp.arange(ROPE_AXIS_FREQS, dtype=jnp.float32) / ROPE_AXIS_FREQS)
    ang = jnp.stack([row[:, None] * inv, col[:, None] * inv], axis=1)
    return jnp.cos(ang).astype(dtype), jnp.sin(ang).astype(dtype)


def apply_axial_rope(x, cos, sin):
    B, S, H, _ = x.shape
    xr = x.reshape(B, S, H, 2, 2, ROPE_AXIS_FREQS)
    x1 = xr[..., 0, :]
    x2 = xr[..., 1, :]
    c = cos[None, :, None]
    s = sin[None, :, None]
    out = jnp.stack([x1 * c - x2 * s, x2 * c + x1 * s], axis=-2)
    return out.reshape(B, S, H, HEAD_DIM)


def blockwise_attention(q, k, v):
    B, S = q.shape[0], q.shape[1]
    nb = S // Q_BLOCK
    qb = q.reshape(B, nb, Q_BLOCK, N_KV_HEADS, Q_PER_KV, HEAD_DIM).transpose(1, 0, 2, 3, 4, 5)
    scale = HEAD_DIM ** -0.5

    def one_block(q_blk):
        s = jnp.einsum('bqkgd,bskd->bkgqs', q_blk, k, preferred_element_type=jnp.float32) * scale
        pr = jax.nn.softmax(s, axis=-1)
        return jnp.einsum('bkgqs,bskd->bqkgd', pr.astype(v.dtype), v)

    o = lax.map(one_block, qb)
    return o.transpose(1, 0, 2, 3, 4, 5).reshape(B, S, ATTN_WIDTH)


def spatial_gating(u, v, g_v, w_s, b_s):
    B, S, _ = u.shape
    n = S // CHUNK
    u = jax.nn.gelu(u, approximate=False)
    v = rms_norm(jax.nn.gelu(v, approximate=False), g_v)
    vc = v.reshape(B, n, CHUNK, GMLP_GROUPS, GMLP_GROUP_DIM)
    mixed = jnp.einsum('gpq,bnqgc->bnpgc', w_s, vc) + b_s.T[None, None, :, :, None]
    return u * mixed.reshape(B, S, GMLP_WIDTH)


def encoder_trunk(x, p, g_ffn1, w_ffn1_gu, w_ffn1_down, g_mix, w_in, g_q, g_k, g_gmlp_v, w_spatial, b_spatial,
                  w_branch_attn, w_branch_gmlp, w_out, g_ffn2, w_ffn2_gu, w_ffn2_down, g_ple, w_ple_gate, w_ple, g_final):
    B, S, _ = x.shape
    cos, sin = axial_rope_tables(S, x.dtype)
    for i in range(DEPTH):
        x = x + 0.5 * swiglu(rms_norm(x, g_ffn1[i]), w_ffn1_gu[i], w_ffn1_down[i])
        h = rms_norm(x, g_mix[i])
        q, k, v, gu, gv, gate_a, gate_b = jnp.split(h @ w_in[i], IN_OFFSETS, axis=-1)
        q = apply_axial_rope(rms_norm(q.reshape(B, S, N_HEADS, HEAD_DIM), g_q[i]), cos, sin)
        k = apply_axial_rope(rms_norm(k.reshape(B, S, N_KV_HEADS, HEAD_DIM), g_k[i]), cos, sin)
        v = v.reshape(B, S, N_KV_HEADS, HEAD_DIM)
        a = blockwise_attention(q, k, v)
        sg = spatial_gating(gu, gv, g_gmlp_v[i], w_spatial[i], b_spatial[i])
        merged = (jax.nn.sigmoid(gate_a) * (a @ w_branch_attn[i])
                  + jax.nn.sigmoid(gate_b) * (sg @ w_branch_gmlp[i]))
        x = x + merged @ w_out[i]
        x = x + 0.5 * swiglu(rms_norm(x, g_ffn2[i]), w_ffn2_gu[i], w_ffn2_down[i])
        x = x + jax.nn.sigmoid(rms_norm(x, g_ple[i]) @ w_ple_gate[i]) * (p[i] @ w_ple[i])
    return rms_norm(x, g_final)


def setup_inputs(seed: int = 0) -> dict:
    key = jax.random.key(seed)
    ks = jax.random.split(key, 32)
    f32 = jnp.float32

    def nrm(k, shape, fan_in):
        return jax.random.normal(k, shape, f32) * (fan_in ** -0.5)

    def gain(k, shape):
        return 1.0 + 0.02 * jax.random.normal(k, shape, f32)

    return {
        'x_prompt': jax.random.normal(ks[0], (BATCH, SEQ, D_MODEL), f32),
        'x_sample': jax.random.normal(ks[1], (DEC_BATCH, DEC_SEQ, D_MODEL), f32),
        'p_prompt': jax.random.normal(ks[2], (DEPTH, BATCH, SEQ, PLE_DIM), f32),
        'p_sample': jax.random.normal(ks[3], (DEPTH, DEC_BATCH, DEC_SEQ, PLE_DIM), f32),
        'g_ffn1': gain(ks[4], (DEPTH, D_MODEL)),
        'w_ffn1_gu': nrm(ks[5], (DEPTH, D_MODEL, 2 * D_FF), D_MODEL),
        'w_ffn1_down': nrm(ks[6], (DEPTH, D_FF, D_MODEL), D_FF),
        'g_mix': gain(ks[7], (DEPTH, D_MODEL)),
        'w_in': nrm(ks[8], (DEPTH, D_MODEL, IN_WIDTH), D_MODEL),
        'g_q': gain(ks[9], (DEPTH, HEAD_DIM)),
        'g_k': gain(ks[10], (DEPTH, HEAD_DIM)),
        'g_gmlp_v': gain(ks[11], (DEPTH, GMLP_WIDTH)),
        'w_spatial': nrm(ks[12], (DEPTH, GMLP_GROUPS, CHUNK, CHUNK), CHUNK),
        'b_spatial': 1.0 + 0.02 * jax.random.normal(ks[13], (DEPTH, GMLP_GROUPS, CHUNK), f32),
        'w_branch_attn': nrm(ks[14], (DEPTH, ATTN_WIDTH, D_MODEL), ATTN_WIDTH),
        'w_branch_gmlp': nrm(ks[15], (DEPTH, GMLP_WIDTH, D_MODEL), GMLP_WIDTH),
        'w_out': nrm(ks[16], (DEPTH, D_MODEL, D_MODEL), D_MODEL),
        'g_ffn2': gain(ks[17], (DEPTH, D_MODEL)),
        'w_ffn2_gu': nrm(ks[18], (DEPTH, D_MODEL, 2 * D_FF), D_MODEL),
        'w_ffn2_down': nrm(ks[19], (DEPTH, D_FF, D_MODEL), D_FF),
        'g_ple': gain(ks[20], (DEPTH, D_MODEL)),
        'w_ple_gate': nrm(ks[21], (DEPTH, D_MODEL, D_MODEL), D_MODEL),
        'w_ple': nrm(ks[22], (DEPTH, PLE_DIM, D_MODEL), PLE_DIM),
        'g_final': gain(ks[23], (D_MODEL,)),
    }


def reference(x_prompt, x_sample, p_prompt, p_sample, g_ffn1, w_ffn1_gu, w_ffn1_down, g_mix, w_in, g_q, g_k,
              g_gmlp_v, w_spatial, b_spatial, w_branch_attn, w_branch_gmlp, w_out, g_ffn2, w_ffn2_gu, w_ffn2_down,
              g_ple, w_ple_gate, w_ple, g_final):
    y_prompt = encoder_trunk(x_prompt, p_prompt, g_ffn1, w_ffn1_gu, w_ffn1_down, g_mix, w_in, g_q, g_k, g_gmlp_v,
                             w_spatial, b_spatial, w_branch_attn, w_branch_gmlp, w_out, g_ffn2, w_ffn2_gu,
                             w_ffn2_down, g_ple, w_ple_gate, w_ple, g_final)
    y_sample = encoder_trunk(x_sample, p_sample, g_ffn1, w_ffn1_gu, w_ffn1_down, g_mix, w_in, g_q, g_k, g_gmlp_v,
                             w_spatial, b_spatial, w_branch_attn, w_branch_gmlp, w_out, g_ffn2, w_ffn2_gu,
                             w_ffn2_down, g_ple, w_ple_gate, w_ple, g_final)
    return (y_prompt, y_sample)
```

```python
import math
P0 = 255
P1 = 99
P1SUB = 99
P1T = 0
from contextlib import ExitStack
import numpy as np
import concourse.bass as bass
import concourse.mybir as mybir
from concourse.bass_utils import run_bass_kernel_spmd

F32 = mybir.dt.float32
BF16 = mybir.dt.bfloat16
I32 = mybir.dt.int32
AF = mybir.ActivationFunctionType
ALU = mybir.AluOpType
AX = mybir.AxisListType

D = 1024
DFF = 2816
NJ = DFF // 128
PLE = 256
EPS = 1e-6
T = 512
NSLOT = 3
SLOTW = 4096
PE, ACT, DVE, POOL, SP = "pe", "act", "dve", "pool", "sp"
ENGS = [PE, ACT, DVE, POOL, SP]


class Op:
    __slots__ = ("eng", "fn", "deps", "dma_key", "sem", "val", "signal", "idx")

    def __init__(self, eng, fn, deps, dma_key):
        self.eng, self.fn, self.deps, self.dma_key = eng, fn, deps, dma_key
        self.sem = None
        self.val = 0
        self.signal = False


class Sched:
    def __init__(self):
        self.ops = []
        self.last_write = {}
        self.readers = {}
        self.phase = 0
        self.last_on = {}

    def op(self, eng, fn, reads=(), writes=(), dma_key=None, extra=()):
        deps = set(extra)
        for r in reads:
            w = self.last_write.get(r)
            if w is not None:
                deps.add(w)
        for w_ in writes:
            w = self.last_write.get(w_)
            if w is not None:
                deps.add(w)
            for rd in self.readers.get(w_, ()):
                deps.add(rd)
        o = Op(eng, fn, deps, dma_key)
        o.idx = len(self.ops)
        o.sem = (eng, self.phase) if dma_key is None else ("dma", dma_key)
        self.ops.append(o)
        for r in reads:
            self.readers.setdefault(r, []).append(o)
        for w_ in writes:
            self.last_write[w_] = o
            self.readers[w_] = []
        self.last_on[(eng, dma_key)] = o
        return o

    def barrier(self):
        lasts = [o for o in self.last_on.values() if o.fn is not None]
        for e in ENGS:
            self.op(e, None, extra=lasts)
        self.phase += 1

    def finalize(self):
        for o in self.ops:
            for d in o.deps:
                if d.eng == PE and o.eng == PE and d.dma_key is None and o.dma_key is None:
                    continue
                d.signal = True
        for o in self.ops:
            if o.dma_key is not None and o.fn is not None:
                o.signal = True
        counts = {}
        for o in self.ops:
            if o.signal:
                counts[o.sem] = counts.get(o.sem, 0) + (16 if o.dma_key is not None else 1)
                o.val = counts[o.sem]
        return sorted(counts.keys(), key=str)


def build_program(NCT, NOT, stage=3):
    NCH = NCT * 4
    NPAIR = NCH // 2
    nc = bass.Bass("TRN2", target_bir_lowering=False)

    def din(name, shape, dt=F32):
        return nc.dram_tensor(name, list(shape), dt, kind="ExternalInput").ap()

    xc = din("xc", [NCT * T, D])
    pin = din("pin", [NOT * T, PLE])
    pos = din("pos", [NCT * T, 2])
    kmask = din("kmask", [128, NCH])
    g_ffn1 = din("g_ffn1", [1, D]); w1gu = din("w_ffn1_gu", [1, D, 2 * DFF]); w1d = din("w_ffn1_down", [1, DFF, D])
    g_mix = din("g_mix", [1, D]); w_in = din("w_in", [1, D, 3840])
    g_q = din("g_q", [1, 64]); g_k = din("g_k", [1, 64]); g_gv = din("g_gmlp_v", [1, 512])
    w_sp = din("w_spatial", [1, 8, 128, 128]); b_sp = din("b_spatial", [1, 8, 128])
    w_ba = din("w_branch_attn", [1, 512, D]); w_bg = din("w_branch_gmlp", [1, 512, D]); w_out = din("w_out", [1, D, D])
    g_ffn2 = din("g_ffn2", [1, D]); w2gu = din("w_ffn2_gu", [1, D, 2 * DFF]); w2d = din("w_ffn2_down", [1, DFF, D])
    g_ple = din("g_ple", [1, D]); w_pg = din("w_ple_gate", [1, D, D]); w_pl = din("w_ple", [1, PLE, D])
    g_fin = din("g_final", [1, D])
    y = nc.dram_tensor("y", [NOT * T, D], F32, kind="ExternalOutput").ap()

    def dscr(name, shape, dt=BF16):
        return nc.dram_tensor(name, list(shape), dt).ap()

    s_gu = [dscr("s_gu1", [11, 128, 8, 512]), dscr("s_gu2", [11, 128, 8, 512])]
    s_dn = [dscr("s_dn1", [2, 128, NJ, 512]), dscr("s_dn2", [2, 128, NJ, 512])]
    s_kv = dscr("s_kv", [128, 8, 256])
    s_qgg = dscr("s_qgg", [3, 128, 8, 512])
    s_mg = dscr("s_mg", [8, 128, 3584])
    s_wo = dscr("s_wo", [2, 128, 8, 512])
    s_pg = dscr("s_pg", [2, 128, 8, 512])
    s_pl = dscr("s_pl", [128, 2, 1024])
    x1s = dscr("x1s", [NOT * T, D], F32)
    kts = dscr("kts", [128, NCT * T])
    vss = dscr("vss", [NCT * T, 130])

    S = Sched()

    def sb(name, shape, dt):
        return nc.alloc_sbuf_tensor(name, list(shape), dt)

    ident = sb("ident", [128, 128], BF16)
    gcol = sb("gcol", [128, 4, 8], F32)
    gq_bc = sb("gq_bc", [128, 64], F32)
    gk_bc = sb("gk_bc", [128, 64], F32)
    bspT = sb("bspT", [128, 8], F32)
    wsT = sb("wsT", [128, 8, 128], BF16)
    inv_bc = sb("inv_bc", [128, 16], F32)
    km = sb("km", [128, NCH], F32)
    negpi = sb("negpi", [128, 1], F32)
    epsb = sb("epsb", [128, 1], F32)
    ring = sb("ring", [128, NSLOT, SLOTW], BF16)
    Xb = [sb("X0", [128, 4, D], F32), sb("X1", [128, 4, D], F32)]
    hb = sb("hb", [128, 4, D], BF16)
    hT = sb("hT", [128, 8, T], BF16)
    ss = sb("ss", [128, 8], F32)
    rstd = sb("rstd", [128, 8], F32)
    tmpA = sb("tmpA", [128, T], F32)[:, :]
    tmpB = sb("tmpB", [128, T], F32)[:, :]
    posb = sb("posb", [128, 4, 2], F32)
    ang = sb("ang", [128, 4, 2, 16], F32)
    angm = sb("angm", [128, 4, 2, 16], F32)
    cs = sb("cs", [128, 4, 32], F32)
    sn = sb("sn", [128, 4, 32], F32)
    angki = sb("angki", [128, 4, 32], I32)
    angkf = sb("angkf", [128, 4, 32], F32)
    angr = sb("angr", [128, 4, 32], F32)
    hss = sb("hss", [128, 32], F32)
    hrs = sb("hrs", [128, 32], F32)
    ARENA_BYTES = 124 * 1024
    arena = sb("arena", [128, ARENA_BYTES // 4], F32)
    apos = [0]

    def carve(shape, dt):
        esz = 4 if dt in (F32, I32) else 2
        n = int(np.prod(shape[1:]))
        nbytes = (n * esz + 31) // 32 * 32
        off = apos[0]
        apos[0] += nbytes
        assert apos[0] <= ARENA_BYTES, (apos[0], ARENA_BYTES)
        v = arena[:, off // 4:(off + nbytes) // 4]
        if esz == 2:
            v = v.bitcast(BF16)
        v = v[:, 0:n]
        if len(shape) == 3:
            v = v.rearrange("p (a b) -> p a b", a=shape[1])
        elif len(shape) == 4:
            v = v.rearrange("p (a b c) -> p a b c", a=shape[1], b=shape[2])
        return v

    tp = nc.alloc_psum_tensor("tp", [128, 2048], BF16)
    S0 = nc.alloc_psum_tensor("S0", [128, 1024], F32)
    S1 = nc.alloc_psum_tensor("S1", [128, 1024], F32)
    O0 = nc.alloc_psum_tensor("O0", [128, 512], F32)
    O1 = nc.alloc_psum_tensor("O1", [128, 512], F32)
    G = [S0[:, 0:512], S0[:, 512:1024], S1[:, 0:512], S1[:, 512:1024], O0[:, :], O1[:, :]]
    GN = ["G0", "G1", "G2", "G3", "G4", "G5"]
    tpf = tp[:, :].bitcast(F32)

    def vec(e):
        return nc.vector if e == DVE else nc.gpsimd

    def cast(key, out_ap, in_ap):
        if not (P0 & 8):
            return None
        return S.op(POOL, lambda o=out_ap, i=in_ap: nc.gpsimd.dma_start(out=o, in_=i),
                    writes=[("scr", key)], dma_key="c_" + key)

    def small_load(out_ap, in_ap, res):
        def f():
            with nc.allow_non_contiguous_dma(reason="tiny constant layout load"):
                return nc.sync.dma_start(out=out_ap, in_=in_ap)
        return S.op(SP, f, writes=[res], dma_key="const")

    def bc_row(ap2d):
        return ap2d.partition_broadcast(128).rearrange("p o d -> p (o d)")

    for i, g in enumerate([g_ffn1, g_mix, g_ffn2, g_ple] if P0 & 1 else []):
        small_load(gcol[:, i, :], g.rearrange("o (kc p) -> p (o kc)", p=128), ("gcol", i))
    if P0 & 2:
        small_load(gq_bc[:, :], bc_row(g_q), "gq")
        small_load(gk_bc[:, :], bc_row(g_k), "gk")
    if P0 & 4:
        small_load(bspT[:, :], b_sp.rearrange("o g p -> p (o g)"), "bsp")
    small_load(km[:, :], kmask[:, :], "km")

    def mk_ident():
        nc.gpsimd.memset(ident[:], 0.0)
        return nc.gpsimd.affine_select(out=ident[:], in_=ident[:], pattern=[[-1, 128]], compare_op=ALU.not_equal,
                                       fill=1.0, base=0, channel_multiplier=1)
    S.op(POOL, mk_ident, writes=["ident"])
    S.op(POOL, lambda: nc.gpsimd.memset(negpi[:], -math.pi), writes=["negpi"])
    S.op(POOL, lambda: nc.gpsimd.memset(epsb[:], EPS), writes=["epsb"])
    S.op(POOL, lambda: nc.gpsimd.iota(out=cs[:, 0, 0:16].bitcast(I32), pattern=[[1, 16]], base=0, channel_multiplier=0),
         writes=["cs"])
    S.op(POOL, lambda: nc.gpsimd.tensor_copy(out=sn[:, 0, 0:16], in_=cs[:, 0, 0:16].bitcast(I32)), reads=["cs"], writes=["sn"])
    S.op(ACT, lambda: nc.scalar.activation(out=inv_bc[:, :], in_=sn[:, 0, 0:16], func=AF.Exp, scale=-math.log(10000.0) / 16.0),
         reads=["sn"], writes=["inv"])

    S.op(SP, lambda: nc.sync.dma_start(out=Xb[0][:, 0, :].rearrange("p (g q) -> p g q", g=8),
                                       in_=w_sp[0].rearrange("g p q -> p g q")), writes=["X0"], dma_key="const")
    S.op(DVE, lambda: nc.vector.tensor_copy(out=hb[:, 0, :], in_=Xb[0][:, 0, :]), reads=["X0"], writes=["hb"])
    for g in range(8):
        S.op(PE, lambda g=g: nc.tensor.transpose(out=tp[:, g * 128:(g + 1) * 128], in_=hb[:, 0, g * 128:(g + 1) * 128], identity=ident[:]),
             reads=["hb", "ident"], writes=[("tpb", 0)])
    S.op(DVE, lambda: nc.vector.tensor_copy(out=wsT[:, :, :].rearrange("q g p -> q (g p)"), in_=tp[:, 0:1024]),
         reads=[("tpb", 0)], writes=["wsT"])

    def kcp(ap2d):
        return ap2d.rearrange("(kc p) c -> p kc c", p=128)

    def cast_ffn(idx, wgu, wd):
        for i in range(11):
            cast("gu%d_%d" % (idx, i), s_gu[idx][i, :, :, 0:256], kcp(wgu[0, :, 256 * i:256 * i + 256]))
            cast("gu%d_%d" % (idx, i), s_gu[idx][i, :, :, 256:512], kcp(wgu[0, :, DFF + 256 * i:DFF + 256 * i + 256]))
        for h in range(2):
            cast("dn%d_%d" % (idx, h), s_dn[idx][h], kcp(wd[0, :, 512 * h:512 * h + 512]))

    cast_ffn(0, w1gu, w1d)
    cast("kv", s_kv, kcp(w_in[0, :, 512:768]))
    for g in range(2):
        for c in range(4):
            cast("qgg0", s_qgg[0, :, :, c * 128 + g * 64:c * 128 + g * 64 + 64], kcp(w_in[0, :, (g * 4 + c) * 64:(g * 4 + c) * 64 + 64]))
    for i, c0 in ((1, 768), (2, 1280)):
        cast("qgg%d" % i, s_qgg[i], kcp(w_in[0, :, c0:c0 + 512]))
    for oc in range(8):
        k = "mg%d" % oc
        gts = s_mg[oc, :, 0:2048].rearrange("p (kc c) -> p kc c", kc=8)
        cast(k, gts[:, :, 0:128], kcp(w_in[0, :, 1792 + oc * 128:1792 + oc * 128 + 128]))
        cast(k, gts[:, :, 128:256], kcp(w_in[0, :, 2816 + oc * 128:2816 + oc * 128 + 128]))
        cast(k, s_mg[oc, :, 2048:2560].rearrange("p (kc c) -> p kc c", kc=4), kcp(w_bg[0, :, oc * 128:oc * 128 + 128]))
        cast(k, s_mg[oc, 0:64, 2560:3584].rearrange("p (h c) -> p h c", h=8),
             w_ba[0, :, oc * 128:oc * 128 + 128].rearrange("(h p) c -> p h c", p=64))
    for h in range(2):
        cast("wo%d" % h, s_wo[h], kcp(w_out[0, :, 512 * h:512 * h + 512]))
    cast_ffn(1, w2gu, w2d)
    for h in range(2):
        cast("pg%d" % h, s_pg[h], kcp(w_pg[0, :, 512 * h:512 * h + 512]))
    cast("pl", s_pl, kcp(w_pl[0, :, :]))

    ring_n = [0]

    def ring_load(key, src_ap, width):
        slot = ring_n[0] % NSLOT
        ring_n[0] += 1
        S.op(SP, lambda s=slot, a=src_ap, w=width: nc.sync.dma_start(out=ring[:, s, 0:w], in_=a),
             reads=[("scr", key)], writes=[("ring", slot)], dma_key="ring%d" % slot)
        return slot

    def transpose_T(src, srcres, nkc, outT, outres, gi=None):
        for k0 in range(0, nkc, 2):
            bank = (k0 // 2) % 2
            for kk in range(2):
                kc = k0 + kk
                for s in range(4):
                    col = bank * 1024 + kk * 512 + s * 128
                    S.op(PE, lambda kc=kc, s=s, col=col: nc.tensor.transpose(out=tp[:, col:col + 128],
                                                                            in_=src[:, s, kc * 128:(kc + 1) * 128], identity=ident[:]),
                         reads=[srcres, "ident"], writes=[("tpb", bank)])
            for kk in range(2):
                kc = k0 + kk
                e = ACT if bank == 0 else DVE
                src_ps = tp[:, bank * 1024 + kk * 512: bank * 1024 + (kk + 1) * 512]
                if gi is None:
                    if e == ACT:
                        f = lambda kc=kc, src_ps=src_ps: nc.scalar.copy(out=outT[:, kc, :], in_=src_ps)
                    else:
                        f = lambda kc=kc, src_ps=src_ps: nc.vector.tensor_copy(out=outT[:, kc, :], in_=src_ps)
                    rd = [("tpb", bank)]
                else:
                    if e == ACT:
                        f = lambda kc=kc, src_ps=src_ps: nc.scalar.activation(out=outT[:, kc, :], in_=src_ps, func=AF.Copy,
                                                                               scale=gcol[:, gi, kc:kc + 1])
                    else:
                        f = lambda kc=kc, src_ps=src_ps: nc.vector.tensor_scalar(out=outT[:, kc, :], in0=src_ps,
                                                                                  scalar1=gcol[:, gi, kc:kc + 1], scalar2=None, op0=ALU.mult)
                    rd = [("tpb", bank), ("gcol", gi)]
                S.op(e, f, reads=rd, writes=[(outres, kc)])

    def row_rstd(X, xres, width, nrm):
        for s in range(4):
            S.op(ACT, lambda s=s: nc.scalar.activation(out=hb[:, s, 0:width], in_=X[:, s, 0:width], func=AF.Square, accum_out=ss[:, s:s + 1]),
                 reads=[xres], writes=["hb", ("ss", s)])
        S.op(ACT, lambda: nc.scalar.activation(out=rstd[:, 0:4], in_=ss[:, 0:4], func=AF.Sqrt, scale=1.0 / nrm, bias=epsb[:, 0:1]),
             reads=[("ss", s) for s in range(4)] + ["epsb"], writes=["rstd"])
        S.op(DVE, lambda: nc.vector.reciprocal(out=rstd[:, 0:4], in_=rstd[:, 0:4]), reads=["rstd"], writes=["rstd"])

    def rmsnorm_T(X, xres, gi):
        row_rstd(X, xres, D, D)
        if P1SUB < 2:
            return
        for s in range(4):
            S.op(DVE, lambda s=s: nc.vector.tensor_scalar(out=hb[:, s, :], in0=X[:, s, :], scalar1=rstd[:, s:s + 1], scalar2=None, op0=ALU.mult),
                 reads=[xres, "rstd"], writes=["hb"])
        if P1SUB < 3:
            return
        transpose_T(hb, "hb", 8, hT, "hT", gi)

    def ffn(idx, X, xres, hid, dn):
        for i in range(11):
            slot = ring_load("gu%d_%d" % (idx, i), s_gu[idx][i].rearrange("p kc c -> p (kc c)"), 4096)
            rv = ring[:, slot, :].rearrange("p (kc c) -> p kc c", kc=8)
            for jj in range(2):
                j = 2 * i + jj
                ga, gb = (0, 1) if j % 2 == 0 else (2, 3)
                for kc in range(8):
                    S.op(PE, lambda kc=kc, jj=jj, ga=ga, rv=rv: nc.tensor.matmul(G[ga], lhsT=rv[:, kc, jj * 128:(jj + 1) * 128], rhs=hT[:, kc, :],
                                                                                  start=(kc == 0), stop=(kc == 7)),
                         reads=[("ring", slot), ("hT", kc)], writes=[GN[ga]])
                for kc in range(8):
                    S.op(PE, lambda kc=kc, jj=jj, gb=gb, rv=rv: nc.tensor.matmul(G[gb], lhsT=rv[:, kc, 256 + jj * 128:256 + (jj + 1) * 128], rhs=hT[:, kc, :],
                                                                                  start=(kc == 0), stop=(kc == 7)),
                         reads=[("ring", slot), ("hT", kc)], writes=[GN[gb]])
                tt, tn = (tmpA, "tmpA") if j % 2 == 0 else (tmpB, "tmpB")
                S.op(ACT, lambda ga=ga, tt=tt: nc.scalar.activation(out=tt[:, :], in_=G[ga], func=AF.Silu), reads=[GN[ga]], writes=[tn])
                S.op(DVE, lambda j=j, gb=gb, tt=tt: nc.vector.tensor_tensor(out=hid[:, j, :], in0=tt[:, :], in1=G[gb], op=ALU.mult),
                     reads=[tn, GN[gb]], writes=[("hid", j)])
        for h in range(2):
            S.op(SP, lambda h=h: nc.sync.dma_start(out=dn[h], in_=s_dn[idx][h]),
                 reads=[("scr", "dn%d_%d" % (idx, h))], writes=[("dn", h)], dma_key="dn%d" % h)
            for s in range(4):
                b = 4 + (s % 2)
                for j in range(NJ):
                    S.op(PE, lambda j=j, s=s, h=h, b=b: nc.tensor.matmul(G[b], lhsT=hid[:, j, s * 128:(s + 1) * 128], rhs=dn[h][:, j, :],
                                                                         start=(j == 0), stop=(j == NJ - 1)),
                         reads=[("hid", j), ("dn", h)], writes=[GN[b]])
                S.op(DVE, lambda s=s, h=h, b=b: nc.vector.scalar_tensor_tensor(out=X[:, s, h * 512:(h + 1) * 512], in0=G[b], scalar=0.5,
                                                                               in1=X[:, s, h * 512:(h + 1) * 512], op0=ALU.mult, op1=ALU.add),
                     reads=[GN[b], xres], writes=[xres])

    def rope_tables(t, nh, tabc, tabs, tabres):
        S.op(SP, lambda: nc.sync.dma_start(out=posb[:, :, :], in_=pos[t * T:(t + 1) * T, :].rearrange("(s p) a -> p s a", p=128)),
             writes=["posb"], dma_key="posb")
        for a in range(2):
            S.op(POOL, lambda a=a: nc.gpsimd.tensor_tensor(out=ang[:, :, a, :], in0=posb[:, :, a:a + 1].to_broadcast([128, 4, 16]),
                                                           in1=inv_bc[:, :].unsqueeze(1).to_broadcast([128, 4, 16]), op=ALU.mult),
                 reads=["posb", "inv"], writes=["ang"])
        angf = ang[:, :, :, :].rearrange("p s a f -> p s (a f)")
        angmf = angm[:, :, :, :].rearrange("p s a f -> p s (a f)")
        TWO_PI = 2.0 * math.pi

        def sin_of(dst, shift):
            S.op(DVE, lambda: nc.vector.tensor_scalar(out=angmf, in0=angf, scalar1=shift, scalar2=1.0 / TWO_PI, op0=ALU.add, op1=ALU.mult),
                 reads=["ang"], writes=["angm"])
            S.op(DVE, lambda: nc.vector.tensor_copy(out=angki[:, :, :], in_=angmf), reads=["angm"], writes=["angki"])
            S.op(DVE, lambda: nc.vector.tensor_copy(out=angkf[:, :, :], in_=angki[:, :, :]), reads=["angki"], writes=["angkf"])
            S.op(DVE, lambda: nc.vector.tensor_scalar(out=angmf, in0=angf, scalar1=shift, scalar2=None, op0=ALU.add),
                 reads=["ang", "angki"], writes=["angm"])
            S.op(DVE, lambda: nc.vector.scalar_tensor_tensor(out=angr[:, :, :], in0=angkf[:, :, :], scalar=-TWO_PI, in1=angmf, op0=ALU.mult, op1=ALU.add),
                 reads=["angkf", "angm"], writes=["angr"])
            S.op(DVE, lambda: nc.vector.tensor_scalar(out=angmf, in0=angr[:, :, :], scalar1=math.pi, scalar2=TWO_PI, op0=ALU.is_gt, op1=ALU.mult),
                 reads=["angr"], writes=["angm"])
            S.op(DVE, lambda: nc.vector.tensor_tensor(out=angr[:, :, :], in0=angr[:, :, :], in1=angmf, op=ALU.subtract),
                 reads=["angr", "angm"], writes=["angr"])
            S.op(ACT, lambda: nc.scalar.activation(out=dst[:, :, :], in_=angr[:, :, :], func=AF.Sin), reads=["angr"], writes=[("cs" if dst is cs else "sn")])

        sin_of(sn, 0.0)
        sin_of(cs, 0.5 * math.pi)
        S.op(POOL, lambda: nc.gpsimd.tensor_copy(out=tabc, in_=cs[:, :, :].unsqueeze(2).to_broadcast([128, 4, nh, 32])),
             reads=["cs"], writes=[tabres + "c"])
        S.op(POOL, lambda: nc.gpsimd.tensor_copy(out=tabs, in_=sn[:, :, :].unsqueeze(2).to_broadcast([128, 4, nh, 32])),
             reads=["sn"], writes=[tabres + "s"])

    def head_norm_rope(e, src, srcres, nh, gbc, gres, tabc, tabs, tabres, sq, sqres, dst, dstres):
        V = vec(e)
        SH = 4 * nh
        x3 = src.rearrange("p s (h d) -> p (s h) d", h=nh)
        sq3 = sq.rearrange("p s (h d) -> p (s h) d", h=nh)
        S.op(e, lambda: V.tensor_tensor(out=sq3, in0=x3, in1=x3, op=ALU.mult), reads=[srcres], writes=[sqres])
        S.op(DVE, lambda: nc.vector.tensor_reduce(out=hss[:, 0:SH], in_=sq3, axis=AX.X, op=ALU.add), reads=[sqres], writes=["hss"])
        S.op(ACT, lambda: nc.scalar.activation(out=hrs[:, 0:SH], in_=hss[:, 0:SH], func=AF.Sqrt, scale=1.0 / 64, bias=epsb[:, 0:1]),
             reads=["hss", "epsb"], writes=["hrs"])
        S.op(DVE, lambda: nc.vector.reciprocal(out=hrs[:, 0:SH], in_=hrs[:, 0:SH]), reads=["hrs"], writes=["hrs"])
        S.op(e, lambda: V.tensor_tensor(out=x3, in0=x3, in1=hrs[:, 0:SH].unsqueeze(2).to_broadcast([128, SH, 64]), op=ALU.mult),
             reads=[srcres, "hrs"], writes=[srcres])
        S.op(e, lambda: V.tensor_tensor(out=x3, in0=x3, in1=gbc[:, :].unsqueeze(1).to_broadcast([128, SH, 64]), op=ALU.mult),
             reads=[srcres, gres], writes=[srcres])
        pat = "p s (h a r f) -> p (s h) a r f"
        x5 = src.rearrange(pat, h=nh, a=2, r=2)
        q5 = sq.rearrange(pat, h=nh, a=2, r=2)
        d5 = dst.rearrange(pat, h=nh, a=2, r=2)
        xa, xb_ = x5[:, :, :, 0, :], x5[:, :, :, 1, :]
        ta, tb_ = q5[:, :, :, 0, :], q5[:, :, :, 1, :]
        c4 = tabc.rearrange("p s h (a f) -> p (s h) a f", a=2)
        s4 = tabs.rearrange("p s h (a f) -> p (s h) a f", a=2)
        oa, ob = d5[:, :, :, 0, :], d5[:, :, :, 1, :]
        S.op(e, lambda: V.tensor_tensor(out=ta, in0=xb_, in1=s4, op=ALU.mult), reads=[srcres, tabres + "s"], writes=[sqres])
        S.op(e, lambda: V.tensor_tensor(out=tb_, in0=xa, in1=s4, op=ALU.mult), reads=[srcres, tabres + "s"], writes=[sqres])
        S.op(e, lambda: V.tensor_tensor(out=xa, in0=xa, in1=c4, op=ALU.mult), reads=[srcres, sqres, tabres + "c"], writes=[srcres])
        S.op(e, lambda: V.tensor_tensor(out=xb_, in0=xb_, in1=c4, op=ALU.mult), reads=[srcres, sqres, tabres + "c"], writes=[srcres])
        S.op(e, lambda: V.tensor_tensor(out=oa, in0=xa, in1=ta, op=ALU.subtract), reads=[srcres, sqres], writes=[dstres])
        S.op(e, lambda: V.tensor_tensor(out=ob, in0=xb_, in1=tb_, op=ALU.add), reads=[srcres, sqres], writes=[dstres])

    def tok_rows(ap2d):
        return ap2d.rearrange("(s p) d -> p s d", p=128)

    dbg = {}

    apos[0] = 0
    hid = carve([128, NJ, T], BF16)
    dn = [carve([128, NJ, 512], BF16), carve([128, NJ, 512], BF16)]
    kvw = carve([128, 8, 256], BF16)
    kvs = carve([128, 2, 4, 128], F32)
    ksq = carve([128, 4, 128], F32)
    tkc = carve([128, 4, 2, 32], F32)
    tks = carve([128, 4, 2, 32], F32)
    krb = carve([128, 4, 128], BF16)
    kTb = [carve([128, T], BF16), carve([128, T], BF16)]
    vsb = [carve([128, 4, 130], BF16), carve([128, 4, 130], BF16)]

    S.op(SP, lambda: nc.sync.dma_start(out=kvw, in_=s_kv), reads=[("scr", "kv")], writes=["kvw"], dma_key="kvw")

    for t in range(NCT if stage >= 1 else 0):
        b = t % 2
        X, xres = Xb[b], "X%d" % b
        S.op(SP, lambda b=b, t=t: nc.sync.dma_start(out=Xb[b][:, :, :], in_=tok_rows(xc[t * T:(t + 1) * T, :])),
             writes=[xres], dma_key="xl%d" % b)
        if P1 >= 1:
            rmsnorm_T(X, xres, 0)
        if P1 >= 2:
            ffn(0, X, xres, hid, dn)
        if t < NOT:
            S.op(SP, lambda b=b, t=t: nc.sync.dma_start(out=tok_rows(x1s[t * T:(t + 1) * T, :]), in_=Xb[b][:, :, :]),
                 reads=[xres], writes=[("x1s", t)], dma_key="xs%d" % b)
        if P1 < 3:
            continue
        rmsnorm_T(X, xres, 1)
        if P1 < 4:
            continue
        rope_tables(t, 2, tkc, tks, "tk")
        if P1 < 5:
            continue
        for s in range(4):
            gb_ = s % 2
            for kc in range(8):
                S.op(PE, lambda kc=kc, s=s, gb_=gb_: nc.tensor.matmul(G[gb_][:, 0:256], lhsT=hT[:, kc, s * 128:(s + 1) * 128], rhs=kvw[:, kc, :],
                                                                       start=(kc == 0), stop=(kc == 7)),
                     reads=[("hT", kc), "kvw"], writes=[GN[gb_]])
            S.op(ACT, lambda s=s, gb_=gb_: nc.scalar.copy(out=kvs[:, :, s, :], in_=G[gb_][:, 0:256].rearrange("p (a d) -> p a d", a=2)), reads=[GN[gb_]], writes=["kvs"])
        if P1 < 6:
            continue
        vb, vres = vsb[b], "vsb%d" % b
        kmt = km[:, t * 4:(t + 1) * 4]
        vb4 = vb.rearrange("p s (h e) -> p s h e", h=2)
        S.op(POOL, lambda vb4=vb4, kmt=kmt: nc.gpsimd.tensor_tensor(
            out=vb4[:, :, :, 0:64], in0=kvs[:, 1, :, :].rearrange("p s (h d) -> p s h d", h=2),
            in1=kmt.unsqueeze(2).unsqueeze(3).to_broadcast([128, 4, 2, 64]), op=ALU.mult),
            reads=["kvs", "km"], writes=[vres])
        S.op(POOL, lambda vb4=vb4, kmt=kmt: nc.gpsimd.tensor_copy(out=vb4[:, :, :, 64], in_=kmt.unsqueeze(2).to_broadcast([128, 4, 2])),
             reads=["km"], writes=[vres])
        S.op(SP, lambda vb=vb, t=t: nc.sync.dma_start(out=vss[t * T:(t + 1) * T, :].rearrange("(s p) e -> p s e", p=128), in_=vb),
             reads=[vres], writes=[("vss", t)], dma_key="vst%d" % b)
        if P1 < 7:
            continue
        head_norm_rope(POOL, kvs[:, 0, :, :], "kvs", 2, gk_bc, "gk", tkc, tks, "tk", ksq, "ksq", krb, "krb")
        for s in range(4):
            S.op(PE, lambda s=s: nc.tensor.transpose(out=tp[:, s * 128:(s + 1) * 128], in_=krb[:, s, :], identity=ident[:]),
                 reads=["krb", "ident"], writes=[("tpb", 0)])
        kb_, kres = kTb[b], "kTb%d" % b
        S.op(DVE, lambda kb_=kb_: nc.vector.tensor_copy(out=kb_, in_=tp[:, 0:512]), reads=[("tpb", 0)], writes=[kres])
        S.op(SP, lambda kb_=kb_, t=t: nc.sync.dma_start(out=kts[:, t * T:(t + 1) * T], in_=kb_),
             reads=[kres], writes=[("kts", t)], dma_key="kst%d" % b)

    S.barrier()
    apos[0] = 0
    kT = carve([128, NCT * T], BF16)
    vA = carve([128, NCH, 130], BF16)
    qf = carve([128, 4, 512], F32)
    qsq = hb[:, :, :].rearrange("p s d -> p (s d)").bitcast(F32).rearrange("p (s d) -> p s d", s=4)
    tqc = carve([128, 4, 8, 32], F32)
    tqs = carve([128, 4, 8, 32], F32)
    qrb = carve([128, 4, 512], BF16)
    qT = carve([128, 4, T], BF16)
    ub = carve([128, 4, 512], BF16)
    vnb = tqc.rearrange("p s h f -> p (s h f)").bitcast(BF16)[:, 0:2048].rearrange("p (s d) -> p s d", s=4)
    sgb = tqs.rearrange("p s h f -> p (s h f)").bitcast(BF16)[:, 0:2048].rearrange("p (s d) -> p s d", s=4)
    sgT = carve([128, 4, T], BF16)
    aT = carve([128, 8, T], BF16)
    mT = carve([128, 8, T], BF16)
    PT = [carve([128, 1024], BF16), carve([128, 1024], BF16)]
    rden = carve([128, T], F32)
    ones1 = carve([128, 64], F32)
    onT = carve([128, T], F32)
    ggv_bc = carve([128, 512], F32)
    S.op(POOL, lambda: nc.gpsimd.memset(ones1, 1.0), writes=["ones1"])
    small_load(ggv_bc, bc_row(g_gv), "ggv")

    for c in range(0, NCT if stage >= 2 else 0, 8):
        n = min(8, NCT - c)
        S.op(SP, lambda c=c, n=n: nc.sync.dma_start(out=kT[:, c * T:(c + n) * T], in_=kts[:, c * T:(c + n) * T]),
             reads=[("kts", t) for t in range(c, c + n)], writes=["kT"], dma_key="kTl")
        S.op(SP, lambda c=c, n=n: nc.sync.dma_start(out=vA[:, c * 4:(c + n) * 4, :],
                                                    in_=vss[c * T:(c + n) * T, :].rearrange("(ch p) e -> p ch e", p=128)),
             reads=[("vss", t) for t in range(c, c + n)], writes=["vA"], dma_key="vAl")

    def attention():
        heads = [(c, g) for c in range(4) for g in range(2)]
        seq = [(hi, i) for hi in range(8) for i in range(NPAIR)]

        def qk(n):
            hi, i = seq[n]
            c, g = heads[hi]
            sb_ = n % 2
            Sx = (S0, S1)[sb_]
            for u in range(2):
                ch = 2 * i + u
                S.op(PE, lambda u=u, ch=ch, c=c, g=g, Sx=Sx: nc.tensor.matmul(
                    Sx[:, u * 512:(u + 1) * 512], lhsT=kT[g * 64:(g + 1) * 64, ch * 128:(ch + 1) * 128],
                    rhs=qT[g * 64:(g + 1) * 64, c, :], start=True, stop=True),
                    reads=["kT", ("qT", c)], writes=["S%d" % sb_])

        qk(0)
        for n in range(len(seq)):
            hi, i = seq[n]
            c, g = heads[hi]
            sb_ = n % 2
            Sx = (S0, S1)[sb_]
            ob = hi % 2
            Ox = (O0, O1)[ob]
            if n + 1 < len(seq):
                qk(n + 1)
            S.op(ACT, lambda Sx=Sx, sb_=sb_: nc.scalar.activation(out=PT[sb_], in_=Sx[:, :], func=AF.Exp, scale=0.125),
                 reads=["S%d" % sb_], writes=["PT%d" % sb_])
            for u in range(2):
                ch = 2 * i + u
                S.op(PE, lambda u=u, ch=ch, g=g, Ox=Ox, sb_=sb_, i=i: nc.tensor.matmul(
                    Ox[0:65, :], lhsT=vA[:, ch, g * 65:(g + 1) * 65], rhs=PT[sb_][:, u * 512:(u + 1) * 512],
                    start=(i == 0 and u == 0), stop=(i == NPAIR - 1 and u == 1)),
                    reads=["vA", "PT%d" % sb_], writes=["O%d" % ob])
            if i == NPAIR - 1:
                h_true = g * 4 + c
                S.op(DVE, lambda Ox=Ox: nc.vector.reciprocal(out=rden[64:65, :], in_=Ox[64:65, :]), reads=["O%d" % ob], writes=["rden"])
                S.op(PE, lambda: nc.tensor.matmul(tpf[0:64, 0:512], lhsT=ones1[64:65, 0:64], rhs=rden[64:65, :], start=True, stop=True),
                     reads=["rden", "ones1"], writes=[("tpb", 0)])
                S.op(ACT, lambda: nc.scalar.copy(out=onT[0:64, :], in_=tpf[0:64, 0:512]), reads=[("tpb", 0)], writes=["onT"])
                S.op(DVE, lambda Ox=Ox, h_true=h_true: nc.vector.tensor_tensor(out=aT[0:64, h_true, :], in0=Ox[0:64, :], in1=onT[0:64, :], op=ALU.mult),
                     reads=["O%d" % ob, "onT"], writes=[("aT", h_true)])

    for t in range(NOT if stage >= 2 else 0):
        b = t % 2
        X, xres = Xb[b], "X%d" % b
        S.op(SP, lambda b=b, t=t: nc.sync.dma_start(out=Xb[b][:, :, :], in_=tok_rows(x1s[t * T:(t + 1) * T, :])),
             reads=[("x1s", t)], writes=[xres], dma_key="xl%d" % b)
        rmsnorm_T(X, xres, 1)
        rope_tables(t, 8, tqc, tqs, "tq")
        for pi in range(3):
            slot = ring_load("qgg%d" % pi, s_qgg[pi].rearrange("p kc c -> p (kc c)"), 4096)
            rv = ring[:, slot, :].rearrange("p (kc c) -> p kc c", kc=8)
            for s in range(4):
                gb_ = s % 2
                for kc in range(8):
                    S.op(PE, lambda kc=kc, s=s, gb_=gb_, rv=rv: nc.tensor.matmul(G[gb_], lhsT=hT[:, kc, s * 128:(s + 1) * 128], rhs=rv[:, kc, :],
                                                                                  start=(kc == 0), stop=(kc == 7)),
                         reads=[("hT", kc), ("ring", slot)], writes=[GN[gb_]])
                if pi == 0:
                    S.op(ACT, lambda s=s, gb_=gb_: nc.scalar.copy(out=qf[:, s, :], in_=G[gb_]), reads=[GN[gb_]], writes=["qf"])
                elif pi == 1:
                    S.op(ACT, lambda s=s, gb_=gb_: nc.scalar.activation(out=ub[:, s, :], in_=G[gb_], func=AF.Gelu), reads=[GN[gb_]], writes=["ub"])
                else:
                    S.op(ACT, lambda s=s, gb_=gb_: nc.scalar.activation(out=qf[:, s, :], in_=G[gb_], func=AF.Gelu), reads=[GN[gb_]], writes=["qf"])
            if pi == 0:
                head_norm_rope(POOL, qf, "qf", 8, gq_bc, "gq", tqc, tqs, "tq", qsq, "hb", qrb, "qrb")
                transpose_T(qrb, "qrb", 4, qT, "qT")
            if pi == 2:
                row_rstd(qf, "qf", 512, 512)
                for s in range(4):
                    S.op(DVE, lambda s=s: nc.vector.scalar_tensor_tensor(out=vnb[:, s, :], in0=qf[:, s, :], scalar=rstd[:, s:s + 1], in1=ggv_bc,
                                                                         op0=ALU.mult, op1=ALU.mult),
                         reads=["qf", "rstd", "ggv"], writes=["tqc"])
                for s in range(4):
                    gb_ = 2 + (s % 2)
                    for g in range(8):
                        S.op(PE, lambda s=s, g=g, gb_=gb_: nc.tensor.matmul(G[gb_][:, g * 64:(g + 1) * 64], lhsT=wsT[:, g, :], rhs=vnb[:, s, g * 64:(g + 1) * 64],
                                                                             start=True, stop=True),
                             reads=["tqc", "wsT"], writes=[GN[gb_]])
                    tt, tn = (tmpA, "tmpA") if s % 2 == 0 else (tmpB, "tmpB")
                    S.op(DVE, lambda gb_=gb_, tt=tt: nc.vector.tensor_tensor(out=tt.rearrange("p (g c) -> p g c", g=8),
                                                                             in0=G[gb_].rearrange("p (g c) -> p g c", g=8),
                                                                             in1=bspT[:, :].unsqueeze(2).to_broadcast([128, 8, 64]), op=ALU.add),
                         reads=[GN[gb_], "bsp"], writes=[tn])
                    S.op(POOL, lambda s=s, tt=tt: nc.gpsimd.tensor_tensor(out=sgb[:, s, :], in0=tt, in1=ub[:, s, :], op=ALU.mult),
                         reads=[tn, "ub"], writes=["tqs"])
                transpose_T(sgb, "tqs", 4, sgT, "sgT")
        attention()
        for oc in range(8):
            slot = ring_load("mg%d" % oc, s_mg[oc], 3584)
            rg = ring[:, slot, 0:2048].rearrange("p (kc c) -> p kc c", kc=8)
            g0, g1 = (0, 1) if oc % 2 == 0 else (2, 3)
            for kc in range(8):
                S.op(PE, lambda kc=kc, rg=rg, g0=g0: nc.tensor.matmul(G[g0], lhsT=rg[:, kc, 0:128], rhs=hT[:, kc, :], start=(kc == 0), stop=(kc == 7)),
                     reads=[("ring", slot), ("hT", kc)], writes=[GN[g0]])
            for kc in range(8):
                S.op(PE, lambda kc=kc, rg=rg, g1=g1: nc.tensor.matmul(G[g1], lhsT=rg[:, kc, 128:256], rhs=hT[:, kc, :], start=(kc == 0), stop=(kc == 7)),
                     reads=[("ring", slot), ("hT", kc)], writes=[GN[g1]])
            for h in range(8):
                S.op(PE, lambda h=h, slot=slot: nc.tensor.matmul(G[4], lhsT=ring[0:64, slot, 2560 + h * 128:2560 + (h + 1) * 128], rhs=aT[0:64, h, :],
                                                                 start=(h == 0), stop=(h == 7)),
                     reads=[("ring", slot), ("aT", h)], writes=[GN[4]])
            for kc in range(4):
                S.op(PE, lambda kc=kc, slot=slot: nc.tensor.matmul(G[5], lhsT=ring[:, slot, 2048 + kc * 128:2048 + (kc + 1) * 128], rhs=sgT[:, kc, :],
                                                                   start=(kc == 0), stop=(kc == 3)),
                     reads=[("ring", slot), ("sgT", kc)], writes=[GN[5]])
            S.op(ACT, lambda g0=g0: nc.scalar.activation(out=tmpA, in_=G[g0], func=AF.Sigmoid), reads=[GN[g0]], writes=["tmpA"])
            S.op(ACT, lambda g1=g1: nc.scalar.activation(out=tmpB, in_=G[g1], func=AF.Sigmoid), reads=[GN[g1]], writes=["tmpB"])
            S.op(DVE, lambda: nc.vector.tensor_tensor(out=tmpA, in0=tmpA, in1=G[4], op=ALU.mult), reads=["tmpA", GN[4]], writes=["tmpA"])
            S.op(DVE, lambda: nc.vector.tensor_tensor(out=tmpB, in0=tmpB, in1=G[5], op=ALU.mult), reads=["tmpB", GN[5]], writes=["tmpB"])
            S.op(POOL, lambda oc=oc: nc.gpsimd.tensor_tensor(out=mT[:, oc, :], in0=tmpA, in1=tmpB, op=ALU.add),
                 reads=["tmpA", "tmpB"], writes=[("mT", oc)])
        for h in range(2):
            slot = ring_load("wo%d" % h, s_wo[h].rearrange("p kc c -> p (kc c)"), 4096)
            rv = ring[:, slot, :].rearrange("p (kc c) -> p kc c", kc=8)
            for s in range(4):
                gb_ = s % 2
                for kc in range(8):
                    S.op(PE, lambda kc=kc, s=s, gb_=gb_, rv=rv: nc.tensor.matmul(G[gb_], lhsT=mT[:, kc, s * 128:(s + 1) * 128], rhs=rv[:, kc, :],
                                                                                  start=(kc == 0), stop=(kc == 7)),
                         reads=[("mT", kc), ("ring", slot)], writes=[GN[gb_]])
                S.op(DVE, lambda s=s, h=h, gb_=gb_, X=X: nc.vector.tensor_tensor(out=X[:, s, h * 512:(h + 1) * 512], in0=G[gb_],
                                                                                  in1=X[:, s, h * 512:(h + 1) * 512], op=ALU.add),
                     reads=[GN[gb_], xres], writes=[xres])
        S.op(SP, lambda b=b, t=t: nc.sync.dma_start(out=tok_rows(x1s[t * T:(t + 1) * T, :]), in_=Xb[b][:, :, :]),
             reads=[xres], writes=[("x1s", t)], dma_key="xs%d" % b)

    S.barrier()
    apos[0] = 0
    hid = carve([128, NJ, T], BF16)
    dn = [carve([128, NJ, 512], BF16), carve([128, NJ, 512], BF16)]
    pf = carve([128, 4, PLE], F32)
    pbf = carve([128, 4, PLE], BF16)
    pT = carve([128, 2, T], BF16)
    gfin_bc = carve([128, D], F32)
    small_load(gfin_bc, bc_row(g_fin), "gfin")
    out_ops = []
    for t in range(NOT if stage >= 3 else 0):
        b = t % 2
        X, xres = Xb[b], "X%d" % b
        S.op(SP, lambda b=b, t=t: nc.sync.dma_start(out=Xb[b][:, :, :], in_=tok_rows(x1s[t * T:(t + 1) * T, :])),
             reads=[("x1s", t)], writes=[xres], dma_key="xl%d" % b)
        rmsnorm_T(X, xres, 2)
        ffn(1, X, xres, hid, dn)
        rmsnorm_T(X, xres, 3)
        S.op(SP, lambda t=t: nc.sync.dma_start(out=pf, in_=tok_rows(pin[t * T:(t + 1) * T, :])), writes=["pf"], dma_key="pfl")
        S.op(POOL, lambda: nc.gpsimd.tensor_copy(out=pbf, in_=pf), reads=["pf"], writes=["pbf"])
        transpose_T(pbf, "pbf", 2, pT, "pT")
        slotp = ring_load("pl", s_pl.rearrange("p kc c -> p (kc c)"), 2048)
        rvp = ring[:, slotp, 0:2048].rearrange("p (kc c) -> p kc c", kc=2)
        for h in range(2):
            slot = ring_load("pg%d" % h, s_pg[h].rearrange("p kc c -> p (kc c)"), 4096)
            rv = ring[:, slot, :].rearrange("p (kc c) -> p kc c", kc=8)
            for s in range(4):
                g0, g1 = (0, 1) if s % 2 == 0 else (2, 3)
                for kc in range(8):
                    S.op(PE, lambda kc=kc, s=s, g0=g0, rv=rv: nc.tensor.matmul(G[g0], lhsT=hT[:, kc, s * 128:(s + 1) * 128], rhs=rv[:, kc, :],
                                                                                start=(kc == 0), stop=(kc == 7)),
                         reads=[("hT", kc), ("ring", slot)], writes=[GN[g0]])
                for k2 in range(2):
                    S.op(PE, lambda k2=k2, s=s, g1=g1, h=h, rvp=rvp: nc.tensor.matmul(G[g1], lhsT=pT[:, k2, s * 128:(s + 1) * 128],
                                                                                       rhs=rvp[:, k2, h * 512:(h + 1) * 512],
                                                                                       start=(k2 == 0), stop=(k2 == 1)),
                         reads=[("pT", k2), ("ring", slotp)], writes=[GN[g1]])
                tt, tn = (tmpA, "tmpA") if s % 2 == 0 else (tmpB, "tmpB")
                S.op(ACT, lambda g0=g0, tt=tt: nc.scalar.activation(out=tt, in_=G[g0], func=AF.Sigmoid), reads=[GN[g0]], writes=[tn])
                S.op(DVE, lambda g1=g1, tt=tt: nc.vector.tensor_tensor(out=tt, in0=tt, in1=G[g1], op=ALU.mult), reads=[tn, GN[g1]], writes=[tn])
                S.op(POOL, lambda s=s, h=h, tt=tt, X=X: nc.gpsimd.tensor_tensor(out=X[:, s, h * 512:(h + 1) * 512], in0=X[:, s, h * 512:(h + 1) * 512],
                                                                                 in1=tt, op=ALU.add),
                     reads=[tn, xres], writes=[xres])
        row_rstd(X, xres, D, D)
        for s in range(4):
            S.op(DVE, lambda s=s, X=X: nc.vector.scalar_tensor_tensor(out=X[:, s, :], in0=X[:, s, :], scalar=rstd[:, s:s + 1], in1=gfin_bc,
                                                                      op0=ALU.mult, op1=ALU.mult),
                 reads=[xres, "rstd", "gfin"], writes=[xres])
        out_ops.append(S.op(SP, lambda b=b, t=t: nc.sync.dma_start(out=tok_rows(y[t * T:(t + 1) * T, :]), in_=Xb[b][:, :, :]),
                            reads=[xres], dma_key="ys%d" % b))
    if stage < 3 and stage >= 1:
        out_ops.append(S.op(SP, lambda: nc.sync.dma_start(out=y[:, :], in_=x1s[:, :]), reads=[("x1s", t) for t in range(NOT)], dma_key="dbg"))
    S.op(SP, None, extra=out_ops)

    semkeys = S.finalize()
    by_eng = {e: [o for o in S.ops if o.eng == e] for e in ENGS}
    with ExitStack() as es:
        sems = {k: es.enter_context(nc.semaphore("s%d" % i)) for i, k in enumerate(semkeys)}
        block = es.enter_context(nc.Block())

        def emit(engname, eng):
            waited = {}
            for o in by_eng[engname]:
                need = {}
                for d in o.deps:
                    if not d.signal:
                        continue
                    if d.eng == PE and o.eng == PE and d.dma_key is None and o.dma_key is None:
                        continue
                    if need.get(d.sem, 0) < d.val:
                        need[d.sem] = d.val
                for k, v in need.items():
                    if waited.get(k, 0) < v:
                        eng.wait_ge(sems[k], v)
                        waited[k] = v
                if o.fn is not None:
                    ins = o.fn()
                    if o.signal:
                        ins.then_inc(sems[o.sem], 16 if o.dma_key is not None else 1)
                else:
                    assert not o.signal

        @block.tensor
        def _(e):
            emit(PE, e)

        @block.scalar
        def _(e):
            emit(ACT, e)

        @block.vector
        def _(e):
            emit(DVE, e)

        @block.gpsimd
        def _(e):
            emit(POOL, e)

        @block.sync
        def _(e):
            emit(SP, e)
    return nc


NCT_FULL, NOT_FULL = 32, 8
WNAMES = ["g_ffn1", "w_ffn1_gu", "w_ffn1_down", "g_mix", "w_in", "g_q", "g_k", "g_gmlp_v", "w_spatial", "b_spatial",
          "w_branch_attn", "w_branch_gmlp", "w_out", "g_ffn2", "w_ffn2_gu", "w_ffn2_down", "g_ple", "w_ple_gate", "w_ple", "g_final"]


def _pos_table(tok_idx):
    tok_idx = np.asarray(tok_idx, np.int64)
    return np.stack([tok_idx // 64, tok_idx % 64], axis=1).astype(np.float32)


def kernel(**inputs):
    xp = np.asarray(inputs["x_prompt"], np.float32)
    xs = np.asarray(inputs["x_sample"], np.float32)
    pp = np.asarray(inputs["p_prompt"], np.float32)
    ps = np.asarray(inputs["p_sample"], np.float32)
    w = {k: np.ascontiguousarray(np.asarray(inputs[k], np.float32)) for k in WNAMES}
    w["g_final"] = w["g_final"].reshape(1, D)
    NTOK = NCT_FULL * T
    own = NOT_FULL * T
    in_maps = []
    for c in range(8):
        if c < 4:
            order = [c] + [(c + k) % 4 for k in range(1, 4)]
            xcx = np.concatenate([xp[o] for o in order], axis=0)
            posi = np.concatenate([np.arange(own)] * 4)
            msk = np.zeros(NTOK, np.float32); msk[:own] = 1.0
            pc = pp[0, c]
        else:
            q = c - 4
            order = [q] + [(q + k) % 4 for k in range(1, 4)]
            xcx = np.concatenate([xs[0, o * own:(o + 1) * own] for o in order], axis=0)
            posi = np.concatenate([np.arange(o * own, (o + 1) * own) for o in order])
            msk = np.ones(NTOK, np.float32)
            pc = ps[0, 0, q * own:(q + 1) * own]
        m = {"xc": np.ascontiguousarray(xcx), "pin": np.ascontiguousarray(pc), "pos": _pos_table(posi),
             "kmask": np.ascontiguousarray(msk.reshape(NTOK // 128, 128).T)}
        m.update(w)
        in_maps.append(m)
    nc = build_program(NCT_FULL, NOT_FULL)
    res = run_bass_kernel_spmd(nc, in_maps, core_ids=list(range(8)))
    ys = [np.asarray(res.results[c]["y"], np.float32) for c in range(8)]
    y_prompt = np.stack(ys[0:4], axis=0)
    y_sample = np.concatenate(ys[4:8], axis=0)[None]
    return (y_prompt, y_sample)
```

```python
import math
P0 = 255
P1 = 99
P1SUB = 99
P1T = 0
from contextlib import ExitStack
import numpy as np
import concourse.bass as bass
import concourse.mybir as mybir
from concourse.bass_utils import run_bass_kernel_spmd

F32 = mybir.dt.float32
BF16 = mybir.dt.bfloat16
I32 = mybir.dt.int32
AF = mybir.ActivationFunctionType
ALU = mybir.AluOpType
AX = mybir.AxisListType

D = 1024
DFF = 2816
NJ = DFF // 128
PLE = 256
EPS = 1e-6
T = 512
NSLOT = 3
SLOTW = 4096
PE, ACT, DVE, POOL, SP = "pe", "act", "dve", "pool", "sp"
ENGS = [PE, ACT, DVE, POOL, SP]


class Op:
    __slots__ = ("eng", "fn", "deps", "dma_key", "sem", "val", "signal", "idx")

    def __init__(self, eng, fn, deps, dma_key):
        self.eng, self.fn, self.deps, self.dma_key = eng, fn, deps, dma_key
        self.sem = None
        self.val = 0
        self.signal = False


class Sched:
    def __init__(self):
        self.ops = []
        self.last_write = {}
        self.readers = {}
        self.phase = 0
        self.last_on = {}

    def op(self, eng, fn, reads=(), writes=(), dma_key=None, extra=()):
        deps = set(extra)
        for r in reads:
            w = self.last_write.get(r)
            if w is not None:
                deps.add(w)
        for w_ in writes:
            w = self.last_write.get(w_)
            if w is not None:
                deps.add(w)
            for rd in self.readers.get(w_, ()):
                deps.add(rd)
        o = Op(eng, fn, deps, dma_key)
        o.idx = len(self.ops)
        o.sem = (eng, self.phase) if dma_key is None else ("dma", dma_key)
        self.ops.append(o)
        for r in reads:
            self.readers.setdefault(r, []).append(o)
        for w_ in writes:
            self.last_write[w_] = o
            self.readers[w_] = []
        self.last_on[(eng, dma_key)] = o
        return o

    def barrier(self):
        lasts = [o for o in self.last_on.values() if o.fn is not None]
        for e in ENGS:
            self.op(e, None, extra=lasts)
        self.phase += 1

    def finalize(self):
        for o in self.ops:
            for d in o.deps:
                if d.eng == PE and o.eng == PE and d.dma_key is None and o.dma_key is None:
                    continue
                d.signal = True
        for o in self.ops:
            if o.dma_key is not None and o.fn is not None:
                o.signal = True
        counts = {}
        for o in self.ops:
            if o.signal:
                counts[o.sem] = counts.get(o.sem, 0) + (16 if o.dma_key is not None else 1)
                o.val = counts[o.sem]
        return sorted(counts.keys(), key=str)


def build_program(NCT, NOT, stage=3):
    NCH = NCT * 4
    NPAIR = NCH // 2
    nc = bass.Bass("TRN2", target_bir_lowering=False)

    def din(name, shape, dt=F32):
        return nc.dram_tensor(name, list(shape), dt, kind="ExternalInput").ap()

    xc = din("xc", [NCT * T, D])
    pin = din("pin", [NOT * T, PLE])
    pos = din("pos", [NCT * T, 2])
    kmask = din("kmask", [128, NCH])
    g_ffn1 = din("g_ffn1", [1, D]); w1gu = din("w_ffn1_gu", [1, D, 2 * DFF]); w1d = din("w_ffn1_down", [1, DFF, D])
    g_mix = din("g_mix", [1, D]); w_in = din("w_in", [1, D, 3840])
    g_q = din("g_q", [1, 64]); g_k = din("g_k", [1, 64]); g_gv = din("g_gmlp_v", [1, 512])
    w_sp = din("w_spatial", [1, 8, 128, 128]); b_sp = din("b_spatial", [1, 8, 128])
    w_ba = din("w_branch_attn", [1, 512, D]); w_bg = din("w_branch_gmlp", [1, 512, D]); w_out = din("w_out", [1, D, D])
    g_ffn2 = din("g_ffn2", [1, D]); w2gu = din("w_ffn2_gu", [1, D, 2 * DFF]); w2d = din("w_ffn2_down", [1, DFF, D])
    g_ple = din("g_ple", [1, D]); w_pg = din("w_ple_gate", [1, D, D]); w_pl = din("w_ple", [1, PLE, D])
    g_fin = din("g_final", [1, D])
    y = nc.dram_tensor("y", [NOT * T, D], F32, kind="ExternalOutput").ap()

    def dscr(name, shape, dt=BF16):
        return nc.dram_tensor(name, list(shape), dt).ap()

    s_gu = [dscr("s_gu1", [11, 128, 8, 512]), dscr("s_gu2", [11, 128, 8, 512])]
    s_dn = [dscr("s_dn1", [2, 128, NJ, 512]), dscr("s_dn2", [2, 128, NJ, 512])]
    s_kv = dscr("s_kv", [128, 8, 256])
    s_qgg = dscr("s_qgg", [3, 128, 8, 512])
    s_mg = dscr("s_mg", [8, 128, 3584])
    s_wo = dscr("s_wo", [2, 128, 8, 512])
    s_pg = dscr("s_pg", [2, 128, 8, 512])
    s_pl = dscr("s_pl", [128, 2, 1024])
    x1s = dscr("x1s", [NOT * T, D], F32)
    kts = dscr("kts", [128, NCT * T])
    vss = dscr("vss", [NCT * T, 130])

    S = Sched()

    def sb(name, shape, dt):
        return nc.alloc_sbuf_tensor(name, list(shape), dt)

    ident = sb("ident", [128, 128], BF16)
    gcol = sb("gcol", [128, 4, 8], F32)
    gq_bc = sb("gq_bc", [128, 64], F32)
    gk_bc = sb("gk_bc", [128, 64], F32)
    bspT = sb("bspT", [128, 8], F32)
    wsT = sb("wsT", [128, 8, 128], BF16)
    inv_bc = sb("inv_bc", [128, 16], F32)
    km = sb("km", [128, NCH], F32)
    negpi = sb("negpi", [128, 1], F32)
    epsb = sb("epsb", [128, 1], F32)
    ring = sb("ring", [128, NSLOT, SLOTW], BF16)
    Xb = [sb("X0", [128, 4, D], F32), sb("X1", [128, 4, D], F32)]
    hb = sb("hb", [128, 4, D], BF16)
    hT = sb("hT", [128, 8, T], BF16)
    ss = sb("ss", [128, 8], F32)
    rstd = sb("rstd", [128, 8], F32)
    tmpA = sb("tmpA", [128, T], F32)[:, :]
    tmpB = sb("tmpB", [128, T], F32)[:, :]
    posb = sb("posb", [128, 4, 2], F32)
    ang = sb("ang", [128, 4, 2, 16], F32)
    angm = sb("angm", [128, 4, 2, 16], F32)
    cs = sb("cs", [128, 4, 32], F32)
    sn = sb("sn", [128, 4, 32], F32)
    angki = sb("angki", [128, 4, 32], I32)
    angkf = sb("angkf", [128, 4, 32], F32)
    angr = sb("angr", [128, 4, 32], F32)
    hss = sb("hss", [128, 32], F32)
    hrs = sb("hrs", [128, 32], F32)
    ARENA_BYTES = 124 * 1024
    arena = sb("arena", [128, ARENA_BYTES // 4], F32)
    apos = [0]

    def carve(shape, dt):
        esz = 4 if dt in (F32, I32) else 2
        n = int(np.prod(shape[1:]))
        nbytes = (n * esz + 31) // 32 * 32
        off = apos[0]
        apos[0] += nbytes
        assert apos[0] <= ARENA_BYTES, (apos[0], ARENA_BYTES)
        v = arena[:, off // 4:(off + nbytes) // 4]
        if esz == 2:
            v = v.bitcast(BF16)
        v = v[:, 0:n]
        if len(shape) == 3:
            v = v.rearrange("p (a b) -> p a b", a=shape[1])
        elif len(shape) == 4:
            v = v.rearrange("p (a b c) -> p a b c", a=shape[1], b=shape[2])
        return v

    tp = nc.alloc_psum_tensor("tp", [128, 2048], BF16)
    S0 = nc.alloc_psum_tensor("S0", [128, 1024], F32)
    S1 = nc.alloc_psum_tensor("S1", [128, 1024], F32)
    O0 = nc.alloc_psum_tensor("O0", [128, 512], F32)
    O1 = nc.alloc_psum_tensor("O1", [128, 512], F32)
    G = [S0[:, 0:512], S0[:, 512:1024], S1[:, 0:512], S1[:, 512:1024], O0[:, :], O1[:, :]]
    GN = ["G0", "G1", "G2", "G3", "G4", "G5"]
    tpf = tp[:, :].bitcast(F32)

    def vec(e):
        return nc.vector if e == DVE else nc.gpsimd

    def cast(key, out_ap, in_ap):
        if not (P0 & 8):
            return None
        return S.op(POOL, lambda o=out_ap, i=in_ap: nc.gpsimd.dma_start(out=o, in_=i),
                    writes=[("scr", key)], dma_key="c_" + key)

    def small_load(out_ap, in_ap, res):
        def f():
            with nc.allow_non_contiguous_dma(reason="tiny constant layout load"):
                return nc.sync.dma_start(out=out_ap, in_=in_ap)
        return S.op(SP, f, writes=[res], dma_key="const")

    def bc_row(ap2d):
        return ap2d.partition_broadcast(128).rearrange("p o d -> p (o d)")

    for i, g in enumerate([g_ffn1, g_mix, g_ffn2, g_ple] if P0 & 1 else []):
        small_load(gcol[:, i, :], g.rearrange("o (kc p) -> p (o kc)", p=128), ("gcol", i))
    if P0 & 2:
        small_load(gq_bc[:, :], bc_row(g_q), "gq")
        small_load(gk_bc[:, :], bc_row(g_k), "gk")
    if P0 & 4:
        small_load(bspT[:, :], b_sp.rearrange("o g p -> p (o g)"), "bsp")
    small_load(km[:, :], kmask[:, :], "km")

    def mk_ident():
        nc.gpsimd.memset(ident[:], 0.0)
        return nc.gpsimd.affine_select(out=ident[:], in_=ident[:], pattern=[[-1, 128]], compare_op=ALU.not_equal,
                                       fill=1.0, base=0, channel_multiplier=1)
    S.op(POOL, mk_ident, writes=["ident"])
    S.op(POOL, lambda: nc.gpsimd.memset(negpi[:], -math.pi), writes=["negpi"])
    S.op(POOL, lambda: nc.gpsimd.memset(epsb[:], EPS), writes=["epsb"])
    S.op(POOL, lambda: nc.gpsimd.iota(out=cs[:, 0, 0:16].bitcast(I32), pattern=[[1, 16]], base=0, channel_multiplier=0),
         writes=["cs"])
    S.op(POOL, lambda: nc.gpsimd.tensor_copy(out=sn[:, 0, 0:16], in_=cs[:, 0, 0:16].bitcast(I32)), reads=["cs"], writes=["sn"])
    S.op(ACT, lambda: nc.scalar.activation(out=inv_bc[:, :], in_=sn[:, 0, 0:16], func=AF.Exp, scale=-math.log(10000.0) / 16.0),
         reads=["sn"], writes=["inv"])

    S.op(SP, lambda: nc.sync.dma_start(out=Xb[0][:, 0, :].rearrange("p (g q) -> p g q", g=8),
                                       in_=w_sp[0].rearrange("g p q -> p g q")), writes=["X0"], dma_key="const")
    S.op(DVE, lambda: nc.vector.tensor_copy(out=hb[:, 0, :], in_=Xb[0][:, 0, :]), reads=["X0"], writes=["hb"])
    for g in range(8):
        S.op(PE, lambda g=g: nc.tensor.transpose(out=tp[:, g * 128:(g + 1) * 128], in_=hb[:, 0, g * 128:(g + 1) * 128], identity=ident[:]),
             reads=["hb", "ident"], writes=[("tpb", 0)])
    S.op(DVE, lambda: nc.vector.tensor_copy(out=wsT[:, :, :].rearrange("q g p -> q (g p)"), in_=tp[:, 0:1024]),
         reads=[("tpb", 0)], writes=["wsT"])

    def kcp(ap2d):
        return ap2d.rearrange("(kc p) c -> p kc c", p=128)

    def cast_ffn(idx, wgu, wd):
        for i in range(11):
            cast("gu%d_%d" % (idx, i), s_gu[idx][i, :, :, 0:256], kcp(wgu[0, :, 256 * i:256 * i + 256]))
            cast("gu%d_%d" % (idx, i), s_gu[idx][i, :, :, 256:512], kcp(wgu[0, :, DFF + 256 * i:DFF + 256 * i + 256]))
        for h in range(2):
            cast("dn%d_%d" % (idx, h), s_dn[idx][h], kcp(wd[0, :, 512 * h:512 * h + 512]))

    cast_ffn(0, w1gu, w1d)
    cast("kv", s_kv, kcp(w_in[0, :, 512:768]))
    for g in range(2):
        for c in range(4):
            cast("qgg0", s_qgg[0, :, :, c * 128 + g * 64:c * 128 + g * 64 + 64], kcp(w_in[0, :, (g * 4 + c) * 64:(g * 4 + c) * 64 + 64]))
    for i, c0 in ((1, 768), (2, 1280)):
        cast("qgg%d" % i, s_qgg[i], kcp(w_in[0, :, c0:c0 + 512]))
    for oc in range(8):
        k = "mg%d" % oc
        gts = s_mg[oc, :, 0:2048].rearrange("p (kc c) -> p kc c", kc=8)
        cast(k, gts[:, :, 0:128], kcp(w_in[0, :, 1792 + oc * 128:1792 + oc * 128 + 128]))
        cast(k, gts[:, :, 128:256], kcp(w_in[0, :, 2816 + oc * 128:2816 + oc * 128 + 128]))
        cast(k, s_mg[oc, :, 2048:2560].rearrange("p (kc c) -> p kc c", kc=4), kcp(w_bg[0, :, oc * 128:oc * 128 + 128]))
        cast(k, s_mg[oc, 0:64, 2560:3584].rearrange("p (h c) -> p h c", h=8),
             w_ba[0, :, oc * 128:oc * 128 + 128].rearrange("(h p) c -> p h c", p=64))
    for h in range(2):
        cast("wo%d" % h, s_wo[h], kcp(w_out[0, :, 512 * h:512 * h + 512]))
    cast_ffn(1, w2gu, w2d)
    for h in range(2):
        cast("pg%d" % h, s_pg[h], kcp(w_pg[0, :, 512 * h:512 * h + 512]))
    cast("pl", s_pl, kcp(w_pl[0, :, :]))

    ring_n = [0]

    def ring_load(key, src_ap, width):
        slot = ring_n[0] % NSLOT
        ring_n[0] += 1
        S.op(SP, lambda s=slot, a=src_ap, w=width: nc.sync.dma_start(out=ring[:, s, 0:w], in_=a),
             reads=[("scr", key)], writes=[("ring", slot)], dma_key="ring%d" % slot)
        return slot

    def transpose_T(src, srcres, nkc, outT, outres, gi=None):
        for k0 in range(0, nkc, 2):
            bank = (k0 // 2) % 2
            for kk in range(2):
                kc = k0 + kk
                for s in range(4):
                    col = bank * 1024 + kk * 512 + s * 128
                    S.op(PE, lambda kc=kc, s=s, col=col: nc.tensor.transpose(out=tp[:, col:col + 128],
                                                                            in_=src[:, s, kc * 128:(kc + 1) * 128], identity=ident[:]),
                         reads=[srcres, "ident"], writes=[("tpb", bank)])
            for kk in range(2):
                kc = k0 + kk
                e = ACT if bank == 0 else DVE
                src_ps = tp[:, bank * 1024 + kk * 512: bank * 1024 + (kk + 1) * 512]
                if gi is None:
                    if e == ACT:
                        f = lambda kc=kc, src_ps=src_ps: nc.scalar.copy(out=outT[:, kc, :], in_=src_ps)
                    else:
                        f = lambda kc=kc, src_ps=src_ps: nc.vector.tensor_copy(out=outT[:, kc, :], in_=src_ps)
                    rd = [("tpb", bank)]
                else:
                    if e == ACT:
                        f = lambda kc=kc, src_ps=src_ps: nc.scalar.activation(out=outT[:, kc, :], in_=src_ps, func=AF.Copy,
                                                                               scale=gcol[:, gi, kc:kc + 1])
                    else:
                        f = lambda kc=kc, src_ps=src_ps: nc.vector.tensor_scalar(out=outT[:, kc, :], in0=src_ps,
                                                                                  scalar1=gcol[:, gi, kc:kc + 1], scalar2=None, op0=ALU.mult)
                    rd = [("tpb", bank), ("gcol", gi)]
                S.op(e, f, reads=rd, writes=[(outres, kc)])

    def row_rstd(X, xres, width, nrm):
        for s in range(4):
            S.op(ACT, lambda s=s: nc.scalar.activation(out=hb[:, s, 0:width], in_=X[:, s, 0:width], func=AF.Square, accum_out=ss[:, s:s + 1]),
                 reads=[xres], writes=["hb", ("ss", s)])
        S.op(ACT, lambda: nc.scalar.activation(out=rstd[:, 0:4], in_=ss[:, 0:4], func=AF.Sqrt, scale=1.0 / nrm, bias=epsb[:, 0:1]),
             reads=[("ss", s) for s in range(4)] + ["epsb"], writes=["rstd"])
        S.op(DVE, lambda: nc.vector.reciprocal(out=rstd[:, 0:4], in_=rstd[:, 0:4]), reads=["rstd"], writes=["rstd"])

    def rmsnorm_T(X, xres, gi):
        row_rstd(X, xres, D, D)
        if P1SUB < 2:
            return
        for s in range(4):
            S.op(DVE, lambda s=s: nc.vector.tensor_scalar(out=hb[:, s, :], in0=X[:, s, :], scalar1=rstd[:, s:s + 1], scalar2=None, op0=ALU.mult),
                 reads=[xres, "rstd"], writes=["hb"])
        if P1SUB < 3:
            return
        transpose_T(hb, "hb", 8, hT, "hT", gi)

    def ffn(idx, X, xres, hid, dn):
        for i in range(11):
            slot = ring_load("gu%d_%d" % (idx, i), s_gu[idx][i].rearrange("p kc c -> p (kc c)"), 4096)
            rv = ring[:, slot, :].rearrange("p (kc c) -> p kc c", kc=8)
            for jj in range(2):
                j = 2 * i + jj
                ga, gb = (0, 1) if j % 2 == 0 else (2, 3)
                for kc in range(8):
                    S.op(PE, lambda kc=kc, jj=jj, ga=ga, rv=rv: nc.tensor.matmul(G[ga], lhsT=rv[:, kc, jj * 128:(jj + 1) * 128], rhs=hT[:, kc, :],
                                                                                  start=(kc == 0), stop=(kc == 7)),
                         reads=[("ring", slot), ("hT", kc)], writes=[GN[ga]])
                for kc in range(8):
                    S.op(PE, lambda kc=kc, jj=jj, gb=gb, rv=rv: nc.tensor.matmul(G[gb], lhsT=rv[:, kc, 256 + jj * 128:256 + (jj + 1) * 128], rhs=hT[:, kc, :],
                                                                                  start=(kc == 0), stop=(kc == 7)),
                         reads=[("ring", slot), ("hT", kc)], writes=[GN[gb]])
                tt, tn = (tmpA, "tmpA") if j % 2 == 0 else (tmpB, "tmpB")
                S.op(ACT, lambda ga=ga, tt=tt: nc.scalar.activation(out=tt[:, :], in_=G[ga], func=AF.Silu), reads=[GN[ga]], writes=[tn])
                S.op(DVE, lambda j=j, gb=gb, tt=tt: nc.vector.tensor_tensor(out=hid[:, j, :], in0=tt[:, :], in1=G[gb], op=ALU.mult),
                     reads=[tn, GN[gb]], writes=[("hid", j)])
        for h in range(2):
            S.op(SP, lambda h=h: nc.sync.dma_start(out=dn[h], in_=s_dn[idx][h]),
                 reads=[("scr", "dn%d_%d" % (idx, h))], writes=[("dn", h)], dma_key="dn%d" % h)
            for s in range(4):
                b = 4 + (s % 2)
                for j in range(NJ):
                    S.op(PE, lambda j=j, s=s, h=h, b=b: nc.tensor.matmul(G[b], lhsT=hid[:, j, s * 128:(s + 1) * 128], rhs=dn[h][:, j, :],
                                                                         start=(j == 0), stop=(j == NJ - 1)),
                         reads=[("hid", j), ("dn", h)], writes=[GN[b]])
                S.op(DVE, lambda s=s, h=h, b=b: nc.vector.scalar_tensor_tensor(out=X[:, s, h * 512:(h + 1) * 512], in0=G[b], scalar=0.5,
                                                                               in1=X[:, s, h * 512:(h + 1) * 512], op0=ALU.mult, op1=ALU.add),
                     reads=[GN[b], xres], writes=[xres])

    def rope_tables(t, nh, tabc, tabs, tabres):
        S.op(SP, lambda: nc.sync.dma_start(out=posb[:, :, :], in_=pos[t * T:(t + 1) * T, :].rearrange("(s p) a -> p s a", p=128)),
             writes=["posb"], dma_key="posb")
        for a in range(2):
            S.op(POOL, lambda a=a: nc.gpsimd.tensor_tensor(out=ang[:, :, a, :], in0=posb[:, :, a:a + 1].to_broadcast([128, 4, 16]),
                                                           in1=inv_bc[:, :].unsqueeze(1).to_broadcast([128, 4, 16]), op=ALU.mult),
                 reads=["posb", "inv"], writes=["ang"])
        angf = ang[:, :, :, :].rearrange("p s a f -> p s (a f)")
        angmf = angm[:, :, :, :].rearrange("p s a f -> p s (a f)")
        TWO_PI = 2.0 * math.pi

        def sin_of(dst, shift):
            S.op(DVE, lambda: nc.vector.tensor_scalar(out=angmf, in0=angf, scalar1=shift, scalar2=1.0 / TWO_PI, op0=ALU.add, op1=ALU.mult),
                 reads=["ang"], writes=["angm"])
            S.op(DVE, lambda: nc.vector.tensor_copy(out=angki[:, :, :], in_=angmf), reads=["angm"], writes=["angki"])
            S.op(DVE, lambda: nc.vector.tensor_copy(out=angkf[:, :, :], in_=angki[:, :, :]), reads=["angki"], writes=["angkf"])
            S.op(DVE, lambda: nc.vector.tensor_scalar(out=angmf, in0=angf, scalar1=shift, scalar2=None, op0=ALU.add),
                 reads=["ang", "angki"], writes=["angm"])
            S.op(DVE, lambda: nc.vector.scalar_tensor_tensor(out=angr[:, :, :], in0=angkf[:, :, :], scalar=-TWO_PI, in1=angmf, op0=ALU.mult, op1=ALU.add),
                 reads=["angkf", "angm"], writes=["angr"])
            S.op(DVE, lambda: nc.vector.tensor_scalar(out=angmf, in0=angr[:, :, :], scalar1=math.pi, scalar2=TWO_PI, op0=ALU.is_gt, op1=ALU.mult),
                 reads=["angr"], writes=["angm"])
            S.op(DVE, lambda: nc.vector.tensor_tensor(out=angr[:, :, :], in0=angr[:, :, :], in1=angmf, op=ALU.subtract),
                 reads=["angr", "angm"], writes=["angr"])
            S.op(ACT, lambda: nc.scalar.activation(out=dst[:, :, :], in_=angr[:, :, :], func=AF.Sin), reads=["angr"], writes=[("cs" if dst is cs else "sn")])

        sin_of(sn, 0.0)
        sin_of(cs, 0.5 * math.pi)
        S.op(POOL, lambda: nc.gpsimd.tensor_copy(out=tabc, in_=cs[:, :, :].unsqueeze(2).to_broadcast([128, 4, nh, 32])),
             reads=["cs"], writes=[tabres + "c"])
        S.op(POOL, lambda: nc.gpsimd.tensor_copy(out=tabs, in_=sn[:, :, :].unsqueeze(2).to_broadcast([128, 4, nh, 32])),
             reads=["sn"], writes=[tabres + "s"])

    def head_norm_rope(e, src, srcres, nh, gbc, gres, tabc, tabs, tabres, sq, sqres, dst, dstres):
        V = vec(e)
        SH = 4 * nh
        x3 = src.rearrange("p s (h d) -> p (s h) d", h=nh)
        sq3 = sq.rearrange("p s (h d) -> p (s h) d", h=nh)
        S.op(e, lambda: V.tensor_tensor(out=sq3, in0=x3, in1=x3, op=ALU.mult), reads=[srcres], writes=[sqres])
        S.op(DVE, lambda: nc.vector.tensor_reduce(out=hss[:, 0:SH], in_=sq3, axis=AX.X, op=ALU.add), reads=[sqres], writes=["hss"])
        S.op(ACT, lambda: nc.scalar.activation(out=hrs[:, 0:SH], in_=hss[:, 0:SH], func=AF.Sqrt, scale=1.0 / 64, bias=epsb[:, 0:1]),
             reads=["hss", "epsb"], writes=["hrs"])
        S.op(DVE, lambda: nc.vector.reciprocal(out=hrs[:, 0:SH], in_=hrs[:, 0:SH]), reads=["hrs"], writes=["hrs"])
        S.op(e, lambda: V.tensor_tensor(out=x3, in0=x3, in1=hrs[:, 0:SH].unsqueeze(2).to_broadcast([128, SH, 64]), op=ALU.mult),
             reads=[srcres, "hrs"], writes=[srcres])
        S.op(e, lambda: V.tensor_tensor(out=x3, in0=x3, in1=gbc[:, :].unsqueeze(1).to_broadcast([128, SH, 64]), op=ALU.mult),
             reads=[srcres, gres], writes=[srcres])
        pat = "p s (h a r f) -> p (s h) a r f"
        x5 = src.rearrange(pat, h=nh, a=2, r=2)
        q5 = sq.rearrange(pat, h=nh, a=2, r=2)
        d5 = dst.rearrange(pat, h=nh, a=2, r=2)
        xa, xb_ = x5[:, :, :, 0, :], x5[:, :, :, 1, :]
        ta, tb_ = q5[:, :, :, 0, :], q5[:, :, :, 1, :]
        c4 = tabc.rearrange("p s h (a f) -> p (s h) a f", a=2)
        s4 = tabs.rearrange("p s h (a f) -> p (s h) a f", a=2)
        oa, ob = d5[:, :, :, 0, :], d5[:, :, :, 1, :]
        S.op(e, lambda: V.tensor_tensor(out=ta, in0=xb_, in1=s4, op=ALU.mult), reads=[srcres, tabres + "s"], writes=[sqres])
        S.op(e, lambda: V.tensor_tensor(out=tb_, in0=xa, in1=s4, op=ALU.mult), reads=[srcres, tabres + "s"], writes=[sqres])
        S.op(e, lambda: V.tensor_tensor(out=xa, in0=xa, in1=c4, op=ALU.mult), reads=[srcres, sqres, tabres + "c"], writes=[srcres])
        S.op(e, lambda: V.tensor_tensor(out=xb_, in0=xb_, in1=c4, op=ALU.mult), reads=[srcres, sqres, tabres + "c"], writes=[srcres])
        S.op(e, lambda: V.tensor_tensor(out=oa, in0=xa, in1=ta, op=ALU.subtract), reads=[srcres, sqres], writes=[dstres])
        S.op(e, lambda: V.tensor_tensor(out=ob, in0=xb_, in1=tb_, op=ALU.add), reads=[srcres, sqres], writes=[dstres])

    def tok_rows(ap2d):
        return ap2d.rearrange("(s p) d -> p s d", p=128)

    dbg = {}

    apos[0] = 0
    hid = carve([128, NJ, T], BF16)
    dn = [carve([128, NJ, 512], BF16), carve([128, NJ, 512], BF16)]
    kvw = carve([128, 8, 256], BF16)
    kvs = carve([128, 2, 4, 128], F32)
    ksq = carve([128, 4, 128], F32)
    tkc = carve([128, 4, 2, 32], F32)
    tks = carve([128, 4, 2, 32], F32)
    krb = carve([128, 4, 128], BF16)
    kTb = [carve([128, T], BF16), carve([128, T], BF16)]
    vsb = [carve([128, 4, 130], BF16), carve([128, 4, 130], BF16)]

    S.op(SP, lambda: nc.sync.dma_start(out=kvw, in_=s_kv), reads=[("scr", "kv")], writes=["kvw"], dma_key="kvw")

    for t in range(NCT if stage >= 1 else 0):
        b = t % 2
        X, xres = Xb[b], "X%d" % b
        S.op(SP, lambda b=b, t=t: nc.sync.dma_start(out=Xb[b][:, :, :], in_=tok_rows(xc[t * T:(t + 1) * T, :])),
             writes=[xres], dma_key="xl%d" % b)
        if P1 >= 1:
            rmsnorm_T(X, xres, 0)
        if P1 >= 2:
            ffn(0, X, xres, hid, dn)
        if t < NOT:
            S.op(SP, lambda b=b, t=t: nc.sync.dma_start(out=tok_rows(x1s[t * T:(t + 1) * T, :]), in_=Xb[b][:, :, :]),
                 reads=[xres], writes=[("x1s", t)], dma_key="xs%d" % b)
        if P1 < 3:
            continue
        rmsnorm_T(X, xres, 1)
        if P1 < 4:
            continue
        rope_tables(t, 2, tkc, tks, "tk")
        if P1 < 5:
            continue
        for s in range(4):
            gb_ = s % 2
            for kc in range(8):
                S.op(PE, lambda kc=kc, s=s, gb_=gb_: nc.tensor.matmul(G[gb_][:, 0:256], lhsT=hT[:, kc, s * 128:(s + 1) * 128], rhs=kvw[:, kc, :],
                                                                       start=(kc == 0), stop=(kc == 7)),
                     reads=[("hT", kc), "kvw"], writes=[GN[gb_]])
            S.op(ACT, lambda s=s, gb_=gb_: nc.scalar.copy(out=kvs[:, :, s, :], in_=G[gb_][:, 0:256].rearrange("p (a d) -> p a d", a=2)), reads=[GN[gb_]], writes=["kvs"])
        if P1 < 6:
            continue
        vb, vres = vsb[b], "vsb%d" % b
        kmt = km[:, t * 4:(t + 1) * 4]
        vb4 = vb.rearrange("p s (h e) -> p s h e", h=2)
        S.op(POOL, lambda vb4=vb4, kmt=kmt: nc.gpsimd.tensor_tensor(
            out=vb4[:, :, :, 0:64], in0=kvs[:, 1, :, :].rearrange("p s (h d) -> p s h d", h=2),
            in1=kmt.unsqueeze(2).unsqueeze(3).to_broadcast([128, 4, 2, 64]), op=ALU.mult),
            reads=["kvs", "km"], writes=[vres])
        S.op(POOL, lambda vb4=vb4, kmt=kmt: nc.gpsimd.tensor_copy(out=vb4[:, :, :, 64], in_=kmt.unsqueeze(2).to_broadcast([128, 4, 2])),
             reads=["km"], writes=[vres])
        S.op(SP, lambda vb=vb, t=t: nc.sync.dma_start(out=vss[t * T:(t + 1) * T, :].rearrange("(s p) e -> p s e", p=128), in_=vb),
             reads=[vres], writes=[("vss", t)], dma_key="vst%d" % b)
        if P1 < 7:
            continue
        head_norm_rope(POOL, kvs[:, 0, :, :], "kvs", 2, gk_bc, "gk", tkc, tks, "tk", ksq, "ksq", krb, "krb")
        for s in range(4):
            S.op(PE, lambda s=s: nc.tensor.transpose(out=tp[:, s * 128:(s + 1) * 128], in_=krb[:, s, :], identity=ident[:]),
                 reads=["krb", "ident"], writes=[("tpb", 0)])
        kb_, kres = kTb[b], "kTb%d" % b
        S.op(DVE, lambda kb_=kb_: nc.vector.tensor_copy(out=kb_, in_=tp[:, 0:512]), reads=[("tpb", 0)], writes=[kres])
        S.op(SP, lambda kb_=kb_, t=t: nc.sync.dma_start(out=kts[:, t * T:(t + 1) * T], in_=kb_),
             reads=[kres], writes=[("kts", t)], dma_key="kst%d" % b)

    S.barrier()
    apos[0] = 0
    kT = carve([128, NCT * T], BF16)
    vAf = carve([128, NCH * 130 + 64], BF16)
    vA = vAf[:, 0:NCH * 130].rearrange("p (ch e) -> p ch e", e=130)
    qf = carve([128, 4, 512], F32)
    qsq = hb[:, :, :].rearrange("p s d -> p (s d)").bitcast(F32).rearrange("p (s d) -> p s d", s=4)
    tqc = carve([128, 4, 8, 32], F32)
    tqs = carve([128, 4, 8, 32], F32)
    qrb = carve([128, 4, 512], BF16)
    qTp = Xb[1][:, 0:2, :].rearrange("p a d -> p (a d)").bitcast(BF16).rearrange("p (h t) -> p h t", h=8)
    ub = carve([128, 4, 512], BF16)
    vnb = tqc.rearrange("p s h f -> p (s h f)").bitcast(BF16)[:, 0:2048].rearrange("p (s d) -> p s d", s=4)
    sgb = tqs.rearrange("p s h f -> p (s h f)").bitcast(BF16)[:, 0:2048].rearrange("p (s d) -> p s d", s=4)
    sgT = carve([128, 4, T], BF16)
    aT = carve([128, 8, T], BF16)
    mT = carve([128, 8, T], BF16)
    PT = [carve([128, 1024], BF16), carve([128, 1024], BF16)]
    rden = carve([128, T], F32)
    ones1 = carve([128, 64], F32)
    onT = carve([128, T], F32)
    ggv_bc = carve([128, 512], F32)
    S.op(POOL, lambda: nc.gpsimd.memset(ones1, 1.0), writes=["ones1"])
    S.op(POOL, lambda: nc.gpsimd.memset(qTp, 0.0), writes=[("qT", c) for c in range(4)])
    S.op(POOL, lambda: nc.gpsimd.memset(vAf[:, NCH * 130:NCH * 130 + 64], 0.0), writes=["vApad"])
    small_load(ggv_bc, bc_row(g_gv), "ggv")

    for c in range(0, NCT if stage >= 2 else 0, 8):
        n = min(8, NCT - c)
        S.op(SP, lambda c=c, n=n: nc.sync.dma_start(out=kT[:, c * T:(c + n) * T], in_=kts[:, c * T:(c + n) * T]),
             reads=[("kts", t) for t in range(c, c + n)], writes=["kT"], dma_key="kTl")
        S.op(SP, lambda c=c, n=n: nc.sync.dma_start(out=vA[:, c * 4:(c + n) * 4, :],
                                                    in_=vss[c * T:(c + n) * T, :].rearrange("(ch p) e -> p ch e", p=128)),
             reads=[("vss", t) for t in range(c, c + n)], writes=["vA"], dma_key="vAl")

    def attention():
        heads = [(c, g) for c in range(4) for g in range(2)]
        seq = [(hi, i) for hi in range(8) for i in range(NPAIR)]

        def qk(n):
            hi, i = seq[n]
            c, g = heads[hi]
            sb_ = n % 2
            Sx = (S0, S1)[sb_]
            for u in range(2):
                ch = 2 * i + u
                S.op(PE, lambda u=u, ch=ch, c=c, g=g, Sx=Sx: nc.tensor.matmul(
                    Sx[:, u * 512:(u + 1) * 512], lhsT=kT[:, ch * 128:(ch + 1) * 128],
                    rhs=qTp[:, c * 2 + g, :], start=True, stop=True),
                    reads=["kT", ("qT", c)], writes=["S%d" % sb_])

        qk(0)
        for n in range(len(seq)):
            hi, i = seq[n]
            c, g = heads[hi]
            sb_ = n % 2
            Sx = (S0, S1)[sb_]
            ob = hi % 2
            Ox = (O0, O1)[ob]
            if n + 1 < len(seq):
                qk(n + 1)
            S.op(ACT, lambda Sx=Sx, sb_=sb_: nc.scalar.activation(out=PT[sb_], in_=Sx[:, :], func=AF.Exp, scale=0.125),
                 reads=["S%d" % sb_], writes=["PT%d" % sb_])
            for u in range(2):
                ch = 2 * i + u
                S.op(PE, lambda u=u, ch=ch, g=g, Ox=Ox, sb_=sb_, i=i: nc.tensor.matmul(
                    Ox[:, :], lhsT=vAf[:, ch * 130 + g * 65:ch * 130 + g * 65 + 128], rhs=PT[sb_][:, u * 512:(u + 1) * 512],
                    start=(i == 0 and u == 0), stop=(i == NPAIR - 1 and u == 1)),
                    reads=["vA", "vApad", "PT%d" % sb_], writes=["O%d" % ob])
            if i == NPAIR - 1:
                h_true = g * 4 + c
                S.op(DVE, lambda Ox=Ox: nc.vector.reciprocal(out=rden[64:65, :], in_=Ox[64:65, :]), reads=["O%d" % ob], writes=["rden"])
                S.op(PE, lambda: nc.tensor.matmul(tpf[0:64, 0:512], lhsT=ones1[64:65, 0:64], rhs=rden[64:65, :], start=True, stop=True),
                     reads=["rden", "ones1"], writes=[("tpb", 0)])
                S.op(ACT, lambda: nc.scalar.copy(out=onT[0:64, :], in_=tpf[0:64, 0:512]), reads=[("tpb", 0)], writes=["onT"])
                S.op(DVE, lambda Ox=Ox, h_true=h_true: nc.vector.tensor_tensor(out=aT[0:64, h_true, :], in0=Ox[0:64, :], in1=onT[0:64, :], op=ALU.mult),
                     reads=["O%d" % ob, "onT"], writes=[("aT", h_true)])

    for t in range(NOT if stage >= 2 else 0):
        b = 0
        X, xres = Xb[b], "X%d" % b
        S.op(SP, lambda b=b, t=t: nc.sync.dma_start(out=Xb[b][:, :, :], in_=tok_rows(x1s[t * T:(t + 1) * T, :])),
             reads=[("x1s", t)], writes=[xres], dma_key="xl%d" % b)
        rmsnorm_T(X, xres, 1)
        rope_tables(t, 8, tqc, tqs, "tq")
        for pi in range(3):
            slot = ring_load("qgg%d" % pi, s_qgg[pi].rearrange("p kc c -> p (kc c)"), 4096)
            rv = ring[:, slot, :].rearrange("p (kc c) -> p kc c", kc=8)
            for s in range(4):
                gb_ = s % 2
                for kc in range(8):
                    S.op(PE, lambda kc=kc, s=s, gb_=gb_, rv=rv: nc.tensor.matmul(G[gb_], lhsT=hT[:, kc, s * 128:(s + 1) * 128], rhs=rv[:, kc, :],
                                                                                  start=(kc == 0), stop=(kc == 7)),
                         reads=[("hT", kc), ("ring", slot)], writes=[GN[gb_]])
                if pi == 0:
                    S.op(ACT, lambda s=s, gb_=gb_: nc.scalar.copy(out=qf[:, s, :], in_=G[gb_]), reads=[GN[gb_]], writes=["qf"])
                elif pi == 1:
                    S.op(ACT, lambda s=s, gb_=gb_: nc.scalar.activation(out=ub[:, s, :], in_=G[gb_], func=AF.Gelu), reads=[GN[gb_]], writes=["ub"])
                else:
                    S.op(ACT, lambda s=s, gb_=gb_: nc.scalar.activation(out=qf[:, s, :], in_=G[gb_], func=AF.Gelu), reads=[GN[gb_]], writes=["qf"])
            if pi == 0:
                head_norm_rope(POOL, qf, "qf", 8, gq_bc, "gq", tqc, tqs, "tq", qsq, "hb", qrb, "qrb")
                for c0 in range(0, 4, 2):
                    bank = (c0 // 2) % 2
                    for kk in range(2):
                        c = c0 + kk
                        for s in range(4):
                            col = bank * 1024 + kk * 512 + s * 128
                            S.op(PE, lambda c=c, s=s, col=col: nc.tensor.transpose(out=tp[:, col:col + 128], in_=qrb[:, s, c * 128:(c + 1) * 128],
                                                                                    identity=ident[:]),
                                 reads=["qrb", "ident"], writes=[("tpb", bank)])
                    for kk in range(2):
                        c = c0 + kk
                        for g in range(2):
                            src_ps = tp[g * 64:(g + 1) * 64, bank * 1024 + kk * 512: bank * 1024 + (kk + 1) * 512]
                            dst = qTp[g * 64:(g + 1) * 64, c * 2 + g, :]
                            if bank == 0:
                                S.op(ACT, lambda src_ps=src_ps, dst=dst: nc.scalar.copy(out=dst, in_=src_ps), reads=[("tpb", bank)], writes=[("qT", c)])
                            else:
                                S.op(DVE, lambda src_ps=src_ps, dst=dst: nc.vector.tensor_copy(out=dst, in_=src_ps), reads=[("tpb", bank)], writes=[("qT", c)])
            if pi == 2:
                row_rstd(qf, "qf", 512, 512)
                for s in range(4):
                    S.op(DVE, lambda s=s: nc.vector.scalar_tensor_tensor(out=vnb[:, s, :], in0=qf[:, s, :], scalar=rstd[:, s:s + 1], in1=ggv_bc,
                                                                         op0=ALU.mult, op1=ALU.mult),
                         reads=["qf", "rstd", "ggv"], writes=["tqc"])
                for s in range(4):
                    gb_ = 2 + (s % 2)
                    for g in range(8):
                        S.op(PE, lambda s=s, g=g, gb_=gb_: nc.tensor.matmul(G[gb_][:, g * 64:(g + 1) * 64], lhsT=wsT[:, g, :], rhs=vnb[:, s, g * 64:(g + 1) * 64],
                                                                             start=True, stop=True),
                             reads=["tqc", "wsT"], writes=[GN[gb_]])
                    tt, tn = (tmpA, "tmpA") if s % 2 == 0 else (tmpB, "tmpB")
                    S.op(DVE, lambda gb_=gb_, tt=tt: nc.vector.tensor_tensor(out=tt.rearrange("p (g c) -> p g c", g=8),
                                                                             in0=G[gb_].rearrange("p (g c) -> p g c", g=8),
                                                                             in1=bspT[:, :].unsqueeze(2).to_broadcast([128, 8, 64]), op=ALU.add),
                         reads=[GN[gb_], "bsp"], writes=[tn])
                    S.op(POOL, lambda s=s, tt=tt: nc.gpsimd.tensor_tensor(out=sgb[:, s, :], in0=tt, in1=ub[:, s, :], op=ALU.mult),
                         reads=[tn, "ub"], writes=["tqs"])
                transpose_T(sgb, "tqs", 4, sgT, "sgT")
        attention()
        for oc in range(8):
            slot = ring_load("mg%d" % oc, s_mg[oc], 3584)
            rg = ring[:, slot, 0:2048].rearrange("p (kc c) -> p kc c", kc=8)
            g0, g1 = (0, 1) if oc % 2 == 0 else (2, 3)
            for kc in range(8):
                S.op(PE, lambda kc=kc, rg=rg, g0=g0: nc.tensor.matmul(G[g0], lhsT=rg[:, kc, 0:128], rhs=hT[:, kc, :], start=(kc == 0), stop=(kc == 7)),
                     reads=[("ring", slot), ("hT", kc)], writes=[GN[g0]])
            for kc in range(8):
                S.op(PE, lambda kc=kc, rg=rg, g1=g1: nc.tensor.matmul(G[g1], lhsT=rg[:, kc, 128:256], rhs=hT[:, kc, :], start=(kc == 0), stop=(kc == 7)),
                     reads=[("ring", slot), ("hT", kc)], writes=[GN[g1]])
            for h in range(8):
                S.op(PE, lambda h=h, slot=slot: nc.tensor.matmul(G[4], lhsT=ring[0:64, slot, 2560 + h * 128:2560 + (h + 1) * 128], rhs=aT[0:64, h, :],
                                                                 start=(h == 0), stop=(h == 7)),
                     reads=[("ring", slot), ("aT", h)], writes=[GN[4]])
            for kc in range(4):
                S.op(PE, lambda kc=kc, slot=slot: nc.tensor.matmul(G[5], lhsT=ring[:, slot, 2048 + kc * 128:2048 + (kc + 1) * 128], rhs=sgT[:, kc, :],
                                                                   start=(kc == 0), stop=(kc == 3)),
                     reads=[("ring", slot), ("sgT", kc)], writes=[GN[5]])
            S.op(ACT, lambda g0=g0: nc.scalar.activation(out=tmpA, in_=G[g0], func=AF.Sigmoid), reads=[GN[g0]], writes=["tmpA"])
            S.op(ACT, lambda g1=g1: nc.scalar.activation(out=tmpB, in_=G[g1], func=AF.Sigmoid), reads=[GN[g1]], writes=["tmpB"])
            S.op(DVE, lambda: nc.vector.tensor_tensor(out=tmpA, in0=tmpA, in1=G[4], op=ALU.mult), reads=["tmpA", GN[4]], writes=["tmpA"])
            S.op(DVE, lambda: nc.vector.tensor_tensor(out=tmpB, in0=tmpB, in1=G[5], op=ALU.mult), reads=["tmpB", GN[5]], writes=["tmpB"])
            S.op(POOL, lambda oc=oc: nc.gpsimd.tensor_tensor(out=mT[:, oc, :], in0=tmpA, in1=tmpB, op=ALU.add),
                 reads=["tmpA", "tmpB"], writes=[("mT", oc)])
        for h in range(2):
            slot = ring_load("wo%d" % h, s_wo[h].rearrange("p kc c -> p (kc c)"), 4096)
            rv = ring[:, slot, :].rearrange("p (kc c) -> p kc c", kc=8)
            for s in range(4):
                gb_ = s % 2
                for kc in range(8):
                    S.op(PE, lambda kc=kc, s=s, gb_=gb_, rv=rv: nc.tensor.matmul(G[gb_], lhsT=mT[:, kc, s * 128:(s + 1) * 128], rhs=rv[:, kc, :],
                                                                                  start=(kc == 0), stop=(kc == 7)),
                         reads=[("mT", kc), ("ring", slot)], writes=[GN[gb_]])
                S.op(DVE, lambda s=s, h=h, gb_=gb_, X=X: nc.vector.tensor_tensor(out=X[:, s, h * 512:(h + 1) * 512], in0=G[gb_],
                                                                                  in1=X[:, s, h * 512:(h + 1) * 512], op=ALU.add),
                     reads=[GN[gb_], xres], writes=[xres])
        S.op(SP, lambda b=b, t=t: nc.sync.dma_start(out=tok_rows(x1s[t * T:(t + 1) * T, :]), in_=Xb[b][:, :, :]),
             reads=[xres], writes=[("x1s", t)], dma_key="xs%d" % b)

    S.barrier()
    apos[0] = 0
    hid = carve([128, NJ, T], BF16)
    dn = [carve([128, NJ, 512], BF16), carve([128, NJ, 512], BF16)]
    pf = carve([128, 4, PLE], F32)
    pbf = carve([128, 4, PLE], BF16)
    pT = carve([128, 2, T], BF16)
    gfin_bc = carve([128, D], F32)
    small_load(gfin_bc, bc_row(g_fin), "gfin")
    out_ops = []
    for t in range(NOT if stage >= 3 else 0):
        b = t % 2
        X, xres = Xb[b], "X%d" % b
        S.op(SP, lambda b=b, t=t: nc.sync.dma_start(out=Xb[b][:, :, :], in_=tok_rows(x1s[t * T:(t + 1) * T, :])),
             reads=[("x1s", t)], writes=[xres], dma_key="xl%d" % b)
        rmsnorm_T(X, xres, 2)
        ffn(1, X, xres, hid, dn)
        rmsnorm_T(X, xres, 3)
        S.op(SP, lambda t=t: nc.sync.dma_start(out=pf, in_=tok_rows(pin[t * T:(t + 1) * T, :])), writes=["pf"], dma_key="pfl")
        S.op(POOL, lambda: nc.gpsimd.tensor_copy(out=pbf, in_=pf), reads=["pf"], writes=["pbf"])
        transpose_T(pbf, "pbf", 2, pT, "pT")
        slotp = ring_load("pl", s_pl.rearrange("p kc c -> p (kc c)"), 2048)
        rvp = ring[:, slotp, 0:2048].rearrange("p (kc c) -> p kc c", kc=2)
        for h in range(2):
            slot = ring_load("pg%d" % h, s_pg[h].rearrange("p kc c -> p (kc c)"), 4096)
            rv = ring[:, slot, :].rearrange("p (kc c) -> p kc c", kc=8)
            for s in range(4):
                g0, g1 = (0, 1) if s % 2 == 0 else (2, 3)
                for kc in range(8):
                    S.op(PE, lambda kc=kc, s=s, g0=g0, rv=rv: nc.tensor.matmul(G[g0], lhsT=hT[:, kc, s * 128:(s + 1) * 128], rhs=rv[:, kc, :],
                                                                                start=(kc == 0), stop=(kc == 7)),
                         reads=[("hT", kc), ("ring", slot)], writes=[GN[g0]])
                for k2 in range(2):
                    S.op(PE, lambda k2=k2, s=s, g1=g1, h=h, rvp=rvp: nc.tensor.matmul(G[g1], lhsT=pT[:, k2, s * 128:(s + 1) * 128],
                                                                                       rhs=rvp[:, k2, h * 512:(h + 1) * 512],
                                                                                       start=(k2 == 0), stop=(k2 == 1)),
                         reads=[("pT", k2), ("ring", slotp)], writes=[GN[g1]])
                tt, tn = (tmpA, "tmpA") if s % 2 == 0 else (tmpB, "tmpB")
                S.op(ACT, lambda g0=g0, tt=tt: nc.scalar.activation(out=tt, in_=G[g0], func=AF.Sigmoid), reads=[GN[g0]], writes=[tn])
                S.op(DVE, lambda g1=g1, tt=tt: nc.vector.tensor_tensor(out=tt, in0=tt, in1=G[g1], op=ALU.mult), reads=[tn, GN[g1]], writes=[tn])
                S.op(POOL, lambda s=s, h=h, tt=tt, X=X: nc.gpsimd.tensor_tensor(out=X[:, s, h * 512:(h + 1) * 512], in0=X[:, s, h * 512:(h + 1) * 512],
                                                                                 in1=tt, op=ALU.add),
                     reads=[tn, xres], writes=[xres])
        row_rstd(X, xres, D, D)
        for s in range(4):
            S.op(DVE, lambda s=s, X=X: nc.vector.scalar_tensor_tensor(out=X[:, s, :], in0=X[:, s, :], scalar=rstd[:, s:s + 1], in1=gfin_bc,
                                                                      op0=ALU.mult, op1=ALU.mult),
                 reads=[xres, "rstd", "gfin"], writes=[xres])
        out_ops.append(S.op(SP, lambda b=b, t=t: nc.sync.dma_start(out=tok_rows(y[t * T:(t + 1) * T, :]), in_=Xb[b][:, :, :]),
                            reads=[xres], dma_key="ys%d" % b))
    if stage < 3 and stage >= 1:
        out_ops.append(S.op(SP, lambda: nc.sync.dma_start(out=y[:, :], in_=x1s[:, :]), reads=[("x1s", t) for t in range(NOT)], dma_key="dbg"))
    S.op(SP, None, extra=out_ops)

    semkeys = S.finalize()
    by_eng = {e: [o for o in S.ops if o.eng == e] for e in ENGS}
    with ExitStack() as es:
        sems = {k: es.enter_context(nc.semaphore("s%d" % i)) for i, k in enumerate(semkeys)}
        block = es.enter_context(nc.Block())

        def emit(engname, eng):
            waited = {}
            for o in by_eng[engname]:
                need = {}
                for d in o.deps:
                    if not d.signal:
                        continue
                    if d.eng == PE and o.eng == PE and d.dma_key is None and o.dma_key is None:
                        continue
                    if need.get(d.sem, 0) < d.val:
                        need[d.sem] = d.val
                for k, v in need.items():
                    if waited.get(k, 0) < v:
                        eng.wait_ge(sems[k], v)
                        waited[k] = v
                if o.fn is not None:
                    ins = o.fn()
                    if o.signal:
                        ins.then_inc(sems[o.sem], 16 if o.dma_key is not None else 1)
                else:
                    assert not o.signal

        @block.tensor
        def _(e):
            emit(PE, e)

        @block.scalar
        def _(e):
            emit(ACT, e)

        @block.vector
        def _(e):
            emit(DVE, e)

        @block.gpsimd
        def _(e):
            emit(POOL, e)

        @block.sync
        def _(e):
            emit(SP, e)
    return nc


NCT_FULL, NOT_FULL = 32, 8
WNAMES = ["g_ffn1", "w_ffn1_gu", "w_ffn1_down", "g_mix", "w_in", "g_q", "g_k", "g_gmlp_v", "w_spatial", "b_spatial",
          "w_branch_attn", "w_branch_gmlp", "w_out", "g_ffn2", "w_ffn2_gu", "w_ffn2_down", "g_ple", "w_ple_gate", "w_ple", "g_final"]


def _pos_table(tok_idx):
    tok_idx = np.asarray(tok_idx, np.int64)
    return np.stack([tok_idx // 64, tok_idx % 64], axis=1).astype(np.float32)


def kernel(**inputs):
    xp = np.asarray(inputs["x_prompt"], np.float32)
    xs = np.asarray(inputs["x_sample"], np.float32)
    pp = np.asarray(inputs["p_prompt"], np.float32)
    ps = np.asarray(inputs["p_sample"], np.float32)
    w = {k: np.ascontiguousarray(np.asarray(inputs[k], np.float32)) for k in WNAMES}
    w["g_final"] = w["g_final"].reshape(1, D)
    NTOK = NCT_FULL * T
    own = NOT_FULL * T
    in_maps = []
    for c in range(8):
        if c < 4:
            order = [c] + [(c + k) % 4 for k in range(1, 4)]
            xcx = np.concatenate([xp[o] for o in order], axis=0)
            posi = np.concatenate([np.arange(own)] * 4)
            msk = np.zeros(NTOK, np.float32); msk[:own] = 1.0
            pc = pp[0, c]
        else:
            q = c - 4
            order = [q] + [(q + k) % 4 for k in range(1, 4)]
            xcx = np.concatenate([xs[0, o * own:(o + 1) * own] for o in order], axis=0)
            posi = np.concatenate([np.arange(o * own, (o + 1) * own) for o in order])
            msk = np.ones(NTOK, np.float32)
            pc = ps[0, 0, q * own:(q + 1) * own]
        m = {"xc": np.ascontiguousarray(xcx), "pin": np.ascontiguousarray(pc), "pos": _pos_table(posi),
             "kmask": np.ascontiguousarray(msk.reshape(NTOK // 128, 128).T)}
        m.update(w)
        in_maps.append(m)
    nc = build_program(NCT_FULL, NOT_FULL)
    res = run_bass_kernel_spmd(nc, in_maps, core_ids=list(range(8)))
    ys = [np.asarray(res.results[c]["y"], np.float32) for c in range(8)]
    y_prompt = np.stack(ys[0:4], axis=0)
    y_sample = np.concatenate(ys[4:8], axis=0)[None]
    return (y_prompt, y_sample)
```

```python
import math
SCHEDULE = True
P0 = 255
P1 = 99
P1SUB = 99
P1T = 0
from contextlib import ExitStack
import numpy as np
import concourse.bass as bass
import concourse.mybir as mybir
from concourse.bass_utils import run_bass_kernel_spmd

F32 = mybir.dt.float32
BF16 = mybir.dt.bfloat16
I32 = mybir.dt.int32
AF = mybir.ActivationFunctionType
ALU = mybir.AluOpType
AX = mybir.AxisListType

D = 1024
DFF = 2816
NJ = DFF // 128
PLE = 256
EPS = 1e-6
T = 512
NSLOT = 3
SLOTW = 4096
PE, ACT, DVE, POOL, SP = "pe", "act", "dve", "pool", "sp"
ENGS = [PE, ACT, DVE, POOL, SP]


class Op:
    __slots__ = ("eng", "fn", "deps", "dma_key", "sem", "val", "signal", "idx", "cost", "phase", "t0", "t1")

    def __init__(self, eng, fn, deps, dma_key, cost):
        self.eng, self.fn, self.deps, self.dma_key, self.cost = eng, fn, deps, dma_key, cost
        self.sem = None
        self.val = 0
        self.signal = False


DEFAULT_COST = {PE: 216, ACT: 600, DVE: 650, POOL: 900, SP: 2500}


class Sched:
    def __init__(self):
        self.ops = []
        self.last_write = {}
        self.readers = {}
        self.phase = 0
        self.since_barrier = []
        self.cur_barrier = {}

    def op(self, eng, fn, reads=(), writes=(), dma_key=None, extra=(), cost=None):
        deps = set(extra)
        for r in reads:
            w = self.last_write.get(r)
            if w is not None:
                deps.add(w)
        for w_ in writes:
            w = self.last_write.get(w_)
            if w is not None:
                deps.add(w)
            for rd in self.readers.get(w_, ()):
                deps.add(rd)
        bar = self.cur_barrier.get(eng)
        if bar is not None:
            deps.add(bar)
        o = Op(eng, fn, deps, dma_key, DEFAULT_COST[eng] if cost is None else cost)
        o.idx = len(self.ops)
        o.phase = self.phase
        o.sem = (eng, self.phase) if dma_key is None else ("dma", dma_key)
        self.ops.append(o)
        for r in reads:
            self.readers.setdefault(r, []).append(o)
        for w_ in writes:
            self.last_write[w_] = o
            self.readers[w_] = []
        if fn is not None:
            self.since_barrier.append(o)
        return o

    def barrier(self):
        prev = list(self.since_barrier)
        self.since_barrier = []
        for e in ENGS:
            self.cur_barrier[e] = self.op(e, None, extra=prev, cost=0)
        self.phase += 1

    def schedule(self):
        import heapq
        n = len(self.ops)
        succ = [[] for _ in range(n)]
        indeg = [0] * n
        for o in self.ops:
            indeg[o.idx] = len(o.deps)
            for d in o.deps:
                succ[d.idx].append(o)
        pending = {e: [] for e in ENGS}
        avail = {e: [] for e in ENGS}
        free = {e: 0.0 for e in ENGS}
        ready_t = [0.0] * n
        for o in self.ops:
            if indeg[o.idx] == 0:
                heapq.heappush(pending[o.eng], (0.0, o.idx))
        order = []
        done = 0
        while done < n:
            best = None
            for e in ENGS:
                pe_, av = pending[e], avail[e]
                while pe_ and pe_[0][0] <= free[e]:
                    heapq.heappush(av, heapq.heappop(pe_)[1])
                if av:
                    cand = (free[e], av[0], e, True)
                elif pe_:
                    cand = (pe_[0][0], pe_[0][1], e, False)
                else:
                    continue
                if best is None or cand[:2] < best[:2]:
                    best = cand
            assert best is not None, "dependency cycle"
            start, idx, e, from_av = best
            if from_av:
                heapq.heappop(avail[e])
            else:
                heapq.heappop(pending[e])
            o = self.ops[idx]
            o.t0 = start
            if o.dma_key is not None:
                free[e] = start + 60.0
                o.t1 = start + o.cost
            else:
                o.t1 = start + o.cost
                free[e] = o.t1
            order.append(o)
            done += 1
            for sc in succ[idx]:
                if ready_t[sc.idx] < o.t1:
                    ready_t[sc.idx] = o.t1
                indeg[sc.idx] -= 1
                if indeg[sc.idx] == 0:
                    heapq.heappush(pending[sc.eng], (ready_t[sc.idx], sc.idx))
        self.ops = order
        self.est_ns = max(o.t1 for o in order)

    def finalize(self):
        for o in self.ops:
            for d in o.deps:
                if d.eng == PE and o.eng == PE and d.dma_key is None and o.dma_key is None:
                    continue
                if d.fn is None:
                    continue
                d.signal = True
        for o in self.ops:
            if o.dma_key is not None and o.fn is not None:
                o.signal = True
        counts = {}
        for o in self.ops:
            if o.signal:
                counts[o.sem] = counts.get(o.sem, 0) + (16 if o.dma_key is not None else 1)
                o.val = counts[o.sem]
        return sorted(counts.keys(), key=str)


def build_program(NCT, NOT, stage=3):
    NCH = NCT * 4
    NPAIR = NCH // 2
    nc = bass.Bass("TRN2", target_bir_lowering=False)

    def din(name, shape, dt=F32):
        return nc.dram_tensor(name, list(shape), dt, kind="ExternalInput").ap()

    xc = din("xc", [NCT * T, D])
    pin = din("pin", [NOT * T, PLE])
    pos = din("pos", [NCT * T, 2])
    kmask = din("kmask", [128, NCH])
    g_ffn1 = din("g_ffn1", [1, D]); w1gu = din("w_ffn1_gu", [1, D, 2 * DFF]); w1d = din("w_ffn1_down", [1, DFF, D])
    g_mix = din("g_mix", [1, D]); w_in = din("w_in", [1, D, 3840])
    g_q = din("g_q", [1, 64]); g_k = din("g_k", [1, 64]); g_gv = din("g_gmlp_v", [1, 512])
    w_sp = din("w_spatial", [1, 8, 128, 128]); b_sp = din("b_spatial", [1, 8, 128])
    w_ba = din("w_branch_attn", [1, 512, D]); w_bg = din("w_branch_gmlp", [1, 512, D]); w_out = din("w_out", [1, D, D])
    g_ffn2 = din("g_ffn2", [1, D]); w2gu = din("w_ffn2_gu", [1, D, 2 * DFF]); w2d = din("w_ffn2_down", [1, DFF, D])
    g_ple = din("g_ple", [1, D]); w_pg = din("w_ple_gate", [1, D, D]); w_pl = din("w_ple", [1, PLE, D])
    g_fin = din("g_final", [1, D])
    y = nc.dram_tensor("y", [NOT * T, D], F32, kind="ExternalOutput").ap()

    def dscr(name, shape, dt=BF16):
        return nc.dram_tensor(name, list(shape), dt).ap()

    s_gu = [dscr("s_gu1", [11, 128, 8, 512]), dscr("s_gu2", [11, 128, 8, 512])]
    s_dn = [dscr("s_dn1", [2, 128, NJ, 512]), dscr("s_dn2", [2, 128, NJ, 512])]
    s_kv = dscr("s_kv", [128, 8, 256])
    s_qgg = dscr("s_qgg", [3, 128, 8, 512])
    s_mg = dscr("s_mg", [8, 128, 3584])
    s_wo = dscr("s_wo", [2, 128, 8, 512])
    s_pg = dscr("s_pg", [2, 128, 8, 512])
    s_pl = dscr("s_pl", [128, 2, 1024])
    x1s = dscr("x1s", [NOT * T, D], F32)
    kts = dscr("kts", [128, NCT * T])
    vss = dscr("vss", [NCT * T, 130])

    S = Sched()

    def sb(name, shape, dt):
        return nc.alloc_sbuf_tensor(name, list(shape), dt)

    ident = sb("ident", [128, 128], BF16)
    gcol = sb("gcol", [128, 4, 8], F32)
    gq_bc = sb("gq_bc", [128, 64], F32)
    gk_bc = sb("gk_bc", [128, 64], F32)
    bspT = sb("bspT", [128, 8], F32)
    wsT = sb("wsT", [128, 8, 128], BF16)
    inv_bc = sb("inv_bc", [128, 16], F32)
    km = sb("km", [128, NCH], F32)
    negpi = sb("negpi", [128, 1], F32)
    epsb = sb("epsb", [128, 1], F32)
    ring = sb("ring", [128, NSLOT, SLOTW], BF16)
    Xb = [sb("X0", [128, 4, D], F32), sb("X1", [128, 4, D], F32)]
    hb = sb("hb", [128, 4, D], BF16)
    hT = sb("hT", [128, 8, T], BF16)
    ss = sb("ss", [128, 8], F32)
    rstd = sb("rstd", [128, 8], F32)
    tmpA = sb("tmpA", [128, T], F32)[:, :]
    tmpB = sb("tmpB", [128, T], F32)[:, :]
    posb = sb("posb", [128, 4, 2], F32)
    ang = sb("ang", [128, 4, 2, 16], F32)
    angm = sb("angm", [128, 4, 2, 16], F32)
    cs = sb("cs", [128, 4, 32], F32)
    sn = sb("sn", [128, 4, 32], F32)
    angki = sb("angki", [128, 4, 32], I32)
    angkf = sb("angkf", [128, 4, 32], F32)
    angr = sb("angr", [128, 4, 32], F32)
    hss = sb("hss", [128, 32], F32)
    hrs = sb("hrs", [128, 32], F32)
    ARENA_BYTES = 124 * 1024
    arena = sb("arena", [128, ARENA_BYTES // 4], F32)
    apos = [0]

    def carve(shape, dt):
        esz = 4 if dt in (F32, I32) else 2
        n = int(np.prod(shape[1:]))
        nbytes = (n * esz + 31) // 32 * 32
        off = apos[0]
        apos[0] += nbytes
        assert apos[0] <= ARENA_BYTES, (apos[0], ARENA_BYTES)
        v = arena[:, off // 4:(off + nbytes) // 4]
        if esz == 2:
            v = v.bitcast(BF16)
        v = v[:, 0:n]
        if len(shape) == 3:
            v = v.rearrange("p (a b) -> p a b", a=shape[1])
        elif len(shape) == 4:
            v = v.rearrange("p (a b c) -> p a b c", a=shape[1], b=shape[2])
        return v

    tp = nc.alloc_psum_tensor("tp", [128, 2048], BF16)
    S0 = nc.alloc_psum_tensor("S0", [128, 1024], F32)
    S1 = nc.alloc_psum_tensor("S1", [128, 1024], F32)
    O0 = nc.alloc_psum_tensor("O0", [128, 512], F32)
    O1 = nc.alloc_psum_tensor("O1", [128, 512], F32)
    G = [S0[:, 0:512], S0[:, 512:1024], S1[:, 0:512], S1[:, 512:1024], O0[:, :], O1[:, :]]
    GN = ["G0", "G1", "G2", "G3", "G4", "G5"]
    tpf = tp[:, :].bitcast(F32)

    def vec(e):
        return nc.vector if e == DVE else nc.gpsimd

    def cast(key, out_ap, in_ap):
        if not (P0 & 8):
            return None
        return S.op(POOL, lambda o=out_ap, i=in_ap: nc.gpsimd.dma_start(out=o, in_=i),
                    writes=[("scr", key)], dma_key="c_" + key, cost=9000)

    def small_load(out_ap, in_ap, res):
        def f():
            with nc.allow_non_contiguous_dma(reason="tiny constant layout load"):
                return nc.sync.dma_start(out=out_ap, in_=in_ap)
        return S.op(SP, f, writes=[res], dma_key="k_" + str(res).replace("'", "").replace(" ", ""))

    def bc_row(ap2d):
        return ap2d.partition_broadcast(128).rearrange("p o d -> p (o d)")

    for i, g in enumerate([g_ffn1, g_mix, g_ffn2, g_ple] if P0 & 1 else []):
        small_load(gcol[:, i, :], g.rearrange("o (kc p) -> p (o kc)", p=128), ("gcol", i))
    if P0 & 2:
        small_load(gq_bc[:, :], bc_row(g_q), "gq")
        small_load(gk_bc[:, :], bc_row(g_k), "gk")
    if P0 & 4:
        small_load(bspT[:, :], b_sp.rearrange("o g p -> p (o g)"), "bsp")
    small_load(km[:, :], kmask[:, :], "km")

    def mk_ident():
        nc.gpsimd.memset(ident[:], 0.0)
        return nc.gpsimd.affine_select(out=ident[:], in_=ident[:], pattern=[[-1, 128]], compare_op=ALU.not_equal,
                                       fill=1.0, base=0, channel_multiplier=1)
    S.op(POOL, mk_ident, writes=["ident"])
    S.op(POOL, lambda: nc.gpsimd.memset(negpi[:], -math.pi), writes=["negpi"])
    S.op(POOL, lambda: nc.gpsimd.memset(epsb[:], EPS), writes=["epsb"])
    S.op(POOL, lambda: nc.gpsimd.iota(out=cs[:, 0, 0:16].bitcast(I32), pattern=[[1, 16]], base=0, channel_multiplier=0),
         writes=["cs"])
    S.op(POOL, lambda: nc.gpsimd.tensor_copy(out=sn[:, 0, 0:16], in_=cs[:, 0, 0:16].bitcast(I32)), reads=["cs"], writes=["sn"])
    S.op(ACT, lambda: nc.scalar.activation(out=inv_bc[:, :], in_=sn[:, 0, 0:16], func=AF.Exp, scale=-math.log(10000.0) / 16.0),
         reads=["sn"], writes=["inv"])

    S.op(SP, lambda: nc.sync.dma_start(out=Xb[0][:, 0, :].rearrange("p (g q) -> p g q", g=8),
                                       in_=w_sp[0].rearrange("g p q -> p g q")), writes=["X0"], dma_key="const")
    S.op(DVE, lambda: nc.vector.tensor_copy(out=hb[:, 0, :], in_=Xb[0][:, 0, :]), reads=["X0"], writes=["hb"])
    for g in range(8):
        S.op(PE, lambda g=g: nc.tensor.transpose(out=tp[:, g * 128:(g + 1) * 128], in_=hb[:, 0, g * 128:(g + 1) * 128], identity=ident[:]),
             reads=["hb", "ident"], writes=[("tpb", 0)], cost=118)
    S.op(DVE, lambda: nc.vector.tensor_copy(out=wsT[:, :, :].rearrange("q g p -> q (g p)"), in_=tp[:, 0:1024]),
         reads=[("tpb", 0)], writes=["wsT"])

    def kcp(ap2d):
        return ap2d.rearrange("(kc p) c -> p kc c", p=128)

    def cast_ffn(idx, wgu, wd):
        for i in range(11):
            cast("gu%d_%d" % (idx, i), s_gu[idx][i, :, :, 0:256], kcp(wgu[0, :, 256 * i:256 * i + 256]))
            cast("gu%d_%d" % (idx, i), s_gu[idx][i, :, :, 256:512], kcp(wgu[0, :, DFF + 256 * i:DFF + 256 * i + 256]))
        for h in range(2):
            cast("dn%d_%d" % (idx, h), s_dn[idx][h], kcp(wd[0, :, 512 * h:512 * h + 512]))

    cast_ffn(0, w1gu, w1d)
    cast("kv", s_kv, kcp(w_in[0, :, 512:768]))
    for g in range(2):
        for c in range(4):
            cast("qgg0", s_qgg[0, :, :, c * 128 + g * 64:c * 128 + g * 64 + 64], kcp(w_in[0, :, (g * 4 + c) * 64:(g * 4 + c) * 64 + 64]))
    for i, c0 in ((1, 768), (2, 1280)):
        cast("qgg%d" % i, s_qgg[i], kcp(w_in[0, :, c0:c0 + 512]))
    for oc in range(8):
        k = "mg%d" % oc
        gts = s_mg[oc, :, 0:2048].rearrange("p (kc c) -> p kc c", kc=8)
        cast(k, gts[:, :, 0:128], kcp(w_in[0, :, 1792 + oc * 128:1792 + oc * 128 + 128]))
        cast(k, gts[:, :, 128:256], kcp(w_in[0, :, 2816 + oc * 128:2816 + oc * 128 + 128]))
        cast(k, s_mg[oc, :, 2048:2560].rearrange("p (kc c) -> p kc c", kc=4), kcp(w_bg[0, :, oc * 128:oc * 128 + 128]))
        cast(k, s_mg[oc, 0:64, 2560:3584].rearrange("p (h c) -> p h c", h=8),
             w_ba[0, :, oc * 128:oc * 128 + 128].rearrange("(h p) c -> p h c", p=64))
    for h in range(2):
        cast("wo%d" % h, s_wo[h], kcp(w_out[0, :, 512 * h:512 * h + 512]))
    cast_ffn(1, w2gu, w2d)
    for h in range(2):
        cast("pg%d" % h, s_pg[h], kcp(w_pg[0, :, 512 * h:512 * h + 512]))
    cast("pl", s_pl, kcp(w_pl[0, :, :]))

    ring_n = [0]

    def ring_load(key, src_ap, width):
        slot = ring_n[0] % NSLOT
        ring_n[0] += 1
        S.op(SP, lambda s=slot, a=src_ap, w=width: nc.sync.dma_start(out=ring[:, s, 0:w], in_=a),
             reads=[("scr", key)], writes=[("ring", slot)], dma_key="ring%d" % slot, cost=2000 + width * 256 // 180)
        return slot

    def transpose_T(src, srcres, nkc, outT, outres, gi=None):
        for k0 in range(0, nkc, 2):
            bank = (k0 // 2) % 2
            for kk in range(2):
                kc = k0 + kk
                for s in range(4):
                    col = bank * 1024 + kk * 512 + s * 128
                    S.op(PE, lambda kc=kc, s=s, col=col: nc.tensor.transpose(out=tp[:, col:col + 128],
                                                                            in_=src[:, s, kc * 128:(kc + 1) * 128], identity=ident[:]),
                         reads=[srcres, "ident"], writes=[("tpb", bank)], cost=118)
            for kk in range(2):
                kc = k0 + kk
                e = ACT if bank == 0 else DVE
                src_ps = tp[:, bank * 1024 + kk * 512: bank * 1024 + (kk + 1) * 512]
                if gi is None:
                    if e == ACT:
                        f = lambda kc=kc, src_ps=src_ps: nc.scalar.copy(out=outT[:, kc, :], in_=src_ps)
                    else:
                        f = lambda kc=kc, src_ps=src_ps: nc.vector.tensor_copy(out=outT[:, kc, :], in_=src_ps)
                    rd = [("tpb", bank)]
                else:
                    if e == ACT:
                        f = lambda kc=kc, src_ps=src_ps: nc.scalar.activation(out=outT[:, kc, :], in_=src_ps, func=AF.Copy,
                                                                               scale=gcol[:, gi, kc:kc + 1])
                    else:
                        f = lambda kc=kc, src_ps=src_ps: nc.vector.tensor_scalar(out=outT[:, kc, :], in0=src_ps,
                                                                                  scalar1=gcol[:, gi, kc:kc + 1], scalar2=None, op0=ALU.mult)
                    rd = [("tpb", bank), ("gcol", gi)]
                S.op(e, f, reads=rd, writes=[(outres, kc)])

    def row_rstd(X, xres, width, nrm):
        for s in range(4):
            S.op(ACT, lambda s=s: nc.scalar.activation(out=hb[:, s, 0:width], in_=X[:, s, 0:width], func=AF.Square, accum_out=ss[:, s:s + 1]),
                 reads=[xres], writes=["hb", ("ss", s)], cost=1056 if width > 512 else 843)
        S.op(ACT, lambda: nc.scalar.activation(out=rstd[:, 0:4], in_=ss[:, 0:4], func=AF.Sqrt, scale=1.0 / nrm, bias=epsb[:, 0:1]),
             reads=[("ss", s) for s in range(4)] + ["epsb"], writes=["rstd"], cost=300)
        S.op(DVE, lambda: nc.vector.reciprocal(out=rstd[:, 0:4], in_=rstd[:, 0:4]), reads=["rstd"], writes=["rstd"], cost=190)

    def rmsnorm_T(X, xres, gi):
        row_rstd(X, xres, D, D)
        if P1SUB < 2:
            return
        for s in range(4):
            S.op(DVE, lambda s=s: nc.vector.tensor_scalar(out=hb[:, s, :], in0=X[:, s, :], scalar1=rstd[:, s:s + 1], scalar2=None, op0=ALU.mult),
                 reads=[xres, "rstd"], writes=["hb"])
        if P1SUB < 3:
            return
        transpose_T(hb, "hb", 8, hT, "hT", gi)

    def ffn(idx, X, xres, hid, dn):
        for i in range(11):
            slot = ring_load("gu%d_%d" % (idx, i), s_gu[idx][i].rearrange("p kc c -> p (kc c)"), 4096)
            rv = ring[:, slot, :].rearrange("p (kc c) -> p kc c", kc=8)
            for jj in range(2):
                j = 2 * i + jj
                ga, gb = (0, 1) if j % 2 == 0 else (2, 3)
                for kc in range(8):
                    S.op(PE, lambda kc=kc, jj=jj, ga=ga, rv=rv: nc.tensor.matmul(G[ga], lhsT=rv[:, kc, jj * 128:(jj + 1) * 128], rhs=hT[:, kc, :],
                                                                                  start=(kc == 0), stop=(kc == 7)),
                         reads=[("ring", slot), ("hT", kc)], writes=[GN[ga]])
                for kc in range(8):
                    S.op(PE, lambda kc=kc, jj=jj, gb=gb, rv=rv: nc.tensor.matmul(G[gb], lhsT=rv[:, kc, 256 + jj * 128:256 + (jj + 1) * 128], rhs=hT[:, kc, :],
                                                                                  start=(kc == 0), stop=(kc == 7)),
                         reads=[("ring", slot), ("hT", kc)], writes=[GN[gb]])
                tt, tn = (tmpA, "tmpA") if j % 2 == 0 else (tmpB, "tmpB")
                S.op(ACT, lambda ga=ga, tt=tt: nc.scalar.activation(out=tt[:, :], in_=G[ga], func=AF.Silu), reads=[GN[ga]], writes=[tn])
                S.op(DVE, lambda j=j, gb=gb, tt=tt: nc.vector.tensor_tensor(out=hid[:, j, :], in0=tt[:, :], in1=G[gb], op=ALU.mult),
                     reads=[tn, GN[gb]], writes=[("hid", j)])
        for h in range(2):
            S.op(SP, lambda h=h: nc.sync.dma_start(out=dn[h], in_=s_dn[idx][h]),
                 reads=[("scr", "dn%d_%d" % (idx, h))], writes=[("dn", h)], dma_key="dn%d" % h, cost=18000)
            for s in range(4):
                b = 4 + (s % 2)
                for j in range(NJ):
                    S.op(PE, lambda j=j, s=s, h=h, b=b: nc.tensor.matmul(G[b], lhsT=hid[:, j, s * 128:(s + 1) * 128], rhs=dn[h][:, j, :],
                                                                         start=(j == 0), stop=(j == NJ - 1)),
                         reads=[("hid", j), ("dn", h)], writes=[GN[b]])
                S.op(DVE, lambda s=s, h=h, b=b: nc.vector.scalar_tensor_tensor(out=X[:, s, h * 512:(h + 1) * 512], in0=G[b], scalar=0.5,
                                                                               in1=X[:, s, h * 512:(h + 1) * 512], op0=ALU.mult, op1=ALU.add),
                     reads=[GN[b], xres], writes=[xres])

    def rope_tables(t, nh, tabc, tabs, tabres):
        S.op(SP, lambda: nc.sync.dma_start(out=posb[:, :, :], in_=pos[t * T:(t + 1) * T, :].rearrange("(s p) a -> p s a", p=128)),
             writes=["posb"], dma_key="posb")
        for a in range(2):
            S.op(POOL, lambda a=a: nc.gpsimd.tensor_tensor(out=ang[:, :, a, :], in0=posb[:, :, a:a + 1].to_broadcast([128, 4, 16]),
                                                           in1=inv_bc[:, :].unsqueeze(1).to_broadcast([128, 4, 16]), op=ALU.mult),
                 reads=["posb", "inv"], writes=["ang"])
        angf = ang[:, :, :, :].rearrange("p s a f -> p s (a f)")
        angmf = angm[:, :, :, :].rearrange("p s a f -> p s (a f)")
        TWO_PI = 2.0 * math.pi

        def sin_of(dst, shift):
            S.op(DVE, lambda: nc.vector.tensor_scalar(out=angmf, in0=angf, scalar1=shift, scalar2=1.0 / TWO_PI, op0=ALU.add, op1=ALU.mult),
                 reads=["ang"], writes=["angm"])
            S.op(DVE, lambda: nc.vector.tensor_copy(out=angki[:, :, :], in_=angmf), reads=["angm"], writes=["angki"])
            S.op(DVE, lambda: nc.vector.tensor_copy(out=angkf[:, :, :], in_=angki[:, :, :]), reads=["angki"], writes=["angkf"])
            S.op(DVE, lambda: nc.vector.tensor_scalar(out=angmf, in0=angf, scalar1=shift, scalar2=None, op0=ALU.add),
                 reads=["ang", "angki"], writes=["angm"])
            S.op(DVE, lambda: nc.vector.scalar_tensor_tensor(out=angr[:, :, :], in0=angkf[:, :, :], scalar=-TWO_PI, in1=angmf, op0=ALU.mult, op1=ALU.add),
                 reads=["angkf", "angm"], writes=["angr"])
            S.op(DVE, lambda: nc.vector.tensor_scalar(out=angmf, in0=angr[:, :, :], scalar1=math.pi, scalar2=TWO_PI, op0=ALU.is_gt, op1=ALU.mult),
                 reads=["angr"], writes=["angm"])
            S.op(DVE, lambda: nc.vector.tensor_tensor(out=angr[:, :, :], in0=angr[:, :, :], in1=angmf, op=ALU.subtract),
                 reads=["angr", "angm"], writes=["angr"])
            S.op(ACT, lambda: nc.scalar.activation(out=dst[:, :, :], in_=angr[:, :, :], func=AF.Sin), reads=["angr"], writes=[("cs" if dst is cs else "sn")])

        sin_of(sn, 0.0)
        sin_of(cs, 0.5 * math.pi)
        S.op(POOL, lambda: nc.gpsimd.tensor_copy(out=tabc, in_=cs[:, :, :].unsqueeze(2).to_broadcast([128, 4, nh, 32])),
             reads=["cs"], writes=[tabres + "c"])
        S.op(POOL, lambda: nc.gpsimd.tensor_copy(out=tabs, in_=sn[:, :, :].unsqueeze(2).to_broadcast([128, 4, nh, 32])),
             reads=["sn"], writes=[tabres + "s"])

    def head_norm_rope(e, src, srcres, nh, gbc, gres, tabc, tabs, tabres, sq, sqres, dst, dstres):
        V = vec(e)
        SH = 4 * nh
        cb = 250 + SH * 64 * (0.9 if e == POOL else 0.55)
        ch = 250 + SH * 32 * (0.9 if e == POOL else 0.55)
        x3 = src.rearrange("p s (h d) -> p (s h) d", h=nh)
        sq3 = sq.rearrange("p s (h d) -> p (s h) d", h=nh)
        S.op(e, lambda: V.tensor_tensor(out=sq3, in0=x3, in1=x3, op=ALU.mult), reads=[srcres], writes=[sqres], cost=cb)
        S.op(DVE, lambda: nc.vector.tensor_reduce(out=hss[:, 0:SH], in_=sq3, axis=AX.X, op=ALU.add), reads=[sqres], writes=["hss"], cost=250 + SH * 64 * 0.55)
        S.op(ACT, lambda: nc.scalar.activation(out=hrs[:, 0:SH], in_=hss[:, 0:SH], func=AF.Sqrt, scale=1.0 / 64, bias=epsb[:, 0:1]),
             reads=["hss", "epsb"], writes=["hrs"], cost=300)
        S.op(DVE, lambda: nc.vector.reciprocal(out=hrs[:, 0:SH], in_=hrs[:, 0:SH]), reads=["hrs"], writes=["hrs"], cost=190)
        S.op(e, lambda: V.tensor_tensor(out=x3, in0=x3, in1=hrs[:, 0:SH].unsqueeze(2).to_broadcast([128, SH, 64]), op=ALU.mult),
             reads=[srcres, "hrs"], writes=[srcres], cost=cb)
        S.op(e, lambda: V.tensor_tensor(out=x3, in0=x3, in1=gbc[:, :].unsqueeze(1).to_broadcast([128, SH, 64]), op=ALU.mult),
             reads=[srcres, gres], writes=[srcres], cost=cb)
        pat = "p s (h a r f) -> p (s h) a r f"
        x5 = src.rearrange(pat, h=nh, a=2, r=2)
        q5 = sq.rearrange(pat, h=nh, a=2, r=2)
        d5 = dst.rearrange(pat, h=nh, a=2, r=2)
        xa, xb_ = x5[:, :, :, 0, :], x5[:, :, :, 1, :]
        ta, tb_ = q5[:, :, :, 0, :], q5[:, :, :, 1, :]
        c4 = tabc.rearrange("p s h (a f) -> p (s h) a f", a=2)
        s4 = tabs.rearrange("p s h (a f) -> p (s h) a f", a=2)
        oa, ob = d5[:, :, :, 0, :], d5[:, :, :, 1, :]
        S.op(e, lambda: V.tensor_tensor(out=ta, in0=xb_, in1=s4, op=ALU.mult), reads=[srcres, tabres + "s"], writes=[sqres], cost=ch)
        S.op(e, lambda: V.tensor_tensor(out=tb_, in0=xa, in1=s4, op=ALU.mult), reads=[srcres, tabres + "s"], writes=[sqres], cost=ch)
        S.op(e, lambda: V.tensor_tensor(out=xa, in0=xa, in1=c4, op=ALU.mult), reads=[srcres, sqres, tabres + "c"], writes=[srcres], cost=ch)
        S.op(e, lambda: V.tensor_tensor(out=xb_, in0=xb_, in1=c4, op=ALU.mult), reads=[srcres, sqres, tabres + "c"], writes=[srcres], cost=ch)
        S.op(e, lambda: V.tensor_tensor(out=oa, in0=xa, in1=ta, op=ALU.subtract), reads=[srcres, sqres], writes=[dstres], cost=ch)
        S.op(e, lambda: V.tensor_tensor(out=ob, in0=xb_, in1=tb_, op=ALU.add), reads=[srcres, sqres], writes=[dstres], cost=ch)

    def tok_rows(ap2d):
        return ap2d.rearrange("(s p) d -> p s d", p=128)

    dbg = {}

    apos[0] = 0
    hid = carve([128, NJ, T], BF16)
    dn = [carve([128, NJ, 512], BF16), carve([128, NJ, 512], BF16)]
    kvw = carve([128, 8, 256], BF16)
    kvs = carve([128, 2, 4, 128], F32)
    ksq = carve([128, 4, 128], F32)
    tkc = carve([128, 4, 2, 32], F32)
    tks = carve([128, 4, 2, 32], F32)
    krb = carve([128, 4, 128], BF16)
    kTb = [carve([128, T], BF16), carve([128, T], BF16)]
    vsb = [carve([128, 4, 130], BF16), carve([128, 4, 130], BF16)]

    S.op(SP, lambda: nc.sync.dma_start(out=kvw, in_=s_kv), reads=[("scr", "kv")], writes=["kvw"], dma_key="kvw")

    for t in range(NCT if stage >= 1 else 0):
        b = t % 2
        X, xres = Xb[b], "X%d" % b
        S.op(SP, lambda b=b, t=t: nc.sync.dma_start(out=Xb[b][:, :, :], in_=tok_rows(xc[t * T:(t + 1) * T, :])),
             writes=[xres], dma_key="xl%d" % b, cost=13600)
        if P1 >= 1:
            rmsnorm_T(X, xres, 0)
        if P1 >= 2:
            ffn(0, X, xres, hid, dn)
        if t < NOT:
            S.op(SP, lambda b=b, t=t: nc.sync.dma_start(out=tok_rows(x1s[t * T:(t + 1) * T, :]), in_=Xb[b][:, :, :]),
                 reads=[xres], writes=[("x1s", t)], dma_key="xs%d" % b, cost=13600)
        if P1 < 3:
            continue
        rmsnorm_T(X, xres, 1)
        if P1 < 4:
            continue
        rope_tables(t, 2, tkc, tks, "tk")
        if P1 < 5:
            continue
        for s in range(4):
            gb_ = s % 2
            for kc in range(8):
                S.op(PE, lambda kc=kc, s=s, gb_=gb_: nc.tensor.matmul(G[gb_][:, 0:256], lhsT=hT[:, kc, s * 128:(s + 1) * 128], rhs=kvw[:, kc, :],
                                                                       start=(kc == 0), stop=(kc == 7)),
                     reads=[("hT", kc), "kvw"], writes=[GN[gb_]], cost=200)
            S.op(ACT, lambda s=s, gb_=gb_: nc.scalar.copy(out=kvs[:, :, s, :], in_=G[gb_][:, 0:256].rearrange("p (a d) -> p a d", a=2)), reads=[GN[gb_]], writes=["kvs"])
        if P1 < 6:
            continue
        vb, vres = vsb[b], "vsb%d" % b
        kmt = km[:, t * 4:(t + 1) * 4]
        vb4 = vb.rearrange("p s (h e) -> p s h e", h=2)
        S.op(POOL, lambda vb4=vb4, kmt=kmt: nc.gpsimd.tensor_tensor(
            out=vb4[:, :, :, 0:64], in0=kvs[:, 1, :, :].rearrange("p s (h d) -> p s h d", h=2),
            in1=kmt.unsqueeze(2).unsqueeze(3).to_broadcast([128, 4, 2, 64]), op=ALU.mult),
            reads=["kvs", "km"], writes=[vres])
        S.op(POOL, lambda vb4=vb4, kmt=kmt: nc.gpsimd.tensor_copy(out=vb4[:, :, :, 64], in_=kmt.unsqueeze(2).to_broadcast([128, 4, 2])),
             reads=["km"], writes=[vres])
        S.op(SP, lambda vb=vb, t=t: nc.sync.dma_start(out=vss[t * T:(t + 1) * T, :].rearrange("(s p) e -> p s e", p=128), in_=vb),
             reads=[vres], writes=[("vss", t)], dma_key="vst%d" % b)
        if P1 < 7:
            continue
        head_norm_rope(POOL, kvs[:, 0, :, :], "kvs", 2, gk_bc, "gk", tkc, tks, "tk", ksq, "ksq", krb, "krb")
        for s in range(4):
            S.op(PE, lambda s=s: nc.tensor.transpose(out=tp[:, s * 128:(s + 1) * 128], in_=krb[:, s, :], identity=ident[:]),
                 reads=["krb", "ident"], writes=[("tpb", 0)], cost=118)
        kb_, kres = kTb[b], "kTb%d" % b
        S.op(DVE, lambda kb_=kb_: nc.vector.tensor_copy(out=kb_, in_=tp[:, 0:512]), reads=[("tpb", 0)], writes=[kres])
        S.op(SP, lambda kb_=kb_, t=t: nc.sync.dma_start(out=kts[:, t * T:(t + 1) * T], in_=kb_),
             reads=[kres], writes=[("kts", t)], dma_key="kst%d" % b)

    S.barrier()
    apos[0] = 0
    kT = carve([128, NCT * T], BF16)
    vAf = carve([128, NCH * 130 + 64], BF16)
    vA = vAf[:, 0:NCH * 130].rearrange("p (ch e) -> p ch e", e=130)
    qf = carve([128, 4, 512], F32)
    qsq = hb[:, :, :].rearrange("p s d -> p (s d)").bitcast(F32).rearrange("p (s d) -> p s d", s=4)
    tqc = carve([128, 4, 8, 32], F32)
    tqs = carve([128, 4, 8, 32], F32)
    qrb = carve([128, 4, 512], BF16)
    qTp = Xb[1][:, 0:2, :].rearrange("p a d -> p (a d)").bitcast(BF16).rearrange("p (h t) -> p h t", h=8)
    ub = carve([128, 4, 512], BF16)
    vnb = tqc.rearrange("p s h f -> p (s h f)").bitcast(BF16)[:, 0:2048].rearrange("p (s d) -> p s d", s=4)
    sgb = tqs.rearrange("p s h f -> p (s h f)").bitcast(BF16)[:, 0:2048].rearrange("p (s d) -> p s d", s=4)
    sgT = carve([128, 4, T], BF16)
    aT = carve([128, 8, T], BF16)
    mT = carve([128, 8, T], BF16)
    PT = [carve([128, 1024], BF16), carve([128, 1024], BF16)]
    rden = carve([128, T], F32)
    ones1 = carve([128, 64], F32)
    onT = carve([128, T], F32)
    ggv_bc = carve([128, 512], F32)
    S.op(POOL, lambda: nc.gpsimd.memset(ones1, 1.0), writes=["ones1"])
    S.op(POOL, lambda: nc.gpsimd.memset(qTp, 0.0), writes=[("qT", c) for c in range(4)])
    S.op(POOL, lambda: nc.gpsimd.memset(vAf[:, NCH * 130:NCH * 130 + 64], 0.0), writes=["vApad"])
    small_load(ggv_bc, bc_row(g_gv), "ggv")

    for c in range(0, NCT if stage >= 2 else 0, 8):
        n = min(8, NCT - c)
        S.op(SP, lambda c=c, n=n: nc.sync.dma_start(out=kT[:, c * T:(c + n) * T], in_=kts[:, c * T:(c + n) * T]),
             reads=[("kts", t) for t in range(c, c + n)], writes=["kT"], dma_key="kTl", cost=9000)
        S.op(SP, lambda c=c, n=n: nc.sync.dma_start(out=vA[:, c * 4:(c + n) * 4, :],
                                                    in_=vss[c * T:(c + n) * T, :].rearrange("(ch p) e -> p ch e", p=128)),
             reads=[("vss", t) for t in range(c, c + n)], writes=["vA"], dma_key="vAl", cost=15000)

    def attention():
        heads = [(c, g) for c in range(4) for g in range(2)]
        seq = [(hi, i) for hi in range(8) for i in range(NPAIR)]

        def qk(n):
            hi, i = seq[n]
            c, g = heads[hi]
            sb_ = n % 2
            Sx = (S0, S1)[sb_]
            for u in range(2):
                ch = 2 * i + u
                S.op(PE, lambda u=u, ch=ch, c=c, g=g, Sx=Sx: nc.tensor.matmul(
                    Sx[:, u * 512:(u + 1) * 512], lhsT=kT[:, ch * 128:(ch + 1) * 128],
                    rhs=qTp[:, c * 2 + g, :], start=True, stop=True),
                    reads=["kT", ("qT", c)], writes=[GN[2 * sb_], GN[2 * sb_ + 1]])

        qk(0)
        for n in range(len(seq)):
            hi, i = seq[n]
            c, g = heads[hi]
            sb_ = n % 2
            Sx = (S0, S1)[sb_]
            ob = hi % 2
            Ox = (O0, O1)[ob]
            if n + 1 < len(seq):
                qk(n + 1)
            S.op(ACT, lambda Sx=Sx, sb_=sb_: nc.scalar.activation(out=PT[sb_], in_=Sx[:, :], func=AF.Exp, scale=0.125),
                 reads=[GN[2 * sb_], GN[2 * sb_ + 1]], writes=["PT%d" % sb_], cost=1023)
            for u in range(2):
                ch = 2 * i + u
                S.op(PE, lambda u=u, ch=ch, g=g, Ox=Ox, sb_=sb_, i=i: nc.tensor.matmul(
                    Ox[:, :], lhsT=vAf[:, ch * 130 + g * 65:ch * 130 + g * 65 + 128], rhs=PT[sb_][:, u * 512:(u + 1) * 512],
                    start=(i == 0 and u == 0), stop=(i == NPAIR - 1 and u == 1)),
                    reads=["vA", "vApad", "PT%d" % sb_], writes=[GN[4 + ob]])
            if i == NPAIR - 1:
                h_true = g * 4 + c
                S.op(DVE, lambda Ox=Ox: nc.vector.reciprocal(out=rden[64:65, :], in_=Ox[64:65, :]), reads=[GN[4 + ob]], writes=["rden"], cost=2472)
                S.op(PE, lambda: nc.tensor.matmul(tpf[0:64, 0:512], lhsT=ones1[64:65, 0:64], rhs=rden[64:65, :], start=True, stop=True),
                     reads=["rden", "ones1"], writes=[("tpb", 0)], cost=970)
                S.op(ACT, lambda: nc.scalar.copy(out=onT[0:64, :], in_=tpf[0:64, 0:512]), reads=[("tpb", 0)], writes=["onT"])
                S.op(DVE, lambda Ox=Ox, h_true=h_true: nc.vector.tensor_tensor(out=aT[0:64, h_true, :], in0=Ox[0:64, :], in1=onT[0:64, :], op=ALU.mult),
                     reads=[GN[4 + ob], "onT"], writes=[("aT", h_true)])

    for t in range(NOT if stage >= 2 else 0):
        b = 0
        X, xres = Xb[b], "X%d" % b
        S.op(SP, lambda b=b, t=t: nc.sync.dma_start(out=Xb[b][:, :, :], in_=tok_rows(x1s[t * T:(t + 1) * T, :])),
             reads=[("x1s", t)], writes=[xres], dma_key="xl%d" % b, cost=13600)
        rmsnorm_T(X, xres, 1)
        rope_tables(t, 8, tqc, tqs, "tq")
        for pi in range(3):
            slot = ring_load("qgg%d" % pi, s_qgg[pi].rearrange("p kc c -> p (kc c)"), 4096)
            rv = ring[:, slot, :].rearrange("p (kc c) -> p kc c", kc=8)
            for s in range(4):
                gb_ = s % 2
                for kc in range(8):
                    S.op(PE, lambda kc=kc, s=s, gb_=gb_, rv=rv: nc.tensor.matmul(G[gb_], lhsT=hT[:, kc, s * 128:(s + 1) * 128], rhs=rv[:, kc, :],
                                                                                  start=(kc == 0), stop=(kc == 7)),
                         reads=[("hT", kc), ("ring", slot)], writes=[GN[gb_]])
                if pi == 0:
                    S.op(ACT, lambda s=s, gb_=gb_: nc.scalar.copy(out=qf[:, s, :], in_=G[gb_]), reads=[GN[gb_]], writes=["qf"])
                elif pi == 1:
                    S.op(ACT, lambda s=s, gb_=gb_: nc.scalar.activation(out=ub[:, s, :], in_=G[gb_], func=AF.Gelu), reads=[GN[gb_]], writes=["ub"])
                else:
                    S.op(ACT, lambda s=s, gb_=gb_: nc.scalar.activation(out=qf[:, s, :], in_=G[gb_], func=AF.Gelu), reads=[GN[gb_]], writes=["qf"])
            if pi == 0:
                head_norm_rope(POOL, qf, "qf", 8, gq_bc, "gq", tqc, tqs, "tq", qsq, "hb", qrb, "qrb")
                for c0 in range(0, 4, 2):
                    bank = (c0 // 2) % 2
                    for kk in range(2):
                        c = c0 + kk
                        for s in range(4):
                            col = bank * 1024 + kk * 512 + s * 128
                            S.op(PE, lambda c=c, s=s, col=col: nc.tensor.transpose(out=tp[:, col:col + 128], in_=qrb[:, s, c * 128:(c + 1) * 128],
                                                                                    identity=ident[:]),
                                 reads=["qrb", "ident"], writes=[("tpb", bank)], cost=118)
                    for kk in range(2):
                        c = c0 + kk
                        for g in range(2):
                            src_ps = tp[g * 64:(g + 1) * 64, bank * 1024 + kk * 512: bank * 1024 + (kk + 1) * 512]
                            dst = qTp[g * 64:(g + 1) * 64, c * 2 + g, :]
                            if bank == 0:
                                S.op(ACT, lambda src_ps=src_ps, dst=dst: nc.scalar.copy(out=dst, in_=src_ps), reads=[("tpb", bank)], writes=[("qT", c)])
                            else:
                                S.op(DVE, lambda src_ps=src_ps, dst=dst: nc.vector.tensor_copy(out=dst, in_=src_ps), reads=[("tpb", bank)], writes=[("qT", c)])
            if pi == 2:
                row_rstd(qf, "qf", 512, 512)
                for s in range(4):
                    S.op(DVE, lambda s=s: nc.vector.scalar_tensor_tensor(out=vnb[:, s, :], in0=qf[:, s, :], scalar=rstd[:, s:s + 1], in1=ggv_bc,
                                                                         op0=ALU.mult, op1=ALU.mult),
                         reads=["qf", "rstd", "ggv"], writes=["tqc"])
                for s in range(4):
                    gb_ = 2 + (s % 2)
                    for g in range(8):
                        S.op(PE, lambda s=s, g=g, gb_=gb_: nc.tensor.matmul(G[gb_][:, g * 64:(g + 1) * 64], lhsT=wsT[:, g, :], rhs=vnb[:, s, g * 64:(g + 1) * 64],
                                                                             start=True, stop=True),
                             reads=["tqc", "wsT"], writes=[GN[gb_]], cost=70)
                    tt, tn = (tmpA, "tmpA") if s % 2 == 0 else (tmpB, "tmpB")
                    S.op(DVE, lambda gb_=gb_, tt=tt: nc.vector.tensor_tensor(out=tt.rearrange("p (g c) -> p g c", g=8),
                                                                             in0=G[gb_].rearrange("p (g c) -> p g c", g=8),
                                                                             in1=bspT[:, :].unsqueeze(2).to_broadcast([128, 8, 64]), op=ALU.add),
                         reads=[GN[gb_], "bsp"], writes=[tn])
                    S.op(POOL, lambda s=s, tt=tt: nc.gpsimd.tensor_tensor(out=sgb[:, s, :], in0=tt, in1=ub[:, s, :], op=ALU.mult),
                         reads=[tn, "ub"], writes=["tqs"])
                transpose_T(sgb, "tqs", 4, sgT, "sgT")
        attention()
        for oc in range(8):
            slot = ring_load("mg%d" % oc, s_mg[oc], 3584)
            rg = ring[:, slot, 0:2048].rearrange("p (kc c) -> p kc c", kc=8)
            g0, g1 = (0, 1) if oc % 2 == 0 else (2, 3)
            for kc in range(8):
                S.op(PE, lambda kc=kc, rg=rg, g0=g0: nc.tensor.matmul(G[g0], lhsT=rg[:, kc, 0:128], rhs=hT[:, kc, :], start=(kc == 0), stop=(kc == 7)),
                     reads=[("ring", slot), ("hT", kc)], writes=[GN[g0]])
            for kc in range(8):
                S.op(PE, lambda kc=kc, rg=rg, g1=g1: nc.tensor.matmul(G[g1], lhsT=rg[:, kc, 128:256], rhs=hT[:, kc, :], start=(kc == 0), stop=(kc == 7)),
                     reads=[("ring", slot), ("hT", kc)], writes=[GN[g1]])
            for h in range(8):
                S.op(PE, lambda h=h, slot=slot: nc.tensor.matmul(G[4], lhsT=ring[0:64, slot, 2560 + h * 128:2560 + (h + 1) * 128], rhs=aT[0:64, h, :],
                                                                 start=(h == 0), stop=(h == 7)),
                     reads=[("ring", slot), ("aT", h)], writes=[GN[4]])
            for kc in range(4):
                S.op(PE, lambda kc=kc, slot=slot: nc.tensor.matmul(G[5], lhsT=ring[:, slot, 2048 + kc * 128:2048 + (kc + 1) * 128], rhs=sgT[:, kc, :],
                                                                   start=(kc == 0), stop=(kc == 3)),
                     reads=[("ring", slot), ("sgT", kc)], writes=[GN[5]])
            S.op(ACT, lambda g0=g0: nc.scalar.activation(out=tmpA, in_=G[g0], func=AF.Sigmoid), reads=[GN[g0]], writes=["tmpA"])
            S.op(ACT, lambda g1=g1: nc.scalar.activation(out=tmpB, in_=G[g1], func=AF.Sigmoid), reads=[GN[g1]], writes=["tmpB"])
            S.op(DVE, lambda: nc.vector.tensor_tensor(out=tmpA, in0=tmpA, in1=G[4], op=ALU.mult), reads=["tmpA", GN[4]], writes=["tmpA"])
            S.op(DVE, lambda: nc.vector.tensor_tensor(out=tmpB, in0=tmpB, in1=G[5], op=ALU.mult), reads=["tmpB", GN[5]], writes=["tmpB"])
            S.op(POOL, lambda oc=oc: nc.gpsimd.tensor_tensor(out=mT[:, oc, :], in0=tmpA, in1=tmpB, op=ALU.add),
                 reads=["tmpA", "tmpB"], writes=[("mT", oc)])
        for h in range(2):
            slot = ring_load("wo%d" % h, s_wo[h].rearrange("p kc c -> p (kc c)"), 4096)
            rv = ring[:, slot, :].rearrange("p (kc c) -> p kc c", kc=8)
            for s in range(4):
                gb_ = s % 2
                for kc in range(8):
                    S.op(PE, lambda kc=kc, s=s, gb_=gb_, rv=rv: nc.tensor.matmul(G[gb_], lhsT=mT[:, kc, s * 128:(s + 1) * 128], rhs=rv[:, kc, :],
                                                                                  start=(kc == 0), stop=(kc == 7)),
                         reads=[("mT", kc), ("ring", slot)], writes=[GN[gb_]])
                S.op(DVE, lambda s=s, h=h, gb_=gb_, X=X: nc.vector.tensor_tensor(out=X[:, s, h * 512:(h + 1) * 512], in0=G[gb_],
                                                                                  in1=X[:, s, h * 512:(h + 1) * 512], op=ALU.add),
                     reads=[GN[gb_], xres], writes=[xres])
        S.op(SP, lambda b=b, t=t: nc.sync.dma_start(out=tok_rows(x1s[t * T:(t + 1) * T, :]), in_=Xb[b][:, :, :]),
             reads=[xres], writes=[("x1s", t)], dma_key="xs%d" % b, cost=13600)

    S.barrier()
    apos[0] = 0
    hid = carve([128, NJ, T], BF16)
    dn = [carve([128, NJ, 512], BF16), carve([128, NJ, 512], BF16)]
    pf = carve([128, 4, PLE], F32)
    pbf = carve([128, 4, PLE], BF16)
    pT = carve([128, 2, T], BF16)
    gfin_bc = carve([128, D], F32)
    small_load(gfin_bc, bc_row(g_fin), "gfin")
    out_ops = []
    for t in range(NOT if stage >= 3 else 0):
        b = t % 2
        X, xres = Xb[b], "X%d" % b
        S.op(SP, lambda b=b, t=t: nc.sync.dma_start(out=Xb[b][:, :, :], in_=tok_rows(x1s[t * T:(t + 1) * T, :])),
             reads=[("x1s", t)], writes=[xres], dma_key="xl%d" % b, cost=13600)
        rmsnorm_T(X, xres, 2)
        ffn(1, X, xres, hid, dn)
        rmsnorm_T(X, xres, 3)
        S.op(SP, lambda t=t: nc.sync.dma_start(out=pf, in_=tok_rows(pin[t * T:(t + 1) * T, :])), writes=["pf"], dma_key="pfl")
        S.op(POOL, lambda: nc.gpsimd.tensor_copy(out=pbf, in_=pf), reads=["pf"], writes=["pbf"])
        transpose_T(pbf, "pbf", 2, pT, "pT")
        slotp = ring_load("pl", s_pl.rearrange("p kc c -> p (kc c)"), 2048)
        rvp = ring[:, slotp, 0:2048].rearrange("p (kc c) -> p kc c", kc=2)
        for h in range(2):
            slot = ring_load("pg%d" % h, s_pg[h].rearrange("p kc c -> p (kc c)"), 4096)
            rv = ring[:, slot, :].rearrange("p (kc c) -> p kc c", kc=8)
            for s in range(4):
                g0, g1 = (0, 1) if s % 2 == 0 else (2, 3)
                for kc in range(8):
                    S.op(PE, lambda kc=kc, s=s, g0=g0, rv=rv: nc.tensor.matmul(G[g0], lhsT=hT[:, kc, s * 128:(s + 1) * 128], rhs=rv[:, kc, :],
                                                                                start=(kc == 0), stop=(kc == 7)),
                         reads=[("hT", kc), ("ring", slot)], writes=[GN[g0]])
                for k2 in range(2):
                    S.op(PE, lambda k2=k2, s=s, g1=g1, h=h, rvp=rvp: nc.tensor.matmul(G[g1], lhsT=pT[:, k2, s * 128:(s + 1) * 128],
                                                                                       rhs=rvp[:, k2, h * 512:(h + 1) * 512],
                                                                                       start=(k2 == 0), stop=(k2 == 1)),
                         reads=[("pT", k2), ("ring", slotp)], writes=[GN[g1]])
                tt, tn = (tmpA, "tmpA") if s % 2 == 0 else (tmpB, "tmpB")
                S.op(ACT, lambda g0=g0, tt=tt: nc.scalar.activation(out=tt, in_=G[g0], func=AF.Sigmoid), reads=[GN[g0]], writes=[tn])
                S.op(DVE, lambda g1=g1, tt=tt: nc.vector.tensor_tensor(out=tt, in0=tt, in1=G[g1], op=ALU.mult), reads=[tn, GN[g1]], writes=[tn])
                S.op(POOL, lambda s=s, h=h, tt=tt, X=X: nc.gpsimd.tensor_tensor(out=X[:, s, h * 512:(h + 1) * 512], in0=X[:, s, h * 512:(h + 1) * 512],
                                                                                 in1=tt, op=ALU.add),
                     reads=[tn, xres], writes=[xres])
        row_rstd(X, xres, D, D)
        for s in range(4):
            S.op(DVE, lambda s=s, X=X: nc.vector.scalar_tensor_tensor(out=X[:, s, :], in0=X[:, s, :], scalar=rstd[:, s:s + 1], in1=gfin_bc,
                                                                      op0=ALU.mult, op1=ALU.mult),
                 reads=[xres, "rstd", "gfin"], writes=[xres])
        out_ops.append(S.op(SP, lambda b=b, t=t: nc.sync.dma_start(out=tok_rows(y[t * T:(t + 1) * T, :]), in_=Xb[b][:, :, :]),
                            reads=[xres], dma_key="ys%d" % b, cost=13600))
    if stage < 3 and stage >= 1:
        out_ops.append(S.op(SP, lambda: nc.sync.dma_start(out=y[:, :], in_=x1s[:, :]), reads=[("x1s", t) for t in range(NOT)], dma_key="dbg"))
    S.op(SP, None, extra=out_ops)

    if SCHEDULE:
        S.schedule()
        build_program.last_est_ns = S.est_ns
    semkeys = S.finalize()
    by_eng = {e: [o for o in S.ops if o.eng == e] for e in ENGS}
    with ExitStack() as es:
        sems = {k: es.enter_context(nc.semaphore("s%d" % i)) for i, k in enumerate(semkeys)}
        block = es.enter_context(nc.Block())

        def emit(engname, eng):
            waited = {}
            for o in by_eng[engname]:
                need = {}
                for d in o.deps:
                    if not d.signal or d.fn is None:
                        continue
                    if d.eng == PE and o.eng == PE and d.dma_key is None and o.dma_key is None:
                        continue
                    if need.get(d.sem, 0) < d.val:
                        need[d.sem] = d.val
                for k, v in need.items():
                    if waited.get(k, 0) < v:
                        eng.wait_ge(sems[k], v)
                        waited[k] = v
                if o.fn is not None:
                    ins = o.fn()
                    if o.signal:
                        ins.then_inc(sems[o.sem], 16 if o.dma_key is not None else 1)
                else:
                    assert not o.signal

        @block.tensor
        def _(e):
            emit(PE, e)

        @block.scalar
        def _(e):
            emit(ACT, e)

        @block.vector
        def _(e):
            emit(DVE, e)

        @block.gpsimd
        def _(e):
            emit(POOL, e)

        @block.sync
        def _(e):
            emit(SP, e)
    return nc


NCT_FULL, NOT_FULL = 32, 8
WNAMES = ["g_ffn1", "w_ffn1_gu", "w_ffn1_down", "g_mix", "w_in", "g_q", "g_k", "g_gmlp_v", "w_spatial", "b_spatial",
          "w_branch_attn", "w_branch_gmlp", "w_out", "g_ffn2", "w_ffn2_gu", "w_ffn2_down", "g_ple", "w_ple_gate", "w_ple", "g_final"]


def _pos_table(tok_idx):
    tok_idx = np.asarray(tok_idx, np.int64)
    return np.stack([tok_idx // 64, tok_idx % 64], axis=1).astype(np.float32)


def kernel(**inputs):
    xp = np.asarray(inputs["x_prompt"], np.float32)
    xs = np.asarray(inputs["x_sample"], np.float32)
    pp = np.asarray(inputs["p_prompt"], np.float32)
    ps = np.asarray(inputs["p_sample"], np.float32)
    w = {k: np.ascontiguousarray(np.asarray(inputs[k], np.float32)) for k in WNAMES}
    w["g_final"] = w["g_final"].reshape(1, D)
    NTOK = NCT_FULL * T
    own = NOT_FULL * T
    in_maps = []
    for c in range(8):
        if c < 4:
            order = [c] + [(c + k) % 4 for k in range(1, 4)]
            xcx = np.concatenate([xp[o] for o in order], axis=0)
            posi = np.concatenate([np.arange(own)] * 4)
            msk = np.zeros(NTOK, np.float32); msk[:own] = 1.0
            pc = pp[0, c]
        else:
            q = c - 4
            order = [q] + [(q + k) % 4 for k in range(1, 4)]
            xcx = np.concatenate([xs[0, o * own:(o + 1) * own] for o in order], axis=0)
            posi = np.concatenate([np.arange(o * own, (o + 1) * own) for o in order])
            msk = np.ones(NTOK, np.float32)
            pc = ps[0, 0, q * own:(q + 1) * own]
        m = {"xc": np.ascontiguousarray(xcx), "pin": np.ascontiguousarray(pc), "pos": _pos_table(posi),
             "kmask": np.ascontiguousarray(msk.reshape(NTOK // 128, 128).T)}
        m.update(w)
        in_maps.append(m)
    nc = build_program(NCT_FULL, NOT_FULL)
    res = run_bass_kernel_spmd(nc, in_maps, core_ids=list(range(8)))
    ys = [np.asarray(res.results[c]["y"], np.float32) for c in range(8)]
    y_prompt = np.stack(ys[0:4], axis=0)
    y_sample = np.concatenate(ys[4:8], axis=0)[None]
    return (y_prompt, y_sample)
```

```python
import math
SCHEDULE = True
P0 = 255
P1 = 99
P1SUB = 99
P1T = 0
from contextlib import ExitStack
import numpy as np
import concourse.bass as bass
import concourse.mybir as mybir
from concourse.bass_utils import run_bass_kernel_spmd

F32 = mybir.dt.float32
BF16 = mybir.dt.bfloat16
I32 = mybir.dt.int32
AF = mybir.ActivationFunctionType
ALU = mybir.AluOpType
AX = mybir.AxisListType

D = 1024
DFF = 2816
NJ = DFF // 128
PLE = 256
EPS = 1e-6
T = 512
NSLOT = 3
SLOTW = 4096
PE, ACT, DVE, POOL, SP = "pe", "act", "dve", "pool", "sp"
ENGS = [PE, ACT, DVE, POOL, SP]


class Op:
    __slots__ = ("eng", "fn", "deps", "dma_key", "sem", "val", "signal", "idx", "cost", "phase", "t0", "t1")

    def __init__(self, eng, fn, deps, dma_key, cost):
        self.eng, self.fn, self.deps, self.dma_key, self.cost = eng, fn, deps, dma_key, cost
        self.sem = None
        self.val = 0
        self.signal = False


DEFAULT_COST = {PE: 216, ACT: 600, DVE: 650, POOL: 900, SP: 2500}


class Sched:
    def __init__(self):
        self.ops = []
        self.last_write = {}
        self.readers = {}
        self.phase = 0
        self.since_barrier = []
        self.cur_barrier = {}

    def op(self, eng, fn, reads=(), writes=(), dma_key=None, extra=(), cost=None):
        deps = set(extra)
        for r in reads:
            w = self.last_write.get(r)
            if w is not None:
                deps.add(w)
        for w_ in writes:
            w = self.last_write.get(w_)
            if w is not None:
                deps.add(w)
            for rd in self.readers.get(w_, ()):
                deps.add(rd)
        bar = self.cur_barrier.get(eng)
        if bar is not None:
            deps.add(bar)
        o = Op(eng, fn, deps, dma_key, DEFAULT_COST[eng] if cost is None else cost)
        o.idx = len(self.ops)
        o.phase = self.phase
        o.sem = (eng, self.phase) if dma_key is None else ("dma", dma_key)
        self.ops.append(o)
        for r in reads:
            self.readers.setdefault(r, []).append(o)
        for w_ in writes:
            self.last_write[w_] = o
            self.readers[w_] = []
        if fn is not None:
            self.since_barrier.append(o)
        return o

    def barrier(self):
        prev = list(self.since_barrier)
        self.since_barrier = []
        for e in ENGS:
            self.cur_barrier[e] = self.op(e, None, extra=prev, cost=0)
        self.phase += 1

    def schedule(self):
        import heapq
        n = len(self.ops)
        succ = [[] for _ in range(n)]
        indeg = [0] * n
        for o in self.ops:
            indeg[o.idx] = len(o.deps)
            for d in o.deps:
                succ[d.idx].append(o)
        pending = {e: [] for e in ENGS}
        avail = {e: [] for e in ENGS}
        free = {e: 0.0 for e in ENGS}
        ready_t = [0.0] * n
        for o in self.ops:
            if indeg[o.idx] == 0:
                heapq.heappush(pending[o.eng], (0.0, o.idx))
        order = []
        done = 0
        while done < n:
            best = None
            for e in ENGS:
                pe_, av = pending[e], avail[e]
                while pe_ and pe_[0][0] <= free[e]:
                    heapq.heappush(av, heapq.heappop(pe_)[1])
                if av:
                    cand = (free[e], av[0], e, True)
                elif pe_:
                    cand = (pe_[0][0], pe_[0][1], e, False)
                else:
                    continue
                if best is None or cand[:2] < best[:2]:
                    best = cand
            assert best is not None, "dependency cycle"
            start, idx, e, from_av = best
            if from_av:
                heapq.heappop(avail[e])
            else:
                heapq.heappop(pending[e])
            o = self.ops[idx]
            o.t0 = start
            if o.dma_key is not None:
                free[e] = start + 60.0
                o.t1 = start + o.cost
            else:
                o.t1 = start + o.cost
                free[e] = o.t1
            order.append(o)
            done += 1
            for sc in succ[idx]:
                if ready_t[sc.idx] < o.t1:
                    ready_t[sc.idx] = o.t1
                indeg[sc.idx] -= 1
                if indeg[sc.idx] == 0:
                    heapq.heappush(pending[sc.eng], (ready_t[sc.idx], sc.idx))
        self.ops = order
        self.est_ns = max(o.t1 for o in order)

    def finalize(self):
        for o in self.ops:
            for d in o.deps:
                if d.eng == PE and o.eng == PE and d.dma_key is None and o.dma_key is None:
                    continue
                if d.fn is None:
                    continue
                d.signal = True
        for o in self.ops:
            if o.dma_key is not None and o.fn is not None:
                o.signal = True
        counts = {}
        for o in self.ops:
            if o.signal:
                counts[o.sem] = counts.get(o.sem, 0) + (16 if o.dma_key is not None else 1)
                o.val = counts[o.sem]
        return sorted(counts.keys(), key=str)


def build_program(NCT, NOT, stage=3):
    NCH = NCT * 4
    NPAIR = NCH // 2
    nc = bass.Bass("TRN2", target_bir_lowering=False)

    def din(name, shape, dt=F32):
        return nc.dram_tensor(name, list(shape), dt, kind="ExternalInput").ap()

    xc = din("xc", [NCT * T, D])
    pin = din("pin", [NOT * T, PLE])
    pos = din("pos", [NCT * T, 2])
    kmask = din("kmask", [128, NCH])
    g_ffn1 = din("g_ffn1", [1, D]); w1gu = din("w_ffn1_gu", [1, D, 2 * DFF]); w1d = din("w_ffn1_down", [1, DFF, D])
    g_mix = din("g_mix", [1, D]); w_in = din("w_in", [1, D, 3840])
    g_q = din("g_q", [1, 64]); g_k = din("g_k", [1, 64]); g_gv = din("g_gmlp_v", [1, 512])
    w_sp = din("w_spatial", [1, 8, 128, 128]); b_sp = din("b_spatial", [1, 8, 128])
    w_ba = din("w_branch_attn", [1, 512, D]); w_bg = din("w_branch_gmlp", [1, 512, D]); w_out = din("w_out", [1, D, D])
    g_ffn2 = din("g_ffn2", [1, D]); w2gu = din("w_ffn2_gu", [1, D, 2 * DFF]); w2d = din("w_ffn2_down", [1, DFF, D])
    g_ple = din("g_ple", [1, D]); w_pg = din("w_ple_gate", [1, D, D]); w_pl = din("w_ple", [1, PLE, D])
    g_fin = din("g_final", [1, D])
    y = nc.dram_tensor("y", [NOT * T, D], F32, kind="ExternalOutput").ap()

    def dscr(name, shape, dt=BF16):
        return nc.dram_tensor(name, list(shape), dt).ap()

    s_gu = [dscr("s_gu1", [11, 128, 8, 512]), dscr("s_gu2", [11, 128, 8, 512])]
    s_dn = [dscr("s_dn1", [2, 128, NJ, 512]), dscr("s_dn2", [2, 128, NJ, 512])]
    s_kv = dscr("s_kv", [128, 8, 256])
    s_qgg = dscr("s_qgg", [3, 128, 8, 512])
    s_mg = dscr("s_mg", [8, 128, 3584])
    s_wo = dscr("s_wo", [2, 128, 8, 512])
    s_pg = dscr("s_pg", [2, 128, 8, 512])
    s_pl = dscr("s_pl", [128, 2, 1024])
    x1s = dscr("x1s", [NOT * T, D], F32)
    kts = dscr("kts", [128, NCT * T])
    vss = dscr("vss", [NCT * T, 130])

    S = Sched()

    def sb(name, shape, dt):
        return nc.alloc_sbuf_tensor(name, list(shape), dt)

    ident = sb("ident", [128, 128], BF16)
    gcol = sb("gcol", [128, 4, 8], F32)
    gq_bc = sb("gq_bc", [128, 64], F32)
    gk_bc = sb("gk_bc", [128, 64], F32)
    bspT = sb("bspT", [128, 8], F32)
    wsT = sb("wsT", [128, 8, 128], BF16)
    inv_bc = sb("inv_bc", [128, 16], F32)
    km = sb("km", [128, NCH], F32)
    negpi = sb("negpi", [128, 1], F32)
    epsb = sb("epsb", [128, 1], F32)
    ring = sb("ring", [128, NSLOT, SLOTW], BF16)
    Xb = [sb("X0", [128, 4, D], F32), sb("X1", [128, 4, D], F32)]
    hb = sb("hb", [128, 4, D], BF16)
    hT = sb("hT", [128, 8, T], BF16)
    ss = sb("ss", [128, 8], F32)
    rstd = sb("rstd", [128, 8], F32)
    tmpA = sb("tmpA", [128, T], F32)[:, :]
    tmpB = sb("tmpB", [128, T], F32)[:, :]
    posb = sb("posb", [128, 4, 2], F32)
    ang = sb("ang", [128, 4, 2, 16], F32)
    angm = sb("angm", [128, 4, 2, 16], F32)
    cs = sb("cs", [128, 4, 32], F32)
    sn = sb("sn", [128, 4, 32], F32)
    angki = sb("angki", [128, 4, 32], I32)
    angkf = sb("angkf", [128, 4, 32], F32)
    angr = sb("angr", [128, 4, 32], F32)
    hss = sb("hss", [128, 32], F32)
    hrs = sb("hrs", [128, 32], F32)
    ARENA_BYTES = 124 * 1024
    arena = sb("arena", [128, ARENA_BYTES // 4], F32)
    apos = [0]

    def carve(shape, dt):
        esz = 4 if dt in (F32, I32) else 2
        n = int(np.prod(shape[1:]))
        nbytes = (n * esz + 31) // 32 * 32
        off = apos[0]
        apos[0] += nbytes
        assert apos[0] <= ARENA_BYTES, (apos[0], ARENA_BYTES)
        v = arena[:, off // 4:(off + nbytes) // 4]
        if esz == 2:
            v = v.bitcast(BF16)
        v = v[:, 0:n]
        if len(shape) == 3:
            v = v.rearrange("p (a b) -> p a b", a=shape[1])
        elif len(shape) == 4:
            v = v.rearrange("p (a b c) -> p a b c", a=shape[1], b=shape[2])
        return v

    tp = nc.alloc_psum_tensor("tp", [128, 2048], BF16)
    S0 = nc.alloc_psum_tensor("S0", [128, 1024], F32)
    S1 = nc.alloc_psum_tensor("S1", [128, 1024], F32)
    O0 = nc.alloc_psum_tensor("O0", [128, 512], F32)
    O1 = nc.alloc_psum_tensor("O1", [128, 512], F32)
    G = [S0[:, 0:512], S0[:, 512:1024], S1[:, 0:512], S1[:, 512:1024], O0[:, :], O1[:, :]]
    GN = ["G0", "G1", "G2", "G3", "G4", "G5"]
    tpf = tp[:, :].bitcast(F32)

    def vec(e):
        return nc.vector if e == DVE else nc.gpsimd

    def cast(key, out_ap, in_ap):
        if not (P0 & 8):
            return None
        return S.op(POOL, lambda o=out_ap, i=in_ap: nc.gpsimd.dma_start(out=o, in_=i),
                    writes=[("scr", key)], dma_key="c_" + key, cost=9000)

    def small_load(out_ap, in_ap, res):
        def f():
            with nc.allow_non_contiguous_dma(reason="tiny constant layout load"):
                return nc.sync.dma_start(out=out_ap, in_=in_ap)
        return S.op(SP, f, writes=[res], dma_key="k_" + str(res).replace("'", "").replace(" ", ""))

    def bc_row(ap2d):
        return ap2d.partition_broadcast(128).rearrange("p o d -> p (o d)")

    for i, g in enumerate([g_ffn1, g_mix, g_ffn2, g_ple] if P0 & 1 else []):
        small_load(gcol[:, i, :], g.rearrange("o (kc p) -> p (o kc)", p=128), ("gcol", i))
    if P0 & 2:
        small_load(gq_bc[:, :], bc_row(g_q), "gq")
        small_load(gk_bc[:, :], bc_row(g_k), "gk")
    if P0 & 4:
        small_load(bspT[:, :], b_sp.rearrange("o g p -> p (o g)"), "bsp")
    small_load(km[:, :], kmask[:, :], "km")

    def mk_ident():
        nc.gpsimd.memset(ident[:], 0.0)
        return nc.gpsimd.affine_select(out=ident[:], in_=ident[:], pattern=[[-1, 128]], compare_op=ALU.not_equal,
                                       fill=1.0, base=0, channel_multiplier=1)
    S.op(POOL, mk_ident, writes=["ident"])
    S.op(POOL, lambda: nc.gpsimd.memset(negpi[:], -math.pi), writes=["negpi"])
    S.op(POOL, lambda: nc.gpsimd.memset(epsb[:], EPS), writes=["epsb"])
    S.op(POOL, lambda: nc.gpsimd.iota(out=cs[:, 0, 0:16].bitcast(I32), pattern=[[1, 16]], base=0, channel_multiplier=0),
         writes=["cs"])
    S.op(POOL, lambda: nc.gpsimd.tensor_copy(out=sn[:, 0, 0:16], in_=cs[:, 0, 0:16].bitcast(I32)), reads=["cs"], writes=["sn"])
    S.op(ACT, lambda: nc.scalar.activation(out=inv_bc[:, :], in_=sn[:, 0, 0:16], func=AF.Exp, scale=-math.log(10000.0) / 16.0),
         reads=["sn"], writes=["inv"])

    S.op(SP, lambda: nc.sync.dma_start(out=Xb[0][:, 0, :].rearrange("p (g q) -> p g q", g=8),
                                       in_=w_sp[0].rearrange("g p q -> p g q")), writes=["X0"], dma_key="const")
    S.op(DVE, lambda: nc.vector.tensor_copy(out=hb[:, 0, :], in_=Xb[0][:, 0, :]), reads=["X0"], writes=["hb"])
    for g in range(8):
        S.op(PE, lambda g=g: nc.tensor.transpose(out=tp[:, g * 128:(g + 1) * 128], in_=hb[:, 0, g * 128:(g + 1) * 128], identity=ident[:]),
             reads=["hb", "ident"], writes=[("tpb", 0)], cost=118)
    S.op(DVE, lambda: nc.vector.tensor_copy(out=wsT[:, :, :].rearrange("q g p -> q (g p)"), in_=tp[:, 0:1024]),
         reads=[("tpb", 0)], writes=["wsT"])

    def kcp(ap2d):
        return ap2d.rearrange("(kc p) c -> p kc c", p=128)

    def cast_ffn(idx, wgu, wd):
        for i in range(11):
            cast("gu%d_%d" % (idx, i), s_gu[idx][i, :, :, 0:256], kcp(wgu[0, :, 256 * i:256 * i + 256]))
            cast("gu%d_%d" % (idx, i), s_gu[idx][i, :, :, 256:512], kcp(wgu[0, :, DFF + 256 * i:DFF + 256 * i + 256]))
        for h in range(2):
            cast("dn%d_%d" % (idx, h), s_dn[idx][h], kcp(wd[0, :, 512 * h:512 * h + 512]))

    cast_ffn(0, w1gu, w1d)
    cast("kv", s_kv, kcp(w_in[0, :, 512:768]))
    for g in range(2):
        for c in range(4):
            cast("qgg0", s_qgg[0, :, :, c * 128 + g * 64:c * 128 + g * 64 + 64], kcp(w_in[0, :, (g * 4 + c) * 64:(g * 4 + c) * 64 + 64]))
    for i, c0 in ((1, 768), (2, 1280)):
        cast("qgg%d" % i, s_qgg[i], kcp(w_in[0, :, c0:c0 + 512]))
    for oc in range(8):
        k = "mg%d" % oc
        gts = s_mg[oc, :, 0:2048].rearrange("p (kc c) -> p kc c", kc=8)
        cast(k, gts[:, :, 0:128], kcp(w_in[0, :, 1792 + oc * 128:1792 + oc * 128 + 128]))
        cast(k, gts[:, :, 128:256], kcp(w_in[0, :, 2816 + oc * 128:2816 + oc * 128 + 128]))
        cast(k, s_mg[oc, :, 2048:2560].rearrange("p (kc c) -> p kc c", kc=4), kcp(w_bg[0, :, oc * 128:oc * 128 + 128]))
        cast(k, s_mg[oc, 0:64, 2560:3584].rearrange("p (h c) -> p h c", h=8),
             w_ba[0, :, oc * 128:oc * 128 + 128].rearrange("(h p) c -> p h c", p=64))
    for h in range(2):
        cast("wo%d" % h, s_wo[h], kcp(w_out[0, :, 512 * h:512 * h + 512]))
    cast_ffn(1, w2gu, w2d)
    for h in range(2):
        cast("pg%d" % h, s_pg[h], kcp(w_pg[0, :, 512 * h:512 * h + 512]))
    cast("pl", s_pl, kcp(w_pl[0, :, :]))

    ring_n = [0]

    def ring_load(key, src_ap, width):
        slot = ring_n[0] % NSLOT
        ring_n[0] += 1
        S.op(SP, lambda s=slot, a=src_ap, w=width: nc.sync.dma_start(out=ring[:, s, 0:w], in_=a),
             reads=[("scr", key)], writes=[("ring", slot)], dma_key="ring%d" % slot, cost=2000 + width * 256 // 180)
        return slot

    def transpose_T(src, srcres, nkc, outT, outres, gi=None):
        for k0 in range(0, nkc, 2):
            bank = (k0 // 2) % 2
            for kk in range(2):
                kc = k0 + kk
                for s in range(4):
                    col = bank * 1024 + kk * 512 + s * 128
                    S.op(PE, lambda kc=kc, s=s, col=col: nc.tensor.transpose(out=tp[:, col:col + 128],
                                                                            in_=src[:, s, kc * 128:(kc + 1) * 128], identity=ident[:]),
                         reads=[srcres, "ident"], writes=[("tpb", bank)], cost=118)
            for kk in range(2):
                kc = k0 + kk
                e = ACT if bank == 0 else DVE
                src_ps = tp[:, bank * 1024 + kk * 512: bank * 1024 + (kk + 1) * 512]
                if gi is None:
                    if e == ACT:
                        f = lambda kc=kc, src_ps=src_ps: nc.scalar.copy(out=outT[:, kc, :], in_=src_ps)
                    else:
                        f = lambda kc=kc, src_ps=src_ps: nc.vector.tensor_copy(out=outT[:, kc, :], in_=src_ps)
                    rd = [("tpb", bank)]
                else:
                    if e == ACT:
                        f = lambda kc=kc, src_ps=src_ps: nc.scalar.activation(out=outT[:, kc, :], in_=src_ps, func=AF.Copy,
                                                                               scale=gcol[:, gi, kc:kc + 1])
                    else:
                        f = lambda kc=kc, src_ps=src_ps: nc.vector.tensor_scalar(out=outT[:, kc, :], in0=src_ps,
                                                                                  scalar1=gcol[:, gi, kc:kc + 1], scalar2=None, op0=ALU.mult)
                    rd = [("tpb", bank), ("gcol", gi)]
                S.op(e, f, reads=rd, writes=[(outres, kc)])

    def row_rstd(X, xres, width, nrm, hbuf=None, hbres="hb", c0=0):
        hbuf = hb if hbuf is None else hbuf
        for s in range(4):
            S.op(ACT, lambda s=s: nc.scalar.activation(out=hbuf[:, s, 0:width], in_=X[:, s, 0:width], func=AF.Square, accum_out=ss[:, c0 + s:c0 + s + 1]),
                 reads=[xres], writes=[hbres, ("ss", c0 + s)], cost=1056 if width > 512 else 843)
        S.op(ACT, lambda: nc.scalar.activation(out=rstd[:, c0:c0 + 4], in_=ss[:, c0:c0 + 4], func=AF.Sqrt, scale=1.0 / nrm, bias=epsb[:, 0:1]),
             reads=[("ss", c0 + s) for s in range(4)] + ["epsb"], writes=[("rstd", c0)], cost=300)
        S.op(DVE, lambda: nc.vector.reciprocal(out=rstd[:, c0:c0 + 4], in_=rstd[:, c0:c0 + 4]), reads=[("rstd", c0)], writes=[("rstd", c0)], cost=190)

    def rmsnorm_T(X, xres, gi, alt=None):
        hbuf, hbres, outT, outres = (hb, "hb", hT, "hT") if alt is None else alt
        c0 = 0 if alt is None else 4
        row_rstd(X, xres, D, D, hbuf, hbres, c0)
        for s in range(4):
            S.op(DVE, lambda s=s: nc.vector.tensor_scalar(out=hbuf[:, s, :], in0=X[:, s, :], scalar1=rstd[:, c0 + s:c0 + s + 1], scalar2=None, op0=ALU.mult),
                 reads=[xres, ("rstd", c0)], writes=[hbres])
        transpose_T(hbuf, hbres, 8, outT, outres, gi)

    def ffn(idx, X, xres, hid, dn):
        for i in range(11):
            slot = ring_load("gu%d_%d" % (idx, i), s_gu[idx][i].rearrange("p kc c -> p (kc c)"), 4096)
            rv = ring[:, slot, :].rearrange("p (kc c) -> p kc c", kc=8)
            for jj in range(2):
                j = 2 * i + jj
                ga, gb = (0, 1) if j % 2 == 0 else (2, 3)
                for kc in range(8):
                    S.op(PE, lambda kc=kc, jj=jj, ga=ga, rv=rv: nc.tensor.matmul(G[ga], lhsT=rv[:, kc, jj * 128:(jj + 1) * 128], rhs=hT[:, kc, :],
                                                                                  start=(kc == 0), stop=(kc == 7)),
                         reads=[("ring", slot), ("hT", kc)], writes=[GN[ga]])
                for kc in range(8):
                    S.op(PE, lambda kc=kc, jj=jj, gb=gb, rv=rv: nc.tensor.matmul(G[gb], lhsT=rv[:, kc, 256 + jj * 128:256 + (jj + 1) * 128], rhs=hT[:, kc, :],
                                                                                  start=(kc == 0), stop=(kc == 7)),
                         reads=[("ring", slot), ("hT", kc)], writes=[GN[gb]])
                tt, tn = (tmpA, "tmpA") if j % 2 == 0 else (tmpB, "tmpB")
                S.op(ACT, lambda ga=ga, tt=tt: nc.scalar.activation(out=tt[:, :], in_=G[ga], func=AF.Silu), reads=[GN[ga]], writes=[tn])
                S.op(DVE, lambda j=j, gb=gb, tt=tt: nc.vector.tensor_tensor(out=hid[:, j, :], in0=tt[:, :], in1=G[gb], op=ALU.mult),
                     reads=[tn, GN[gb]], writes=[("hid", j)])
        for h in range(2):
            S.op(SP, lambda h=h: nc.sync.dma_start(out=dn[h], in_=s_dn[idx][h]),
                 reads=[("scr", "dn%d_%d" % (idx, h))], writes=[("dn", h)], dma_key="dn%d" % h, cost=18000)
            for s in range(4):
                b = 4 + (s % 2)
                for j in range(NJ):
                    S.op(PE, lambda j=j, s=s, h=h, b=b: nc.tensor.matmul(G[b], lhsT=hid[:, j, s * 128:(s + 1) * 128], rhs=dn[h][:, j, :],
                                                                         start=(j == 0), stop=(j == NJ - 1)),
                         reads=[("hid", j), ("dn", h)], writes=[GN[b]])
                S.op(DVE, lambda s=s, h=h, b=b: nc.vector.scalar_tensor_tensor(out=X[:, s, h * 512:(h + 1) * 512], in0=G[b], scalar=0.5,
                                                                               in1=X[:, s, h * 512:(h + 1) * 512], op0=ALU.mult, op1=ALU.add),
                     reads=[GN[b], xres], writes=[xres])

    def rope_tables(t, nh, tabc, tabs, tabres):
        S.op(SP, lambda: nc.sync.dma_start(out=posb[:, :, :], in_=pos[t * T:(t + 1) * T, :].rearrange("(s p) a -> p s a", p=128)),
             writes=["posb"], dma_key="posb")
        for a in range(2):
            S.op(POOL, lambda a=a: nc.gpsimd.tensor_tensor(out=ang[:, :, a, :], in0=posb[:, :, a:a + 1].to_broadcast([128, 4, 16]),
                                                           in1=inv_bc[:, :].unsqueeze(1).to_broadcast([128, 4, 16]), op=ALU.mult),
                 reads=["posb", "inv"], writes=["ang"])
        angf = ang[:, :, :, :].rearrange("p s a f -> p s (a f)")
        angmf = angm[:, :, :, :].rearrange("p s a f -> p s (a f)")
        TWO_PI = 2.0 * math.pi

        def sin_of(dst, shift):
            S.op(DVE, lambda: nc.vector.tensor_scalar(out=angmf, in0=angf, scalar1=shift, scalar2=1.0 / TWO_PI, op0=ALU.add, op1=ALU.mult),
                 reads=["ang"], writes=["angm"])
            S.op(DVE, lambda: nc.vector.tensor_copy(out=angki[:, :, :], in_=angmf), reads=["angm"], writes=["angki"])
            S.op(DVE, lambda: nc.vector.tensor_copy(out=angkf[:, :, :], in_=angki[:, :, :]), reads=["angki"], writes=["angkf"])
            S.op(DVE, lambda: nc.vector.tensor_scalar(out=angmf, in0=angf, scalar1=shift, scalar2=None, op0=ALU.add),
                 reads=["ang", "angki"], writes=["angm"])
            S.op(DVE, lambda: nc.vector.scalar_tensor_tensor(out=angr[:, :, :], in0=angkf[:, :, :], scalar=-TWO_PI, in1=angmf, op0=ALU.mult, op1=ALU.add),
                 reads=["angkf", "angm"], writes=["angr"])
            S.op(DVE, lambda: nc.vector.tensor_scalar(out=angmf, in0=angr[:, :, :], scalar1=math.pi, scalar2=TWO_PI, op0=ALU.is_gt, op1=ALU.mult),
                 reads=["angr"], writes=["angm"])
            S.op(DVE, lambda: nc.vector.tensor_tensor(out=angr[:, :, :], in0=angr[:, :, :], in1=angmf, op=ALU.subtract),
                 reads=["angr", "angm"], writes=["angr"])
            S.op(ACT, lambda: nc.scalar.activation(out=dst[:, :, :], in_=angr[:, :, :], func=AF.Sin), reads=["angr"], writes=[("cs" if dst is cs else "sn")])

        sin_of(sn, 0.0)
        sin_of(cs, 0.5 * math.pi)
        S.op(POOL, lambda: nc.gpsimd.tensor_copy(out=tabc, in_=cs[:, :, :].unsqueeze(2).to_broadcast([128, 4, nh, 32])),
             reads=["cs"], writes=[tabres + "c"])
        S.op(POOL, lambda: nc.gpsimd.tensor_copy(out=tabs, in_=sn[:, :, :].unsqueeze(2).to_broadcast([128, 4, nh, 32])),
             reads=["sn"], writes=[tabres + "s"])

    def head_norm_rope(e, src, srcres, nh, gbc, gres, tabc, tabs, tabres, sq, sqres, dst, dstres):
        V = vec(e)
        SH = 4 * nh
        cb = 250 + SH * 64 * (0.9 if e == POOL else 0.55)
        ch = 250 + SH * 32 * (0.9 if e == POOL else 0.55)
        x3 = src.rearrange("p s (h d) -> p (s h) d", h=nh)
        sq3 = sq.rearrange("p s (h d) -> p (s h) d", h=nh)
        S.op(e, lambda: V.tensor_tensor(out=sq3, in0=x3, in1=x3, op=ALU.mult), reads=[srcres], writes=[sqres], cost=cb)
        S.op(DVE, lambda: nc.vector.tensor_reduce(out=hss[:, 0:SH], in_=sq3, axis=AX.X, op=ALU.add), reads=[sqres], writes=["hss"], cost=250 + SH * 64 * 0.55)
        S.op(ACT, lambda: nc.scalar.activation(out=hrs[:, 0:SH], in_=hss[:, 0:SH], func=AF.Sqrt, scale=1.0 / 64, bias=epsb[:, 0:1]),
             reads=["hss", "epsb"], writes=["hrs"], cost=300)
        S.op(DVE, lambda: nc.vector.reciprocal(out=hrs[:, 0:SH], in_=hrs[:, 0:SH]), reads=["hrs"], writes=["hrs"], cost=190)
        S.op(e, lambda: V.tensor_tensor(out=x3, in0=x3, in1=hrs[:, 0:SH].unsqueeze(2).to_broadcast([128, SH, 64]), op=ALU.mult),
             reads=[srcres, "hrs"], writes=[srcres], cost=cb)
        S.op(e, lambda: V.tensor_tensor(out=x3, in0=x3, in1=gbc[:, :].unsqueeze(1).to_broadcast([128, SH, 64]), op=ALU.mult),
             reads=[srcres, gres], writes=[srcres], cost=cb)
        pat = "p s (h a r f) -> p (s h) a r f"
        x5 = src.rearrange(pat, h=nh, a=2, r=2)
        q5 = sq.rearrange(pat, h=nh, a=2, r=2)
        d5 = dst.rearrange(pat, h=nh, a=2, r=2)
        xa, xb_ = x5[:, :, :, 0, :], x5[:, :, :, 1, :]
        ta, tb_ = q5[:, :, :, 0, :], q5[:, :, :, 1, :]
        c4 = tabc.rearrange("p s h (a f) -> p (s h) a f", a=2)
        s4 = tabs.rearrange("p s h (a f) -> p (s h) a f", a=2)
        oa, ob = d5[:, :, :, 0, :], d5[:, :, :, 1, :]
        S.op(e, lambda: V.tensor_tensor(out=ta, in0=xb_, in1=s4, op=ALU.mult), reads=[srcres, tabres + "s"], writes=[sqres], cost=ch)
        S.op(e, lambda: V.tensor_tensor(out=tb_, in0=xa, in1=s4, op=ALU.mult), reads=[srcres, tabres + "s"], writes=[sqres], cost=ch)
        S.op(e, lambda: V.tensor_tensor(out=xa, in0=xa, in1=c4, op=ALU.mult), reads=[srcres, sqres, tabres + "c"], writes=[srcres], cost=ch)
        S.op(e, lambda: V.tensor_tensor(out=xb_, in0=xb_, in1=c4, op=ALU.mult), reads=[srcres, sqres, tabres + "c"], writes=[srcres], cost=ch)
        S.op(e, lambda: V.tensor_tensor(out=oa, in0=xa, in1=ta, op=ALU.subtract), reads=[srcres, sqres], writes=[dstres], cost=ch)
        S.op(e, lambda: V.tensor_tensor(out=ob, in0=xb_, in1=tb_, op=ALU.add), reads=[srcres, sqres], writes=[dstres], cost=ch)

    def tok_rows(ap2d):
        return ap2d.rearrange("(s p) d -> p s d", p=128)

    dbg = {}

    apos[0] = 0
    hid = carve([128, NJ, T], BF16)
    dn = [carve([128, NJ, 512], BF16), carve([128, NJ, 512], BF16)]
    kvw = carve([128, 8, 256], BF16)
    kvs = carve([128, 2, 4, 128], F32)
    ksq = carve([128, 4, 128], F32)
    tkc = carve([128, 4, 2, 32], F32)
    tks = carve([128, 4, 2, 32], F32)
    krb = carve([128, 4, 128], BF16)
    kTb = [carve([128, T], BF16), carve([128, T], BF16)]
    vsb = [carve([128, 4, 130], BF16), carve([128, 4, 130], BF16)]
    hb2 = carve([128, 4, D], BF16)
    hT2 = carve([128, 8, T], BF16)
    ALT = (hb2, "hb2", hT2, "hT2")

    S.op(SP, lambda: nc.sync.dma_start(out=kvw, in_=s_kv), reads=[("scr", "kv")], writes=["kvw"], dma_key="kvw")

    for t in range(NCT if stage >= 1 else 0):
        b = t % 2
        X, xres = Xb[b], "X%d" % b
        S.op(SP, lambda b=b, t=t: nc.sync.dma_start(out=Xb[b][:, :, :], in_=tok_rows(xc[t * T:(t + 1) * T, :])),
             writes=[xres], dma_key="xl%d" % b, cost=13600)
        if P1 >= 1:
            rmsnorm_T(X, xres, 0)
        if P1 >= 2:
            ffn(0, X, xres, hid, dn)
        if t < NOT:
            S.op(SP, lambda b=b, t=t: nc.sync.dma_start(out=tok_rows(x1s[t * T:(t + 1) * T, :]), in_=Xb[b][:, :, :]),
                 reads=[xres], writes=[("x1s", t)], dma_key="xs%d" % b, cost=13600)
        if P1 < 3:
            continue
        rmsnorm_T(X, xres, 1, ALT)
        if P1 < 4:
            continue
        rope_tables(t, 2, tkc, tks, "tk")
        if P1 < 5:
            continue
        for s in range(4):
            gb_ = s % 2
            for kc in range(8):
                S.op(PE, lambda kc=kc, s=s, gb_=gb_: nc.tensor.matmul(G[gb_][:, 0:256], lhsT=hT2[:, kc, s * 128:(s + 1) * 128], rhs=kvw[:, kc, :],
                                                                       start=(kc == 0), stop=(kc == 7)),
                     reads=[("hT2", kc), "kvw"], writes=[GN[gb_]], cost=200)
            S.op(ACT, lambda s=s, gb_=gb_: nc.scalar.copy(out=kvs[:, :, s, :], in_=G[gb_][:, 0:256].rearrange("p (a d) -> p a d", a=2)), reads=[GN[gb_]], writes=["kvs"])
        if P1 < 6:
            continue
        vb, vres = vsb[b], "vsb%d" % b
        kmt = km[:, t * 4:(t + 1) * 4]
        vb4 = vb.rearrange("p s (h e) -> p s h e", h=2)
        S.op(POOL, lambda vb4=vb4, kmt=kmt: nc.gpsimd.tensor_tensor(
            out=vb4[:, :, :, 0:64], in0=kvs[:, 1, :, :].rearrange("p s (h d) -> p s h d", h=2),
            in1=kmt.unsqueeze(2).unsqueeze(3).to_broadcast([128, 4, 2, 64]), op=ALU.mult),
            reads=["kvs", "km"], writes=[vres])
        S.op(POOL, lambda vb4=vb4, kmt=kmt: nc.gpsimd.tensor_copy(out=vb4[:, :, :, 64], in_=kmt.unsqueeze(2).to_broadcast([128, 4, 2])),
             reads=["km"], writes=[vres])
        S.op(SP, lambda vb=vb, t=t: nc.sync.dma_start(out=vss[t * T:(t + 1) * T, :].rearrange("(s p) e -> p s e", p=128), in_=vb),
             reads=[vres], writes=[("vss", t)], dma_key="vst%d" % b)
        if P1 < 7:
            continue
        head_norm_rope(POOL, kvs[:, 0, :, :], "kvs", 2, gk_bc, "gk", tkc, tks, "tk", ksq, "ksq", krb, "krb")
        for s in range(4):
            S.op(PE, lambda s=s: nc.tensor.transpose(out=tp[:, s * 128:(s + 1) * 128], in_=krb[:, s, :], identity=ident[:]),
                 reads=["krb", "ident"], writes=[("tpb", 0)], cost=118)
        kb_, kres = kTb[b], "kTb%d" % b
        S.op(DVE, lambda kb_=kb_: nc.vector.tensor_copy(out=kb_, in_=tp[:, 0:512]), reads=[("tpb", 0)], writes=[kres])
        S.op(SP, lambda kb_=kb_, t=t: nc.sync.dma_start(out=kts[:, t * T:(t + 1) * T], in_=kb_),
             reads=[kres], writes=[("kts", t)], dma_key="kst%d" % b)

    S.barrier()
    apos[0] = 0
    kT = carve([128, NCT * T], BF16)
    vAf = carve([128, NCH * 130 + 64], BF16)
    vA = vAf[:, 0:NCH * 130].rearrange("p (ch e) -> p ch e", e=130)
    qf = carve([128, 4, 512], F32)
    qsq = hb[:, :, :].rearrange("p s d -> p (s d)").bitcast(F32).rearrange("p (s d) -> p s d", s=4)
    tqc = carve([128, 4, 8, 32], F32)
    tqs = carve([128, 4, 8, 32], F32)
    qrb = carve([128, 4, 512], BF16)
    qTp = Xb[1][:, 0:2, :].rearrange("p a d -> p (a d)").bitcast(BF16).rearrange("p (h t) -> p h t", h=8)
    ub = carve([128, 4, 512], BF16)
    vnb = tqc.rearrange("p s h f -> p (s h f)").bitcast(BF16)[:, 0:2048].rearrange("p (s d) -> p s d", s=4)
    sgb = tqs.rearrange("p s h f -> p (s h f)").bitcast(BF16)[:, 0:2048].rearrange("p (s d) -> p s d", s=4)
    sgT = carve([128, 4, T], BF16)
    aT = carve([128, 8, T], BF16)
    mT = carve([128, 8, T], BF16)
    PT = [carve([128, 1024], BF16), carve([128, 1024], BF16)]
    rden = carve([128, T], F32)
    ones1 = carve([128, 64], F32)
    onT = carve([128, T], F32)
    ggv_bc = carve([128, 512], F32)
    S.op(POOL, lambda: nc.gpsimd.memset(ones1, 1.0), writes=["ones1"])
    S.op(POOL, lambda: nc.gpsimd.memset(qTp, 0.0), writes=[("qT", c) for c in range(4)])
    S.op(POOL, lambda: nc.gpsimd.memset(vAf[:, NCH * 130:NCH * 130 + 64], 0.0), writes=["vApad"])
    small_load(ggv_bc, bc_row(g_gv), "ggv")

    for c in range(0, NCT if stage >= 2 else 0, 8):
        n = min(8, NCT - c)
        S.op(SP, lambda c=c, n=n: nc.sync.dma_start(out=kT[:, c * T:(c + n) * T], in_=kts[:, c * T:(c + n) * T]),
             reads=[("kts", t) for t in range(c, c + n)], writes=["kT"], dma_key="kTl", cost=9000)
        S.op(SP, lambda c=c, n=n: nc.sync.dma_start(out=vA[:, c * 4:(c + n) * 4, :],
                                                    in_=vss[c * T:(c + n) * T, :].rearrange("(ch p) e -> p ch e", p=128)),
             reads=[("vss", t) for t in range(c, c + n)], writes=["vA"], dma_key="vAl", cost=15000)

    def attention():
        heads = [(c, g) for c in range(4) for g in range(2)]
        seq = [(hi, i) for hi in range(8) for i in range(NPAIR)]

        def qk(n):
            hi, i = seq[n]
            c, g = heads[hi]
            sb_ = n % 2
            Sx = (S0, S1)[sb_]
            for u in range(2):
                ch = 2 * i + u
                S.op(PE, lambda u=u, ch=ch, c=c, g=g, Sx=Sx: nc.tensor.matmul(
                    Sx[:, u * 512:(u + 1) * 512], lhsT=kT[:, ch * 128:(ch + 1) * 128],
                    rhs=qTp[:, c * 2 + g, :], start=True, stop=True),
                    reads=["kT", ("qT", c)], writes=[GN[2 * sb_], GN[2 * sb_ + 1]])

        qk(0)
        for n in range(len(seq)):
            hi, i = seq[n]
            c, g = heads[hi]
            sb_ = n % 2
            Sx = (S0, S1)[sb_]
            ob = hi % 2
            Ox = (O0, O1)[ob]
            if n + 1 < len(seq):
                qk(n + 1)
            S.op(ACT, lambda Sx=Sx, sb_=sb_: nc.scalar.activation(out=PT[sb_], in_=Sx[:, :], func=AF.Exp, scale=0.125),
                 reads=[GN[2 * sb_], GN[2 * sb_ + 1]], writes=["PT%d" % sb_], cost=1023)
            for u in range(2):
                ch = 2 * i + u
                S.op(PE, lambda u=u, ch=ch, g=g, Ox=Ox, sb_=sb_, i=i: nc.tensor.matmul(
                    Ox[:, :], lhsT=vAf[:, ch * 130 + g * 65:ch * 130 + g * 65 + 128], rhs=PT[sb_][:, u * 512:(u + 1) * 512],
                    start=(i == 0 and u == 0), stop=(i == NPAIR - 1 and u == 1)),
                    reads=["vA", "vApad", "PT%d" % sb_], writes=[GN[4 + ob]])
            if i == NPAIR - 1:
                h_true = g * 4 + c
                S.op(DVE, lambda Ox=Ox: nc.vector.reciprocal(out=rden[64:65, :], in_=Ox[64:65, :]), reads=[GN[4 + ob]], writes=["rden"], cost=2472)
                S.op(PE, lambda: nc.tensor.matmul(tpf[0:64, 0:512], lhsT=ones1[64:65, 0:64], rhs=rden[64:65, :], start=True, stop=True),
                     reads=["rden", "ones1"], writes=[("tpb", 0)], cost=970)
                S.op(ACT, lambda: nc.scalar.copy(out=onT[0:64, :], in_=tpf[0:64, 0:512]), reads=[("tpb", 0)], writes=["onT"])
                S.op(DVE, lambda Ox=Ox, h_true=h_true: nc.vector.tensor_tensor(out=aT[0:64, h_true, :], in0=Ox[0:64, :], in1=onT[0:64, :], op=ALU.mult),
                     reads=[GN[4 + ob], "onT"], writes=[("aT", h_true)])

    for t in range(NOT if stage >= 2 else 0):
        b = 0
        X, xres = Xb[b], "X%d" % b
        S.op(SP, lambda b=b, t=t: nc.sync.dma_start(out=Xb[b][:, :, :], in_=tok_rows(x1s[t * T:(t + 1) * T, :])),
             reads=[("x1s", t)], writes=[xres], dma_key="xl%d" % b, cost=13600)
        rmsnorm_T(X, xres, 1)
        rope_tables(t, 8, tqc, tqs, "tq")
        for pi in range(3):
            slot = ring_load("qgg%d" % pi, s_qgg[pi].rearrange("p kc c -> p (kc c)"), 4096)
            rv = ring[:, slot, :].rearrange("p (kc c) -> p kc c", kc=8)
            for s in range(4):
                gb_ = s % 2
                for kc in range(8):
                    S.op(PE, lambda kc=kc, s=s, gb_=gb_, rv=rv: nc.tensor.matmul(G[gb_], lhsT=hT[:, kc, s * 128:(s + 1) * 128], rhs=rv[:, kc, :],
                                                                                  start=(kc == 0), stop=(kc == 7)),
                         reads=[("hT", kc), ("ring", slot)], writes=[GN[gb_]])
                if pi == 0:
                    S.op(ACT, lambda s=s, gb_=gb_: nc.scalar.copy(out=qf[:, s, :], in_=G[gb_]), reads=[GN[gb_]], writes=["qf"])
                elif pi == 1:
                    S.op(ACT, lambda s=s, gb_=gb_: nc.scalar.activation(out=ub[:, s, :], in_=G[gb_], func=AF.Gelu), reads=[GN[gb_]], writes=["ub"])
                else:
                    S.op(ACT, lambda s=s, gb_=gb_: nc.scalar.activation(out=qf[:, s, :], in_=G[gb_], func=AF.Gelu), reads=[GN[gb_]], writes=["qf"])
            if pi == 0:
                head_norm_rope(POOL, qf, "qf", 8, gq_bc, "gq", tqc, tqs, "tq", qsq, "hb", qrb, "qrb")
                for c0 in range(0, 4, 2):
                    bank = (c0 // 2) % 2
                    for kk in range(2):
                        c = c0 + kk
                        for s in range(4):
                            col = bank * 1024 + kk * 512 + s * 128
                            S.op(PE, lambda c=c, s=s, col=col: nc.tensor.transpose(out=tp[:, col:col + 128], in_=qrb[:, s, c * 128:(c + 1) * 128],
                                                                                    identity=ident[:]),
                                 reads=["qrb", "ident"], writes=[("tpb", bank)], cost=118)
                    for kk in range(2):
                        c = c0 + kk
                        for g in range(2):
                            src_ps = tp[g * 64:(g + 1) * 64, bank * 1024 + kk * 512: bank * 1024 + (kk + 1) * 512]
                            dst = qTp[g * 64:(g + 1) * 64, c * 2 + g, :]
                            if bank == 0:
                                S.op(ACT, lambda src_ps=src_ps, dst=dst: nc.scalar.copy(out=dst, in_=src_ps), reads=[("tpb", bank)], writes=[("qT", c)])
                            else:
                                S.op(DVE, lambda src_ps=src_ps, dst=dst: nc.vector.tensor_copy(out=dst, in_=src_ps), reads=[("tpb", bank)], writes=[("qT", c)])
            if pi == 2:
                row_rstd(qf, "qf", 512, 512)
                for s in range(4):
                    S.op(DVE, lambda s=s: nc.vector.scalar_tensor_tensor(out=vnb[:, s, :], in0=qf[:, s, :], scalar=rstd[:, s:s + 1], in1=ggv_bc,
                                                                         op0=ALU.mult, op1=ALU.mult),
                         reads=["qf", ("rstd", 0), "ggv"], writes=["tqc"])
                for s in range(4):
                    gb_ = 2 + (s % 2)
                    for g in range(8):
                        S.op(PE, lambda s=s, g=g, gb_=gb_: nc.tensor.matmul(G[gb_][:, g * 64:(g + 1) * 64], lhsT=wsT[:, g, :], rhs=vnb[:, s, g * 64:(g + 1) * 64],
                                                                             start=True, stop=True),
                             reads=["tqc", "wsT"], writes=[GN[gb_]], cost=70)
                    tt, tn = (tmpA, "tmpA") if s % 2 == 0 else (tmpB, "tmpB")
                    S.op(DVE, lambda gb_=gb_, tt=tt: nc.vector.tensor_tensor(out=tt.rearrange("p (g c) -> p g c", g=8),
                                                                             in0=G[gb_].rearrange("p (g c) -> p g c", g=8),
                                                                             in1=bspT[:, :].unsqueeze(2).to_broadcast([128, 8, 64]), op=ALU.add),
                         reads=[GN[gb_], "bsp"], writes=[tn])
                    S.op(POOL, lambda s=s, tt=tt: nc.gpsimd.tensor_tensor(out=sgb[:, s, :], in0=tt, in1=ub[:, s, :], op=ALU.mult),
                         reads=[tn, "ub"], writes=["tqs"])
                transpose_T(sgb, "tqs", 4, sgT, "sgT")
        attention()
        for oc in range(8):
            slot = ring_load("mg%d" % oc, s_mg[oc], 3584)
            rg = ring[:, slot, 0:2048].rearrange("p (kc c) -> p kc c", kc=8)
            g0, g1 = (0, 1) if oc % 2 == 0 else (2, 3)
            for kc in range(8):
                S.op(PE, lambda kc=kc, rg=rg, g0=g0: nc.tensor.matmul(G[g0], lhsT=rg[:, kc, 0:128], rhs=hT[:, kc, :], start=(kc == 0), stop=(kc == 7)),
                     reads=[("ring", slot), ("hT", kc)], writes=[GN[g0]])
            for kc in range(8):
                S.op(PE, lambda kc=kc, rg=rg, g1=g1: nc.tensor.matmul(G[g1], lhsT=rg[:, kc, 128:256], rhs=hT[:, kc, :], start=(kc == 0), stop=(kc == 7)),
                     reads=[("ring", slot), ("hT", kc)], writes=[GN[g1]])
            for h in range(8):
                S.op(PE, lambda h=h, slot=slot: nc.tensor.matmul(G[4], lhsT=ring[0:64, slot, 2560 + h * 128:2560 + (h + 1) * 128], rhs=aT[0:64, h, :],
                                                                 start=(h == 0), stop=(h == 7)),
                     reads=[("ring", slot), ("aT", h)], writes=[GN[4]])
            for kc in range(4):
                S.op(PE, lambda kc=kc, slot=slot: nc.tensor.matmul(G[5], lhsT=ring[:, slot, 2048 + kc * 128:2048 + (kc + 1) * 128], rhs=sgT[:, kc, :],
                                                                   start=(kc == 0), stop=(kc == 3)),
                     reads=[("ring", slot), ("sgT", kc)], writes=[GN[5]])
            S.op(ACT, lambda g0=g0: nc.scalar.activation(out=tmpA, in_=G[g0], func=AF.Sigmoid), reads=[GN[g0]], writes=["tmpA"])
            S.op(ACT, lambda g1=g1: nc.scalar.activation(out=tmpB, in_=G[g1], func=AF.Sigmoid), reads=[GN[g1]], writes=["tmpB"])
            S.op(DVE, lambda: nc.vector.tensor_tensor(out=tmpA, in0=tmpA, in1=G[4], op=ALU.mult), reads=["tmpA", GN[4]], writes=["tmpA"])
            S.op(DVE, lambda: nc.vector.tensor_tensor(out=tmpB, in0=tmpB, in1=G[5], op=ALU.mult), reads=["tmpB", GN[5]], writes=["tmpB"])
            S.op(POOL, lambda oc=oc: nc.gpsimd.tensor_tensor(out=mT[:, oc, :], in0=tmpA, in1=tmpB, op=ALU.add),
                 reads=["tmpA", "tmpB"], writes=[("mT", oc)])
        for h in range(2):
            slot = ring_load("wo%d" % h, s_wo[h].rearrange("p kc c -> p (kc c)"), 4096)
            rv = ring[:, slot, :].rearrange("p (kc c) -> p kc c", kc=8)
            for s in range(4):
                gb_ = s % 2
                for kc in range(8):
                    S.op(PE, lambda kc=kc, s=s, gb_=gb_, rv=rv: nc.tensor.matmul(G[gb_], lhsT=mT[:, kc, s * 128:(s + 1) * 128], rhs=rv[:, kc, :],
                                                                                  start=(kc == 0), stop=(kc == 7)),
                         reads=[("mT", kc), ("ring", slot)], writes=[GN[gb_]])
                S.op(DVE, lambda s=s, h=h, gb_=gb_, X=X: nc.vector.tensor_tensor(out=X[:, s, h * 512:(h + 1) * 512], in0=G[gb_],
                                                                                  in1=X[:, s, h * 512:(h + 1) * 512], op=ALU.add),
                     reads=[GN[gb_], xres], writes=[xres])
        S.op(SP, lambda b=b, t=t: nc.sync.dma_start(out=tok_rows(x1s[t * T:(t + 1) * T, :]), in_=Xb[b][:, :, :]),
             reads=[xres], writes=[("x1s", t)], dma_key="xs%d" % b, cost=13600)

    S.barrier()
    apos[0] = 0
    hid = carve([128, NJ, T], BF16)
    dn = [carve([128, NJ, 512], BF16), carve([128, NJ, 512], BF16)]
    pf = carve([128, 4, PLE], F32)
    pbf = carve([128, 4, PLE], BF16)
    pT = carve([128, 2, T], BF16)
    gfin_bc = carve([128, D], F32)
    hb3 = carve([128, 4, D], BF16)
    hT3 = carve([128, 8, T], BF16)
    ALT = (hb3, "hb3", hT3, "hT3")
    small_load(gfin_bc, bc_row(g_fin), "gfin")
    out_ops = []
    for t in range(NOT if stage >= 3 else 0):
        b = t % 2
        X, xres = Xb[b], "X%d" % b
        S.op(SP, lambda b=b, t=t: nc.sync.dma_start(out=Xb[b][:, :, :], in_=tok_rows(x1s[t * T:(t + 1) * T, :])),
             reads=[("x1s", t)], writes=[xres], dma_key="xl%d" % b, cost=13600)
        rmsnorm_T(X, xres, 2)
        ffn(1, X, xres, hid, dn)
        rmsnorm_T(X, xres, 3, ALT)
        S.op(SP, lambda t=t: nc.sync.dma_start(out=pf, in_=tok_rows(pin[t * T:(t + 1) * T, :])), writes=["pf"], dma_key="pfl")
        S.op(POOL, lambda: nc.gpsimd.tensor_copy(out=pbf, in_=pf), reads=["pf"], writes=["pbf"])
        transpose_T(pbf, "pbf", 2, pT, "pT")
        slotp = ring_load("pl", s_pl.rearrange("p kc c -> p (kc c)"), 2048)
        rvp = ring[:, slotp, 0:2048].rearrange("p (kc c) -> p kc c", kc=2)
        for h in range(2):
            slot = ring_load("pg%d" % h, s_pg[h].rearrange("p kc c -> p (kc c)"), 4096)
            rv = ring[:, slot, :].rearrange("p (kc c) -> p kc c", kc=8)
            for s in range(4):
                g0, g1 = (0, 1) if s % 2 == 0 else (2, 3)
                for kc in range(8):
                    S.op(PE, lambda kc=kc, s=s, g0=g0, rv=rv: nc.tensor.matmul(G[g0], lhsT=hT3[:, kc, s * 128:(s + 1) * 128], rhs=rv[:, kc, :],
                                                                                start=(kc == 0), stop=(kc == 7)),
                         reads=[("hT3", kc), ("ring", slot)], writes=[GN[g0]])
                for k2 in range(2):
                    S.op(PE, lambda k2=k2, s=s, g1=g1, h=h, rvp=rvp: nc.tensor.matmul(G[g1], lhsT=pT[:, k2, s * 128:(s + 1) * 128],
                                                                                       rhs=rvp[:, k2, h * 512:(h + 1) * 512],
                                                                                       start=(k2 == 0), stop=(k2 == 1)),
                         reads=[("pT", k2), ("ring", slotp)], writes=[GN[g1]])
                tt, tn = (tmpA, "tmpA") if s % 2 == 0 else (tmpB, "tmpB")
                S.op(ACT, lambda g0=g0, tt=tt: nc.scalar.activation(out=tt, in_=G[g0], func=AF.Sigmoid), reads=[GN[g0]], writes=[tn])
                S.op(DVE, lambda g1=g1, tt=tt: nc.vector.tensor_tensor(out=tt, in0=tt, in1=G[g1], op=ALU.mult), reads=[tn, GN[g1]], writes=[tn])
                S.op(POOL, lambda s=s, h=h, tt=tt, X=X: nc.gpsimd.tensor_tensor(out=X[:, s, h * 512:(h + 1) * 512], in0=X[:, s, h * 512:(h + 1) * 512],
                                                                                 in1=tt, op=ALU.add),
                     reads=[tn, xres], writes=[xres])
        row_rstd(X, xres, D, D)
        for s in range(4):
            S.op(DVE, lambda s=s, X=X: nc.vector.scalar_tensor_tensor(out=X[:, s, :], in0=X[:, s, :], scalar=rstd[:, s:s + 1], in1=gfin_bc,
                                                                      op0=ALU.mult, op1=ALU.mult),
                 reads=[xres, ("rstd", 0), "gfin"], writes=[xres])
        out_ops.append(S.op(SP, lambda b=b, t=t: nc.sync.dma_start(out=tok_rows(y[t * T:(t + 1) * T, :]), in_=Xb[b][:, :, :]),
                            reads=[xres], dma_key="ys%d" % b, cost=13600))
    if stage < 3 and stage >= 1:
        out_ops.append(S.op(SP, lambda: nc.sync.dma_start(out=y[:, :], in_=x1s[:, :]), reads=[("x1s", t) for t in range(NOT)], dma_key="dbg"))
    S.op(SP, None, extra=out_ops)

    if SCHEDULE:
        S.schedule()
        build_program.last_est_ns = S.est_ns
    semkeys = S.finalize()
    by_eng = {e: [o for o in S.ops if o.eng == e] for e in ENGS}
    with ExitStack() as es:
        sems = {k: es.enter_context(nc.semaphore("s%d" % i)) for i, k in enumerate(semkeys)}
        block = es.enter_context(nc.Block())

        def emit(engname, eng):
            waited = {}
            for o in by_eng[engname]:
                need = {}
                for d in o.deps:
                    if not d.signal or d.fn is None:
                        continue
                    if d.eng == PE and o.eng == PE and d.dma_key is None and o.dma_key is None:
                        continue
                    if need.get(d.sem, 0) < d.val:
                        need[d.sem] = d.val
                for k, v in need.items():
                    if waited.get(k, 0) < v:
                        eng.wait_ge(sems[k], v)
                        waited[k] = v
                if o.fn is not None:
                    ins = o.fn()
                    if o.signal:
                        ins.then_inc(sems[o.sem], 16 if o.dma_key is not None else 1)
                else:
                    assert not o.signal

        @block.tensor
        def _(e):
            emit(PE, e)

        @block.scalar
        def _(e):
            emit(ACT, e)

        @block.vector
        def _(e):
            emit(DVE, e)

        @block.gpsimd
        def _(e):
            emit(POOL, e)

        @block.sync
        def _(e):
            emit(SP, e)
    return nc


NCT_FULL, NOT_FULL = 32, 8
WNAMES = ["g_ffn1", "w_ffn1_gu", "w_ffn1_down", "g_mix", "w_in", "g_q", "g_k", "g_gmlp_v", "w_spatial", "b_spatial",
          "w_branch_attn", "w_branch_gmlp", "w_out", "g_ffn2", "w_ffn2_gu", "w_ffn2_down", "g_ple", "w_ple_gate", "w_ple", "g_final"]


def _pos_table(tok_idx):
    tok_idx = np.asarray(tok_idx, np.int64)
    return np.stack([tok_idx // 64, tok_idx % 64], axis=1).astype(np.float32)


def kernel(**inputs):
    xp = np.asarray(inputs["x_prompt"], np.float32)
    xs = np.asarray(inputs["x_sample"], np.float32)
    pp = np.asarray(inputs["p_prompt"], np.float32)
    ps = np.asarray(inputs["p_sample"], np.float32)
    w = {k: np.ascontiguousarray(np.asarray(inputs[k], np.float32)) for k in WNAMES}
    w["g_final"] = w["g_final"].reshape(1, D)
    NTOK = NCT_FULL * T
    own = NOT_FULL * T
    in_maps = []
    for c in range(8):
        if c < 4:
            order = [c] + [(c + k) % 4 for k in range(1, 4)]
            xcx = np.concatenate([xp[o] for o in order], axis=0)
            posi = np.concatenate([np.arange(own)] * 4)
            msk = np.zeros(NTOK, np.float32); msk[:own] = 1.0
            pc = pp[0, c]
        else:
            q = c - 4
            order = [q] + [(q + k) % 4 for k in range(1, 4)]
            xcx = np.concatenate([xs[0, o * own:(o + 1) * own] for o in order], axis=0)
            posi = np.concatenate([np.arange(o * own, (o + 1) * own) for o in order])
            msk = np.ones(NTOK, np.float32)
            pc = ps[0, 0, q * own:(q + 1) * own]
        m = {"xc": np.ascontiguousarray(xcx), "pin": np.ascontiguousarray(pc), "pos": _pos_table(posi),
             "kmask": np.ascontiguousarray(msk.reshape(NTOK // 128, 128).T)}
        m.update(w)
        in_maps.append(m)
    nc = build_program(NCT_FULL, NOT_FULL)
    res = run_bass_kernel_spmd(nc, in_maps, core_ids=list(range(8)))
    ys = [np.asarray(res.results[c]["y"], np.float32) for c in range(8)]
    y_prompt = np.stack(ys[0:4], axis=0)
    y_sample = np.concatenate(ys[4:8], axis=0)[None]
    return (y_prompt, y_sample)
```

```python
import math
SCHEDULE = True
P0 = 255
P1 = 99
P1SUB = 99
P1T = 0
from contextlib import ExitStack
import numpy as np
import concourse.bass as bass
import concourse.mybir as mybir
from concourse.bass_utils import run_bass_kernel_spmd

F32 = mybir.dt.float32
BF16 = mybir.dt.bfloat16
I32 = mybir.dt.int32
AF = mybir.ActivationFunctionType
ALU = mybir.AluOpType
AX = mybir.AxisListType

D = 1024
DFF = 2816
NJ = DFF // 128
PLE = 256
EPS = 1e-6
T = 512
NSLOT = 3
SLOTW = 4096
PE, ACT, DVE, POOL, SP = "pe", "act", "dve", "pool", "sp"
ENGS = [PE, ACT, DVE, POOL, SP]


class Op:
    __slots__ = ("eng", "fn", "deps", "dma_key", "sem", "val", "signal", "idx", "cost", "phase", "t0", "t1")

    def __init__(self, eng, fn, deps, dma_key, cost):
        self.eng, self.fn, self.deps, self.dma_key, self.cost = eng, fn, deps, dma_key, cost
        self.sem = None
        self.val = 0
        self.signal = False


DEFAULT_COST = {PE: 216, ACT: 600, DVE: 650, POOL: 900, SP: 2500}


class Sched:
    def __init__(self):
        self.ops = []
        self.last_write = {}
        self.readers = {}
        self.phase = 0
        self.since_barrier = []
        self.cur_barrier = {}

    def op(self, eng, fn, reads=(), writes=(), dma_key=None, extra=(), cost=None):
        deps = set(extra)
        for r in reads:
            w = self.last_write.get(r)
            if w is not None:
                deps.add(w)
        for w_ in writes:
            w = self.last_write.get(w_)
            if w is not None:
                deps.add(w)
            for rd in self.readers.get(w_, ()):
                deps.add(rd)
        bar = self.cur_barrier.get(eng)
        if bar is not None:
            deps.add(bar)
        o = Op(eng, fn, deps, dma_key, DEFAULT_COST[eng] if cost is None else cost)
        o.idx = len(self.ops)
        o.phase = self.phase
        o.sem = (eng, self.phase) if dma_key is None else ("dma", dma_key)
        self.ops.append(o)
        for r in reads:
            self.readers.setdefault(r, []).append(o)
        for w_ in writes:
            self.last_write[w_] = o
            self.readers[w_] = []
        if fn is not None:
            self.since_barrier.append(o)
        return o

    def barrier(self):
        prev = list(self.since_barrier)
        self.since_barrier = []
        for e in ENGS:
            self.cur_barrier[e] = self.op(e, None, extra=prev, cost=0)
        self.phase += 1

    def schedule(self):
        import heapq
        n = len(self.ops)
        succ = [[] for _ in range(n)]
        indeg = [0] * n
        for o in self.ops:
            indeg[o.idx] = len(o.deps)
            for d in o.deps:
                succ[d.idx].append(o)
        pending = {e: [] for e in ENGS}
        avail = {e: [] for e in ENGS}
        free = {e: 0.0 for e in ENGS}
        ready_t = [0.0] * n
        for o in self.ops:
            if indeg[o.idx] == 0:
                heapq.heappush(pending[o.eng], (0.0, o.idx))
        order = []
        done = 0
        while done < n:
            best = None
            for e in ENGS:
                pe_, av = pending[e], avail[e]
                while pe_ and pe_[0][0] <= free[e]:
                    heapq.heappush(av, heapq.heappop(pe_)[1])
                if av:
                    cand = (free[e], av[0], e, True)
                elif pe_:
                    cand = (pe_[0][0], pe_[0][1], e, False)
                else:
                    continue
                if best is None or cand[:2] < best[:2]:
                    best = cand
            assert best is not None, "dependency cycle"
            start, idx, e, from_av = best
            if from_av:
                heapq.heappop(avail[e])
            else:
                heapq.heappop(pending[e])
            o = self.ops[idx]
            o.t0 = start
            if o.dma_key is not None:
                free[e] = start + 60.0
                o.t1 = start + o.cost
            else:
                o.t1 = start + o.cost
                free[e] = o.t1
            order.append(o)
            done += 1
            for sc in succ[idx]:
                if ready_t[sc.idx] < o.t1:
                    ready_t[sc.idx] = o.t1
                indeg[sc.idx] -= 1
                if indeg[sc.idx] == 0:
                    heapq.heappush(pending[sc.eng], (ready_t[sc.idx], sc.idx))
        self.ops = order
        self.est_ns = max(o.t1 for o in order)

    def finalize(self):
        for o in self.ops:
            for d in o.deps:
                if d.eng == PE and o.eng == PE and d.dma_key is None and o.dma_key is None:
                    continue
                if d.fn is None:
                    continue
                d.signal = True
        for o in self.ops:
            if o.dma_key is not None and o.fn is not None:
                o.signal = True
        counts = {}
        for o in self.ops:
            if o.signal:
                counts[o.sem] = counts.get(o.sem, 0) + (16 if o.dma_key is not None else 1)
                o.val = counts[o.sem]
        return sorted(counts.keys(), key=str)


def build_program(NCT, NOT, stage=3):
    NCH = NCT * 4
    NPAIR = NCH // 2
    nc = bass.Bass("TRN2", target_bir_lowering=False)

    def din(name, shape, dt=F32):
        return nc.dram_tensor(name, list(shape), dt, kind="ExternalInput").ap()

    xc = din("xc", [NCT * T, D])
    pin = din("pin", [NOT * T, PLE])
    pos = din("pos", [NCT * T, 2])
    kmask = din("kmask", [128, NCH])
    g_ffn1 = din("g_ffn1", [1, D]); w1gu = din("w_ffn1_gu", [1, D, 2 * DFF]); w1d = din("w_ffn1_down", [1, DFF, D])
    g_mix = din("g_mix", [1, D]); w_in = din("w_in", [1, D, 3840])
    g_q = din("g_q", [1, 64]); g_k = din("g_k", [1, 64]); g_gv = din("g_gmlp_v", [1, 512])
    w_sp = din("w_spatial", [1, 8, 128, 128]); b_sp = din("b_spatial", [1, 8, 128])
    w_ba = din("w_branch_attn", [1, 512, D]); w_bg = din("w_branch_gmlp", [1, 512, D]); w_out = din("w_out", [1, D, D])
    g_ffn2 = din("g_ffn2", [1, D]); w2gu = din("w_ffn2_gu", [1, D, 2 * DFF]); w2d = din("w_ffn2_down", [1, DFF, D])
    g_ple = din("g_ple", [1, D]); w_pg = din("w_ple_gate", [1, D, D]); w_pl = din("w_ple", [1, PLE, D])
    g_fin = din("g_final", [1, D])
    y = nc.dram_tensor("y", [NOT * T, D], F32, kind="ExternalOutput").ap()

    def dscr(name, shape, dt=BF16):
        return nc.dram_tensor(name, list(shape), dt).ap()

    s_gu = [dscr("s_gu1", [11, 128, 8, 512]), dscr("s_gu2", [11, 128, 8, 512])]
    s_dn = [dscr("s_dn1", [2, 128, NJ, 512]), dscr("s_dn2", [2, 128, NJ, 512])]
    s_kv = dscr("s_kv", [128, 8, 256])
    s_qgg = dscr("s_qgg", [3, 128, 8, 512])
    s_mg = dscr("s_mg", [8, 128, 3584])
    s_wo = dscr("s_wo", [2, 128, 8, 512])
    s_pg = dscr("s_pg", [2, 128, 8, 512])
    s_pl = dscr("s_pl", [128, 2, 1024])
    x1s = dscr("x1s", [NOT * T, D], F32)
    kts = dscr("kts", [128, NCT * T])
    vss = dscr("vss", [NCT * T, 130])

    S = Sched()

    def sb(name, shape, dt):
        return nc.alloc_sbuf_tensor(name, list(shape), dt)

    ident = sb("ident", [128, 128], BF16)
    gcol = sb("gcol", [128, 4, 8], F32)
    gq_bc = sb("gq_bc", [128, 64], F32)
    gk_bc = sb("gk_bc", [128, 64], F32)
    bspT = sb("bspT", [128, 8], F32)
    wsT = sb("wsT", [128, 8, 128], BF16)
    inv_bc = sb("inv_bc", [128, 16], F32)
    km = sb("km", [128, NCH], F32)
    negpi = sb("negpi", [128, 1], F32)
    epsb = sb("epsb", [128, 1], F32)
    ring = sb("ring", [128, NSLOT, SLOTW], BF16)
    Xb = [sb("X0", [128, 4, D], F32), sb("X1", [128, 4, D], F32)]
    hb = sb("hb", [128, 4, D], BF16)
    hT = sb("hT", [128, 8, T], BF16)
    ss = sb("ss", [128, 8], F32)
    rstd = sb("rstd", [128, 8], F32)
    tmpA = sb("tmpA", [128, T], F32)[:, :]
    tmpB = sb("tmpB", [128, T], F32)[:, :]
    posb = sb("posb", [128, 4, 2], F32)
    ang = sb("ang", [128, 4, 2, 16], F32)
    angm = sb("angm", [128, 4, 2, 16], F32)
    cs = sb("cs", [128, 4, 32], F32)
    sn = sb("sn", [128, 4, 32], F32)
    angki = sb("angki", [128, 4, 32], I32)
    angkf = sb("angkf", [128, 4, 32], F32)
    angr = sb("angr", [128, 4, 32], F32)
    hss = sb("hss", [128, 32], F32)
    hrs = sb("hrs", [128, 32], F32)
    ARENA_BYTES = 124 * 1024
    arena = sb("arena", [128, ARENA_BYTES // 4], F32)
    apos = [0]

    def carve(shape, dt):
        esz = 4 if dt in (F32, I32) else 2
        n = int(np.prod(shape[1:]))
        nbytes = (n * esz + 31) // 32 * 32
        off = apos[0]
        apos[0] += nbytes
        assert apos[0] <= ARENA_BYTES, (apos[0], ARENA_BYTES)
        v = arena[:, off // 4:(off + nbytes) // 4]
        if esz == 2:
            v = v.bitcast(BF16)
        v = v[:, 0:n]
        if len(shape) == 3:
            v = v.rearrange("p (a b) -> p a b", a=shape[1])
        elif len(shape) == 4:
            v = v.rearrange("p (a b c) -> p a b c", a=shape[1], b=shape[2])
        return v

    tp = nc.alloc_psum_tensor("tp", [128, 2048], BF16)
    S0 = nc.alloc_psum_tensor("S0", [128, 1024], F32)
    S1 = nc.alloc_psum_tensor("S1", [128, 1024], F32)
    O0 = nc.alloc_psum_tensor("O0", [128, 512], F32)
    O1 = nc.alloc_psum_tensor("O1", [128, 512], F32)
    G = [S0[:, 0:512], S0[:, 512:1024], S1[:, 0:512], S1[:, 512:1024], O0[:, :], O1[:, :]]
    GN = ["G0", "G1", "G2", "G3", "G4", "G5"]
    tpf = tp[:, :].bitcast(F32)

    def vec(e):
        return nc.vector if e == DVE else nc.gpsimd

    def cast(key, out_ap, in_ap):
        if not (P0 & 8):
            return None
        return S.op(POOL, lambda o=out_ap, i=in_ap: nc.gpsimd.dma_start(out=o, in_=i),
                    writes=[("scr", key)], dma_key="c_" + key, cost=9000)

    def small_load(out_ap, in_ap, res):
        def f():
            with nc.allow_non_contiguous_dma(reason="tiny constant layout load"):
                return nc.sync.dma_start(out=out_ap, in_=in_ap)
        return S.op(SP, f, writes=[res], dma_key="k_" + str(res).replace("'", "").replace(" ", ""))

    def bc_row(ap2d):
        return ap2d.partition_broadcast(128).rearrange("p o d -> p (o d)")

    for i, g in enumerate([g_ffn1, g_mix, g_ffn2, g_ple] if P0 & 1 else []):
        small_load(gcol[:, i, :], g.rearrange("o (kc p) -> p (o kc)", p=128), ("gcol", i))
    if P0 & 2:
        small_load(gq_bc[:, :], bc_row(g_q), "gq")
        small_load(gk_bc[:, :], bc_row(g_k), "gk")
    if P0 & 4:
        small_load(bspT[:, :], b_sp.rearrange("o g p -> p (o g)"), "bsp")
    small_load(km[:, :], kmask[:, :], "km")

    def mk_ident():
        nc.gpsimd.memset(ident[:], 0.0)
        return nc.gpsimd.affine_select(out=ident[:], in_=ident[:], pattern=[[-1, 128]], compare_op=ALU.not_equal,
                                       fill=1.0, base=0, channel_multiplier=1)
    S.op(POOL, mk_ident, writes=["ident"])
    S.op(POOL, lambda: nc.gpsimd.memset(negpi[:], -math.pi), writes=["negpi"])
    S.op(POOL, lambda: nc.gpsimd.memset(epsb[:], EPS), writes=["epsb"])
    S.op(POOL, lambda: nc.gpsimd.iota(out=cs[:, 0, 0:16].bitcast(I32), pattern=[[1, 16]], base=0, channel_multiplier=0),
         writes=["cs"])
    S.op(POOL, lambda: nc.gpsimd.tensor_copy(out=sn[:, 0, 0:16], in_=cs[:, 0, 0:16].bitcast(I32)), reads=["cs"], writes=["sn"])
    S.op(ACT, lambda: nc.scalar.activation(out=inv_bc[:, :], in_=sn[:, 0, 0:16], func=AF.Exp, scale=-math.log(10000.0) / 16.0),
         reads=["sn"], writes=["inv"])

    S.op(SP, lambda: nc.sync.dma_start(out=Xb[0][:, 0, :].rearrange("p (g q) -> p g q", g=8),
                                       in_=w_sp[0].rearrange("g p q -> p g q")), writes=["X0"], dma_key="const")
    S.op(DVE, lambda: nc.vector.tensor_copy(out=hb[:, 0, :], in_=Xb[0][:, 0, :]), reads=["X0"], writes=["hb"])
    for g in range(8):
        S.op(PE, lambda g=g: nc.tensor.transpose(out=tp[:, g * 128:(g + 1) * 128], in_=hb[:, 0, g * 128:(g + 1) * 128], identity=ident[:]),
             reads=["hb", "ident"], writes=[("tpb", 0)], cost=118)
    S.op(DVE, lambda: nc.vector.tensor_copy(out=wsT[:, :, :].rearrange("q g p -> q (g p)"), in_=tp[:, 0:1024]),
         reads=[("tpb", 0)], writes=["wsT"])

    def kcp(ap2d):
        return ap2d.rearrange("(kc p) c -> p kc c", p=128)

    def cast_ffn(idx, wgu, wd):
        for i in range(11):
            cast("gu%d_%d" % (idx, i), s_gu[idx][i, :, :, 0:256], kcp(wgu[0, :, 256 * i:256 * i + 256]))
            cast("gu%d_%d" % (idx, i), s_gu[idx][i, :, :, 256:512], kcp(wgu[0, :, DFF + 256 * i:DFF + 256 * i + 256]))
        for h in range(2):
            cast("dn%d_%d" % (idx, h), s_dn[idx][h], kcp(wd[0, :, 512 * h:512 * h + 512]))

    cast_ffn(0, w1gu, w1d)
    cast("kv", s_kv, kcp(w_in[0, :, 512:768]))
    for g in range(2):
        for c in range(4):
            cast("qgg0", s_qgg[0, :, :, c * 128 + g * 64:c * 128 + g * 64 + 64], kcp(w_in[0, :, (g * 4 + c) * 64:(g * 4 + c) * 64 + 64]))
    for i, c0 in ((1, 768), (2, 1280)):
        cast("qgg%d" % i, s_qgg[i], kcp(w_in[0, :, c0:c0 + 512]))
    for oc in range(8):
        k = "mg%d" % oc
        gts = s_mg[oc, :, 0:2048].rearrange("p (kc c) -> p kc c", kc=8)
        cast(k, gts[:, :, 0:128], kcp(w_in[0, :, 1792 + oc * 128:1792 + oc * 128 + 128]))
        cast(k, gts[:, :, 128:256], kcp(w_in[0, :, 2816 + oc * 128:2816 + oc * 128 + 128]))
        cast(k, s_mg[oc, :, 2048:2560].rearrange("p (kc c) -> p kc c", kc=4), kcp(w_bg[0, :, oc * 128:oc * 128 + 128]))
        cast(k, s_mg[oc, 0:64, 2560:3584].rearrange("p (h c) -> p h c", h=8),
             w_ba[0, :, oc * 128:oc * 128 + 128].rearrange("(h p) c -> p h c", p=64))
    for h in range(2):
        cast("wo%d" % h, s_wo[h], kcp(w_out[0, :, 512 * h:512 * h + 512]))
    cast_ffn(1, w2gu, w2d)
    for h in range(2):
        cast("pg%d" % h, s_pg[h], kcp(w_pg[0, :, 512 * h:512 * h + 512]))
    cast("pl", s_pl, kcp(w_pl[0, :, :]))

    ring_n = [0]

    def ring_load(key, src_ap, width):
        slot = ring_n[0] % NSLOT
        ring_n[0] += 1
        S.op(SP, lambda s=slot, a=src_ap, w=width: nc.sync.dma_start(out=ring[:, s, 0:w], in_=a),
             reads=[("scr", key)], writes=[("ring", slot)], dma_key="ring%d" % slot, cost=2000 + width * 256 // 180)
        return slot

    def transpose_T(src, srcres, nkc, outT, outres, gi=None):
        for k0 in range(0, nkc, 2):
            bank = (k0 // 2) % 2
            for kk in range(2):
                kc = k0 + kk
                for s in range(4):
                    col = bank * 1024 + kk * 512 + s * 128
                    S.op(PE, lambda kc=kc, s=s, col=col: nc.tensor.transpose(out=tp[:, col:col + 128],
                                                                            in_=src[:, s, kc * 128:(kc + 1) * 128], identity=ident[:]),
                         reads=[srcres, "ident"], writes=[("tpb", bank)], cost=118)
            for kk in range(2):
                kc = k0 + kk
                e = ACT if bank == 0 else DVE
                src_ps = tp[:, bank * 1024 + kk * 512: bank * 1024 + (kk + 1) * 512]
                if gi is None:
                    if e == ACT:
                        f = lambda kc=kc, src_ps=src_ps: nc.scalar.copy(out=outT[:, kc, :], in_=src_ps)
                    else:
                        f = lambda kc=kc, src_ps=src_ps: nc.vector.tensor_copy(out=outT[:, kc, :], in_=src_ps)
                    rd = [("tpb", bank)]
                else:
                    if e == ACT:
                        f = lambda kc=kc, src_ps=src_ps: nc.scalar.activation(out=outT[:, kc, :], in_=src_ps, func=AF.Copy,
                                                                               scale=gcol[:, gi, kc:kc + 1])
                    else:
                        f = lambda kc=kc, src_ps=src_ps: nc.vector.tensor_scalar(out=outT[:, kc, :], in0=src_ps,
                                                                                  scalar1=gcol[:, gi, kc:kc + 1], scalar2=None, op0=ALU.mult)
                    rd = [("tpb", bank), ("gcol", gi)]
                S.op(e, f, reads=rd, writes=[(outres, kc)])

    def row_rstd(X, xres, width, nrm, hbuf=None, hbres="hb", c0=0):
        hbuf = hb if hbuf is None else hbuf
        for s in range(4):
            S.op(ACT, lambda s=s: nc.scalar.activation(out=hbuf[:, s, 0:width], in_=X[:, s, 0:width], func=AF.Square, accum_out=ss[:, c0 + s:c0 + s + 1]),
                 reads=[xres], writes=[hbres, ("ss", c0 + s)], cost=1056 if width > 512 else 843)
        S.op(ACT, lambda: nc.scalar.activation(out=rstd[:, c0:c0 + 4], in_=ss[:, c0:c0 + 4], func=AF.Sqrt, scale=1.0 / nrm, bias=epsb[:, 0:1]),
             reads=[("ss", c0 + s) for s in range(4)] + ["epsb"], writes=[("rstd", c0)], cost=300)
        S.op(DVE, lambda: nc.vector.reciprocal(out=rstd[:, c0:c0 + 4], in_=rstd[:, c0:c0 + 4]), reads=[("rstd", c0)], writes=[("rstd", c0)], cost=190)

    def rmsnorm_T(X, xres, gi, alt=None):
        hbuf, hbres, outT, outres = (hb, "hb", hT, "hT") if alt is None else alt
        c0 = 0 if alt is None else 4
        row_rstd(X, xres, D, D, hbuf, hbres, c0)
        for s in range(4):
            S.op(DVE, lambda s=s: nc.vector.tensor_scalar(out=hbuf[:, s, :], in0=X[:, s, :], scalar1=rstd[:, c0 + s:c0 + s + 1], scalar2=None, op0=ALU.mult),
                 reads=[xres, ("rstd", c0)], writes=[hbres])
        transpose_T(hbuf, hbres, 8, outT, outres, gi)

    def ffn(idx, X, xres, hid, dn):
        for i in range(11):
            slot = ring_load("gu%d_%d" % (idx, i), s_gu[idx][i].rearrange("p kc c -> p (kc c)"), 4096)
            rv = ring[:, slot, :].rearrange("p (kc c) -> p kc c", kc=8)
            for jj in range(2):
                j = 2 * i + jj
                ga, gb = (0, 1) if j % 2 == 0 else (2, 3)
                for kc in range(8):
                    S.op(PE, lambda kc=kc, jj=jj, ga=ga, rv=rv: nc.tensor.matmul(G[ga], lhsT=rv[:, kc, jj * 128:(jj + 1) * 128], rhs=hT[:, kc, :],
                                                                                  start=(kc == 0), stop=(kc == 7)),
                         reads=[("ring", slot), ("hT", kc)], writes=[GN[ga]])
                for kc in range(8):
                    S.op(PE, lambda kc=kc, jj=jj, gb=gb, rv=rv: nc.tensor.matmul(G[gb], lhsT=rv[:, kc, 256 + jj * 128:256 + (jj + 1) * 128], rhs=hT[:, kc, :],
                                                                                  start=(kc == 0), stop=(kc == 7)),
                         reads=[("ring", slot), ("hT", kc)], writes=[GN[gb]])
                tt, tn = (tmpA, "tmpA") if j % 2 == 0 else (tmpB, "tmpB")
                S.op(ACT, lambda ga=ga, tt=tt: nc.scalar.activation(out=tt[:, :], in_=G[ga], func=AF.Silu), reads=[GN[ga]], writes=[tn])
                S.op(DVE, lambda j=j, gb=gb, tt=tt: nc.vector.tensor_tensor(out=hid[:, j, :], in0=tt[:, :], in1=G[gb], op=ALU.mult),
                     reads=[tn, GN[gb]], writes=[("hid", j)])
        for h in range(2):
            S.op(SP, lambda h=h: nc.sync.dma_start(out=dn[h], in_=s_dn[idx][h]),
                 reads=[("scr", "dn%d_%d" % (idx, h))], writes=[("dn", h)], dma_key="dn%d" % h, cost=18000)
            for s in range(4):
                b = 4 + (s % 2)
                for j in range(NJ):
                    S.op(PE, lambda j=j, s=s, h=h, b=b: nc.tensor.matmul(G[b], lhsT=hid[:, j, s * 128:(s + 1) * 128], rhs=dn[h][:, j, :],
                                                                         start=(j == 0), stop=(j == NJ - 1)),
                         reads=[("hid", j), ("dn", h)], writes=[GN[b]])
                S.op(DVE, lambda s=s, h=h, b=b: nc.vector.scalar_tensor_tensor(out=X[:, s, h * 512:(h + 1) * 512], in0=G[b], scalar=0.5,
                                                                               in1=X[:, s, h * 512:(h + 1) * 512], op0=ALU.mult, op1=ALU.add),
                     reads=[GN[b], xres], writes=[xres])

    def rope_tables(t, nh, tabc, tabs, tabres):
        S.op(SP, lambda: nc.sync.dma_start(out=posb[:, :, :], in_=pos[t * T:(t + 1) * T, :].rearrange("(s p) a -> p s a", p=128)),
             writes=["posb"], dma_key="posb")
        for a in range(2):
            S.op(POOL, lambda a=a: nc.gpsimd.tensor_tensor(out=ang[:, :, a, :], in0=posb[:, :, a:a + 1].to_broadcast([128, 4, 16]),
                                                           in1=inv_bc[:, :].unsqueeze(1).to_broadcast([128, 4, 16]), op=ALU.mult),
                 reads=["posb", "inv"], writes=["ang"])
        angf = ang[:, :, :, :].rearrange("p s a f -> p s (a f)")
        angmf = angm[:, :, :, :].rearrange("p s a f -> p s (a f)")
        TWO_PI = 2.0 * math.pi

        def sin_of(dst, shift):
            S.op(DVE, lambda: nc.vector.tensor_scalar(out=angmf, in0=angf, scalar1=shift, scalar2=1.0 / TWO_PI, op0=ALU.add, op1=ALU.mult),
                 reads=["ang"], writes=["angm"])
            S.op(DVE, lambda: nc.vector.tensor_copy(out=angki[:, :, :], in_=angmf), reads=["angm"], writes=["angki"])
            S.op(DVE, lambda: nc.vector.tensor_copy(out=angkf[:, :, :], in_=angki[:, :, :]), reads=["angki"], writes=["angkf"])
            S.op(DVE, lambda: nc.vector.tensor_scalar(out=angmf, in0=angf, scalar1=shift, scalar2=None, op0=ALU.add),
                 reads=["ang", "angki"], writes=["angm"])
            S.op(DVE, lambda: nc.vector.scalar_tensor_tensor(out=angr[:, :, :], in0=angkf[:, :, :], scalar=-TWO_PI, in1=angmf, op0=ALU.mult, op1=ALU.add),
                 reads=["angkf", "angm"], writes=["angr"])
            S.op(DVE, lambda: nc.vector.tensor_scalar(out=angmf, in0=angr[:, :, :], scalar1=math.pi, scalar2=TWO_PI, op0=ALU.is_gt, op1=ALU.mult),
                 reads=["angr"], writes=["angm"])
            S.op(DVE, lambda: nc.vector.tensor_tensor(out=angr[:, :, :], in0=angr[:, :, :], in1=angmf, op=ALU.subtract),
                 reads=["angr", "angm"], writes=["angr"])
            S.op(ACT, lambda: nc.scalar.activation(out=dst[:, :, :], in_=angr[:, :, :], func=AF.Sin), reads=["angr"], writes=[("cs" if dst is cs else "sn")])

        sin_of(sn, 0.0)
        sin_of(cs, 0.5 * math.pi)
        S.op(POOL, lambda: nc.gpsimd.tensor_copy(out=tabc, in_=cs[:, :, :].unsqueeze(2).to_broadcast([128, 4, nh, 32])),
             reads=["cs"], writes=[tabres + "c"])
        S.op(POOL, lambda: nc.gpsimd.tensor_copy(out=tabs, in_=sn[:, :, :].unsqueeze(2).to_broadcast([128, 4, nh, 32])),
             reads=["sn"], writes=[tabres + "s"])

    def head_norm_rope(e, src, srcres, nh, gbc, gres, tabc, tabs, tabres, sq, sqres, dst, dstres):
        V = vec(e)
        SH = 4 * nh
        cb = 250 + SH * 64 * (0.9 if e == POOL else 0.55)
        ch = 250 + SH * 32 * (0.9 if e == POOL else 0.55)
        x3 = src.rearrange("p s (h d) -> p (s h) d", h=nh)
        sq3 = sq.rearrange("p s (h d) -> p (s h) d", h=nh)
        S.op(e, lambda: V.tensor_tensor(out=sq3, in0=x3, in1=x3, op=ALU.mult), reads=[srcres], writes=[sqres], cost=cb)
        S.op(DVE, lambda: nc.vector.tensor_reduce(out=hss[:, 0:SH], in_=sq3, axis=AX.X, op=ALU.add), reads=[sqres], writes=["hss"], cost=250 + SH * 64 * 0.55)
        S.op(ACT, lambda: nc.scalar.activation(out=hrs[:, 0:SH], in_=hss[:, 0:SH], func=AF.Sqrt, scale=1.0 / 64, bias=epsb[:, 0:1]),
             reads=["hss", "epsb"], writes=["hrs"], cost=300)
        S.op(DVE, lambda: nc.vector.reciprocal(out=hrs[:, 0:SH], in_=hrs[:, 0:SH]), reads=["hrs"], writes=["hrs"], cost=190)
        S.op(e, lambda: V.tensor_tensor(out=x3, in0=x3, in1=hrs[:, 0:SH].unsqueeze(2).to_broadcast([128, SH, 64]), op=ALU.mult),
             reads=[srcres, "hrs"], writes=[srcres], cost=cb)
        S.op(e, lambda: V.tensor_tensor(out=x3, in0=x3, in1=gbc[:, :].unsqueeze(1).to_broadcast([128, SH, 64]), op=ALU.mult),
             reads=[srcres, gres], writes=[srcres], cost=cb)
        pat = "p s (h a r f) -> p (s h) a r f"
        x5 = src.rearrange(pat, h=nh, a=2, r=2)
        q5 = sq.rearrange(pat, h=nh, a=2, r=2)
        d5 = dst.rearrange(pat, h=nh, a=2, r=2)
        xa, xb_ = x5[:, :, :, 0, :], x5[:, :, :, 1, :]
        ta, tb_ = q5[:, :, :, 0, :], q5[:, :, :, 1, :]
        c4 = tabc.rearrange("p s h (a f) -> p (s h) a f", a=2)
        s4 = tabs.rearrange("p s h (a f) -> p (s h) a f", a=2)
        oa, ob = d5[:, :, :, 0, :], d5[:, :, :, 1, :]
        S.op(e, lambda: V.tensor_tensor(out=ta, in0=xb_, in1=s4, op=ALU.mult), reads=[srcres, tabres + "s"], writes=[sqres], cost=ch)
        S.op(e, lambda: V.tensor_tensor(out=tb_, in0=xa, in1=s4, op=ALU.mult), reads=[srcres, tabres + "s"], writes=[sqres], cost=ch)
        S.op(e, lambda: V.tensor_tensor(out=xa, in0=xa, in1=c4, op=ALU.mult), reads=[srcres, sqres, tabres + "c"], writes=[srcres], cost=ch)
        S.op(e, lambda: V.tensor_tensor(out=xb_, in0=xb_, in1=c4, op=ALU.mult), reads=[srcres, sqres, tabres + "c"], writes=[srcres], cost=ch)
        S.op(e, lambda: V.tensor_tensor(out=oa, in0=xa, in1=ta, op=ALU.subtract), reads=[srcres, sqres], writes=[dstres], cost=ch)
        S.op(e, lambda: V.tensor_tensor(out=ob, in0=xb_, in1=tb_, op=ALU.add), reads=[srcres, sqres], writes=[dstres], cost=ch)

    def tok_rows(ap2d):
        return ap2d.rearrange("(s p) d -> p s d", p=128)

    dbg = {}

    apos[0] = 0
    hid = carve([128, NJ, T], BF16)
    dn = [carve([128, NJ, 512], BF16), carve([128, NJ, 512], BF16)]
    kvw = carve([128, 8, 256], BF16)
    kvs = carve([128, 2, 4, 128], F32)
    ksq = carve([128, 4, 128], F32)
    tkc = carve([128, 4, 2, 32], F32)
    tks = carve([128, 4, 2, 32], F32)
    krb = carve([128, 4, 128], BF16)
    kTb = [carve([128, T], BF16), carve([128, T], BF16)]
    vsb = [carve([128, 4, 130], BF16), carve([128, 4, 130], BF16)]
    hb2 = carve([128, 4, D], BF16)
    hT2 = carve([128, 8, T], BF16)
    ALT = (hb2, "hb2", hT2, "hT2")

    S.op(SP, lambda: nc.sync.dma_start(out=kvw, in_=s_kv), reads=[("scr", "kv")], writes=["kvw"], dma_key="kvw")

    for t in range(NCT if stage >= 1 else 0):
        b = t % 2
        X, xres = Xb[b], "X%d" % b
        S.op(SP, lambda b=b, t=t: nc.sync.dma_start(out=Xb[b][:, :, :], in_=tok_rows(xc[t * T:(t + 1) * T, :])),
             writes=[xres], dma_key="xl%d" % b, cost=13600)
        if P1 >= 1:
            rmsnorm_T(X, xres, 0)
        if P1 >= 2:
            ffn(0, X, xres, hid, dn)
        if t < NOT:
            S.op(SP, lambda b=b, t=t: nc.sync.dma_start(out=tok_rows(x1s[t * T:(t + 1) * T, :]), in_=Xb[b][:, :, :]),
                 reads=[xres], writes=[("x1s", t)], dma_key="xs%d" % b, cost=13600)
        if P1 < 3:
            continue
        rmsnorm_T(X, xres, 1, ALT)
        if P1 < 4:
            continue
        rope_tables(t, 2, tkc, tks, "tk")
        if P1 < 5:
            continue
        for s in range(4):
            gb_ = s % 2
            for kc in range(8):
                S.op(PE, lambda kc=kc, s=s, gb_=gb_: nc.tensor.matmul(G[gb_][:, 0:256], lhsT=hT2[:, kc, s * 128:(s + 1) * 128], rhs=kvw[:, kc, :],
                                                                       start=(kc == 0), stop=(kc == 7)),
                     reads=[("hT2", kc), "kvw"], writes=[GN[gb_]], cost=200)
            S.op(ACT, lambda s=s, gb_=gb_: nc.scalar.copy(out=kvs[:, :, s, :], in_=G[gb_][:, 0:256].rearrange("p (a d) -> p a d", a=2)), reads=[GN[gb_]], writes=["kvs"])
        if P1 < 6:
            continue
        vb, vres = vsb[b], "vsb%d" % b
        kmt = km[:, t * 4:(t + 1) * 4]
        vb4 = vb.rearrange("p s (h e) -> p s h e", h=2)
        S.op(POOL, lambda vb4=vb4, kmt=kmt: nc.gpsimd.tensor_tensor(
            out=vb4[:, :, :, 0:64], in0=kvs[:, 1, :, :].rearrange("p s (h d) -> p s h d", h=2),
            in1=kmt.unsqueeze(2).unsqueeze(3).to_broadcast([128, 4, 2, 64]), op=ALU.mult),
            reads=["kvs", "km"], writes=[vres])
        S.op(POOL, lambda vb4=vb4, kmt=kmt: nc.gpsimd.tensor_copy(out=vb4[:, :, :, 64], in_=kmt.unsqueeze(2).to_broadcast([128, 4, 2])),
             reads=["km"], writes=[vres])
        S.op(SP, lambda vb=vb, t=t: nc.sync.dma_start(out=vss[t * T:(t + 1) * T, :].rearrange("(s p) e -> p s e", p=128), in_=vb),
             reads=[vres], writes=[("vss", t)], dma_key="vst%d" % b)
        if P1 < 7:
            continue
        head_norm_rope(POOL, kvs[:, 0, :, :], "kvs", 2, gk_bc, "gk", tkc, tks, "tk", ksq, "ksq", krb, "krb")
        for s in range(4):
            S.op(PE, lambda s=s: nc.tensor.transpose(out=tp[:, s * 128:(s + 1) * 128], in_=krb[:, s, :], identity=ident[:]),
                 reads=["krb", "ident"], writes=[("tpb", 0)], cost=118)
        kb_, kres = kTb[b], "kTb%d" % b
        S.op(DVE, lambda kb_=kb_: nc.vector.tensor_copy(out=kb_, in_=tp[:, 0:512]), reads=[("tpb", 0)], writes=[kres])
        S.op(SP, lambda kb_=kb_, t=t: nc.sync.dma_start(out=kts[:, t * T:(t + 1) * T], in_=kb_),
             reads=[kres], writes=[("kts", t)], dma_key="kst%d" % b)

    S.barrier()
    apos[0] = 0
    kT = carve([128, NCT * T], BF16)
    vAf = carve([128, NCH * 130 + 64], BF16)
    vA = vAf[:, 0:NCH * 130].rearrange("p (ch e) -> p ch e", e=130)
    qf = carve([128, 4, 512], F32)
    qsq = hb[:, :, :].rearrange("p s d -> p (s d)").bitcast(F32).rearrange("p (s d) -> p s d", s=4)
    tqc = carve([128, 4, 8, 32], F32)
    tqs = carve([128, 4, 8, 32], F32)
    qrb = carve([128, 4, 512], BF16)
    qTp = Xb[1][:, 0:2, :].rearrange("p a d -> p (a d)").bitcast(BF16).rearrange("p (h t) -> p h t", h=8)
    ub = carve([128, 4, 512], BF16)
    vnb = tqc.rearrange("p s h f -> p (s h f)").bitcast(BF16)[:, 0:2048].rearrange("p (s d) -> p s d", s=4)
    sgb = tqs.rearrange("p s h f -> p (s h f)").bitcast(BF16)[:, 0:2048].rearrange("p (s d) -> p s d", s=4)
    sgT = carve([128, 4, T], BF16)
    aT = carve([128, 8, T], BF16)
    mT = carve([128, 8, T], BF16)
    PT = [carve([128, 1024], BF16), carve([128, 1024], BF16), carve([128, 1024], BF16)]
    rden = carve([128, T], F32)
    ones1 = carve([128, 64], F32)
    onT = carve([128, T], F32)
    ggv_bc = carve([128, 512], F32)
    S.op(POOL, lambda: nc.gpsimd.memset(ones1, 1.0), writes=["ones1"])
    S.op(POOL, lambda: nc.gpsimd.memset(qTp, 0.0), writes=[("qT", c) for c in range(4)])
    S.op(POOL, lambda: nc.gpsimd.memset(vAf[:, NCH * 130:NCH * 130 + 64], 0.0), writes=["vApad"])
    small_load(ggv_bc, bc_row(g_gv), "ggv")

    for c in range(0, NCT if stage >= 2 else 0, 8):
        n = min(8, NCT - c)
        S.op(SP, lambda c=c, n=n: nc.sync.dma_start(out=kT[:, c * T:(c + n) * T], in_=kts[:, c * T:(c + n) * T]),
             reads=[("kts", t) for t in range(c, c + n)], writes=["kT"], dma_key="kTl", cost=9000)
        S.op(SP, lambda c=c, n=n: nc.sync.dma_start(out=vA[:, c * 4:(c + n) * 4, :],
                                                    in_=vss[c * T:(c + n) * T, :].rearrange("(ch p) e -> p ch e", p=128)),
             reads=[("vss", t) for t in range(c, c + n)], writes=["vA"], dma_key="vAl", cost=15000)

    def attention():
        heads = [(c, g) for c in range(4) for g in range(2)]
        seq = [(hi, i) for hi in range(8) for i in range(NPAIR)]
        Sv = [S0[:, :], S1[:, :], tpf]
        Sres = [[GN[0], GN[1]], [GN[2], GN[3]], [("tpb", 0), ("tpb", 1)]]

        def qk(n):
            hi, i = seq[n]
            c, g = heads[hi]
            sb_ = n % 3
            Sx = Sv[sb_]
            for u in range(2):
                ch = 2 * i + u
                S.op(PE, lambda u=u, ch=ch, c=c, g=g, Sx=Sx: nc.tensor.matmul(
                    Sx[:, u * 512:(u + 1) * 512], lhsT=kT[:, ch * 128:(ch + 1) * 128],
                    rhs=qTp[:, c * 2 + g, :], start=True, stop=True),
                    reads=["kT", ("qT", c)], writes=[Sres[sb_][u]])

        qk(0)
        qk(1)
        for n in range(len(seq)):
            hi, i = seq[n]
            c, g = heads[hi]
            sb_ = n % 3
            Sx = Sv[sb_]
            ob = hi % 2
            Ox = (O0, O1)[ob]
            if n + 2 < len(seq):
                qk(n + 2)
            S.op(ACT, lambda Sx=Sx, sb_=sb_: nc.scalar.activation(out=PT[sb_], in_=Sx, func=AF.Exp, scale=0.125),
                 reads=Sres[sb_], writes=["PT%d" % sb_], cost=1023)
            for u in range(2):
                ch = 2 * i + u
                S.op(PE, lambda u=u, ch=ch, g=g, Ox=Ox, sb_=sb_, i=i: nc.tensor.matmul(
                    Ox[:, :], lhsT=vAf[:, ch * 130 + g * 65:ch * 130 + g * 65 + 128], rhs=PT[sb_][:, u * 512:(u + 1) * 512],
                    start=(i == 0 and u == 0), stop=(i == NPAIR - 1 and u == 1)),
                    reads=["vA", "vApad", "PT%d" % sb_], writes=[GN[4 + ob]])
            if i == NPAIR - 1:
                h_true = g * 4 + c
                S.op(DVE, lambda Ox=Ox: nc.vector.reciprocal(out=rden[64:65, :], in_=Ox[64:65, :]), reads=[GN[4 + ob]], writes=["rden"], cost=2472)
                S.op(PE, lambda Sx=Sx: nc.tensor.matmul(Sx[0:64, 0:512], lhsT=ones1[64:65, 0:64], rhs=rden[64:65, :], start=True, stop=True),
                     reads=["rden", "ones1"], writes=[Sres[sb_][0]], cost=970)
                S.op(ACT, lambda Sx=Sx: nc.scalar.copy(out=onT[0:64, :], in_=Sx[0:64, 0:512]), reads=[Sres[sb_][0]], writes=["onT"])
                S.op(DVE, lambda Ox=Ox, h_true=h_true: nc.vector.tensor_tensor(out=aT[0:64, h_true, :], in0=Ox[0:64, :], in1=onT[0:64, :], op=ALU.mult),
                     reads=[GN[4 + ob], "onT"], writes=[("aT", h_true)])

    for t in range(NOT if stage >= 2 else 0):
        b = 0
        X, xres = Xb[b], "X%d" % b
        S.op(SP, lambda b=b, t=t: nc.sync.dma_start(out=Xb[b][:, :, :], in_=tok_rows(x1s[t * T:(t + 1) * T, :])),
             reads=[("x1s", t)], writes=[xres], dma_key="xl%d" % b, cost=13600)
        rmsnorm_T(X, xres, 1)
        rope_tables(t, 8, tqc, tqs, "tq")
        for pi in range(3):
            slot = ring_load("qgg%d" % pi, s_qgg[pi].rearrange("p kc c -> p (kc c)"), 4096)
            rv = ring[:, slot, :].rearrange("p (kc c) -> p kc c", kc=8)
            for s in range(4):
                gb_ = s % 2
                for kc in range(8):
                    S.op(PE, lambda kc=kc, s=s, gb_=gb_, rv=rv: nc.tensor.matmul(G[gb_], lhsT=hT[:, kc, s * 128:(s + 1) * 128], rhs=rv[:, kc, :],
                                                                                  start=(kc == 0), stop=(kc == 7)),
                         reads=[("hT", kc), ("ring", slot)], writes=[GN[gb_]])
                if pi == 0:
                    S.op(ACT, lambda s=s, gb_=gb_: nc.scalar.copy(out=qf[:, s, :], in_=G[gb_]), reads=[GN[gb_]], writes=["qf"])
                elif pi == 1:
                    S.op(ACT, lambda s=s, gb_=gb_: nc.scalar.activation(out=ub[:, s, :], in_=G[gb_], func=AF.Gelu), reads=[GN[gb_]], writes=["ub"])
                else:
                    S.op(ACT, lambda s=s, gb_=gb_: nc.scalar.activation(out=qf[:, s, :], in_=G[gb_], func=AF.Gelu), reads=[GN[gb_]], writes=["qf"])
            if pi == 0:
                head_norm_rope(DVE, qf, "qf", 8, gq_bc, "gq", tqc, tqs, "tq", qsq, "hb", qrb, "qrb")
                for c0 in range(0, 4, 2):
                    bank = (c0 // 2) % 2
                    for kk in range(2):
                        c = c0 + kk
                        for s in range(4):
                            col = bank * 1024 + kk * 512 + s * 128
                            S.op(PE, lambda c=c, s=s, col=col: nc.tensor.transpose(out=tp[:, col:col + 128], in_=qrb[:, s, c * 128:(c + 1) * 128],
                                                                                    identity=ident[:]),
                                 reads=["qrb", "ident"], writes=[("tpb", bank)], cost=118)
                    for kk in range(2):
                        c = c0 + kk
                        for g in range(2):
                            src_ps = tp[g * 64:(g + 1) * 64, bank * 1024 + kk * 512: bank * 1024 + (kk + 1) * 512]
                            dst = qTp[g * 64:(g + 1) * 64, c * 2 + g, :]
                            if bank == 0:
                                S.op(ACT, lambda src_ps=src_ps, dst=dst: nc.scalar.copy(out=dst, in_=src_ps), reads=[("tpb", bank)], writes=[("qT", c)])
                            else:
                                S.op(DVE, lambda src_ps=src_ps, dst=dst: nc.vector.tensor_copy(out=dst, in_=src_ps), reads=[("tpb", bank)], writes=[("qT", c)])
            if pi == 2:
                row_rstd(qf, "qf", 512, 512)
                for s in range(4):
                    S.op(DVE, lambda s=s: nc.vector.scalar_tensor_tensor(out=vnb[:, s, :], in0=qf[:, s, :], scalar=rstd[:, s:s + 1], in1=ggv_bc,
                                                                         op0=ALU.mult, op1=ALU.mult),
                         reads=["qf", ("rstd", 0), "ggv"], writes=["tqc"])
                for s in range(4):
                    gb_ = 2 + (s % 2)
                    for g in range(8):
                        S.op(PE, lambda s=s, g=g, gb_=gb_: nc.tensor.matmul(G[gb_][:, g * 64:(g + 1) * 64], lhsT=wsT[:, g, :], rhs=vnb[:, s, g * 64:(g + 1) * 64],
                                                                             start=True, stop=True),
                             reads=["tqc", "wsT"], writes=[GN[gb_]], cost=70)
                    tt, tn = (tmpA, "tmpA") if s % 2 == 0 else (tmpB, "tmpB")
                    S.op(DVE, lambda gb_=gb_, tt=tt: nc.vector.tensor_tensor(out=tt.rearrange("p (g c) -> p g c", g=8),
                                                                             in0=G[gb_].rearrange("p (g c) -> p g c", g=8),
                                                                             in1=bspT[:, :].unsqueeze(2).to_broadcast([128, 8, 64]), op=ALU.add),
                         reads=[GN[gb_], "bsp"], writes=[tn])
                    S.op(POOL, lambda s=s, tt=tt: nc.gpsimd.tensor_tensor(out=sgb[:, s, :], in0=tt, in1=ub[:, s, :], op=ALU.mult),
                         reads=[tn, "ub"], writes=["tqs"])
                transpose_T(sgb, "tqs", 4, sgT, "sgT")
        attention()
        for oc in range(8):
            slot = ring_load("mg%d" % oc, s_mg[oc], 3584)
            rg = ring[:, slot, 0:2048].rearrange("p (kc c) -> p kc c", kc=8)
            g0, g1 = (0, 1) if oc % 2 == 0 else (2, 3)
            for kc in range(8):
                S.op(PE, lambda kc=kc, rg=rg, g0=g0: nc.tensor.matmul(G[g0], lhsT=rg[:, kc, 0:128], rhs=hT[:, kc, :], start=(kc == 0), stop=(kc == 7)),
                     reads=[("ring", slot), ("hT", kc)], writes=[GN[g0]])
            for kc in range(8):
                S.op(PE, lambda kc=kc, rg=rg, g1=g1: nc.tensor.matmul(G[g1], lhsT=rg[:, kc, 128:256], rhs=hT[:, kc, :], start=(kc == 0), stop=(kc == 7)),
                     reads=[("ring", slot), ("hT", kc)], writes=[GN[g1]])
            for h in range(8):
                S.op(PE, lambda h=h, slot=slot: nc.tensor.matmul(G[4], lhsT=ring[0:64, slot, 2560 + h * 128:2560 + (h + 1) * 128], rhs=aT[0:64, h, :],
                                                                 start=(h == 0), stop=(h == 7)),
                     reads=[("ring", slot), ("aT", h)], writes=[GN[4]])
            for kc in range(4):
                S.op(PE, lambda kc=kc, slot=slot: nc.tensor.matmul(G[5], lhsT=ring[:, slot, 2048 + kc * 128:2048 + (kc + 1) * 128], rhs=sgT[:, kc, :],
                                                                   start=(kc == 0), stop=(kc == 3)),
                     reads=[("ring", slot), ("sgT", kc)], writes=[GN[5]])
            S.op(ACT, lambda g0=g0: nc.scalar.activation(out=tmpA, in_=G[g0], func=AF.Sigmoid), reads=[GN[g0]], writes=["tmpA"])
            S.op(ACT, lambda g1=g1: nc.scalar.activation(out=tmpB, in_=G[g1], func=AF.Sigmoid), reads=[GN[g1]], writes=["tmpB"])
            S.op(DVE, lambda: nc.vector.tensor_tensor(out=tmpA, in0=tmpA, in1=G[4], op=ALU.mult), reads=["tmpA", GN[4]], writes=["tmpA"])
            S.op(DVE, lambda: nc.vector.tensor_tensor(out=tmpB, in0=tmpB, in1=G[5], op=ALU.mult), reads=["tmpB", GN[5]], writes=["tmpB"])
            S.op(POOL, lambda oc=oc: nc.gpsimd.tensor_tensor(out=mT[:, oc, :], in0=tmpA, in1=tmpB, op=ALU.add),
                 reads=["tmpA", "tmpB"], writes=[("mT", oc)])
        for h in range(2):
            slot = ring_load("wo%d" % h, s_wo[h].rearrange("p kc c -> p (kc c)"), 4096)
            rv = ring[:, slot, :].rearrange("p (kc c) -> p kc c", kc=8)
            for s in range(4):
                gb_ = s % 2
                for kc in range(8):
                    S.op(PE, lambda kc=kc, s=s, gb_=gb_, rv=rv: nc.tensor.matmul(G[gb_], lhsT=mT[:, kc, s * 128:(s + 1) * 128], rhs=rv[:, kc, :],
                                                                                  start=(kc == 0), stop=(kc == 7)),
                         reads=[("mT", kc), ("ring", slot)], writes=[GN[gb_]])
                S.op(DVE, lambda s=s, h=h, gb_=gb_, X=X: nc.vector.tensor_tensor(out=X[:, s, h * 512:(h + 1) * 512], in0=G[gb_],
                                                                                  in1=X[:, s, h * 512:(h + 1) * 512], op=ALU.add),
                     reads=[GN[gb_], xres], writes=[xres])
        S.op(SP, lambda b=b, t=t: nc.sync.dma_start(out=tok_rows(x1s[t * T:(t + 1) * T, :]), in_=Xb[b][:, :, :]),
             reads=[xres], writes=[("x1s", t)], dma_key="xs%d" % b, cost=13600)

    S.barrier()
    apos[0] = 0
    hid = carve([128, NJ, T], BF16)
    dn = [carve([128, NJ, 512], BF16), carve([128, NJ, 512], BF16)]
    pf = carve([128, 4, PLE], F32)
    pbf = carve([128, 4, PLE], BF16)
    pT = carve([128, 2, T], BF16)
    gfin_bc = carve([128, D], F32)
    hb3 = carve([128, 4, D], BF16)
    hT3 = carve([128, 8, T], BF16)
    ALT = (hb3, "hb3", hT3, "hT3")
    small_load(gfin_bc, bc_row(g_fin), "gfin")
    out_ops = []
    for t in range(NOT if stage >= 3 else 0):
        b = t % 2
        X, xres = Xb[b], "X%d" % b
        S.op(SP, lambda b=b, t=t: nc.sync.dma_start(out=Xb[b][:, :, :], in_=tok_rows(x1s[t * T:(t + 1) * T, :])),
             reads=[("x1s", t)], writes=[xres], dma_key="xl%d" % b, cost=13600)
        rmsnorm_T(X, xres, 2)
        ffn(1, X, xres, hid, dn)
        rmsnorm_T(X, xres, 3, ALT)
        S.op(SP, lambda t=t: nc.sync.dma_start(out=pf, in_=tok_rows(pin[t * T:(t + 1) * T, :])), writes=["pf"], dma_key="pfl")
        S.op(POOL, lambda: nc.gpsimd.tensor_copy(out=pbf, in_=pf), reads=["pf"], writes=["pbf"])
        transpose_T(pbf, "pbf", 2, pT, "pT")
        slotp = ring_load("pl", s_pl.rearrange("p kc c -> p (kc c)"), 2048)
        rvp = ring[:, slotp, 0:2048].rearrange("p (kc c) -> p kc c", kc=2)
        for h in range(2):
            slot = ring_load("pg%d" % h, s_pg[h].rearrange("p kc c -> p (kc c)"), 4096)
            rv = ring[:, slot, :].rearrange("p (kc c) -> p kc c", kc=8)
            for s in range(4):
                g0, g1 = (0, 1) if s % 2 == 0 else (2, 3)
                for kc in range(8):
                    S.op(PE, lambda kc=kc, s=s, g0=g0, rv=rv: nc.tensor.matmul(G[g0], lhsT=hT3[:, kc, s * 128:(s + 1) * 128], rhs=rv[:, kc, :],
                                                                                start=(kc == 0), stop=(kc == 7)),
                         reads=[("hT3", kc), ("ring", slot)], writes=[GN[g0]])
                for k2 in range(2):
                    S.op(PE, lambda k2=k2, s=s, g1=g1, h=h, rvp=rvp: nc.tensor.matmul(G[g1], lhsT=pT[:, k2, s * 128:(s + 1) * 128],
                                                                                       rhs=rvp[:, k2, h * 512:(h + 1) * 512],
                                                                                       start=(k2 == 0), stop=(k2 == 1)),
                         reads=[("pT", k2), ("ring", slotp)], writes=[GN[g1]])
                tt, tn = (tmpA, "tmpA") if s % 2 == 0 else (tmpB, "tmpB")
                S.op(ACT, lambda g0=g0, tt=tt: nc.scalar.activation(out=tt, in_=G[g0], func=AF.Sigmoid), reads=[GN[g0]], writes=[tn])
                S.op(DVE, lambda g1=g1, tt=tt: nc.vector.tensor_tensor(out=tt, in0=tt, in1=G[g1], op=ALU.mult), reads=[tn, GN[g1]], writes=[tn])
                S.op(POOL, lambda s=s, h=h, tt=tt, X=X: nc.gpsimd.tensor_tensor(out=X[:, s, h * 512:(h + 1) * 512], in0=X[:, s, h * 512:(h + 1) * 512],
                                                                                 in1=tt, op=ALU.add),
                     reads=[tn, xres], writes=[xres])
        row_rstd(X, xres, D, D)
        for s in range(4):
            S.op(DVE, lambda s=s, X=X: nc.vector.scalar_tensor_tensor(out=X[:, s, :], in0=X[:, s, :], scalar=rstd[:, s:s + 1], in1=gfin_bc,
                                                                      op0=ALU.mult, op1=ALU.mult),
                 reads=[xres, ("rstd", 0), "gfin"], writes=[xres])
        out_ops.append(S.op(SP, lambda b=b, t=t: nc.sync.dma_start(out=tok_rows(y[t * T:(t + 1) * T, :]), in_=Xb[b][:, :, :]),
                            reads=[xres], dma_key="ys%d" % b, cost=13600))
    if stage < 3 and stage >= 1:
        out_ops.append(S.op(SP, lambda: nc.sync.dma_start(out=y[:, :], in_=x1s[:, :]), reads=[("x1s", t) for t in range(NOT)], dma_key="dbg"))
    S.op(SP, None, extra=out_ops)

    if SCHEDULE:
        S.schedule()
        build_program.last_est_ns = S.est_ns
    semkeys = S.finalize()
    by_eng = {e: [o for o in S.ops if o.eng == e] for e in ENGS}
    with ExitStack() as es:
        sems = {k: es.enter_context(nc.semaphore("s%d" % i)) for i, k in enumerate(semkeys)}
        block = es.enter_context(nc.Block())

        def emit(engname, eng):
            waited = {}
            for o in by_eng[engname]:
                need = {}
                for d in o.deps:
                    if not d.signal or d.fn is None:
                        continue
                    if d.eng == PE and o.eng == PE and d.dma_key is None and o.dma_key is None:
                        continue
                    if need.get(d.sem, 0) < d.val:
                        need[d.sem] = d.val
                for k, v in need.items():
                    if waited.get(k, 0) < v:
                        eng.wait_ge(sems[k], v)
                        waited[k] = v
                if o.fn is not None:
                    ins = o.fn()
                    if o.signal:
                        ins.then_inc(sems[o.sem], 16 if o.dma_key is not None else 1)
                else:
                    assert not o.signal

        @block.tensor
        def _(e):
            emit(PE, e)

        @block.scalar
        def _(e):
            emit(ACT, e)

        @block.vector
        def _(e):
            emit(DVE, e)

        @block.gpsimd
        def _(e):
            emit(POOL, e)

        @block.sync
        def _(e):
            emit(SP, e)
    return nc


NCT_FULL, NOT_FULL = 32, 8
WNAMES = ["g_ffn1", "w_ffn1_gu", "w_ffn1_down", "g_mix", "w_in", "g_q", "g_k", "g_gmlp_v", "w_spatial", "b_spatial",
          "w_branch_attn", "w_branch_gmlp", "w_out", "g_ffn2", "w_ffn2_gu", "w_ffn2_down", "g_ple", "w_ple_gate", "w_ple", "g_final"]


def _pos_table(tok_idx):
    tok_idx = np.asarray(tok_idx, np.int64)
    return np.stack([tok_idx // 64, tok_idx % 64], axis=1).astype(np.float32)


def kernel(**inputs):
    xp = np.asarray(inputs["x_prompt"], np.float32)
    xs = np.asarray(inputs["x_sample"], np.float32)
    pp = np.asarray(inputs["p_prompt"], np.float32)
    ps = np.asarray(inputs["p_sample"], np.float32)
    w = {k: np.ascontiguousarray(np.asarray(inputs[k], np.float32)) for k in WNAMES}
    w["g_final"] = w["g_final"].reshape(1, D)
    NTOK = NCT_FULL * T
    own = NOT_FULL * T
    in_maps = []
    for c in range(8):
        if c < 4:
            order = [c] + [(c + k) % 4 for k in range(1, 4)]
            xcx = np.concatenate([xp[o] for o in order], axis=0)
            posi = np.concatenate([np.arange(own)] * 4)
            msk = np.zeros(NTOK, np.float32); msk[:own] = 1.0
            pc = pp[0, c]
        else:
            q = c - 4
            order = [q] + [(q + k) % 4 for k in range(1, 4)]
            xcx = np.concatenate([xs[0, o * own:(o + 1) * own] for o in order], axis=0)
            posi = np.concatenate([np.arange(o * own, (o + 1) * own) for o in order])
            msk = np.ones(NTOK, np.float32)
            pc = ps[0, 0, q * own:(q + 1) * own]
        m = {"xc": np.ascontiguousarray(xcx), "pin": np.ascontiguousarray(pc), "pos": _pos_table(posi),
             "kmask": np.ascontiguousarray(msk.reshape(NTOK // 128, 128).T)}
        m.update(w)
        in_maps.append(m)
    nc = build_program(NCT_FULL, NOT_FULL)
    res = run_bass_kernel_spmd(nc, in_maps, core_ids=list(range(8)))
    ys = [np.asarray(res.results[c]["y"], np.float32) for c in range(8)]
    y_prompt = np.stack(ys[0:4], axis=0)
    y_sample = np.concatenate(ys[4:8], axis=0)[None]
    return (y_prompt, y_sample)
```

```python
import math
SCHEDULE = True
P0 = 255
P1 = 99
P1SUB = 99
P1T = 0
from contextlib import ExitStack
import numpy as np
import concourse.bass as bass
import concourse.mybir as mybir
from concourse.bass_utils import run_bass_kernel_spmd

F32 = mybir.dt.float32
BF16 = mybir.dt.bfloat16
I32 = mybir.dt.int32
AF = mybir.ActivationFunctionType
ALU = mybir.AluOpType
AX = mybir.AxisListType

D = 1024
DFF = 2816
NJ = DFF // 128
PLE = 256
EPS = 1e-6
T = 512
NSLOT = 3
SLOTW = 4096
PE, ACT, DVE, POOL, SP = "pe", "act", "dve", "pool", "sp"
ENGS = [PE, ACT, DVE, POOL, SP]


class Op:
    __slots__ = ("eng", "fn", "deps", "dma_key", "sem", "val", "signal", "idx", "cost", "phase", "t0", "t1")

    def __init__(self, eng, fn, deps, dma_key, cost):
        self.eng, self.fn, self.deps, self.dma_key, self.cost = eng, fn, deps, dma_key, cost
        self.sem = None
        self.val = 0
        self.signal = False


DEFAULT_COST = {PE: 216, ACT: 600, DVE: 650, POOL: 900, SP: 2500}


class Sched:
    def __init__(self):
        self.ops = []
        self.last_write = {}
        self.readers = {}
        self.phase = 0
        self.since_barrier = []
        self.cur_barrier = {}

    def op(self, eng, fn, reads=(), writes=(), dma_key=None, extra=(), cost=None):
        deps = set(extra)
        for r in reads:
            w = self.last_write.get(r)
            if w is not None:
                deps.add(w)
        for w_ in writes:
            w = self.last_write.get(w_)
            if w is not None:
                deps.add(w)
            for rd in self.readers.get(w_, ()):
                deps.add(rd)
        bar = self.cur_barrier.get(eng)
        if bar is not None:
            deps.add(bar)
        o = Op(eng, fn, deps, dma_key, DEFAULT_COST[eng] if cost is None else cost)
        o.idx = len(self.ops)
        o.phase = self.phase
        o.sem = (eng, self.phase) if dma_key is None else ("dma", dma_key)
        self.ops.append(o)
        for r in reads:
            self.readers.setdefault(r, []).append(o)
        for w_ in writes:
            self.last_write[w_] = o
            self.readers[w_] = []
        if fn is not None:
            self.since_barrier.append(o)
        return o

    def barrier(self):
        prev = list(self.since_barrier)
        self.since_barrier = []
        for e in ENGS:
            self.cur_barrier[e] = self.op(e, None, extra=prev, cost=0)
        self.phase += 1

    def schedule(self):
        import heapq
        n = len(self.ops)
        succ = [[] for _ in range(n)]
        indeg = [0] * n
        for o in self.ops:
            indeg[o.idx] = len(o.deps)
            for d in o.deps:
                succ[d.idx].append(o)
        pending = {e: [] for e in ENGS}
        avail = {e: [] for e in ENGS}
        free = {e: 0.0 for e in ENGS}
        ready_t = [0.0] * n
        for o in self.ops:
            if indeg[o.idx] == 0:
                heapq.heappush(pending[o.eng], (0.0, o.idx))
        order = []
        done = 0
        while done < n:
            best = None
            for e in ENGS:
                pe_, av = pending[e], avail[e]
                while pe_ and pe_[0][0] <= free[e]:
                    heapq.heappush(av, heapq.heappop(pe_)[1])
                if av:
                    cand = (free[e], av[0], e, True)
                elif pe_:
                    cand = (pe_[0][0], pe_[0][1], e, False)
                else:
                    continue
                if best is None or cand[:2] < best[:2]:
                    best = cand
            assert best is not None, "dependency cycle"
            start, idx, e, from_av = best
            if from_av:
                heapq.heappop(avail[e])
            else:
                heapq.heappop(pending[e])
            o = self.ops[idx]
            o.t0 = start
            if o.dma_key is not None:
                free[e] = start + 60.0
                o.t1 = start + o.cost
            else:
                o.t1 = start + o.cost
                free[e] = o.t1
            order.append(o)
            done += 1
            for sc in succ[idx]:
                if ready_t[sc.idx] < o.t1:
                    ready_t[sc.idx] = o.t1
                indeg[sc.idx] -= 1
                if indeg[sc.idx] == 0:
                    heapq.heappush(pending[sc.eng], (ready_t[sc.idx], sc.idx))
        self.ops = order
        self.est_ns = max(o.t1 for o in order)

    def finalize(self):
        for o in self.ops:
            for d in o.deps:
                if d.eng == PE and o.eng == PE and d.dma_key is None and o.dma_key is None:
                    continue
                if d.fn is None:
                    continue
                d.signal = True
        for o in self.ops:
            if o.dma_key is not None and o.fn is not None:
                o.signal = True
        counts = {}
        for o in self.ops:
            if o.signal:
                counts[o.sem] = counts.get(o.sem, 0) + (16 if o.dma_key is not None else 1)
                o.val = counts[o.sem]
        return sorted(counts.keys(), key=str)


def build_program(NCT, NOT, stage=3):
    NCH = NCT * 4
    NPAIR = NCH // 2
    nc = bass.Bass("TRN2", target_bir_lowering=False)

    def din(name, shape, dt=F32):
        return nc.dram_tensor(name, list(shape), dt, kind="ExternalInput").ap()

    xc = din("xc", [NCT * T, D])
    pin = din("pin", [NOT * T, PLE])
    pos = din("pos", [NCT * T, 2])
    kmask = din("kmask", [128, NCH])
    g_ffn1 = din("g_ffn1", [1, D]); w1gu = din("w_ffn1_gu", [1, D, 2 * DFF]); w1d = din("w_ffn1_down", [1, DFF, D])
    g_mix = din("g_mix", [1, D]); w_in = din("w_in", [1, D, 3840])
    g_q = din("g_q", [1, 64]); g_k = din("g_k", [1, 64]); g_gv = din("g_gmlp_v", [1, 512])
    w_sp = din("w_spatial", [1, 8, 128, 128]); b_sp = din("b_spatial", [1, 8, 128])
    w_ba = din("w_branch_attn", [1, 512, D]); w_bg = din("w_branch_gmlp", [1, 512, D]); w_out = din("w_out", [1, D, D])
    g_ffn2 = din("g_ffn2", [1, D]); w2gu = din("w_ffn2_gu", [1, D, 2 * DFF]); w2d = din("w_ffn2_down", [1, DFF, D])
    g_ple = din("g_ple", [1, D]); w_pg = din("w_ple_gate", [1, D, D]); w_pl = din("w_ple", [1, PLE, D])
    g_fin = din("g_final", [1, D])
    y = nc.dram_tensor("y", [NOT * T, D], F32, kind="ExternalOutput").ap()

    def dscr(name, shape, dt=BF16):
        return nc.dram_tensor(name, list(shape), dt).ap()

    s_gu = [dscr("s_gu1", [11, 128, 8, 512]), dscr("s_gu2", [11, 128, 8, 512])]
    s_dn = [dscr("s_dn1", [2, 128, NJ, 512]), dscr("s_dn2", [2, 128, NJ, 512])]
    s_kv = dscr("s_kv", [128, 8, 256])
    s_qgg = dscr("s_qgg", [3, 128, 8, 512])
    s_mg = dscr("s_mg", [8, 128, 3584])
    s_wo = dscr("s_wo", [2, 128, 8, 512])
    s_pg = dscr("s_pg", [2, 128, 8, 512])
    s_pl = dscr("s_pl", [128, 2, 1024])
    x1s = dscr("x1s", [NOT * T, D], F32)
    kts = dscr("kts", [128, NCT * T])
    vss = dscr("vss", [NCT * T, 130])

    S = Sched()

    def sb(name, shape, dt):
        return nc.alloc_sbuf_tensor(name, list(shape), dt)

    ident = sb("ident", [128, 128], BF16)
    gcol = sb("gcol", [128, 4, 8], F32)
    gq_bc = sb("gq_bc", [128, 64], F32)
    gk_bc = sb("gk_bc", [128, 64], F32)
    bspT = sb("bspT", [128, 8], F32)
    wsT = sb("wsT", [128, 8, 128], BF16)
    inv_bc = sb("inv_bc", [128, 16], F32)
    km = sb("km", [128, NCH], F32)
    negpi = sb("negpi", [128, 1], F32)
    epsb = sb("epsb", [128, 1], F32)
    ring = sb("ring", [128, NSLOT, SLOTW], BF16)
    Xb = [sb("X0", [128, 4, D], F32), sb("X1", [128, 4, D], F32)]
    hb = sb("hb", [128, 4, D], BF16)
    hT = sb("hT", [128, 8, T], BF16)
    ss = sb("ss", [128, 8], F32)
    rstd = sb("rstd", [128, 8], F32)
    tmpA = sb("tmpA", [128, T], F32)[:, :]
    tmpB = sb("tmpB", [128, T], F32)[:, :]
    posb = sb("posb", [128, 4, 2], F32)
    ang = sb("ang", [128, 4, 2, 16], F32)
    angm = sb("angm", [128, 4, 2, 16], F32)
    cs = sb("cs", [128, 4, 32], F32)
    sn = sb("sn", [128, 4, 32], F32)
    angki = sb("angki", [128, 4, 32], I32)
    angkf = sb("angkf", [128, 4, 32], F32)
    angr = sb("angr", [128, 4, 32], F32)
    hss = sb("hss", [128, 32], F32)
    hrs = sb("hrs", [128, 32], F32)
    ARENA_BYTES = 124 * 1024
    arena = sb("arena", [128, ARENA_BYTES // 4], F32)
    apos = [0]

    def carve(shape, dt):
        esz = 4 if dt in (F32, I32) else 2
        n = int(np.prod(shape[1:]))
        nbytes = (n * esz + 31) // 32 * 32
        off = apos[0]
        apos[0] += nbytes
        assert apos[0] <= ARENA_BYTES, (apos[0], ARENA_BYTES)
        v = arena[:, off // 4:(off + nbytes) // 4]
        if esz == 2:
            v = v.bitcast(BF16)
        v = v[:, 0:n]
        if len(shape) == 3:
            v = v.rearrange("p (a b) -> p a b", a=shape[1])
        elif len(shape) == 4:
            v = v.rearrange("p (a b c) -> p a b c", a=shape[1], b=shape[2])
        return v

    tp = nc.alloc_psum_tensor("tp", [128, 2048], BF16)
    S0 = nc.alloc_psum_tensor("S0", [128, 1024], F32)
    S1 = nc.alloc_psum_tensor("S1", [128, 1024], F32)
    O0 = nc.alloc_psum_tensor("O0", [128, 512], F32)
    O1 = nc.alloc_psum_tensor("O1", [128, 512], F32)
    G = [S0[:, 0:512], S0[:, 512:1024], S1[:, 0:512], S1[:, 512:1024], O0[:, :], O1[:, :]]
    GN = ["G0", "G1", "G2", "G3", "G4", "G5"]
    tpf = tp[:, :].bitcast(F32)

    def vec(e):
        return nc.vector if e == DVE else nc.gpsimd

    def cast(key, out_ap, in_ap):
        if not (P0 & 8):
            return None
        return S.op(POOL, lambda o=out_ap, i=in_ap: nc.gpsimd.dma_start(out=o, in_=i),
                    writes=[("scr", key)], dma_key="c_" + key, cost=9000)

    def small_load(out_ap, in_ap, res):
        def f():
            with nc.allow_non_contiguous_dma(reason="tiny constant layout load"):
                return nc.sync.dma_start(out=out_ap, in_=in_ap)
        return S.op(SP, f, writes=[res], dma_key="k_" + str(res).replace("'", "").replace(" ", ""))

    def bc_row(ap2d):
        return ap2d.partition_broadcast(128).rearrange("p o d -> p (o d)")

    for i, g in enumerate([g_ffn1, g_mix, g_ffn2, g_ple] if P0 & 1 else []):
        small_load(gcol[:, i, :], g.rearrange("o (kc p) -> p (o kc)", p=128), ("gcol", i))
    if P0 & 2:
        small_load(gq_bc[:, :], bc_row(g_q), "gq")
        small_load(gk_bc[:, :], bc_row(g_k), "gk")
    if P0 & 4:
        small_load(bspT[:, :], b_sp.rearrange("o g p -> p (o g)"), "bsp")
    small_load(km[:, :], kmask[:, :], "km")

    def mk_ident():
        nc.gpsimd.memset(ident[:], 0.0)
        return nc.gpsimd.affine_select(out=ident[:], in_=ident[:], pattern=[[-1, 128]], compare_op=ALU.not_equal,
                                       fill=1.0, base=0, channel_multiplier=1)
    S.op(POOL, mk_ident, writes=["ident"])
    S.op(POOL, lambda: nc.gpsimd.memset(negpi[:], -math.pi), writes=["negpi"])
    S.op(POOL, lambda: nc.gpsimd.memset(epsb[:], EPS), writes=["epsb"])
    S.op(POOL, lambda: nc.gpsimd.iota(out=cs[:, 0, 0:16].bitcast(I32), pattern=[[1, 16]], base=0, channel_multiplier=0),
         writes=["cs"])
    S.op(POOL, lambda: nc.gpsimd.tensor_copy(out=sn[:, 0, 0:16], in_=cs[:, 0, 0:16].bitcast(I32)), reads=["cs"], writes=["sn"])
    S.op(ACT, lambda: nc.scalar.activation(out=inv_bc[:, :], in_=sn[:, 0, 0:16], func=AF.Exp, scale=-math.log(10000.0) / 16.0),
         reads=["sn"], writes=["inv"])

    S.op(SP, lambda: nc.sync.dma_start(out=Xb[0][:, 0, :].rearrange("p (g q) -> p g q", g=8),
                                       in_=w_sp[0].rearrange("g p q -> p g q")), writes=[("X0", 0)], dma_key="const")
    S.op(DVE, lambda: nc.vector.tensor_copy(out=hb[:, 0, :], in_=Xb[0][:, 0, :]), reads=[("X0", 0)], writes=["hb"])
    for g in range(8):
        S.op(PE, lambda g=g: nc.tensor.transpose(out=tp[:, g * 128:(g + 1) * 128], in_=hb[:, 0, g * 128:(g + 1) * 128], identity=ident[:]),
             reads=["hb", "ident"], writes=[("tpb", 0)], cost=118)
    S.op(DVE, lambda: nc.vector.tensor_copy(out=wsT[:, :, :].rearrange("q g p -> q (g p)"), in_=tp[:, 0:1024]),
         reads=[("tpb", 0)], writes=["wsT"])

    def kcp(ap2d):
        return ap2d.rearrange("(kc p) c -> p kc c", p=128)

    def cast_ffn(idx, wgu, wd):
        for i in range(11):
            cast("gu%d_%d" % (idx, i), s_gu[idx][i, :, :, 0:256], kcp(wgu[0, :, 256 * i:256 * i + 256]))
            cast("gu%d_%d" % (idx, i), s_gu[idx][i, :, :, 256:512], kcp(wgu[0, :, DFF + 256 * i:DFF + 256 * i + 256]))
        for h in range(2):
            cast("dn%d_%d" % (idx, h), s_dn[idx][h], kcp(wd[0, :, 512 * h:512 * h + 512]))

    cast_ffn(0, w1gu, w1d)
    cast("kv", s_kv, kcp(w_in[0, :, 512:768]))
    for g in range(2):
        for c in range(4):
            cast("qgg0", s_qgg[0, :, :, c * 128 + g * 64:c * 128 + g * 64 + 64], kcp(w_in[0, :, (g * 4 + c) * 64:(g * 4 + c) * 64 + 64]))
    for i, c0 in ((1, 768), (2, 1280)):
        cast("qgg%d" % i, s_qgg[i], kcp(w_in[0, :, c0:c0 + 512]))
    for oc in range(8):
        k = "mg%d" % oc
        gts = s_mg[oc, :, 0:2048].rearrange("p (kc c) -> p kc c", kc=8)
        cast(k, gts[:, :, 0:128], kcp(w_in[0, :, 1792 + oc * 128:1792 + oc * 128 + 128]))
        cast(k, gts[:, :, 128:256], kcp(w_in[0, :, 2816 + oc * 128:2816 + oc * 128 + 128]))
        cast(k, s_mg[oc, :, 2048:2560].rearrange("p (kc c) -> p kc c", kc=4), kcp(w_bg[0, :, oc * 128:oc * 128 + 128]))
        cast(k, s_mg[oc, 0:64, 2560:3584].rearrange("p (h c) -> p h c", h=8),
             w_ba[0, :, oc * 128:oc * 128 + 128].rearrange("(h p) c -> p h c", p=64))
    for h in range(2):
        cast("wo%d" % h, s_wo[h], kcp(w_out[0, :, 512 * h:512 * h + 512]))
    cast_ffn(1, w2gu, w2d)
    for h in range(2):
        cast("pg%d" % h, s_pg[h], kcp(w_pg[0, :, 512 * h:512 * h + 512]))
    cast("pl", s_pl, kcp(w_pl[0, :, :]))

    ring_n = [0]

    def ring_load(key, src_ap, width):
        slot = ring_n[0] % NSLOT
        ring_n[0] += 1
        S.op(SP, lambda s=slot, a=src_ap, w=width: nc.sync.dma_start(out=ring[:, s, 0:w], in_=a),
             reads=[("scr", key)], writes=[("ring", slot)], dma_key="ring%d" % slot, cost=2000 + width * 256 // 180)
        return slot

    def transpose_T(src, srcres, nkc, outT, outres, gi=None):
        for k0 in range(0, nkc, 2):
            bank = (k0 // 2) % 2
            for kk in range(2):
                kc = k0 + kk
                for s in range(4):
                    col = bank * 1024 + kk * 512 + s * 128
                    S.op(PE, lambda kc=kc, s=s, col=col: nc.tensor.transpose(out=tp[:, col:col + 128],
                                                                            in_=src[:, s, kc * 128:(kc + 1) * 128], identity=ident[:]),
                         reads=[srcres, "ident"], writes=[("tpb", bank)], cost=118)
            for kk in range(2):
                kc = k0 + kk
                e = ACT if bank == 0 else DVE
                src_ps = tp[:, bank * 1024 + kk * 512: bank * 1024 + (kk + 1) * 512]
                if gi is None:
                    if e == ACT:
                        f = lambda kc=kc, src_ps=src_ps: nc.scalar.copy(out=outT[:, kc, :], in_=src_ps)
                    else:
                        f = lambda kc=kc, src_ps=src_ps: nc.vector.tensor_copy(out=outT[:, kc, :], in_=src_ps)
                    rd = [("tpb", bank)]
                else:
                    if e == ACT:
                        f = lambda kc=kc, src_ps=src_ps: nc.scalar.activation(out=outT[:, kc, :], in_=src_ps, func=AF.Copy,
                                                                               scale=gcol[:, gi, kc:kc + 1])
                    else:
                        f = lambda kc=kc, src_ps=src_ps: nc.vector.tensor_scalar(out=outT[:, kc, :], in0=src_ps,
                                                                                  scalar1=gcol[:, gi, kc:kc + 1], scalar2=None, op0=ALU.mult)
                    rd = [("tpb", bank), ("gcol", gi)]
                S.op(e, f, reads=rd, writes=[(outres, kc)])

    def row_rstd(X, xres, width, nrm, hbuf=None, hbres="hb", c0=0):
        hbuf = hb if hbuf is None else hbuf
        for s in range(4):
            S.op(ACT, lambda s=s: nc.scalar.activation(out=hbuf[:, s, 0:width], in_=X[:, s, 0:width], func=AF.Square, accum_out=ss[:, c0 + s:c0 + s + 1]),
                 reads=[(xres, s)], writes=[hbres, ("ss", c0 + s)], cost=1056 if width > 512 else 843)
        S.op(ACT, lambda: nc.scalar.activation(out=rstd[:, c0:c0 + 4], in_=ss[:, c0:c0 + 4], func=AF.Sqrt, scale=1.0 / nrm, bias=epsb[:, 0:1]),
             reads=[("ss", c0 + s) for s in range(4)] + ["epsb"], writes=[("rstd", c0)], cost=300)
        S.op(DVE, lambda: nc.vector.reciprocal(out=rstd[:, c0:c0 + 4], in_=rstd[:, c0:c0 + 4]), reads=[("rstd", c0)], writes=[("rstd", c0)], cost=190)

    def rmsnorm_T(X, xres, gi, alt=None):
        hbuf, hbres, outT, outres = (hb, "hb", hT, "hT") if alt is None else alt
        c0 = 0 if alt is None else 4
        row_rstd(X, xres, D, D, hbuf, hbres, c0)
        for s in range(4):
            S.op(DVE, lambda s=s: nc.vector.tensor_scalar(out=hbuf[:, s, :], in0=X[:, s, :], scalar1=rstd[:, c0 + s:c0 + s + 1], scalar2=None, op0=ALU.mult),
                 reads=[(xres, s), ("rstd", c0)], writes=[hbres])
        transpose_T(hbuf, hbres, 8, outT, outres, gi)

    def ffn(idx, X, xres, hid, dn, mid=None):
        for i in range(11):
            slot = ring_load("gu%d_%d" % (idx, i), s_gu[idx][i].rearrange("p kc c -> p (kc c)"), 4096)
            rv = ring[:, slot, :].rearrange("p (kc c) -> p kc c", kc=8)
            for jj in range(2):
                j = 2 * i + jj
                ga, gb = (0, 1) if j % 2 == 0 else (2, 3)
                for kc in range(8):
                    S.op(PE, lambda kc=kc, jj=jj, ga=ga, rv=rv: nc.tensor.matmul(G[ga], lhsT=rv[:, kc, jj * 128:(jj + 1) * 128], rhs=hT[:, kc, :],
                                                                                  start=(kc == 0), stop=(kc == 7)),
                         reads=[("ring", slot), ("hT", kc)], writes=[GN[ga]])
                for kc in range(8):
                    S.op(PE, lambda kc=kc, jj=jj, gb=gb, rv=rv: nc.tensor.matmul(G[gb], lhsT=rv[:, kc, 256 + jj * 128:256 + (jj + 1) * 128], rhs=hT[:, kc, :],
                                                                                  start=(kc == 0), stop=(kc == 7)),
                         reads=[("ring", slot), ("hT", kc)], writes=[GN[gb]])
                tt, tn = (tmpA, "tmpA") if j % 2 == 0 else (tmpB, "tmpB")
                S.op(ACT, lambda ga=ga, tt=tt: nc.scalar.activation(out=tt[:, :], in_=G[ga], func=AF.Silu), reads=[GN[ga]], writes=[tn])
                S.op(DVE, lambda j=j, gb=gb, tt=tt: nc.vector.tensor_tensor(out=hid[:, j, :], in0=tt[:, :], in1=G[gb], op=ALU.mult),
                     reads=[tn, GN[gb]], writes=[("hid", j)])
        if mid is not None:
            mid()
        for h in range(2):
            S.op(SP, lambda h=h: nc.sync.dma_start(out=dn[h], in_=s_dn[idx][h]),
                 reads=[("scr", "dn%d_%d" % (idx, h))], writes=[("dn", h)], dma_key="dn%d" % h, cost=18000)
            for s in range(4):
                b = 4 + (s % 2)
                for j in range(NJ):
                    S.op(PE, lambda j=j, s=s, h=h, b=b: nc.tensor.matmul(G[b], lhsT=hid[:, j, s * 128:(s + 1) * 128], rhs=dn[h][:, j, :],
                                                                         start=(j == 0), stop=(j == NJ - 1)),
                         reads=[("hid", j), ("dn", h)], writes=[GN[b]])
                S.op(DVE, lambda s=s, h=h, b=b: nc.vector.scalar_tensor_tensor(out=X[:, s, h * 512:(h + 1) * 512], in0=G[b], scalar=0.5,
                                                                               in1=X[:, s, h * 512:(h + 1) * 512], op0=ALU.mult, op1=ALU.add),
                     reads=[GN[b], (xres, s)], writes=[(xres, s)])

    def rope_tables(t, nh, tabc, tabs, tabres):
        S.op(SP, lambda: nc.sync.dma_start(out=posb[:, :, :], in_=pos[t * T:(t + 1) * T, :].rearrange("(s p) a -> p s a", p=128)),
             writes=["posb"], dma_key="posb")
        for a in range(2):
            S.op(POOL, lambda a=a: nc.gpsimd.tensor_tensor(out=ang[:, :, a, :], in0=posb[:, :, a:a + 1].to_broadcast([128, 4, 16]),
                                                           in1=inv_bc[:, :].unsqueeze(1).to_broadcast([128, 4, 16]), op=ALU.mult),
                 reads=["posb", "inv"], writes=["ang"])
        angf = ang[:, :, :, :].rearrange("p s a f -> p s (a f)")
        angmf = angm[:, :, :, :].rearrange("p s a f -> p s (a f)")
        TWO_PI = 2.0 * math.pi

        def sin_of(dst, shift):
            S.op(DVE, lambda: nc.vector.tensor_scalar(out=angmf, in0=angf, scalar1=shift, scalar2=1.0 / TWO_PI, op0=ALU.add, op1=ALU.mult),
                 reads=["ang"], writes=["angm"])
            S.op(DVE, lambda: nc.vector.tensor_copy(out=angki[:, :, :], in_=angmf), reads=["angm"], writes=["angki"])
            S.op(DVE, lambda: nc.vector.tensor_copy(out=angkf[:, :, :], in_=angki[:, :, :]), reads=["angki"], writes=["angkf"])
            S.op(DVE, lambda: nc.vector.tensor_scalar(out=angmf, in0=angf, scalar1=shift, scalar2=None, op0=ALU.add),
                 reads=["ang", "angki"], writes=["angm"])
            S.op(DVE, lambda: nc.vector.scalar_tensor_tensor(out=angr[:, :, :], in0=angkf[:, :, :], scalar=-TWO_PI, in1=angmf, op0=ALU.mult, op1=ALU.add),
                 reads=["angkf", "angm"], writes=["angr"])
            S.op(DVE, lambda: nc.vector.tensor_scalar(out=angmf, in0=angr[:, :, :], scalar1=math.pi, scalar2=TWO_PI, op0=ALU.is_gt, op1=ALU.mult),
                 reads=["angr"], writes=["angm"])
            S.op(DVE, lambda: nc.vector.tensor_tensor(out=angr[:, :, :], in0=angr[:, :, :], in1=angmf, op=ALU.subtract),
                 reads=["angr", "angm"], writes=["angr"])
            S.op(ACT, lambda: nc.scalar.activation(out=dst[:, :, :], in_=angr[:, :, :], func=AF.Sin), reads=["angr"], writes=[("cs" if dst is cs else "sn")])

        sin_of(sn, 0.0)
        sin_of(cs, 0.5 * math.pi)
        S.op(POOL, lambda: nc.gpsimd.tensor_copy(out=tabc, in_=cs[:, :, :].unsqueeze(2).to_broadcast([128, 4, nh, 32])),
             reads=["cs"], writes=[tabres + "c"])
        S.op(POOL, lambda: nc.gpsimd.tensor_copy(out=tabs, in_=sn[:, :, :].unsqueeze(2).to_broadcast([128, 4, nh, 32])),
             reads=["sn"], writes=[tabres + "s"])

    def head_norm_rope(e, src, srcres, nh, gbc, gres, tabc, tabs, tabres, sq, sqres, dst, dstres):
        V = vec(e)
        SH = 4 * nh
        cb = 250 + SH * 64 * (0.9 if e == POOL else 0.55)
        ch = 250 + SH * 32 * (0.9 if e == POOL else 0.55)
        x3 = src.rearrange("p s (h d) -> p (s h) d", h=nh)
        sq3 = sq.rearrange("p s (h d) -> p (s h) d", h=nh)
        S.op(e, lambda: V.tensor_tensor(out=sq3, in0=x3, in1=x3, op=ALU.mult), reads=[srcres], writes=[sqres], cost=cb)
        S.op(DVE, lambda: nc.vector.tensor_reduce(out=hss[:, 0:SH], in_=sq3, axis=AX.X, op=ALU.add), reads=[sqres], writes=["hss"], cost=250 + SH * 64 * 0.55)
        S.op(ACT, lambda: nc.scalar.activation(out=hrs[:, 0:SH], in_=hss[:, 0:SH], func=AF.Sqrt, scale=1.0 / 64, bias=epsb[:, 0:1]),
             reads=["hss", "epsb"], writes=["hrs"], cost=300)
        S.op(DVE, lambda: nc.vector.reciprocal(out=hrs[:, 0:SH], in_=hrs[:, 0:SH]), reads=["hrs"], writes=["hrs"], cost=190)
        S.op(e, lambda: V.tensor_tensor(out=x3, in0=x3, in1=hrs[:, 0:SH].unsqueeze(2).to_broadcast([128, SH, 64]), op=ALU.mult),
             reads=[srcres, "hrs"], writes=[srcres], cost=cb)
        S.op(e, lambda: V.tensor_tensor(out=x3, in0=x3, in1=gbc[:, :].unsqueeze(1).to_broadcast([128, SH, 64]), op=ALU.mult),
             reads=[srcres, gres], writes=[srcres], cost=cb)
        pat = "p s (h a r f) -> p (s h) a r f"
        x5 = src.rearrange(pat, h=nh, a=2, r=2)
        q5 = sq.rearrange(pat, h=nh, a=2, r=2)
        d5 = dst.rearrange(pat, h=nh, a=2, r=2)
        xa, xb_ = x5[:, :, :, 0, :], x5[:, :, :, 1, :]
        ta, tb_ = q5[:, :, :, 0, :], q5[:, :, :, 1, :]
        c4 = tabc.rearrange("p s h (a f) -> p (s h) a f", a=2)
        s4 = tabs.rearrange("p s h (a f) -> p (s h) a f", a=2)
        oa, ob = d5[:, :, :, 0, :], d5[:, :, :, 1, :]
        S.op(e, lambda: V.tensor_tensor(out=ta, in0=xb_, in1=s4, op=ALU.mult), reads=[srcres, tabres + "s"], writes=[sqres], cost=ch)
        S.op(e, lambda: V.tensor_tensor(out=tb_, in0=xa, in1=s4, op=ALU.mult), reads=[srcres, tabres + "s"], writes=[sqres], cost=ch)
        S.op(e, lambda: V.tensor_tensor(out=xa, in0=xa, in1=c4, op=ALU.mult), reads=[srcres, sqres, tabres + "c"], writes=[srcres], cost=ch)
        S.op(e, lambda: V.tensor_tensor(out=xb_, in0=xb_, in1=c4, op=ALU.mult), reads=[srcres, sqres, tabres + "c"], writes=[srcres], cost=ch)
        S.op(e, lambda: V.tensor_tensor(out=oa, in0=xa, in1=ta, op=ALU.subtract), reads=[srcres, sqres], writes=[dstres], cost=ch)
        S.op(e, lambda: V.tensor_tensor(out=ob, in0=xb_, in1=tb_, op=ALU.add), reads=[srcres, sqres], writes=[dstres], cost=ch)

    def XR(xres):
        return [(xres, s) for s in range(4)]

    def tok_rows(ap2d):
        return ap2d.rearrange("(s p) d -> p s d", p=128)

    dbg = {}

    apos[0] = 0
    hid = carve([128, NJ, T], BF16)
    dn = [carve([128, NJ, 512], BF16), carve([128, NJ, 512], BF16)]
    kvw = carve([128, 8, 256], BF16)
    kvs = carve([128, 2, 4, 128], F32)
    ksq = carve([128, 4, 128], F32)
    tkc = carve([128, 4, 2, 32], F32)
    tks = carve([128, 4, 2, 32], F32)
    krb = carve([128, 4, 128], BF16)
    kTb = [carve([128, T], BF16), carve([128, T], BF16)]
    vsb = [carve([128, 4, 130], BF16), carve([128, 4, 130], BF16)]
    hb2 = carve([128, 4, D], BF16)
    hT2 = carve([128, 8, T], BF16)
    ALT = (hb2, "hb2", hT2, "hT2")

    S.op(SP, lambda: nc.sync.dma_start(out=kvw, in_=s_kv), reads=[("scr", "kv")], writes=["kvw"], dma_key="kvw")

    for t in range(NCT if stage >= 1 else 0):
        b = t % 2
        X, xres = Xb[b], "X%d" % b
        def prep1(tt):
            bb = tt % 2
            S.op(SP, lambda bb=bb, tt=tt: nc.sync.dma_start(out=Xb[bb][:, :, :], in_=tok_rows(xc[tt * T:(tt + 1) * T, :])),
                 writes=XR("X%d" % bb), dma_key="xl%d" % bb, cost=13600)
            rmsnorm_T(Xb[bb], "X%d" % bb, 0)
        if t == 0:
            prep1(0)
        ffn(0, X, xres, hid, dn, mid=(lambda t=t: prep1(t + 1)) if t + 1 < NCT else None)
        if t < NOT:
            S.op(SP, lambda b=b, t=t: nc.sync.dma_start(out=tok_rows(x1s[t * T:(t + 1) * T, :]), in_=Xb[b][:, :, :]),
                 reads=XR(xres), writes=[("x1s", t, s) for s in range(4)], dma_key="xs%d" % b, cost=13600)
        if P1 < 3:
            continue
        rmsnorm_T(X, xres, 1, ALT)
        if P1 < 4:
            continue
        rope_tables(t, 2, tkc, tks, "tk")
        if P1 < 5:
            continue
        for s in range(4):
            gb_ = s % 2
            for kc in range(8):
                S.op(PE, lambda kc=kc, s=s, gb_=gb_: nc.tensor.matmul(G[gb_][:, 0:256], lhsT=hT2[:, kc, s * 128:(s + 1) * 128], rhs=kvw[:, kc, :],
                                                                       start=(kc == 0), stop=(kc == 7)),
                     reads=[("hT2", kc), "kvw"], writes=[GN[gb_]], cost=200)
            S.op(ACT, lambda s=s, gb_=gb_: nc.scalar.copy(out=kvs[:, :, s, :], in_=G[gb_][:, 0:256].rearrange("p (a d) -> p a d", a=2)), reads=[GN[gb_]], writes=["kvs"])
        if P1 < 6:
            continue
        vb, vres = vsb[b], "vsb%d" % b
        kmt = km[:, t * 4:(t + 1) * 4]
        vb4 = vb.rearrange("p s (h e) -> p s h e", h=2)
        S.op(POOL, lambda vb4=vb4, kmt=kmt: nc.gpsimd.tensor_tensor(
            out=vb4[:, :, :, 0:64], in0=kvs[:, 1, :, :].rearrange("p s (h d) -> p s h d", h=2),
            in1=kmt.unsqueeze(2).unsqueeze(3).to_broadcast([128, 4, 2, 64]), op=ALU.mult),
            reads=["kvs", "km"], writes=[vres])
        S.op(POOL, lambda vb4=vb4, kmt=kmt: nc.gpsimd.tensor_copy(out=vb4[:, :, :, 64], in_=kmt.unsqueeze(2).to_broadcast([128, 4, 2])),
             reads=["km"], writes=[vres])
        S.op(SP, lambda vb=vb, t=t: nc.sync.dma_start(out=vss[t * T:(t + 1) * T, :].rearrange("(s p) e -> p s e", p=128), in_=vb),
             reads=[vres], writes=[("vss", t)], dma_key="vst%d" % b)
        if P1 < 7:
            continue
        head_norm_rope(POOL, kvs[:, 0, :, :], "kvs", 2, gk_bc, "gk", tkc, tks, "tk", ksq, "ksq", krb, "krb")
        for s in range(4):
            S.op(PE, lambda s=s: nc.tensor.transpose(out=tp[:, s * 128:(s + 1) * 128], in_=krb[:, s, :], identity=ident[:]),
                 reads=["krb", "ident"], writes=[("tpb", 0)], cost=118)
        kb_, kres = kTb[b], "kTb%d" % b
        S.op(DVE, lambda kb_=kb_: nc.vector.tensor_copy(out=kb_, in_=tp[:, 0:512]), reads=[("tpb", 0)], writes=[kres])
        S.op(SP, lambda kb_=kb_, t=t: nc.sync.dma_start(out=kts[:, t * T:(t + 1) * T], in_=kb_),
             reads=[kres], writes=[("kts", t)], dma_key="kst%d" % b)

    S.barrier()
    apos[0] = 0
    kT = carve([128, NCT * T], BF16)
    vAf = carve([128, NCH * 130 + 64], BF16)
    vA = vAf[:, 0:NCH * 130].rearrange("p (ch e) -> p ch e", e=130)
    qf = carve([128, 4, 512], F32)
    qsq = hb[:, :, :].rearrange("p s d -> p (s d)").bitcast(F32).rearrange("p (s d) -> p s d", s=4)
    tqc = carve([128, 4, 8, 32], F32)
    tqs = carve([128, 4, 8, 32], F32)
    qrb = carve([128, 4, 512], BF16)
    qTp = Xb[1][:, 0:2, :].rearrange("p a d -> p (a d)").bitcast(BF16).rearrange("p (h t) -> p h t", h=8)
    ub = carve([128, 4, 512], BF16)
    vnb = tqc.rearrange("p s h f -> p (s h f)").bitcast(BF16)[:, 0:2048].rearrange("p (s d) -> p s d", s=4)
    sgb = tqs.rearrange("p s h f -> p (s h f)").bitcast(BF16)[:, 0:2048].rearrange("p (s d) -> p s d", s=4)
    sgT = carve([128, 4, T], BF16)
    aT = carve([128, 8, T], BF16)
    mT = carve([128, 8, T], BF16)
    PT = [carve([128, 1024], BF16), carve([128, 1024], BF16), carve([128, 1024], BF16)]
    rden = carve([128, T], F32)
    ones1 = carve([128, 64], F32)
    onT = carve([128, T], F32)
    ggv_bc = carve([128, 512], F32)
    S.op(POOL, lambda: nc.gpsimd.memset(ones1, 1.0), writes=["ones1"])
    S.op(POOL, lambda: nc.gpsimd.memset(qTp, 0.0), writes=[("qT", c) for c in range(4)])
    S.op(POOL, lambda: nc.gpsimd.memset(vAf[:, NCH * 130:NCH * 130 + 64], 0.0), writes=["vApad"])
    small_load(ggv_bc, bc_row(g_gv), "ggv")

    for c in range(0, NCT if stage >= 2 else 0, 8):
        n = min(8, NCT - c)
        S.op(SP, lambda c=c, n=n: nc.sync.dma_start(out=kT[:, c * T:(c + n) * T], in_=kts[:, c * T:(c + n) * T]),
             reads=[("kts", t) for t in range(c, c + n)], writes=["kT"], dma_key="kTl", cost=9000)
        S.op(SP, lambda c=c, n=n: nc.sync.dma_start(out=vA[:, c * 4:(c + n) * 4, :],
                                                    in_=vss[c * T:(c + n) * T, :].rearrange("(ch p) e -> p ch e", p=128)),
             reads=[("vss", t) for t in range(c, c + n)], writes=["vA"], dma_key="vAl", cost=15000)

    def attention():
        heads = [(c, g) for c in range(4) for g in range(2)]
        seq = [(hi, i) for hi in range(8) for i in range(NPAIR)]
        Sv = [S0[:, :], S1[:, :], tpf]
        Sres = [[GN[0], GN[1]], [GN[2], GN[3]], [("tpb", 0), ("tpb", 1)]]

        def qk(n):
            hi, i = seq[n]
            c, g = heads[hi]
            sb_ = n % 3
            Sx = Sv[sb_]
            for u in range(2):
                ch = 2 * i + u
                S.op(PE, lambda u=u, ch=ch, c=c, g=g, Sx=Sx: nc.tensor.matmul(
                    Sx[:, u * 512:(u + 1) * 512], lhsT=kT[:, ch * 128:(ch + 1) * 128],
                    rhs=qTp[:, c * 2 + g, :], start=True, stop=True),
                    reads=["kT", ("qT", c)], writes=[Sres[sb_][u]])

        qk(0)
        qk(1)
        for n in range(len(seq)):
            hi, i = seq[n]
            c, g = heads[hi]
            sb_ = n % 3
            Sx = Sv[sb_]
            ob = hi % 2
            Ox = (O0, O1)[ob]
            if n + 2 < len(seq):
                qk(n + 2)
            S.op(ACT, lambda Sx=Sx, sb_=sb_: nc.scalar.activation(out=PT[sb_], in_=Sx, func=AF.Exp, scale=0.125),
                 reads=Sres[sb_], writes=["PT%d" % sb_], cost=1023)
            for u in range(2):
                ch = 2 * i + u
                S.op(PE, lambda u=u, ch=ch, g=g, Ox=Ox, sb_=sb_, i=i: nc.tensor.matmul(
                    Ox[:, :], lhsT=vAf[:, ch * 130 + g * 65:ch * 130 + g * 65 + 128], rhs=PT[sb_][:, u * 512:(u + 1) * 512],
                    start=(i == 0 and u == 0), stop=(i == NPAIR - 1 and u == 1)),
                    reads=["vA", "vApad", "PT%d" % sb_], writes=[GN[4 + ob]])
            if i == NPAIR - 1:
                h_true = g * 4 + c
                S.op(DVE, lambda Ox=Ox: nc.vector.reciprocal(out=rden[64:65, :], in_=Ox[64:65, :]), reads=[GN[4 + ob]], writes=["rden"], cost=2472)
                S.op(PE, lambda Sx=Sx: nc.tensor.matmul(Sx[0:64, 0:512], lhsT=ones1[64:65, 0:64], rhs=rden[64:65, :], start=True, stop=True),
                     reads=["rden", "ones1"], writes=[Sres[sb_][0]], cost=970)
                S.op(ACT, lambda Sx=Sx: nc.scalar.copy(out=onT[0:64, :], in_=Sx[0:64, 0:512]), reads=[Sres[sb_][0]], writes=["onT"])
                S.op(DVE, lambda Ox=Ox, h_true=h_true: nc.vector.tensor_tensor(out=aT[0:64, h_true, :], in0=Ox[0:64, :], in1=onT[0:64, :], op=ALU.mult),
                     reads=[GN[4 + ob], "onT"], writes=[("aT", h_true)])

    for t in range(NOT if stage >= 2 else 0):
        b = 0
        X, xres = Xb[b], "X%d" % b
        for s in range(4):
            S.op(SP, lambda s=s, t=t: nc.sync.dma_start(out=Xb[0][:, s, :], in_=x1s[t * T + s * 128:t * T + (s + 1) * 128, :]),
                 reads=[("x1s", t, s)], writes=[(xres, s)], dma_key="xa%d" % s, cost=5000)
        rmsnorm_T(X, xres, 1)
        rope_tables(t, 8, tqc, tqs, "tq")
        for pi in range(3):
            slot = ring_load("qgg%d" % pi, s_qgg[pi].rearrange("p kc c -> p (kc c)"), 4096)
            rv = ring[:, slot, :].rearrange("p (kc c) -> p kc c", kc=8)
            for s in range(4):
                gb_ = s % 2
                for kc in range(8):
                    S.op(PE, lambda kc=kc, s=s, gb_=gb_, rv=rv: nc.tensor.matmul(G[gb_], lhsT=hT[:, kc, s * 128:(s + 1) * 128], rhs=rv[:, kc, :],
                                                                                  start=(kc == 0), stop=(kc == 7)),
                         reads=[("hT", kc), ("ring", slot)], writes=[GN[gb_]])
                if pi == 0:
                    S.op(ACT, lambda s=s, gb_=gb_: nc.scalar.copy(out=qf[:, s, :], in_=G[gb_]), reads=[GN[gb_]], writes=["qf"])
                elif pi == 1:
                    S.op(ACT, lambda s=s, gb_=gb_: nc.scalar.activation(out=ub[:, s, :], in_=G[gb_], func=AF.Gelu), reads=[GN[gb_]], writes=["ub"])
                else:
                    S.op(ACT, lambda s=s, gb_=gb_: nc.scalar.activation(out=qf[:, s, :], in_=G[gb_], func=AF.Gelu), reads=[GN[gb_]], writes=["qf"])
            if pi == 0:
                head_norm_rope(DVE, qf, "qf", 8, gq_bc, "gq", tqc, tqs, "tq", qsq, "hb", qrb, "qrb")
                for c0 in range(0, 4, 2):
                    bank = (c0 // 2) % 2
                    for kk in range(2):
                        c = c0 + kk
                        for s in range(4):
                            col = bank * 1024 + kk * 512 + s * 128
                            S.op(PE, lambda c=c, s=s, col=col: nc.tensor.transpose(out=tp[:, col:col + 128], in_=qrb[:, s, c * 128:(c + 1) * 128],
                                                                                    identity=ident[:]),
                                 reads=["qrb", "ident"], writes=[("tpb", bank)], cost=118)
                    for kk in range(2):
                        c = c0 + kk
                        for g in range(2):
                            src_ps = tp[g * 64:(g + 1) * 64, bank * 1024 + kk * 512: bank * 1024 + (kk + 1) * 512]
                            dst = qTp[g * 64:(g + 1) * 64, c * 2 + g, :]
                            if bank == 0:
                                S.op(ACT, lambda src_ps=src_ps, dst=dst: nc.scalar.copy(out=dst, in_=src_ps), reads=[("tpb", bank)], writes=[("qT", c)])
                            else:
                                S.op(DVE, lambda src_ps=src_ps, dst=dst: nc.vector.tensor_copy(out=dst, in_=src_ps), reads=[("tpb", bank)], writes=[("qT", c)])
            if pi == 2:
                row_rstd(qf, "qf", 512, 512)
                for s in range(4):
                    S.op(DVE, lambda s=s: nc.vector.scalar_tensor_tensor(out=vnb[:, s, :], in0=qf[:, s, :], scalar=rstd[:, s:s + 1], in1=ggv_bc,
                                                                         op0=ALU.mult, op1=ALU.mult),
                         reads=["qf", ("rstd", 0), "ggv"], writes=["tqc"])
                for s in range(4):
                    gb_ = 2 + (s % 2)
                    for g in range(8):
                        S.op(PE, lambda s=s, g=g, gb_=gb_: nc.tensor.matmul(G[gb_][:, g * 64:(g + 1) * 64], lhsT=wsT[:, g, :], rhs=vnb[:, s, g * 64:(g + 1) * 64],
                                                                             start=True, stop=True),
                             reads=["tqc", "wsT"], writes=[GN[gb_]], cost=70)
                    tt, tn = (tmpA, "tmpA") if s % 2 == 0 else (tmpB, "tmpB")
                    S.op(DVE, lambda gb_=gb_, tt=tt: nc.vector.tensor_tensor(out=tt.rearrange("p (g c) -> p g c", g=8),
                                                                             in0=G[gb_].rearrange("p (g c) -> p g c", g=8),
                                                                             in1=bspT[:, :].unsqueeze(2).to_broadcast([128, 8, 64]), op=ALU.add),
                         reads=[GN[gb_], "bsp"], writes=[tn])
                    S.op(POOL, lambda s=s, tt=tt: nc.gpsimd.tensor_tensor(out=sgb[:, s, :], in0=tt, in1=ub[:, s, :], op=ALU.mult),
                         reads=[tn, "ub"], writes=["tqs"])
                transpose_T(sgb, "tqs", 4, sgT, "sgT")
        attention()
        for oc in range(8):
            slot = ring_load("mg%d" % oc, s_mg[oc], 3584)
            rg = ring[:, slot, 0:2048].rearrange("p (kc c) -> p kc c", kc=8)
            g0, g1 = (0, 1) if oc % 2 == 0 else (2, 3)
            for kc in range(8):
                S.op(PE, lambda kc=kc, rg=rg, g0=g0: nc.tensor.matmul(G[g0], lhsT=rg[:, kc, 0:128], rhs=hT[:, kc, :], start=(kc == 0), stop=(kc == 7)),
                     reads=[("ring", slot), ("hT", kc)], writes=[GN[g0]])
            for kc in range(8):
                S.op(PE, lambda kc=kc, rg=rg, g1=g1: nc.tensor.matmul(G[g1], lhsT=rg[:, kc, 128:256], rhs=hT[:, kc, :], start=(kc == 0), stop=(kc == 7)),
                     reads=[("ring", slot), ("hT", kc)], writes=[GN[g1]])
            for h in range(8):
                S.op(PE, lambda h=h, slot=slot: nc.tensor.matmul(G[4], lhsT=ring[0:64, slot, 2560 + h * 128:2560 + (h + 1) * 128], rhs=aT[0:64, h, :],
                                                                 start=(h == 0), stop=(h == 7)),
                     reads=[("ring", slot), ("aT", h)], writes=[GN[4]])
            for kc in range(4):
                S.op(PE, lambda kc=kc, slot=slot: nc.tensor.matmul(G[5], lhsT=ring[:, slot, 2048 + kc * 128:2048 + (kc + 1) * 128], rhs=sgT[:, kc, :],
                                                                   start=(kc == 0), stop=(kc == 3)),
                     reads=[("ring", slot), ("sgT", kc)], writes=[GN[5]])
            S.op(ACT, lambda g0=g0: nc.scalar.activation(out=tmpA, in_=G[g0], func=AF.Sigmoid), reads=[GN[g0]], writes=["tmpA"])
            S.op(ACT, lambda g1=g1: nc.scalar.activation(out=tmpB, in_=G[g1], func=AF.Sigmoid), reads=[GN[g1]], writes=["tmpB"])
            S.op(DVE, lambda: nc.vector.tensor_tensor(out=tmpA, in0=tmpA, in1=G[4], op=ALU.mult), reads=["tmpA", GN[4]], writes=["tmpA"])
            S.op(DVE, lambda: nc.vector.tensor_tensor(out=tmpB, in0=tmpB, in1=G[5], op=ALU.mult), reads=["tmpB", GN[5]], writes=["tmpB"])
            S.op(POOL, lambda oc=oc: nc.gpsimd.tensor_tensor(out=mT[:, oc, :], in0=tmpA, in1=tmpB, op=ALU.add),
                 reads=["tmpA", "tmpB"], writes=[("mT", oc)])
        wo_slots = [ring_load("wo%d" % h, s_wo[h].rearrange("p kc c -> p (kc c)"), 4096) for h in range(2)]
        for s in range(4):
            for h in range(2):
                slot = wo_slots[h]
                rv = ring[:, slot, :].rearrange("p (kc c) -> p kc c", kc=8)
                gb_ = h
                for kc in range(8):
                    S.op(PE, lambda kc=kc, s=s, gb_=gb_, rv=rv: nc.tensor.matmul(G[gb_], lhsT=mT[:, kc, s * 128:(s + 1) * 128], rhs=rv[:, kc, :],
                                                                                  start=(kc == 0), stop=(kc == 7)),
                         reads=[("mT", kc), ("ring", slot)], writes=[GN[gb_]])
                S.op(DVE, lambda s=s, h=h, gb_=gb_, X=X: nc.vector.tensor_tensor(out=X[:, s, h * 512:(h + 1) * 512], in0=G[gb_],
                                                                                  in1=X[:, s, h * 512:(h + 1) * 512], op=ALU.add),
                     reads=[GN[gb_], (xres, s)], writes=[(xres, s)])
            S.op(SP, lambda s=s, t=t: nc.sync.dma_start(out=x1s[t * T + s * 128:t * T + (s + 1) * 128, :], in_=Xb[0][:, s, :]),
                 reads=[(xres, s)], writes=[("x1s", t, s)], dma_key="xb%d" % s, cost=5000)

    S.barrier()
    apos[0] = 0
    hid = carve([128, NJ, T], BF16)
    dn = [carve([128, NJ, 512], BF16), carve([128, NJ, 512], BF16)]
    pf = carve([128, 4, PLE], F32)
    pbf = carve([128, 4, PLE], BF16)
    pT = carve([128, 2, T], BF16)
    gfin_bc = carve([128, D], F32)
    hb3 = carve([128, 4, D], BF16)
    hT3 = carve([128, 8, T], BF16)
    ALT = (hb3, "hb3", hT3, "hT3")
    small_load(gfin_bc, bc_row(g_fin), "gfin")
    out_ops = []
    for t in range(NOT if stage >= 3 else 0):
        b = t % 2
        X, xres = Xb[b], "X%d" % b
        def prep2(tt):
            bb = tt % 2
            S.op(SP, lambda bb=bb, tt=tt: nc.sync.dma_start(out=Xb[bb][:, :, :], in_=tok_rows(x1s[tt * T:(tt + 1) * T, :])),
                 reads=[("x1s", tt, s) for s in range(4)], writes=XR("X%d" % bb), dma_key="xl%d" % bb, cost=13600)
            rmsnorm_T(Xb[bb], "X%d" % bb, 2)
        if t == 0:
            prep2(0)
        ffn(1, X, xres, hid, dn, mid=(lambda t=t: prep2(t + 1)) if t + 1 < NOT else None)
        rmsnorm_T(X, xres, 3, ALT)
        S.op(SP, lambda t=t: nc.sync.dma_start(out=pf, in_=tok_rows(pin[t * T:(t + 1) * T, :])), writes=["pf"], dma_key="pfl")
        S.op(POOL, lambda: nc.gpsimd.tensor_copy(out=pbf, in_=pf), reads=["pf"], writes=["pbf"])
        transpose_T(pbf, "pbf", 2, pT, "pT")
        slotp = ring_load("pl", s_pl.rearrange("p kc c -> p (kc c)"), 2048)
        rvp = ring[:, slotp, 0:2048].rearrange("p (kc c) -> p kc c", kc=2)
        for h in range(2):
            slot = ring_load("pg%d" % h, s_pg[h].rearrange("p kc c -> p (kc c)"), 4096)
            rv = ring[:, slot, :].rearrange("p (kc c) -> p kc c", kc=8)
            for s in range(4):
                g0, g1 = (0, 1) if s % 2 == 0 else (2, 3)
                for kc in range(8):
                    S.op(PE, lambda kc=kc, s=s, g0=g0, rv=rv: nc.tensor.matmul(G[g0], lhsT=hT3[:, kc, s * 128:(s + 1) * 128], rhs=rv[:, kc, :],
                                                                                start=(kc == 0), stop=(kc == 7)),
                         reads=[("hT3", kc), ("ring", slot)], writes=[GN[g0]])
                for k2 in range(2):
                    S.op(PE, lambda k2=k2, s=s, g1=g1, h=h, rvp=rvp: nc.tensor.matmul(G[g1], lhsT=pT[:, k2, s * 128:(s + 1) * 128],
                                                                                       rhs=rvp[:, k2, h * 512:(h + 1) * 512],
                                                                                       start=(k2 == 0), stop=(k2 == 1)),
                         reads=[("pT", k2), ("ring", slotp)], writes=[GN[g1]])
                tt, tn = (tmpA, "tmpA") if s % 2 == 0 else (tmpB, "tmpB")
                S.op(ACT, lambda g0=g0, tt=tt: nc.scalar.activation(out=tt, in_=G[g0], func=AF.Sigmoid), reads=[GN[g0]], writes=[tn])
                S.op(DVE, lambda g1=g1, tt=tt: nc.vector.tensor_tensor(out=tt, in0=tt, in1=G[g1], op=ALU.mult), reads=[tn, GN[g1]], writes=[tn])
                S.op(POOL, lambda s=s, h=h, tt=tt, X=X: nc.gpsimd.tensor_tensor(out=X[:, s, h * 512:(h + 1) * 512], in0=X[:, s, h * 512:(h + 1) * 512],
                                                                                 in1=tt, op=ALU.add),
                     reads=[tn, (xres, s)], writes=[(xres, s)])
        row_rstd(X, xres, D, D)
        for s in range(4):
            S.op(DVE, lambda s=s, X=X: nc.vector.scalar_tensor_tensor(out=X[:, s, :], in0=X[:, s, :], scalar=rstd[:, s:s + 1], in1=gfin_bc,
                                                                      op0=ALU.mult, op1=ALU.mult),
                 reads=[(xres, s), ("rstd", 0), "gfin"], writes=[(xres, s)])
        out_ops.append(S.op(SP, lambda b=b, t=t: nc.sync.dma_start(out=tok_rows(y[t * T:(t + 1) * T, :]), in_=Xb[b][:, :, :]),
                            reads=XR(xres), dma_key="ys%d" % b, cost=13600))
    if stage < 3 and stage >= 1:
        out_ops.append(S.op(SP, lambda: nc.sync.dma_start(out=y[:, :], in_=x1s[:, :]), reads=[("x1s", t, s) for t in range(NOT) for s in range(4)], dma_key="dbg"))
    S.op(SP, None, extra=out_ops)

    if SCHEDULE:
        S.schedule()
        build_program.last_est_ns = S.est_ns
    semkeys = S.finalize()
    by_eng = {e: [o for o in S.ops if o.eng == e] for e in ENGS}
    with ExitStack() as es:
        sems = {k: es.enter_context(nc.semaphore("s%d" % i)) for i, k in enumerate(semkeys)}
        block = es.enter_context(nc.Block())

        def emit(engname, eng):
            waited = {}
            for o in by_eng[engname]:
                need = {}
                for d in o.deps:
                    if not d.signal or d.fn is None:
                        continue
                    if d.eng == PE and o.eng == PE and d.dma_key is None and o.dma_key is None:
                        continue
                    if need.get(d.sem, 0) < d.val:
                        need[d.sem] = d.val
                for k, v in need.items():
                    if waited.get(k, 0) < v:
                        eng.wait_ge(sems[k], v)
                        waited[k] = v
                if o.fn is not None:
                    ins = o.fn()
                    if o.signal:
                        ins.then_inc(sems[o.sem], 16 if o.dma_key is not None else 1)
                else:
                    assert not o.signal

        @block.tensor
        def _(e):
            emit(PE, e)

        @block.scalar
        def _(e):
            emit(ACT, e)

        @block.vector
        def _(e):
            emit(DVE, e)

        @block.gpsimd
        def _(e):
            emit(POOL, e)

        @block.sync
        def _(e):
            emit(SP, e)
    return nc


NCT_FULL, NOT_FULL = 32, 8
WNAMES = ["g_ffn1", "w_ffn1_gu", "w_ffn1_down", "g_mix", "w_in", "g_q", "g_k", "g_gmlp_v", "w_spatial", "b_spatial",
          "w_branch_attn", "w_branch_gmlp", "w_out", "g_ffn2", "w_ffn2_gu", "w_ffn2_down", "g_ple", "w_ple_gate", "w_ple", "g_final"]


def _pos_table(tok_idx):
    tok_idx = np.asarray(tok_idx, np.int64)
    return np.stack([tok_idx // 64, tok_idx % 64], axis=1).astype(np.float32)


def kernel(**inputs):
    xp = np.asarray(inputs["x_prompt"], np.float32)
    xs = np.asarray(inputs["x_sample"], np.float32)
    pp = np.asarray(inputs["p_prompt"], np.float32)
    ps = np.asarray(inputs["p_sample"], np.float32)
    w = {k: np.ascontiguousarray(np.asarray(inputs[k], np.float32)) for k in WNAMES}
    w["g_final"] = w["g_final"].reshape(1, D)
    NTOK = NCT_FULL * T
    own = NOT_FULL * T
    in_maps = []
    for c in range(8):
        if c < 4:
            order = [c] + [(c + k) % 4 for k in range(1, 4)]
            xcx = np.concatenate([xp[o] for o in order], axis=0)
            posi = np.concatenate([np.arange(own)] * 4)
            msk = np.zeros(NTOK, np.float32); msk[:own] = 1.0
            pc = pp[0, c]
        else:
            q = c - 4
            order = [q] + [(q + k) % 4 for k in range(1, 4)]
            xcx = np.concatenate([xs[0, o * own:(o + 1) * own] for o in order], axis=0)
            posi = np.concatenate([np.arange(o * own, (o + 1) * own) for o in order])
            msk = np.ones(NTOK, np.float32)
            pc = ps[0, 0, q * own:(q + 1) * own]
        m = {"xc": np.ascontiguousarray(xcx), "pin": np.ascontiguousarray(pc), "pos": _pos_table(posi),
             "kmask": np.ascontiguousarray(msk.reshape(NTOK // 128, 128).T)}
        m.update(w)
        in_maps.append(m)
    nc = build_program(NCT_FULL, NOT_FULL)
    res = run_bass_kernel_spmd(nc, in_maps, core_ids=list(range(8)))
    ys = [np.asarray(res.results[c]["y"], np.float32) for c in range(8)]
    y_prompt = np.stack(ys[0:4], axis=0)
    y_sample = np.concatenate(ys[4:8], axis=0)[None]
    return (y_prompt, y_sample)
```

```python
import math
SCHEDULE = True
P0 = 255
P1 = 99
P1SUB = 99
P1T = 0
from contextlib import ExitStack
import numpy as np
import concourse.bass as bass
import concourse.mybir as mybir
from concourse.bass_utils import run_bass_kernel_spmd

F32 = mybir.dt.float32
BF16 = mybir.dt.bfloat16
I32 = mybir.dt.int32
AF = mybir.ActivationFunctionType
ALU = mybir.AluOpType
AX = mybir.AxisListType

D = 1024
DFF = 2816
NJ = DFF // 128
PLE = 256
EPS = 1e-6
T = 512
NSLOT = 3
SLOTW = 4096
PE, ACT, DVE, POOL, SP = "pe", "act", "dve", "pool", "sp"
ENGS = [PE, ACT, DVE, POOL, SP]


class Op:
    __slots__ = ("eng", "fn", "deps", "dma_key", "sem", "val", "signal", "idx", "cost", "phase", "t0", "t1")

    def __init__(self, eng, fn, deps, dma_key, cost):
        self.eng, self.fn, self.deps, self.dma_key, self.cost = eng, fn, deps, dma_key, cost
        self.sem = None
        self.val = 0
        self.signal = False


DEFAULT_COST = {PE: 216, ACT: 600, DVE: 650, POOL: 900, SP: 2500}


class Sched:
    def __init__(self):
        self.ops = []
        self.last_write = {}
        self.readers = {}
        self.phase = 0
        self.since_barrier = []
        self.cur_barrier = {}

    def op(self, eng, fn, reads=(), writes=(), dma_key=None, extra=(), cost=None):
        deps = set(extra)
        for r in reads:
            w = self.last_write.get(r)
            if w is not None:
                deps.add(w)
        for w_ in writes:
            w = self.last_write.get(w_)
            if w is not None:
                deps.add(w)
            for rd in self.readers.get(w_, ()):
                deps.add(rd)
        bar = self.cur_barrier.get(eng)
        if bar is not None:
            deps.add(bar)
        o = Op(eng, fn, deps, dma_key, DEFAULT_COST[eng] if cost is None else cost)
        o.idx = len(self.ops)
        o.phase = self.phase
        o.sem = (eng, self.phase) if dma_key is None else ("dma", dma_key)
        self.ops.append(o)
        for r in reads:
            self.readers.setdefault(r, []).append(o)
        for w_ in writes:
            self.last_write[w_] = o
            self.readers[w_] = []
        if fn is not None:
            self.since_barrier.append(o)
        return o

    def barrier(self):
        prev = list(self.since_barrier)
        self.since_barrier = []
        for e in ENGS:
            self.cur_barrier[e] = self.op(e, None, extra=prev, cost=0)
        self.phase += 1

    def schedule(self):
        import heapq
        n = len(self.ops)
        succ = [[] for _ in range(n)]
        indeg = [0] * n
        for o in self.ops:
            indeg[o.idx] = len(o.deps)
            for d in o.deps:
                succ[d.idx].append(o)
        pending = {e: [] for e in ENGS}
        avail = {e: [] for e in ENGS}
        free = {e: 0.0 for e in ENGS}
        ready_t = [0.0] * n
        for o in self.ops:
            if indeg[o.idx] == 0:
                heapq.heappush(pending[o.eng], (0.0, o.idx))
        order = []
        done = 0
        while done < n:
            best = None
            for e in ENGS:
                pe_, av = pending[e], avail[e]
                while pe_ and pe_[0][0] <= free[e]:
                    heapq.heappush(av, heapq.heappop(pe_)[1])
                if av:
                    cand = (free[e], av[0], e, True)
                elif pe_:
                    cand = (pe_[0][0], pe_[0][1], e, False)
                else:
                    continue
                if best is None or cand[:2] < best[:2]:
                    best = cand
            assert best is not None, "dependency cycle"
            start, idx, e, from_av = best
            if from_av:
                heapq.heappop(avail[e])
            else:
                heapq.heappop(pending[e])
            o = self.ops[idx]
            o.t0 = start
            if o.dma_key is not None:
                free[e] = start + 60.0
                o.t1 = start + o.cost
            else:
                o.t1 = start + o.cost
                free[e] = o.t1
            order.append(o)
            done += 1
            for sc in succ[idx]:
                if ready_t[sc.idx] < o.t1:
                    ready_t[sc.idx] = o.t1
                indeg[sc.idx] -= 1
                if indeg[sc.idx] == 0:
                    heapq.heappush(pending[sc.eng], (ready_t[sc.idx], sc.idx))
        self.ops = order
        self.est_ns = max(o.t1 for o in order)

    def finalize(self):
        for o in self.ops:
            for d in o.deps:
                if d.eng == PE and o.eng == PE and d.dma_key is None and o.dma_key is None:
                    continue
                if d.fn is None:
                    continue
                d.signal = True
        for o in self.ops:
            if o.dma_key is not None and o.fn is not None:
                o.signal = True
        counts = {}
        for o in self.ops:
            if o.signal:
                counts[o.sem] = counts.get(o.sem, 0) + (16 if o.dma_key is not None else 1)
                o.val = counts[o.sem]
        return sorted(counts.keys(), key=str)


def build_program(NCT, NOT, stage=3):
    NCH = NCT * 4
    NPAIR = NCH // 2
    nc = bass.Bass("TRN2", target_bir_lowering=False)

    def din(name, shape, dt=F32):
        return nc.dram_tensor(name, list(shape), dt, kind="ExternalInput").ap()

    xc = din("xc", [NCT * T, D])
    pin = din("pin", [NOT * T, PLE])
    pos = din("pos", [NCT * T, 2])
    kmask = din("kmask", [128, NCH])
    g_ffn1 = din("g_ffn1", [1, D]); w1gu = din("w_ffn1_gu", [1, D, 2 * DFF]); w1d = din("w_ffn1_down", [1, DFF, D])
    g_mix = din("g_mix", [1, D]); w_in = din("w_in", [1, D, 3840])
    g_q = din("g_q", [1, 64]); g_k = din("g_k", [1, 64]); g_gv = din("g_gmlp_v", [1, 512])
    w_sp = din("w_spatial", [1, 8, 128, 128]); b_sp = din("b_spatial", [1, 8, 128])
    w_ba = din("w_branch_attn", [1, 512, D]); w_bg = din("w_branch_gmlp", [1, 512, D]); w_out = din("w_out", [1, D, D])
    g_ffn2 = din("g_ffn2", [1, D]); w2gu = din("w_ffn2_gu", [1, D, 2 * DFF]); w2d = din("w_ffn2_down", [1, DFF, D])
    g_ple = din("g_ple", [1, D]); w_pg = din("w_ple_gate", [1, D, D]); w_pl = din("w_ple", [1, PLE, D])
    g_fin = din("g_final", [1, D])
    y = nc.dram_tensor("y", [NOT * T, D], F32, kind="ExternalOutput").ap()

    def dscr(name, shape, dt=BF16):
        return nc.dram_tensor(name, list(shape), dt).ap()

    s_gu = [dscr("s_gu1", [11, 128, 8, 512]), dscr("s_gu2", [11, 128, 8, 512])]
    s_dn = [dscr("s_dn1", [2, 128, NJ, 512]), dscr("s_dn2", [2, 128, NJ, 512])]
    s_kv = dscr("s_kv", [128, 8, 256])
    s_qgg = dscr("s_qgg", [3, 128, 8, 512])
    s_mg = dscr("s_mg", [8, 128, 3584])
    s_wo = dscr("s_wo", [2, 128, 8, 512])
    s_pg = dscr("s_pg", [2, 128, 8, 512])
    s_pl = dscr("s_pl", [128, 2, 1024])
    x1s = dscr("x1s", [NOT * T, D], F32)
    kts = dscr("kts", [128, NCT * T])
    vss = dscr("vss", [NCT * T, 130])

    S = Sched()

    def sb(name, shape, dt):
        return nc.alloc_sbuf_tensor(name, list(shape), dt)

    ident = sb("ident", [128, 128], BF16)
    gcol = sb("gcol", [128, 4, 8], F32)
    gq_bc = sb("gq_bc", [128, 64], F32)
    gk_bc = sb("gk_bc", [128, 64], F32)
    bspT = sb("bspT", [128, 8], F32)
    wsT = sb("wsT", [128, 8, 128], BF16)
    inv_bc = sb("inv_bc", [128, 16], F32)
    km = sb("km", [128, NCH], F32)
    negpi = sb("negpi", [128, 1], F32)
    epsb = sb("epsb", [128, 1], F32)
    ring = sb("ring", [128, NSLOT, SLOTW], BF16)
    Xb = [sb("X0", [128, 4, D], F32), sb("X1", [128, 4, D], F32)]
    hb = sb("hb", [128, 4, D], BF16)
    hT = sb("hT", [128, 8, T], BF16)
    ss = sb("ss", [128, 8], F32)
    rstd = sb("rstd", [128, 8], F32)
    tmpA = sb("tmpA", [128, T], F32)[:, :]
    tmpB = sb("tmpB", [128, T], F32)[:, :]
    posb = sb("posb", [128, 4, 2], F32)
    ang = sb("ang", [128, 4, 2, 16], F32)
    angm = sb("angm", [128, 4, 2, 16], F32)
    cs = sb("cs", [128, 4, 32], F32)
    sn = sb("sn", [128, 4, 32], F32)
    angki = sb("angki", [128, 4, 32], I32)
    angkf = sb("angkf", [128, 4, 32], F32)
    angr = sb("angr", [128, 4, 32], F32)
    hss = sb("hss", [128, 32], F32)
    hrs = sb("hrs", [128, 32], F32)
    ARENA_BYTES = 123 * 1024
    arena = sb("arena", [128, ARENA_BYTES // 4], F32)
    apos = [0]

    def carve(shape, dt):
        esz = 4 if dt in (F32, I32) else 2
        n = int(np.prod(shape[1:]))
        nbytes = (n * esz + 31) // 32 * 32
        off = apos[0]
        apos[0] += nbytes
        assert apos[0] <= ARENA_BYTES, (apos[0], ARENA_BYTES)
        v = arena[:, off // 4:(off + nbytes) // 4]
        if esz == 2:
            v = v.bitcast(BF16)
        v = v[:, 0:n]
        if len(shape) == 3:
            v = v.rearrange("p (a b) -> p a b", a=shape[1])
        elif len(shape) == 4:
            v = v.rearrange("p (a b c) -> p a b c", a=shape[1], b=shape[2])
        return v

    tp = nc.alloc_psum_tensor("tp", [128, 2048], BF16)
    S0 = nc.alloc_psum_tensor("S0", [128, 1024], F32)
    S1 = nc.alloc_psum_tensor("S1", [128, 1024], F32)
    O0 = nc.alloc_psum_tensor("O0", [128, 512], F32)
    O1 = nc.alloc_psum_tensor("O1", [128, 512], F32)
    G = [S0[:, 0:512], S0[:, 512:1024], S1[:, 0:512], S1[:, 512:1024], O0[:, :], O1[:, :]]
    GN = ["G0", "G1", "G2", "G3", "G4", "G5"]
    tpf = tp[:, :].bitcast(F32)

    def vec(e):
        return nc.vector if e == DVE else nc.gpsimd

    def cast(key, out_ap, in_ap):
        if not (P0 & 8):
            return None
        return S.op(POOL, lambda o=out_ap, i=in_ap: nc.gpsimd.dma_start(out=o, in_=i),
                    writes=[("scr", key)], dma_key="c_" + key, cost=9000)

    def small_load(out_ap, in_ap, res):
        def f():
            with nc.allow_non_contiguous_dma(reason="tiny constant layout load"):
                return nc.sync.dma_start(out=out_ap, in_=in_ap)
        return S.op(SP, f, writes=[res], dma_key="k_" + str(res).replace("'", "").replace(" ", ""))

    def bc_row(ap2d):
        return ap2d.partition_broadcast(128).rearrange("p o d -> p (o d)")

    gl = sb("gl", [40, 128], F32)
    identf = sb("identf", [40, 40], F32)
    for i, g in enumerate([g_ffn1, g_mix, g_ffn2, g_ple]):
        S.op(SP, lambda i=i, g=g: nc.sync.dma_start(out=gl[i * 8:(i + 1) * 8, :], in_=g.rearrange("o (kc p) -> (o kc) p", p=128)),
             writes=[("gl", i)], dma_key="k_gl%d" % i)
    S.op(SP, lambda: nc.sync.dma_start(out=gl[32:40, :], in_=b_sp[0]), writes=[("gl", 4)], dma_key="k_gl4")

    def mk_identf():
        nc.gpsimd.memset(identf[:], 0.0)
        return nc.gpsimd.affine_select(out=identf[:], in_=identf[:], pattern=[[-1, 40]], compare_op=ALU.not_equal,
                                       fill=1.0, base=0, channel_multiplier=1)
    S.op(POOL, mk_identf, writes=["identf"])
    S.op(PE, lambda: nc.tensor.matmul(G[0][:, 0:40], lhsT=gl[0:40, :], rhs=identf[0:40, 0:40], start=True, stop=True),
         reads=[("gl", i) for i in range(5)] + ["identf"], writes=[GN[0]], cost=400)
    S.op(DVE, lambda: nc.vector.tensor_copy(out=gcol[:, :, :].rearrange("p g k -> p (g k)"), in_=G[0][:, 0:32]),
         reads=[GN[0]], writes=[("gcol", i) for i in range(4)], cost=200)
    S.op(DVE, lambda: nc.vector.tensor_copy(out=bspT[:, :], in_=G[0][:, 32:40]), reads=[GN[0]], writes=["bsp"], cost=200)
    if P0 & 2:
        small_load(gq_bc[:, :], bc_row(g_q), "gq")
        small_load(gk_bc[:, :], bc_row(g_k), "gk")
    small_load(km[:, :], kmask[:, :], "km")

    def mk_ident():
        nc.gpsimd.memset(ident[:], 0.0)
        return nc.gpsimd.affine_select(out=ident[:], in_=ident[:], pattern=[[-1, 128]], compare_op=ALU.not_equal,
                                       fill=1.0, base=0, channel_multiplier=1)
    S.op(POOL, mk_ident, writes=["ident"])
    S.op(POOL, lambda: nc.gpsimd.memset(negpi[:], -math.pi), writes=["negpi"])
    S.op(POOL, lambda: nc.gpsimd.memset(epsb[:], EPS), writes=["epsb"])
    S.op(POOL, lambda: nc.gpsimd.iota(out=cs[:, 0, 0:16].bitcast(I32), pattern=[[1, 16]], base=0, channel_multiplier=0),
         writes=["cs"])
    S.op(POOL, lambda: nc.gpsimd.tensor_copy(out=sn[:, 0, 0:16], in_=cs[:, 0, 0:16].bitcast(I32)), reads=["cs"], writes=["sn"])
    S.op(ACT, lambda: nc.scalar.activation(out=inv_bc[:, :], in_=sn[:, 0, 0:16], func=AF.Exp, scale=-math.log(10000.0) / 16.0),
         reads=["sn"], writes=["inv"])

    S.op(SP, lambda: nc.sync.dma_start(out=Xb[0][:, 0, :].rearrange("p (g q) -> p g q", g=8),
                                       in_=w_sp[0].rearrange("g p q -> p g q")), writes=[("X0", 0)], dma_key="const")
    S.op(DVE, lambda: nc.vector.tensor_copy(out=hb[:, 0, :], in_=Xb[0][:, 0, :]), reads=[("X0", 0)], writes=["hb"])
    for g in range(8):
        S.op(PE, lambda g=g: nc.tensor.transpose(out=tp[:, g * 128:(g + 1) * 128], in_=hb[:, 0, g * 128:(g + 1) * 128], identity=ident[:]),
             reads=["hb", "ident"], writes=[("tpb", 0)], cost=118)
    S.op(DVE, lambda: nc.vector.tensor_copy(out=wsT[:, :, :].rearrange("q g p -> q (g p)"), in_=tp[:, 0:1024]),
         reads=[("tpb", 0)], writes=["wsT"])

    def kcp(ap2d):
        return ap2d.rearrange("(kc p) c -> p kc c", p=128)

    def cast_ffn(idx, wgu, wd):
        for i in range(11):
            cast("gu%d_%d" % (idx, i), s_gu[idx][i, :, :, 0:256], kcp(wgu[0, :, 256 * i:256 * i + 256]))
            cast("gu%d_%d" % (idx, i), s_gu[idx][i, :, :, 256:512], kcp(wgu[0, :, DFF + 256 * i:DFF + 256 * i + 256]))
        for h in range(2):
            cast("dn%d_%d" % (idx, h), s_dn[idx][h], kcp(wd[0, :, 512 * h:512 * h + 512]))

    cast_ffn(0, w1gu, w1d)
    cast("kv", s_kv, kcp(w_in[0, :, 512:768]))
    for g in range(2):
        for c in range(4):
            cast("qgg0", s_qgg[0, :, :, c * 128 + g * 64:c * 128 + g * 64 + 64], kcp(w_in[0, :, (g * 4 + c) * 64:(g * 4 + c) * 64 + 64]))
    for i, c0 in ((1, 768), (2, 1280)):
        cast("qgg%d" % i, s_qgg[i], kcp(w_in[0, :, c0:c0 + 512]))
    for oc in range(8):
        k = "mg%d" % oc
        gts = s_mg[oc, :, 0:2048].rearrange("p (kc c) -> p kc c", kc=8)
        cast(k, gts[:, :, 0:128], kcp(w_in[0, :, 1792 + oc * 128:1792 + oc * 128 + 128]))
        cast(k, gts[:, :, 128:256], kcp(w_in[0, :, 2816 + oc * 128:2816 + oc * 128 + 128]))
        cast(k, s_mg[oc, :, 2048:2560].rearrange("p (kc c) -> p kc c", kc=4), kcp(w_bg[0, :, oc * 128:oc * 128 + 128]))
        cast(k, s_mg[oc, 0:64, 2560:3584].rearrange("p (h c) -> p h c", h=8),
             w_ba[0, :, oc * 128:oc * 128 + 128].rearrange("(h p) c -> p h c", p=64))
    for h in range(2):
        cast("wo%d" % h, s_wo[h], kcp(w_out[0, :, 512 * h:512 * h + 512]))
    cast_ffn(1, w2gu, w2d)
    for h in range(2):
        cast("pg%d" % h, s_pg[h], kcp(w_pg[0, :, 512 * h:512 * h + 512]))
    cast("pl", s_pl, kcp(w_pl[0, :, :]))

    ring_n = [0]

    def ring_load(key, src_ap, width):
        slot = ring_n[0] % NSLOT
        ring_n[0] += 1
        S.op(SP, lambda s=slot, a=src_ap, w=width: nc.sync.dma_start(out=ring[:, s, 0:w], in_=a),
             reads=[("scr", key)], writes=[("ring", slot)], dma_key="ring%d" % slot, cost=2000 + width * 256 // 180)
        return slot

    def transpose_T(src, srcres, nkc, outT, outres, gi=None):
        for k0 in range(0, nkc, 2):
            bank = (k0 // 2) % 2
            for kk in range(2):
                kc = k0 + kk
                for s in range(4):
                    col = bank * 1024 + kk * 512 + s * 128
                    S.op(PE, lambda kc=kc, s=s, col=col: nc.tensor.transpose(out=tp[:, col:col + 128],
                                                                            in_=src[:, s, kc * 128:(kc + 1) * 128], identity=ident[:]),
                         reads=[srcres, "ident"], writes=[("tpb", bank)], cost=118)
            for kk in range(2):
                kc = k0 + kk
                e = ACT if bank == 0 else DVE
                src_ps = tp[:, bank * 1024 + kk * 512: bank * 1024 + (kk + 1) * 512]
                if gi is None:
                    if e == ACT:
                        f = lambda kc=kc, src_ps=src_ps: nc.scalar.copy(out=outT[:, kc, :], in_=src_ps)
                    else:
                        f = lambda kc=kc, src_ps=src_ps: nc.vector.tensor_copy(out=outT[:, kc, :], in_=src_ps)
                    rd = [("tpb", bank)]
                else:
                    if e == ACT:
                        f = lambda kc=kc, src_ps=src_ps: nc.scalar.activation(out=outT[:, kc, :], in_=src_ps, func=AF.Copy,
                                                                               scale=gcol[:, gi, kc:kc + 1])
                    else:
                        f = lambda kc=kc, src_ps=src_ps: nc.vector.tensor_scalar(out=outT[:, kc, :], in0=src_ps,
                                                                                  scalar1=gcol[:, gi, kc:kc + 1], scalar2=None, op0=ALU.mult)
                    rd = [("tpb", bank), ("gcol", gi)]
                S.op(e, f, reads=rd, writes=[(outres, kc)])

    def row_rstd(X, xres, width, nrm, hbuf=None, hbres="hb", c0=0):
        hbuf = hb if hbuf is None else hbuf
        for s in range(4):
            S.op(ACT, lambda s=s: nc.scalar.activation(out=hbuf[:, s, 0:width], in_=X[:, s, 0:width], func=AF.Square, accum_out=ss[:, c0 + s:c0 + s + 1]),
                 reads=[(xres, s)], writes=[hbres, ("ss", c0 + s)], cost=1056 if width > 512 else 843)
        S.op(ACT, lambda: nc.scalar.activation(out=rstd[:, c0:c0 + 4], in_=ss[:, c0:c0 + 4], func=AF.Sqrt, scale=1.0 / nrm, bias=epsb[:, 0:1]),
             reads=[("ss", c0 + s) for s in range(4)] + ["epsb"], writes=[("rstd", c0)], cost=300)
        S.op(DVE, lambda: nc.vector.reciprocal(out=rstd[:, c0:c0 + 4], in_=rstd[:, c0:c0 + 4]), reads=[("rstd", c0)], writes=[("rstd", c0)], cost=190)

    def rmsnorm_T(X, xres, gi, alt=None):
        hbuf, hbres, outT, outres = (hb, "hb", hT, "hT") if alt is None else alt
        c0 = 0 if alt is None else 4
        row_rstd(X, xres, D, D, hbuf, hbres, c0)
        for s in range(4):
            S.op(DVE, lambda s=s: nc.vector.tensor_scalar(out=hbuf[:, s, :], in0=X[:, s, :], scalar1=rstd[:, c0 + s:c0 + s + 1], scalar2=None, op0=ALU.mult),
                 reads=[(xres, s), ("rstd", c0)], writes=[hbres])
        transpose_T(hbuf, hbres, 8, outT, outres, gi)

    def ffn(idx, X, xres, hid, dn, mid=None):
        for i in range(11):
            slot = ring_load("gu%d_%d" % (idx, i), s_gu[idx][i].rearrange("p kc c -> p (kc c)"), 4096)
            rv = ring[:, slot, :].rearrange("p (kc c) -> p kc c", kc=8)
            for jj in range(2):
                j = 2 * i + jj
                ga, gb = (0, 1) if j % 2 == 0 else (2, 3)
                for kc in range(8):
                    S.op(PE, lambda kc=kc, jj=jj, ga=ga, rv=rv: nc.tensor.matmul(G[ga], lhsT=rv[:, kc, jj * 128:(jj + 1) * 128], rhs=hT[:, kc, :],
                                                                                  start=(kc == 0), stop=(kc == 7)),
                         reads=[("ring", slot), ("hT", kc)], writes=[GN[ga]])
                for kc in range(8):
                    S.op(PE, lambda kc=kc, jj=jj, gb=gb, rv=rv: nc.tensor.matmul(G[gb], lhsT=rv[:, kc, 256 + jj * 128:256 + (jj + 1) * 128], rhs=hT[:, kc, :],
                                                                                  start=(kc == 0), stop=(kc == 7)),
                         reads=[("ring", slot), ("hT", kc)], writes=[GN[gb]])
                tt, tn = (tmpA, "tmpA") if j % 2 == 0 else (tmpB, "tmpB")
                S.op(ACT, lambda ga=ga, tt=tt: nc.scalar.activation(out=tt[:, :], in_=G[ga], func=AF.Silu), reads=[GN[ga]], writes=[tn])
                S.op(DVE, lambda j=j, gb=gb, tt=tt: nc.vector.tensor_tensor(out=hid[:, j, :], in0=tt[:, :], in1=G[gb], op=ALU.mult),
                     reads=[tn, GN[gb]], writes=[("hid", j)])
        if mid is not None:
            mid()
        for h in range(2):
            S.op(SP, lambda h=h: nc.sync.dma_start(out=dn[h], in_=s_dn[idx][h]),
                 reads=[("scr", "dn%d_%d" % (idx, h))], writes=[("dn", h)], dma_key="dn%d" % h, cost=18000)
            for s in range(4):
                b = 4 + (s % 2)
                for j in range(NJ):
                    S.op(PE, lambda j=j, s=s, h=h, b=b: nc.tensor.matmul(G[b], lhsT=hid[:, j, s * 128:(s + 1) * 128], rhs=dn[h][:, j, :],
                                                                         start=(j == 0), stop=(j == NJ - 1)),
                         reads=[("hid", j), ("dn", h)], writes=[GN[b]])
                S.op(DVE, lambda s=s, h=h, b=b: nc.vector.scalar_tensor_tensor(out=X[:, s, h * 512:(h + 1) * 512], in0=G[b], scalar=0.5,
                                                                               in1=X[:, s, h * 512:(h + 1) * 512], op0=ALU.mult, op1=ALU.add),
                     reads=[GN[b], (xres, s)], writes=[(xres, s)])

    def rope_tables(t, nh, tabc, tabs, tabres):
        S.op(SP, lambda: nc.sync.dma_start(out=posb[:, :, :], in_=pos[t * T:(t + 1) * T, :].rearrange("(s p) a -> p s a", p=128)),
             writes=["posb"], dma_key="posb")
        for a in range(2):
            S.op(POOL, lambda a=a: nc.gpsimd.tensor_tensor(out=ang[:, :, a, :], in0=posb[:, :, a:a + 1].to_broadcast([128, 4, 16]),
                                                           in1=inv_bc[:, :].unsqueeze(1).to_broadcast([128, 4, 16]), op=ALU.mult),
                 reads=["posb", "inv"], writes=["ang"])
        angf = ang[:, :, :, :].rearrange("p s a f -> p s (a f)")
        angmf = angm[:, :, :, :].rearrange("p s a f -> p s (a f)")
        TWO_PI = 2.0 * math.pi

        def sin_of(dst, shift):
            S.op(DVE, lambda: nc.vector.tensor_scalar(out=angmf, in0=angf, scalar1=shift, scalar2=1.0 / TWO_PI, op0=ALU.add, op1=ALU.mult),
                 reads=["ang"], writes=["angm"])
            S.op(DVE, lambda: nc.vector.tensor_copy(out=angki[:, :, :], in_=angmf), reads=["angm"], writes=["angki"])
            S.op(DVE, lambda: nc.vector.tensor_copy(out=angkf[:, :, :], in_=angki[:, :, :]), reads=["angki"], writes=["angkf"])
            S.op(DVE, lambda: nc.vector.tensor_scalar(out=angmf, in0=angf, scalar1=shift, scalar2=None, op0=ALU.add),
                 reads=["ang", "angki"], writes=["angm"])
            S.op(DVE, lambda: nc.vector.scalar_tensor_tensor(out=angr[:, :, :], in0=angkf[:, :, :], scalar=-TWO_PI, in1=angmf, op0=ALU.mult, op1=ALU.add),
                 reads=["angkf", "angm"], writes=["angr"])
            S.op(DVE, lambda: nc.vector.tensor_scalar(out=angmf, in0=angr[:, :, :], scalar1=math.pi, scalar2=TWO_PI, op0=ALU.is_gt, op1=ALU.mult),
                 reads=["angr"], writes=["angm"])
            S.op(DVE, lambda: nc.vector.tensor_tensor(out=angr[:, :, :], in0=angr[:, :, :], in1=angmf, op=ALU.subtract),
                 reads=["angr", "angm"], writes=["angr"])
            S.op(ACT, lambda: nc.scalar.activation(out=dst[:, :, :], in_=angr[:, :, :], func=AF.Sin), reads=["angr"], writes=[("cs" if dst is cs else "sn")])

        sin_of(sn, 0.0)
        sin_of(cs, 0.5 * math.pi)
        S.op(POOL, lambda: nc.gpsimd.tensor_copy(out=tabc, in_=cs[:, :, :].unsqueeze(2).to_broadcast([128, 4, nh, 32])),
             reads=["cs"], writes=[tabres + "c"])
        S.op(POOL, lambda: nc.gpsimd.tensor_copy(out=tabs, in_=sn[:, :, :].unsqueeze(2).to_broadcast([128, 4, nh, 32])),
             reads=["sn"], writes=[tabres + "s"])

    def head_norm_rope(e, src, srcres, nh, gbc, gres, tabc, tabs, tabres, sq, sqres, dst, dstres):
        V = vec(e)
        SH = 4 * nh
        cb = 250 + SH * 64 * (0.9 if e == POOL else 0.55)
        ch = 250 + SH * 32 * (0.9 if e == POOL else 0.55)
        x3 = src.rearrange("p s (h d) -> p (s h) d", h=nh)
        sq3 = sq.rearrange("p s (h d) -> p (s h) d", h=nh)
        S.op(e, lambda: V.tensor_tensor(out=sq3, in0=x3, in1=x3, op=ALU.mult), reads=[srcres], writes=[sqres], cost=cb)
        S.op(DVE, lambda: nc.vector.tensor_reduce(out=hss[:, 0:SH], in_=sq3, axis=AX.X, op=ALU.add), reads=[sqres], writes=["hss"], cost=250 + SH * 64 * 0.55)
        S.op(ACT, lambda: nc.scalar.activation(out=hrs[:, 0:SH], in_=hss[:, 0:SH], func=AF.Sqrt, scale=1.0 / 64, bias=epsb[:, 0:1]),
             reads=["hss", "epsb"], writes=["hrs"], cost=300)
        S.op(DVE, lambda: nc.vector.reciprocal(out=hrs[:, 0:SH], in_=hrs[:, 0:SH]), reads=["hrs"], writes=["hrs"], cost=190)
        S.op(e, lambda: V.tensor_tensor(out=x3, in0=x3, in1=hrs[:, 0:SH].unsqueeze(2).to_broadcast([128, SH, 64]), op=ALU.mult),
             reads=[srcres, "hrs"], writes=[srcres], cost=cb)
        S.op(e, lambda: V.tensor_tensor(out=x3, in0=x3, in1=gbc[:, :].unsqueeze(1).to_broadcast([128, SH, 64]), op=ALU.mult),
             reads=[srcres, gres], writes=[srcres], cost=cb)
        pat = "p s (h a r f) -> p (s h) a r f"
        x5 = src.rearrange(pat, h=nh, a=2, r=2)
        q5 = sq.rearrange(pat, h=nh, a=2, r=2)
        d5 = dst.rearrange(pat, h=nh, a=2, r=2)
        xa, xb_ = x5[:, :, :, 0, :], x5[:, :, :, 1, :]
        ta, tb_ = q5[:, :, :, 0, :], q5[:, :, :, 1, :]
        c4 = tabc.rearrange("p s h (a f) -> p (s h) a f", a=2)
        s4 = tabs.rearrange("p s h (a f) -> p (s h) a f", a=2)
        oa, ob = d5[:, :, :, 0, :], d5[:, :, :, 1, :]
        S.op(e, lambda: V.tensor_tensor(out=ta, in0=xb_, in1=s4, op=ALU.mult), reads=[srcres, tabres + "s"], writes=[sqres], cost=ch)
        S.op(e, lambda: V.tensor_tensor(out=tb_, in0=xa, in1=s4, op=ALU.mult), reads=[srcres, tabres + "s"], writes=[sqres], cost=ch)
        S.op(e, lambda: V.tensor_tensor(out=xa, in0=xa, in1=c4, op=ALU.mult), reads=[srcres, sqres, tabres + "c"], writes=[srcres], cost=ch)
        S.op(e, lambda: V.tensor_tensor(out=xb_, in0=xb_, in1=c4, op=ALU.mult), reads=[srcres, sqres, tabres + "c"], writes=[srcres], cost=ch)
        S.op(e, lambda: V.tensor_tensor(out=oa, in0=xa, in1=ta, op=ALU.subtract), reads=[srcres, sqres], writes=[dstres], cost=ch)
        S.op(e, lambda: V.tensor_tensor(out=ob, in0=xb_, in1=tb_, op=ALU.add), reads=[srcres, sqres], writes=[dstres], cost=ch)

    def XR(xres):
        return [(xres, s) for s in range(4)]

    def tok_rows(ap2d):
        return ap2d.rearrange("(s p) d -> p s d", p=128)

    dbg = {}

    apos[0] = 0
    hid = carve([128, NJ, T], BF16)
    dn = [carve([128, NJ, 512], BF16), carve([128, NJ, 512], BF16)]
    kvw = carve([128, 8, 256], BF16)
    kvs = carve([128, 2, 4, 128], F32)
    ksq = carve([128, 4, 128], F32)
    tkc = carve([128, 4, 2, 32], F32)
    tks = carve([128, 4, 2, 32], F32)
    krb = carve([128, 4, 128], BF16)
    kTb = [carve([128, T], BF16), carve([128, T], BF16)]
    vsb = [carve([128, 4, 130], BF16), carve([128, 4, 130], BF16)]
    hb2 = carve([128, 4, D], BF16)
    hT2 = carve([128, 8, T], BF16)
    ALT = (hb2, "hb2", hT2, "hT2")

    S.op(SP, lambda: nc.sync.dma_start(out=kvw, in_=s_kv), reads=[("scr", "kv")], writes=["kvw"], dma_key="kvw")

    for t in range(NCT if stage >= 1 else 0):
        b = t % 2
        X, xres = Xb[b], "X%d" % b
        def prep1(tt):
            bb = tt % 2
            S.op(SP, lambda bb=bb, tt=tt: nc.sync.dma_start(out=Xb[bb][:, :, :], in_=tok_rows(xc[tt * T:(tt + 1) * T, :])),
                 writes=XR("X%d" % bb), dma_key="xl%d" % bb, cost=13600)
            rmsnorm_T(Xb[bb], "X%d" % bb, 0)
        if t == 0:
            prep1(0)
        ffn(0, X, xres, hid, dn, mid=(lambda t=t: prep1(t + 1)) if t + 1 < NCT else None)
        if t < NOT:
            S.op(SP, lambda b=b, t=t: nc.sync.dma_start(out=tok_rows(x1s[t * T:(t + 1) * T, :]), in_=Xb[b][:, :, :]),
                 reads=XR(xres), writes=[("x1s", t, s) for s in range(4)], dma_key="xs%d" % b, cost=13600)
        if P1 < 3:
            continue
        rmsnorm_T(X, xres, 1, ALT)
        if P1 < 4:
            continue
        rope_tables(t, 2, tkc, tks, "tk")
        if P1 < 5:
            continue
        for s in range(4):
            gb_ = s % 2
            for kc in range(8):
                S.op(PE, lambda kc=kc, s=s, gb_=gb_: nc.tensor.matmul(G[gb_][:, 0:256], lhsT=hT2[:, kc, s * 128:(s + 1) * 128], rhs=kvw[:, kc, :],
                                                                       start=(kc == 0), stop=(kc == 7)),
                     reads=[("hT2", kc), "kvw"], writes=[GN[gb_]], cost=200)
            S.op(ACT, lambda s=s, gb_=gb_: nc.scalar.copy(out=kvs[:, :, s, :], in_=G[gb_][:, 0:256].rearrange("p (a d) -> p a d", a=2)), reads=[GN[gb_]], writes=["kvs"])
        if P1 < 6:
            continue
        vb, vres = vsb[b], "vsb%d" % b
        kmt = km[:, t * 4:(t + 1) * 4]
        vb4 = vb.rearrange("p s (h e) -> p s h e", h=2)
        S.op(POOL, lambda vb4=vb4, kmt=kmt: nc.gpsimd.tensor_tensor(
            out=vb4[:, :, :, 0:64], in0=kvs[:, 1, :, :].rearrange("p s (h d) -> p s h d", h=2),
            in1=kmt.unsqueeze(2).unsqueeze(3).to_broadcast([128, 4, 2, 64]), op=ALU.mult),
            reads=["kvs", "km"], writes=[vres])
        S.op(POOL, lambda vb4=vb4, kmt=kmt: nc.gpsimd.tensor_copy(out=vb4[:, :, :, 64], in_=kmt.unsqueeze(2).to_broadcast([128, 4, 2])),
             reads=["km"], writes=[vres])
        S.op(SP, lambda vb=vb, t=t: nc.sync.dma_start(out=vss[t * T:(t + 1) * T, :].rearrange("(s p) e -> p s e", p=128), in_=vb),
             reads=[vres], writes=[("vss", t)], dma_key="vst%d" % b)
        if P1 < 7:
            continue
        head_norm_rope(POOL, kvs[:, 0, :, :], "kvs", 2, gk_bc, "gk", tkc, tks, "tk", ksq, "ksq", krb, "krb")
        for s in range(4):
            S.op(PE, lambda s=s: nc.tensor.transpose(out=tp[:, s * 128:(s + 1) * 128], in_=krb[:, s, :], identity=ident[:]),
                 reads=["krb", "ident"], writes=[("tpb", 0)], cost=118)
        kb_, kres = kTb[b], "kTb%d" % b
        S.op(DVE, lambda kb_=kb_: nc.vector.tensor_copy(out=kb_, in_=tp[:, 0:512]), reads=[("tpb", 0)], writes=[kres])
        S.op(SP, lambda kb_=kb_, t=t: nc.sync.dma_start(out=kts[:, t * T:(t + 1) * T], in_=kb_),
             reads=[kres], writes=[("kts", t)], dma_key="kst%d" % b)

    S.barrier()
    apos[0] = 0
    kT = carve([128, NCT * T], BF16)
    vAf = carve([128, NCH * 130 + 64], BF16)
    vA = vAf[:, 0:NCH * 130].rearrange("p (ch e) -> p ch e", e=130)
    qf = carve([128, 4, 512], F32)
    qsq = hb[:, :, :].rearrange("p s d -> p (s d)").bitcast(F32).rearrange("p (s d) -> p s d", s=4)
    tqc = carve([128, 4, 8, 32], F32)
    tqs = carve([128, 4, 8, 32], F32)
    qrb = carve([128, 4, 512], BF16)
    qTp = Xb[1][:, 0:2, :].rearrange("p a d -> p (a d)").bitcast(BF16).rearrange("p (h t) -> p h t", h=8)
    ub = carve([128, 4, 512], BF16)
    vnb = tqc.rearrange("p s h f -> p (s h f)").bitcast(BF16)[:, 0:2048].rearrange("p (s d) -> p s d", s=4)
    sgb = tqs.rearrange("p s h f -> p (s h f)").bitcast(BF16)[:, 0:2048].rearrange("p (s d) -> p s d", s=4)
    sgT = carve([128, 4, T], BF16)
    aT = carve([128, 8, T], BF16)
    mT = carve([128, 8, T], BF16)
    PT = [carve([128, 1024], BF16), carve([128, 1024], BF16), carve([128, 1024], BF16)]
    rden = carve([128, T], F32)
    ones1 = carve([128, 64], F32)
    onT = carve([128, T], F32)
    ggv_bc = carve([128, 512], F32)
    S.op(POOL, lambda: nc.gpsimd.memset(ones1, 1.0), writes=["ones1"])
    S.op(POOL, lambda: nc.gpsimd.memset(qTp, 0.0), writes=[("qT", c) for c in range(4)])
    S.op(POOL, lambda: nc.gpsimd.memset(vAf[:, NCH * 130:NCH * 130 + 64], 0.0), writes=["vApad"])
    small_load(ggv_bc, bc_row(g_gv), "ggv")

    for c in range(0, NCT if stage >= 2 else 0, 8):
        n = min(8, NCT - c)
        S.op(SP, lambda c=c, n=n: nc.sync.dma_start(out=kT[:, c * T:(c + n) * T], in_=kts[:, c * T:(c + n) * T]),
             reads=[("kts", t) for t in range(c, c + n)], writes=["kT"], dma_key="kTl", cost=9000)
        S.op(SP, lambda c=c, n=n: nc.sync.dma_start(out=vA[:, c * 4:(c + n) * 4, :],
                                                    in_=vss[c * T:(c + n) * T, :].rearrange("(ch p) e -> p ch e", p=128)),
             reads=[("vss", t) for t in range(c, c + n)], writes=["vA"], dma_key="vAl", cost=15000)

    def attention():
        heads = [(c, g) for c in range(4) for g in range(2)]
        seq = [(hi, i) for hi in range(8) for i in range(NPAIR)]
        Sv = [S0[:, :], S1[:, :], tpf]
        Sres = [[GN[0], GN[1]], [GN[2], GN[3]], [("tpb", 0), ("tpb", 1)]]

        def qk(n):
            hi, i = seq[n]
            c, g = heads[hi]
            sb_ = n % 3
            Sx = Sv[sb_]
            for u in range(2):
                ch = 2 * i + u
                S.op(PE, lambda u=u, ch=ch, c=c, g=g, Sx=Sx: nc.tensor.matmul(
                    Sx[:, u * 512:(u + 1) * 512], lhsT=kT[:, ch * 128:(ch + 1) * 128],
                    rhs=qTp[:, c * 2 + g, :], start=True, stop=True),
                    reads=["kT", ("qT", c)], writes=[Sres[sb_][u]])

        qk(0)
        qk(1)
        for n in range(len(seq)):
            hi, i = seq[n]
            c, g = heads[hi]
            sb_ = n % 3
            Sx = Sv[sb_]
            ob = hi % 2
            Ox = (O0, O1)[ob]
            if n + 2 < len(seq):
                qk(n + 2)
            S.op(ACT, lambda Sx=Sx, sb_=sb_: nc.scalar.activation(out=PT[sb_], in_=Sx, func=AF.Exp, scale=0.125),
                 reads=Sres[sb_], writes=["PT%d" % sb_], cost=1023)
            for u in range(2):
                ch = 2 * i + u
                S.op(PE, lambda u=u, ch=ch, g=g, Ox=Ox, sb_=sb_, i=i: nc.tensor.matmul(
                    Ox[:, :], lhsT=vAf[:, ch * 130 + g * 65:ch * 130 + g * 65 + 128], rhs=PT[sb_][:, u * 512:(u + 1) * 512],
                    start=(i == 0 and u == 0), stop=(i == NPAIR - 1 and u == 1)),
                    reads=["vA", "vApad", "PT%d" % sb_], writes=[GN[4 + ob]])
            if i == NPAIR - 1:
                h_true = g * 4 + c
                S.op(DVE, lambda Ox=Ox: nc.vector.reciprocal(out=rden[64:65, :], in_=Ox[64:65, :]), reads=[GN[4 + ob]], writes=["rden"], cost=2472)
                S.op(PE, lambda Sx=Sx: nc.tensor.matmul(Sx[0:64, 0:512], lhsT=ones1[64:65, 0:64], rhs=rden[64:65, :], start=True, stop=True),
                     reads=["rden", "ones1"], writes=[Sres[sb_][0]], cost=970)
                S.op(ACT, lambda Sx=Sx: nc.scalar.copy(out=onT[0:64, :], in_=Sx[0:64, 0:512]), reads=[Sres[sb_][0]], writes=["onT"])
                S.op(DVE, lambda Ox=Ox, h_true=h_true: nc.vector.tensor_tensor(out=aT[0:64, h_true, :], in0=Ox[0:64, :], in1=onT[0:64, :], op=ALU.mult),
                     reads=[GN[4 + ob], "onT"], writes=[("aT", h_true)])

    for t in range(NOT if stage >= 2 else 0):
        b = 0
        X, xres = Xb[b], "X%d" % b
        for s in range(4):
            S.op(SP, lambda s=s, t=t: nc.sync.dma_start(out=Xb[0][:, s, :], in_=x1s[t * T + s * 128:t * T + (s + 1) * 128, :]),
                 reads=[("x1s", t, s)], writes=[(xres, s)], dma_key="xa%d" % s, cost=5000)
        rmsnorm_T(X, xres, 1)
        rope_tables(t, 8, tqc, tqs, "tq")
        for pi in range(3):
            slot = ring_load("qgg%d" % pi, s_qgg[pi].rearrange("p kc c -> p (kc c)"), 4096)
            rv = ring[:, slot, :].rearrange("p (kc c) -> p kc c", kc=8)
            for s in range(4):
                gb_ = s % 2
                for kc in range(8):
                    S.op(PE, lambda kc=kc, s=s, gb_=gb_, rv=rv: nc.tensor.matmul(G[gb_], lhsT=hT[:, kc, s * 128:(s + 1) * 128], rhs=rv[:, kc, :],
                                                                                  start=(kc == 0), stop=(kc == 7)),
                         reads=[("hT", kc), ("ring", slot)], writes=[GN[gb_]])
                if pi == 0:
                    S.op(ACT, lambda s=s, gb_=gb_: nc.scalar.copy(out=qf[:, s, :], in_=G[gb_]), reads=[GN[gb_]], writes=["qf"])
                elif pi == 1:
                    S.op(ACT, lambda s=s, gb_=gb_: nc.scalar.activation(out=ub[:, s, :], in_=G[gb_], func=AF.Gelu), reads=[GN[gb_]], writes=["ub"])
                else:
                    S.op(ACT, lambda s=s, gb_=gb_: nc.scalar.activation(out=qf[:, s, :], in_=G[gb_], func=AF.Gelu), reads=[GN[gb_]], writes=["qf"])
            if pi == 0:
                head_norm_rope(DVE, qf, "qf", 8, gq_bc, "gq", tqc, tqs, "tq", qsq, "hb", qrb, "qrb")
                for c0 in range(0, 4, 2):
                    bank = (c0 // 2) % 2
                    for kk in range(2):
                        c = c0 + kk
                        for s in range(4):
                            col = bank * 1024 + kk * 512 + s * 128
                            S.op(PE, lambda c=c, s=s, col=col: nc.tensor.transpose(out=tp[:, col:col + 128], in_=qrb[:, s, c * 128:(c + 1) * 128],
                                                                                    identity=ident[:]),
                                 reads=["qrb", "ident"], writes=[("tpb", bank)], cost=118)
                    for kk in range(2):
                        c = c0 + kk
                        for g in range(2):
                            src_ps = tp[g * 64:(g + 1) * 64, bank * 1024 + kk * 512: bank * 1024 + (kk + 1) * 512]
                            dst = qTp[g * 64:(g + 1) * 64, c * 2 + g, :]
                            if bank == 0:
                                S.op(ACT, lambda src_ps=src_ps, dst=dst: nc.scalar.copy(out=dst, in_=src_ps), reads=[("tpb", bank)], writes=[("qT", c)])
                            else:
                                S.op(DVE, lambda src_ps=src_ps, dst=dst: nc.vector.tensor_copy(out=dst, in_=src_ps), reads=[("tpb", bank)], writes=[("qT", c)])
            if pi == 2:
                row_rstd(qf, "qf", 512, 512)
                for s in range(4):
                    S.op(DVE, lambda s=s: nc.vector.scalar_tensor_tensor(out=vnb[:, s, :], in0=qf[:, s, :], scalar=rstd[:, s:s + 1], in1=ggv_bc,
                                                                         op0=ALU.mult, op1=ALU.mult),
                         reads=["qf", ("rstd", 0), "ggv"], writes=["tqc"])
                for s in range(4):
                    gb_ = 2 + (s % 2)
                    for g in range(8):
                        S.op(PE, lambda s=s, g=g, gb_=gb_: nc.tensor.matmul(G[gb_][:, g * 64:(g + 1) * 64], lhsT=wsT[:, g, :], rhs=vnb[:, s, g * 64:(g + 1) * 64],
                                                                             start=True, stop=True),
                             reads=["tqc", "wsT"], writes=[GN[gb_]], cost=70)
                    tt, tn = (tmpA, "tmpA") if s % 2 == 0 else (tmpB, "tmpB")
                    S.op(DVE, lambda gb_=gb_, tt=tt: nc.vector.tensor_tensor(out=tt.rearrange("p (g c) -> p g c", g=8),
                                                                             in0=G[gb_].rearrange("p (g c) -> p g c", g=8),
                                                                             in1=bspT[:, :].unsqueeze(2).to_broadcast([128, 8, 64]), op=ALU.add),
                         reads=[GN[gb_], "bsp"], writes=[tn])
                    S.op(POOL, lambda s=s, tt=tt: nc.gpsimd.tensor_tensor(out=sgb[:, s, :], in0=tt, in1=ub[:, s, :], op=ALU.mult),
                         reads=[tn, "ub"], writes=["tqs"])
                transpose_T(sgb, "tqs", 4, sgT, "sgT")
        attention()
        for oc in range(8):
            slot = ring_load("mg%d" % oc, s_mg[oc], 3584)
            rg = ring[:, slot, 0:2048].rearrange("p (kc c) -> p kc c", kc=8)
            g0, g1 = (0, 1) if oc % 2 == 0 else (2, 3)
            for kc in range(8):
                S.op(PE, lambda kc=kc, rg=rg, g0=g0: nc.tensor.matmul(G[g0], lhsT=rg[:, kc, 0:128], rhs=hT[:, kc, :], start=(kc == 0), stop=(kc == 7)),
                     reads=[("ring", slot), ("hT", kc)], writes=[GN[g0]])
            for kc in range(8):
                S.op(PE, lambda kc=kc, rg=rg, g1=g1: nc.tensor.matmul(G[g1], lhsT=rg[:, kc, 128:256], rhs=hT[:, kc, :], start=(kc == 0), stop=(kc == 7)),
                     reads=[("ring", slot), ("hT", kc)], writes=[GN[g1]])
            for h in range(8):
                S.op(PE, lambda h=h, slot=slot: nc.tensor.matmul(G[4], lhsT=ring[0:64, slot, 2560 + h * 128:2560 + (h + 1) * 128], rhs=aT[0:64, h, :],
                                                                 start=(h == 0), stop=(h == 7)),
                     reads=[("ring", slot), ("aT", h)], writes=[GN[4]])
            for kc in range(4):
                S.op(PE, lambda kc=kc, slot=slot: nc.tensor.matmul(G[5], lhsT=ring[:, slot, 2048 + kc * 128:2048 + (kc + 1) * 128], rhs=sgT[:, kc, :],
                                                                   start=(kc == 0), stop=(kc == 3)),
                     reads=[("ring", slot), ("sgT", kc)], writes=[GN[5]])
            S.op(ACT, lambda g0=g0: nc.scalar.activation(out=tmpA, in_=G[g0], func=AF.Sigmoid), reads=[GN[g0]], writes=["tmpA"])
            S.op(ACT, lambda g1=g1: nc.scalar.activation(out=tmpB, in_=G[g1], func=AF.Sigmoid), reads=[GN[g1]], writes=["tmpB"])
            S.op(DVE, lambda: nc.vector.tensor_tensor(out=tmpA, in0=tmpA, in1=G[4], op=ALU.mult), reads=["tmpA", GN[4]], writes=["tmpA"])
            S.op(DVE, lambda: nc.vector.tensor_tensor(out=tmpB, in0=tmpB, in1=G[5], op=ALU.mult), reads=["tmpB", GN[5]], writes=["tmpB"])
            S.op(POOL, lambda oc=oc: nc.gpsimd.tensor_tensor(out=mT[:, oc, :], in0=tmpA, in1=tmpB, op=ALU.add),
                 reads=["tmpA", "tmpB"], writes=[("mT", oc)])
        wo_slots = [ring_load("wo%d" % h, s_wo[h].rearrange("p kc c -> p (kc c)"), 4096) for h in range(2)]
        for s in range(4):
            for h in range(2):
                slot = wo_slots[h]
                rv = ring[:, slot, :].rearrange("p (kc c) -> p kc c", kc=8)
                gb_ = h
                for kc in range(8):
                    S.op(PE, lambda kc=kc, s=s, gb_=gb_, rv=rv: nc.tensor.matmul(G[gb_], lhsT=mT[:, kc, s * 128:(s + 1) * 128], rhs=rv[:, kc, :],
                                                                                  start=(kc == 0), stop=(kc == 7)),
                         reads=[("mT", kc), ("ring", slot)], writes=[GN[gb_]])
                S.op(DVE, lambda s=s, h=h, gb_=gb_, X=X: nc.vector.tensor_tensor(out=X[:, s, h * 512:(h + 1) * 512], in0=G[gb_],
                                                                                  in1=X[:, s, h * 512:(h + 1) * 512], op=ALU.add),
                     reads=[GN[gb_], (xres, s)], writes=[(xres, s)])
            S.op(SP, lambda s=s, t=t: nc.sync.dma_start(out=x1s[t * T + s * 128:t * T + (s + 1) * 128, :], in_=Xb[0][:, s, :]),
                 reads=[(xres, s)], writes=[("x1s", t, s)], dma_key="xb%d" % s, cost=5000)

    S.barrier()
    apos[0] = 0
    hid = carve([128, NJ, T], BF16)
    dn = [carve([128, NJ, 512], BF16), carve([128, NJ, 512], BF16)]
    pf = carve([128, 4, PLE], F32)
    pbf = carve([128, 4, PLE], BF16)
    pT = carve([128, 2, T], BF16)
    gfin_bc = carve([128, D], F32)
    hb3 = carve([128, 4, D], BF16)
    hT3 = carve([128, 8, T], BF16)
    ALT = (hb3, "hb3", hT3, "hT3")
    small_load(gfin_bc, bc_row(g_fin), "gfin")
    out_ops = []
    for t in range(NOT if stage >= 3 else 0):
        b = t % 2
        X, xres = Xb[b], "X%d" % b
        def prep2(tt):
            bb = tt % 2
            S.op(SP, lambda bb=bb, tt=tt: nc.sync.dma_start(out=Xb[bb][:, :, :], in_=tok_rows(x1s[tt * T:(tt + 1) * T, :])),
                 reads=[("x1s", tt, s) for s in range(4)], writes=XR("X%d" % bb), dma_key="xl%d" % bb, cost=13600)
            rmsnorm_T(Xb[bb], "X%d" % bb, 2)
        if t == 0:
            prep2(0)
        ffn(1, X, xres, hid, dn, mid=(lambda t=t: prep2(t + 1)) if t + 1 < NOT else None)
        rmsnorm_T(X, xres, 3, ALT)
        S.op(SP, lambda t=t: nc.sync.dma_start(out=pf, in_=tok_rows(pin[t * T:(t + 1) * T, :])), writes=["pf"], dma_key="pfl")
        S.op(POOL, lambda: nc.gpsimd.tensor_copy(out=pbf, in_=pf), reads=["pf"], writes=["pbf"])
        transpose_T(pbf, "pbf", 2, pT, "pT")
        slotp = ring_load("pl", s_pl.rearrange("p kc c -> p (kc c)"), 2048)
        rvp = ring[:, slotp, 0:2048].rearrange("p (kc c) -> p kc c", kc=2)
        for h in range(2):
            slot = ring_load("pg%d" % h, s_pg[h].rearrange("p kc c -> p (kc c)"), 4096)
            rv = ring[:, slot, :].rearrange("p (kc c) -> p kc c", kc=8)
            for s in range(4):
                g0, g1 = (0, 1) if s % 2 == 0 else (2, 3)
                for kc in range(8):
                    S.op(PE, lambda kc=kc, s=s, g0=g0, rv=rv: nc.tensor.matmul(G[g0], lhsT=hT3[:, kc, s * 128:(s + 1) * 128], rhs=rv[:, kc, :],
                                                                                start=(kc == 0), stop=(kc == 7)),
                         reads=[("hT3", kc), ("ring", slot)], writes=[GN[g0]])
                for k2 in range(2):
                    S.op(PE, lambda k2=k2, s=s, g1=g1, h=h, rvp=rvp: nc.tensor.matmul(G[g1], lhsT=pT[:, k2, s * 128:(s + 1) * 128],
                                                                                       rhs=rvp[:, k2, h * 512:(h + 1) * 512],
                                                                                       start=(k2 == 0), stop=(k2 == 1)),
                         reads=[("pT", k2), ("ring", slotp)], writes=[GN[g1]])
                tt, tn = (tmpA, "tmpA") if s % 2 == 0 else (tmpB, "tmpB")
                S.op(ACT, lambda g0=g0, tt=tt: nc.scalar.activation(out=tt, in_=G[g0], func=AF.Sigmoid), reads=[GN[g0]], writes=[tn])
                S.op(DVE, lambda g1=g1, tt=tt: nc.vector.tensor_tensor(out=tt, in0=tt, in1=G[g1], op=ALU.mult), reads=[tn, GN[g1]], writes=[tn])
                S.op(POOL, lambda s=s, h=h, tt=tt, X=X: nc.gpsimd.tensor_tensor(out=X[:, s, h * 512:(h + 1) * 512], in0=X[:, s, h * 512:(h + 1) * 512],
                                                                                 in1=tt, op=ALU.add),
                     reads=[tn, (xres, s)], writes=[(xres, s)])
        row_rstd(X, xres, D, D)
        for s in range(4):
            S.op(DVE, lambda s=s, X=X: nc.vector.scalar_tensor_tensor(out=X[:, s, :], in0=X[:, s, :], scalar=rstd[:, s:s + 1], in1=gfin_bc,
                                                                      op0=ALU.mult, op1=ALU.mult),
                 reads=[(xres, s), ("rstd", 0), "gfin"], writes=[(xres, s)])
        out_ops.append(S.op(SP, lambda b=b, t=t: nc.sync.dma_start(out=tok_rows(y[t * T:(t + 1) * T, :]), in_=Xb[b][:, :, :]),
                            reads=XR(xres), dma_key="ys%d" % b, cost=13600))
    if stage < 3 and stage >= 1:
        out_ops.append(S.op(SP, lambda: nc.sync.dma_start(out=y[:, :], in_=x1s[:, :]), reads=[("x1s", t, s) for t in range(NOT) for s in range(4)], dma_key="dbg"))
    S.op(SP, None, extra=out_ops)

    if SCHEDULE:
        S.schedule()
        build_program.last_est_ns = S.est_ns
    semkeys = S.finalize()
    by_eng = {e: [o for o in S.ops if o.eng == e] for e in ENGS}
    with ExitStack() as es:
        sems = {k: es.enter_context(nc.semaphore("s%d" % i)) for i, k in enumerate(semkeys)}
        block = es.enter_context(nc.Block())

        def emit(engname, eng):
            waited = {}
            for o in by_eng[engname]:
                need = {}
                for d in o.deps:
                    if not d.signal or d.fn is None:
                        continue
                    if d.eng == PE and o.eng == PE and d.dma_key is None and o.dma_key is None:
                        continue
                    if need.get(d.sem, 0) < d.val:
                        need[d.sem] = d.val
                for k, v in need.items():
                    if waited.get(k, 0) < v:
                        eng.wait_ge(sems[k], v)
                        waited[k] = v
                if o.fn is not None:
                    ins = o.fn()
                    if o.signal:
                        ins.then_inc(sems[o.sem], 16 if o.dma_key is not None else 1)
                else:
                    assert not o.signal

        @block.tensor
        def _(e):
            emit(PE, e)

        @block.scalar
        def _(e):
            emit(ACT, e)

        @block.vector
        def _(e):
            emit(DVE, e)

        @block.gpsimd
        def _(e):
            emit(POOL, e)

        @block.sync
        def _(e):
            emit(SP, e)
    return nc


NCT_FULL, NOT_FULL = 32, 8
WNAMES = ["g_ffn1", "w_ffn1_gu", "w_ffn1_down", "g_mix", "w_in", "g_q", "g_k", "g_gmlp_v", "w_spatial", "b_spatial",
          "w_branch_attn", "w_branch_gmlp", "w_out", "g_ffn2", "w_ffn2_gu", "w_ffn2_down", "g_ple", "w_ple_gate", "w_ple", "g_final"]


def _pos_table(tok_idx):
    tok_idx = np.asarray(tok_idx, np.int64)
    return np.stack([tok_idx // 64, tok_idx % 64], axis=1).astype(np.float32)


def kernel(**inputs):
    xp = np.asarray(inputs["x_prompt"], np.float32)
    xs = np.asarray(inputs["x_sample"], np.float32)
    pp = np.asarray(inputs["p_prompt"], np.float32)
    ps = np.asarray(inputs["p_sample"], np.float32)
    w = {k: np.ascontiguousarray(np.asarray(inputs[k], np.float32)) for k in WNAMES}
    w["g_final"] = w["g_final"].reshape(1, D)
    NTOK = NCT_FULL * T
    own = NOT_FULL * T
    in_maps = []
    for c in range(8):
        if c < 4:
            order = [c] + [(c + k) % 4 for k in range(1, 4)]
            xcx = np.concatenate([xp[o] for o in order], axis=0)
            posi = np.concatenate([np.arange(own)] * 4)
            msk = np.zeros(NTOK, np.float32); msk[:own] = 1.0
            pc = pp[0, c]
        else:
            q = c - 4
            order = [q] + [(q + k) % 4 for k in range(1, 4)]
            xcx = np.concatenate([xs[0, o * own:(o + 1) * own] for o in order], axis=0)
            posi = np.concatenate([np.arange(o * own, (o + 1) * own) for o in order])
            msk = np.ones(NTOK, np.float32)
            pc = ps[0, 0, q * own:(q + 1) * own]
        m = {"xc": np.ascontiguousarray(xcx), "pin": np.ascontiguousarray(pc), "pos": _pos_table(posi),
             "kmask": np.ascontiguousarray(msk.reshape(NTOK // 128, 128).T)}
        m.update(w)
        in_maps.append(m)
    nc = build_program(NCT_FULL, NOT_FULL)
    res = run_bass_kernel_spmd(nc, in_maps, core_ids=list(range(8)))
    ys = [np.asarray(res.results[c]["y"], np.float32) for c in range(8)]
    y_prompt = np.stack(ys[0:4], axis=0)
    y_sample = np.concatenate(ys[4:8], axis=0)[None]
    return (y_prompt, y_sample)
```

```python
import math
SCHEDULE = True
P0 = 255
P1 = 99
P1SUB = 99
P1T = 0
from contextlib import ExitStack
import numpy as np
import concourse.bass as bass
import concourse.mybir as mybir
from concourse.bass_utils import run_bass_kernel_spmd

F32 = mybir.dt.float32
BF16 = mybir.dt.bfloat16
I32 = mybir.dt.int32
AF = mybir.ActivationFunctionType
ALU = mybir.AluOpType
AX = mybir.AxisListType

D = 1024
DFF = 2816
NJ = DFF // 128
PLE = 256
EPS = 1e-6
T = 512
NSLOT = 3
SLOTW = 4096
PE, ACT, DVE, POOL, SP = "pe", "act", "dve", "pool", "sp"
ENGS = [PE, ACT, DVE, POOL, SP]


class Op:
    __slots__ = ("eng", "fn", "deps", "dma_key", "sem", "val", "signal", "idx", "cost", "phase", "t0", "t1")

    def __init__(self, eng, fn, deps, dma_key, cost):
        self.eng, self.fn, self.deps, self.dma_key, self.cost = eng, fn, deps, dma_key, cost
        self.sem = None
        self.val = 0
        self.signal = False


DEFAULT_COST = {PE: 216, ACT: 600, DVE: 650, POOL: 900, SP: 2500}


class Sched:
    def __init__(self):
        self.ops = []
        self.last_write = {}
        self.readers = {}
        self.phase = 0
        self.since_barrier = []
        self.cur_barrier = {}

    def op(self, eng, fn, reads=(), writes=(), dma_key=None, extra=(), cost=None):
        deps = set(extra)
        for r in reads:
            w = self.last_write.get(r)
            if w is not None:
                deps.add(w)
        for w_ in writes:
            w = self.last_write.get(w_)
            if w is not None:
                deps.add(w)
            for rd in self.readers.get(w_, ()):
                deps.add(rd)
        bar = self.cur_barrier.get(eng)
        if bar is not None:
            deps.add(bar)
        o = Op(eng, fn, deps, dma_key, DEFAULT_COST[eng] if cost is None else cost)
        o.idx = len(self.ops)
        o.phase = self.phase
        o.sem = (eng, self.phase) if dma_key is None else ("dma", dma_key)
        self.ops.append(o)
        for r in reads:
            self.readers.setdefault(r, []).append(o)
        for w_ in writes:
            self.last_write[w_] = o
            self.readers[w_] = []
        if fn is not None:
            self.since_barrier.append(o)
        return o

    def barrier(self):
        prev = list(self.since_barrier)
        self.since_barrier = []
        for e in ENGS:
            self.cur_barrier[e] = self.op(e, None, extra=prev, cost=0)
        self.phase += 1

    def schedule(self):
        import heapq
        n = len(self.ops)
        succ = [[] for _ in range(n)]
        indeg = [0] * n
        for o in self.ops:
            indeg[o.idx] = len(o.deps)
            for d in o.deps:
                succ[d.idx].append(o)
        pending = {e: [] for e in ENGS}
        avail = {e: [] for e in ENGS}
        free = {e: 0.0 for e in ENGS}
        ready_t = [0.0] * n
        for o in self.ops:
            if indeg[o.idx] == 0:
                heapq.heappush(pending[o.eng], (0.0, o.idx))
        order = []
        done = 0
        while done < n:
            best = None
            for e in ENGS:
                pe_, av = pending[e], avail[e]
                while pe_ and pe_[0][0] <= free[e]:
                    heapq.heappush(av, heapq.heappop(pe_)[1])
                if av:
                    cand = (free[e], av[0], e, True)
                elif pe_:
                    cand = (pe_[0][0], pe_[0][1], e, False)
                else:
                    continue
                if best is None or cand[:2] < best[:2]:
                    best = cand
            assert best is not None, "dependency cycle"
            start, idx, e, from_av = best
            if from_av:
                heapq.heappop(avail[e])
            else:
                heapq.heappop(pending[e])
            o = self.ops[idx]
            o.t0 = start
            if o.dma_key is not None:
                free[e] = start + (1100.0 if e == POOL else 350.0)
                o.t1 = start + o.cost
            else:
                o.t1 = start + o.cost
                free[e] = o.t1
            order.append(o)
            done += 1
            for sc in succ[idx]:
                if ready_t[sc.idx] < o.t1:
                    ready_t[sc.idx] = o.t1
                indeg[sc.idx] -= 1
                if indeg[sc.idx] == 0:
                    heapq.heappush(pending[sc.eng], (ready_t[sc.idx], sc.idx))
        self.ops = order
        self.est_ns = max(o.t1 for o in order)

    def finalize(self):
        for o in self.ops:
            for d in o.deps:
                if d.eng == PE and o.eng == PE and d.dma_key is None and o.dma_key is None:
                    continue
                if d.fn is None:
                    continue
                d.signal = True
        for o in self.ops:
            if o.dma_key is not None and o.fn is not None:
                o.signal = True
        counts = {}
        for o in self.ops:
            if o.signal:
                counts[o.sem] = counts.get(o.sem, 0) + (16 if o.dma_key is not None else 1)
                o.val = counts[o.sem]
        return sorted(counts.keys(), key=str)


def build_program(NCT, NOT, stage=3):
    NCH = NCT * 4
    NPAIR = NCH // 2
    nc = bass.Bass("TRN2", target_bir_lowering=False)

    def din(name, shape, dt=F32):
        return nc.dram_tensor(name, list(shape), dt, kind="ExternalInput").ap()

    xc = din("xc", [NCT * T, D])
    pin = din("pin", [NOT * T, PLE])
    pos = din("pos", [NCT * T, 2])
    kmask = din("kmask", [128, NCH])
    g_ffn1 = din("g_ffn1", [1, D]); w1gu = din("w_ffn1_gu", [1, D, 2 * DFF]); w1d = din("w_ffn1_down", [1, DFF, D])
    g_mix = din("g_mix", [1, D]); w_in = din("w_in", [1, D, 3840])
    g_q = din("g_q", [1, 64]); g_k = din("g_k", [1, 64]); g_gv = din("g_gmlp_v", [1, 512])
    w_sp = din("w_spatial", [1, 8, 128, 128]); b_sp = din("b_spatial", [1, 8, 128])
    w_ba = din("w_branch_attn", [1, 512, D]); w_bg = din("w_branch_gmlp", [1, 512, D]); w_out = din("w_out", [1, D, D])
    g_ffn2 = din("g_ffn2", [1, D]); w2gu = din("w_ffn2_gu", [1, D, 2 * DFF]); w2d = din("w_ffn2_down", [1, DFF, D])
    g_ple = din("g_ple", [1, D]); w_pg = din("w_ple_gate", [1, D, D]); w_pl = din("w_ple", [1, PLE, D])
    g_fin = din("g_final", [1, D])
    y = nc.dram_tensor("y", [NOT * T, D], F32, kind="ExternalOutput").ap()

    def dscr(name, shape, dt=BF16):
        return nc.dram_tensor(name, list(shape), dt).ap()

    s_gu = [dscr("s_gu1", [11, 128, 8, 512]), dscr("s_gu2", [11, 128, 8, 512])]
    s_dn = [dscr("s_dn1", [2, 128, NJ, 512]), dscr("s_dn2", [2, 128, NJ, 512])]
    s_kv = dscr("s_kv", [128, 8, 256])
    s_qgg = dscr("s_qgg", [3, 128, 8, 512])
    s_mg = dscr("s_mg", [8, 128, 3584])
    s_wo = dscr("s_wo", [2, 128, 8, 512])
    s_pg = dscr("s_pg", [2, 128, 8, 512])
    s_pl = dscr("s_pl", [128, 2, 1024])
    x1s = dscr("x1s", [NOT * T, D], F32)
    kts = dscr("kts", [128, NCT * T])
    vss = dscr("vss", [NCT * T, 130])

    S = Sched()

    def sb(name, shape, dt):
        return nc.alloc_sbuf_tensor(name, list(shape), dt)

    ident = sb("ident", [128, 128], BF16)
    gcol = sb("gcol", [128, 4, 8], F32)
    gq_bc = sb("gq_bc", [128, 64], F32)
    gk_bc = sb("gk_bc", [128, 64], F32)
    bspT = sb("bspT", [128, 8], F32)
    wsT = sb("wsT", [128, 8, 128], BF16)
    inv_bc = sb("inv_bc", [128, 16], F32)
    km = sb("km", [128, NCH], F32)
    negpi = sb("negpi", [128, 1], F32)
    epsb = sb("epsb", [128, 1], F32)
    ring = sb("ring", [128, NSLOT, SLOTW], BF16)
    Xb = [sb("X0", [128, 4, D], F32), sb("X1", [128, 4, D], F32)]
    hb = sb("hb", [128, 4, D], BF16)
    hT = sb("hT", [128, 8, T], BF16)
    ss = sb("ss", [128, 8], F32)
    rstd = sb("rstd", [128, 8], F32)
    tmpA = sb("tmpA", [128, T], F32)[:, :]
    tmpB = sb("tmpB", [128, T], F32)[:, :]
    posb = sb("posb", [128, 4, 2], F32)
    ang = sb("ang", [128, 4, 2, 16], F32)
    angm = sb("angm", [128, 4, 2, 16], F32)
    cs = sb("cs", [128, 4, 32], F32)
    sn = sb("sn", [128, 4, 32], F32)
    angki = sb("angki", [128, 4, 32], I32)
    angkf = sb("angkf", [128, 4, 32], F32)
    angr = sb("angr", [128, 4, 32], F32)
    hss = sb("hss", [128, 32], F32)
    hrs = sb("hrs", [128, 32], F32)
    ARENA_BYTES = 123 * 1024
    arena = sb("arena", [128, ARENA_BYTES // 4], F32)
    apos = [0]

    def carve(shape, dt):
        esz = 4 if dt in (F32, I32) else 2
        n = int(np.prod(shape[1:]))
        nbytes = (n * esz + 31) // 32 * 32
        off = apos[0]
        apos[0] += nbytes
        assert apos[0] <= ARENA_BYTES, (apos[0], ARENA_BYTES)
        v = arena[:, off // 4:(off + nbytes) // 4]
        if esz == 2:
            v = v.bitcast(BF16)
        v = v[:, 0:n]
        if len(shape) == 3:
            v = v.rearrange("p (a b) -> p a b", a=shape[1])
        elif len(shape) == 4:
            v = v.rearrange("p (a b c) -> p a b c", a=shape[1], b=shape[2])
        return v

    tp = nc.alloc_psum_tensor("tp", [128, 2048], BF16)
    S0 = nc.alloc_psum_tensor("S0", [128, 1024], F32)
    S1 = nc.alloc_psum_tensor("S1", [128, 1024], F32)
    O0 = nc.alloc_psum_tensor("O0", [128, 512], F32)
    O1 = nc.alloc_psum_tensor("O1", [128, 512], F32)
    G = [S0[:, 0:512], S0[:, 512:1024], S1[:, 0:512], S1[:, 512:1024], O0[:, :], O1[:, :]]
    GN = ["G0", "G1", "G2", "G3", "G4", "G5"]
    tpf = tp[:, :].bitcast(F32)

    def vec(e):
        return nc.vector if e == DVE else nc.gpsimd

    def cast(key, out_ap, in_ap):
        if not (P0 & 8):
            return None
        return S.op(POOL, lambda o=out_ap, i=in_ap: nc.gpsimd.dma_start(out=o, in_=i),
                    writes=[("scr", key)], dma_key="c_" + key, cost=9000)

    def small_load(out_ap, in_ap, res):
        def f():
            with nc.allow_non_contiguous_dma(reason="tiny constant layout load"):
                return nc.sync.dma_start(out=out_ap, in_=in_ap)
        return S.op(SP, f, writes=[res], dma_key="k_" + str(res).replace("'", "").replace(" ", ""))

    def bc_row(ap2d):
        return ap2d.partition_broadcast(128).rearrange("p o d -> p (o d)")

    gl = sb("gl", [40, 128], F32)
    identf = sb("identf", [40, 40], F32)
    for i, g in enumerate([g_ffn1, g_mix, g_ffn2, g_ple]):
        S.op(SP, lambda i=i, g=g: nc.sync.dma_start(out=gl[i * 8:(i + 1) * 8, :], in_=g.rearrange("o (kc p) -> (o kc) p", p=128)),
             writes=[("gl", i)], dma_key="k_gl%d" % i)
    S.op(SP, lambda: nc.sync.dma_start(out=gl[32:40, :], in_=b_sp[0]), writes=[("gl", 4)], dma_key="k_gl4")

    def mk_identf():
        nc.gpsimd.memset(identf[:], 0.0)
        return nc.gpsimd.affine_select(out=identf[:], in_=identf[:], pattern=[[-1, 40]], compare_op=ALU.not_equal,
                                       fill=1.0, base=0, channel_multiplier=1)
    S.op(POOL, mk_identf, writes=["identf"])
    S.op(PE, lambda: nc.tensor.matmul(G[0][:, 0:40], lhsT=gl[0:40, :], rhs=identf[0:40, 0:40], start=True, stop=True),
         reads=[("gl", i) for i in range(5)] + ["identf"], writes=[GN[0]], cost=400)
    S.op(DVE, lambda: nc.vector.tensor_copy(out=gcol[:, :, :].rearrange("p g k -> p (g k)"), in_=G[0][:, 0:32]),
         reads=[GN[0]], writes=[("gcol", i) for i in range(4)], cost=200)
    S.op(DVE, lambda: nc.vector.tensor_copy(out=bspT[:, :], in_=G[0][:, 32:40]), reads=[GN[0]], writes=["bsp"], cost=200)
    if P0 & 2:
        small_load(gq_bc[:, :], bc_row(g_q), "gq")
        small_load(gk_bc[:, :], bc_row(g_k), "gk")
    small_load(km[:, :], kmask[:, :], "km")

    def mk_ident():
        nc.gpsimd.memset(ident[:], 0.0)
        return nc.gpsimd.affine_select(out=ident[:], in_=ident[:], pattern=[[-1, 128]], compare_op=ALU.not_equal,
                                       fill=1.0, base=0, channel_multiplier=1)
    S.op(POOL, mk_ident, writes=["ident"])
    S.op(POOL, lambda: nc.gpsimd.memset(negpi[:], -math.pi), writes=["negpi"])
    S.op(POOL, lambda: nc.gpsimd.memset(epsb[:], EPS), writes=["epsb"])
    S.op(POOL, lambda: nc.gpsimd.iota(out=cs[:, 0, 0:16].bitcast(I32), pattern=[[1, 16]], base=0, channel_multiplier=0),
         writes=["cs"])
    S.op(POOL, lambda: nc.gpsimd.tensor_copy(out=sn[:, 0, 0:16], in_=cs[:, 0, 0:16].bitcast(I32)), reads=["cs"], writes=["sn"])
    S.op(ACT, lambda: nc.scalar.activation(out=inv_bc[:, :], in_=sn[:, 0, 0:16], func=AF.Exp, scale=-math.log(10000.0) / 16.0),
         reads=["sn"], writes=["inv"])

    S.op(SP, lambda: nc.sync.dma_start(out=Xb[0][:, 0, :].rearrange("p (g q) -> p g q", g=8),
                                       in_=w_sp[0].rearrange("g p q -> p g q")), writes=[("X0", 0)], dma_key="const")
    S.op(DVE, lambda: nc.vector.tensor_copy(out=hb[:, 0, :], in_=Xb[0][:, 0, :]), reads=[("X0", 0)], writes=["hb"])
    for g in range(8):
        S.op(PE, lambda g=g: nc.tensor.transpose(out=tp[:, g * 128:(g + 1) * 128], in_=hb[:, 0, g * 128:(g + 1) * 128], identity=ident[:]),
             reads=["hb", "ident"], writes=[("tpb", 0)], cost=118)
    S.op(DVE, lambda: nc.vector.tensor_copy(out=wsT[:, :, :].rearrange("q g p -> q (g p)"), in_=tp[:, 0:1024]),
         reads=[("tpb", 0)], writes=["wsT"])

    def kcp(ap2d):
        return ap2d.rearrange("(kc p) c -> p kc c", p=128)

    def cast_ffn(idx, wgu, wd):
        for i in range(11):
            cast("gu%d_%d" % (idx, i), s_gu[idx][i, :, :, 0:256], kcp(wgu[0, :, 256 * i:256 * i + 256]))
            cast("gu%d_%d" % (idx, i), s_gu[idx][i, :, :, 256:512], kcp(wgu[0, :, DFF + 256 * i:DFF + 256 * i + 256]))
        for h in range(2):
            cast("dn%d_%d" % (idx, h), s_dn[idx][h], kcp(wd[0, :, 512 * h:512 * h + 512]))

    cast_ffn(0, w1gu, w1d)
    cast("kv", s_kv, kcp(w_in[0, :, 512:768]))
    for g in range(2):
        for c in range(4):
            cast("qgg0", s_qgg[0, :, :, c * 128 + g * 64:c * 128 + g * 64 + 64], kcp(w_in[0, :, (g * 4 + c) * 64:(g * 4 + c) * 64 + 64]))
    for i, c0 in ((1, 768), (2, 1280)):
        cast("qgg%d" % i, s_qgg[i], kcp(w_in[0, :, c0:c0 + 512]))
    for oc in range(8):
        k = "mg%d" % oc
        gts = s_mg[oc, :, 0:2048].rearrange("p (kc c) -> p kc c", kc=8)
        cast(k, gts[:, :, 0:128], kcp(w_in[0, :, 1792 + oc * 128:1792 + oc * 128 + 128]))
        cast(k, gts[:, :, 128:256], kcp(w_in[0, :, 2816 + oc * 128:2816 + oc * 128 + 128]))
        cast(k, s_mg[oc, :, 2048:2560].rearrange("p (kc c) -> p kc c", kc=4), kcp(w_bg[0, :, oc * 128:oc * 128 + 128]))
        cast(k, s_mg[oc, 0:64, 2560:3584].rearrange("p (h c) -> p h c", h=8),
             w_ba[0, :, oc * 128:oc * 128 + 128].rearrange("(h p) c -> p h c", p=64))
    for h in range(2):
        cast("wo%d" % h, s_wo[h], kcp(w_out[0, :, 512 * h:512 * h + 512]))
    cast_ffn(1, w2gu, w2d)
    for h in range(2):
        cast("pg%d" % h, s_pg[h], kcp(w_pg[0, :, 512 * h:512 * h + 512]))
    cast("pl", s_pl, kcp(w_pl[0, :, :]))

    ring_n = [0]

    def ring_load(key, src_ap, width):
        slot = ring_n[0] % NSLOT
        ring_n[0] += 1
        S.op(SP, lambda s=slot, a=src_ap, w=width: nc.sync.dma_start(out=ring[:, s, 0:w], in_=a),
             reads=[("scr", key)], writes=[("ring", slot)], dma_key="ring%d" % slot, cost=2000 + width * 256 // 180)
        return slot

    def transpose_T(src, srcres, nkc, outT, outres, gi=None):
        for k0 in range(0, nkc, 2):
            bank = (k0 // 2) % 2
            for kk in range(2):
                kc = k0 + kk
                for s in range(4):
                    col = bank * 1024 + kk * 512 + s * 128
                    S.op(PE, lambda kc=kc, s=s, col=col: nc.tensor.transpose(out=tp[:, col:col + 128],
                                                                            in_=src[:, s, kc * 128:(kc + 1) * 128], identity=ident[:]),
                         reads=[srcres, "ident"], writes=[("tpb", bank)], cost=118)
            for kk in range(2):
                kc = k0 + kk
                e = ACT if bank == 0 else DVE
                src_ps = tp[:, bank * 1024 + kk * 512: bank * 1024 + (kk + 1) * 512]
                if gi is None:
                    if e == ACT:
                        f = lambda kc=kc, src_ps=src_ps: nc.scalar.copy(out=outT[:, kc, :], in_=src_ps)
                    else:
                        f = lambda kc=kc, src_ps=src_ps: nc.vector.tensor_copy(out=outT[:, kc, :], in_=src_ps)
                    rd = [("tpb", bank)]
                else:
                    if e == ACT:
                        f = lambda kc=kc, src_ps=src_ps: nc.scalar.activation(out=outT[:, kc, :], in_=src_ps, func=AF.Copy,
                                                                               scale=gcol[:, gi, kc:kc + 1])
                    else:
                        f = lambda kc=kc, src_ps=src_ps: nc.vector.tensor_scalar(out=outT[:, kc, :], in0=src_ps,
                                                                                  scalar1=gcol[:, gi, kc:kc + 1], scalar2=None, op0=ALU.mult)
                    rd = [("tpb", bank), ("gcol", gi)]
                S.op(e, f, reads=rd, writes=[(outres, kc)])

    def row_rstd(X, xres, width, nrm, hbuf=None, hbres="hb", c0=0):
        hbuf = hb if hbuf is None else hbuf
        for s in range(4):
            S.op(ACT, lambda s=s: nc.scalar.activation(out=hbuf[:, s, 0:width], in_=X[:, s, 0:width], func=AF.Square, accum_out=ss[:, c0 + s:c0 + s + 1]),
                 reads=[(xres, s)], writes=[hbres, ("ss", c0 + s)], cost=1056 if width > 512 else 843)
        S.op(ACT, lambda: nc.scalar.activation(out=rstd[:, c0:c0 + 4], in_=ss[:, c0:c0 + 4], func=AF.Sqrt, scale=1.0 / nrm, bias=epsb[:, 0:1]),
             reads=[("ss", c0 + s) for s in range(4)] + ["epsb"], writes=[("rstd", c0)], cost=300)
        S.op(DVE, lambda: nc.vector.reciprocal(out=rstd[:, c0:c0 + 4], in_=rstd[:, c0:c0 + 4]), reads=[("rstd", c0)], writes=[("rstd", c0)], cost=190)

    def rmsnorm_T(X, xres, gi, alt=None):
        hbuf, hbres, outT, outres = (hb, "hb", hT, "hT") if alt is None else alt
        c0 = 0 if alt is None else 4
        row_rstd(X, xres, D, D, hbuf, hbres, c0)
        for s in range(4):
            S.op(DVE, lambda s=s: nc.vector.tensor_scalar(out=hbuf[:, s, :], in0=X[:, s, :], scalar1=rstd[:, c0 + s:c0 + s + 1], scalar2=None, op0=ALU.mult),
                 reads=[(xres, s), ("rstd", c0)], writes=[hbres])
        transpose_T(hbuf, hbres, 8, outT, outres, gi)

    def ffn(idx, X, xres, hid, dn, mid=None):
        for i in range(11):
            slot = ring_load("gu%d_%d" % (idx, i), s_gu[idx][i].rearrange("p kc c -> p (kc c)"), 4096)
            rv = ring[:, slot, :].rearrange("p (kc c) -> p kc c", kc=8)
            for jj in range(2):
                j = 2 * i + jj
                ga, gb = (0, 1) if j % 2 == 0 else (2, 3)
                for kc in range(8):
                    S.op(PE, lambda kc=kc, jj=jj, ga=ga, rv=rv: nc.tensor.matmul(G[ga], lhsT=rv[:, kc, jj * 128:(jj + 1) * 128], rhs=hT[:, kc, :],
                                                                                  start=(kc == 0), stop=(kc == 7)),
                         reads=[("ring", slot), ("hT", kc)], writes=[GN[ga]])
                for kc in range(8):
                    S.op(PE, lambda kc=kc, jj=jj, gb=gb, rv=rv: nc.tensor.matmul(G[gb], lhsT=rv[:, kc, 256 + jj * 128:256 + (jj + 1) * 128], rhs=hT[:, kc, :],
                                                                                  start=(kc == 0), stop=(kc == 7)),
                         reads=[("ring", slot), ("hT", kc)], writes=[GN[gb]])
                tt, tn = (tmpA, "tmpA") if j % 2 == 0 else (tmpB, "tmpB")
                S.op(ACT, lambda ga=ga, tt=tt: nc.scalar.activation(out=tt[:, :], in_=G[ga], func=AF.Silu), reads=[GN[ga]], writes=[tn])
                S.op(DVE, lambda j=j, gb=gb, tt=tt: nc.vector.tensor_tensor(out=hid[:, j, :], in0=tt[:, :], in1=G[gb], op=ALU.mult),
                     reads=[tn, GN[gb]], writes=[("hid", j)])
        if mid is not None:
            mid()
        for h in range(2):
            S.op(SP, lambda h=h: nc.sync.dma_start(out=dn[h], in_=s_dn[idx][h]),
                 reads=[("scr", "dn%d_%d" % (idx, h))], writes=[("dn", h)], dma_key="dn%d" % h, cost=18000)
            for s in range(4):
                b = 4 + (s % 2)
                for j in range(NJ):
                    S.op(PE, lambda j=j, s=s, h=h, b=b: nc.tensor.matmul(G[b], lhsT=hid[:, j, s * 128:(s + 1) * 128], rhs=dn[h][:, j, :],
                                                                         start=(j == 0), stop=(j == NJ - 1)),
                         reads=[("hid", j), ("dn", h)], writes=[GN[b]])
                S.op(DVE, lambda s=s, h=h, b=b: nc.vector.scalar_tensor_tensor(out=X[:, s, h * 512:(h + 1) * 512], in0=G[b], scalar=0.5,
                                                                               in1=X[:, s, h * 512:(h + 1) * 512], op0=ALU.mult, op1=ALU.add),
                     reads=[GN[b], (xres, s)], writes=[(xres, s)])

    def rope_tables(t, nh, tabc, tabs, tabres):
        S.op(SP, lambda: nc.sync.dma_start(out=posb[:, :, :], in_=pos[t * T:(t + 1) * T, :].rearrange("(s p) a -> p s a", p=128)),
             writes=["posb"], dma_key="posb")
        for a in range(2):
            S.op(DVE, lambda a=a: nc.vector.tensor_tensor(out=ang[:, :, a, :], in0=posb[:, :, a:a + 1].to_broadcast([128, 4, 16]),
                                                           in1=inv_bc[:, :].unsqueeze(1).to_broadcast([128, 4, 16]), op=ALU.mult),
                 reads=["posb", "inv"], writes=["ang"], cost=200)
        angf = ang[:, :, :, :].rearrange("p s a f -> p s (a f)")
        angmf = angm[:, :, :, :].rearrange("p s a f -> p s (a f)")
        TWO_PI = 2.0 * math.pi

        def sin_of(dst, shift):
            S.op(DVE, lambda: nc.vector.tensor_scalar(out=angmf, in0=angf, scalar1=shift, scalar2=1.0 / TWO_PI, op0=ALU.add, op1=ALU.mult),
                 reads=["ang"], writes=["angm"])
            S.op(DVE, lambda: nc.vector.tensor_copy(out=angki[:, :, :], in_=angmf), reads=["angm"], writes=["angki"])
            S.op(DVE, lambda: nc.vector.tensor_copy(out=angkf[:, :, :], in_=angki[:, :, :]), reads=["angki"], writes=["angkf"])
            S.op(DVE, lambda: nc.vector.tensor_scalar(out=angmf, in0=angf, scalar1=shift, scalar2=None, op0=ALU.add),
                 reads=["ang", "angki"], writes=["angm"])
            S.op(DVE, lambda: nc.vector.scalar_tensor_tensor(out=angr[:, :, :], in0=angkf[:, :, :], scalar=-TWO_PI, in1=angmf, op0=ALU.mult, op1=ALU.add),
                 reads=["angkf", "angm"], writes=["angr"])
            S.op(DVE, lambda: nc.vector.tensor_scalar(out=angmf, in0=angr[:, :, :], scalar1=math.pi, scalar2=TWO_PI, op0=ALU.is_gt, op1=ALU.mult),
                 reads=["angr"], writes=["angm"])
            S.op(DVE, lambda: nc.vector.tensor_tensor(out=angr[:, :, :], in0=angr[:, :, :], in1=angmf, op=ALU.subtract),
                 reads=["angr", "angm"], writes=["angr"])
            S.op(ACT, lambda: nc.scalar.activation(out=dst[:, :, :], in_=angr[:, :, :], func=AF.Sin), reads=["angr"], writes=[("cs" if dst is cs else "sn")])

        sin_of(sn, 0.0)
        sin_of(cs, 0.5 * math.pi)
        S.op(POOL, lambda: nc.gpsimd.tensor_copy(out=tabc, in_=cs[:, :, :].unsqueeze(2).to_broadcast([128, 4, nh, 32])),
             reads=["cs"], writes=[tabres + "c"])
        S.op(POOL, lambda: nc.gpsimd.tensor_copy(out=tabs, in_=sn[:, :, :].unsqueeze(2).to_broadcast([128, 4, nh, 32])),
             reads=["sn"], writes=[tabres + "s"])

    def head_norm_rope(e, src, srcres, nh, gbc, gres, tabc, tabs, tabres, sq, sqres, dst, dstres):
        V = vec(e)
        SH = 4 * nh
        cb = 250 + SH * 64 * (0.9 if e == POOL else 0.55)
        ch = 250 + SH * 32 * (0.9 if e == POOL else 0.55)
        x3 = src.rearrange("p s (h d) -> p (s h) d", h=nh)
        sq3 = sq.rearrange("p s (h d) -> p (s h) d", h=nh)
        S.op(e, lambda: V.tensor_tensor(out=sq3, in0=x3, in1=x3, op=ALU.mult), reads=[srcres], writes=[sqres], cost=cb)
        S.op(DVE, lambda: nc.vector.tensor_reduce(out=hss[:, 0:SH], in_=sq3, axis=AX.X, op=ALU.add), reads=[sqres], writes=["hss"], cost=250 + SH * 64 * 0.55)
        S.op(ACT, lambda: nc.scalar.activation(out=hrs[:, 0:SH], in_=hss[:, 0:SH], func=AF.Sqrt, scale=1.0 / 64, bias=epsb[:, 0:1]),
             reads=["hss", "epsb"], writes=["hrs"], cost=300)
        S.op(DVE, lambda: nc.vector.reciprocal(out=hrs[:, 0:SH], in_=hrs[:, 0:SH]), reads=["hrs"], writes=["hrs"], cost=190)
        S.op(e, lambda: V.tensor_tensor(out=x3, in0=x3, in1=hrs[:, 0:SH].unsqueeze(2).to_broadcast([128, SH, 64]), op=ALU.mult),
             reads=[srcres, "hrs"], writes=[srcres], cost=cb)
        S.op(e, lambda: V.tensor_tensor(out=x3, in0=x3, in1=gbc[:, :].unsqueeze(1).to_broadcast([128, SH, 64]), op=ALU.mult),
             reads=[srcres, gres], writes=[srcres], cost=cb)
        pat = "p s (h a r f) -> p (s h) a r f"
        x5 = src.rearrange(pat, h=nh, a=2, r=2)
        q5 = sq.rearrange(pat, h=nh, a=2, r=2)
        d5 = dst.rearrange(pat, h=nh, a=2, r=2)
        xa, xb_ = x5[:, :, :, 0, :], x5[:, :, :, 1, :]
        ta, tb_ = q5[:, :, :, 0, :], q5[:, :, :, 1, :]
        c4 = tabc.rearrange("p s h (a f) -> p (s h) a f", a=2)
        s4 = tabs.rearrange("p s h (a f) -> p (s h) a f", a=2)
        oa, ob = d5[:, :, :, 0, :], d5[:, :, :, 1, :]
        S.op(e, lambda: V.tensor_tensor(out=ta, in0=xb_, in1=s4, op=ALU.mult), reads=[srcres, tabres + "s"], writes=[sqres], cost=ch)
        S.op(e, lambda: V.tensor_tensor(out=tb_, in0=xa, in1=s4, op=ALU.mult), reads=[srcres, tabres + "s"], writes=[sqres], cost=ch)
        S.op(e, lambda: V.tensor_tensor(out=xa, in0=xa, in1=c4, op=ALU.mult), reads=[srcres, sqres, tabres + "c"], writes=[srcres], cost=ch)
        S.op(e, lambda: V.tensor_tensor(out=xb_, in0=xb_, in1=c4, op=ALU.mult), reads=[srcres, sqres, tabres + "c"], writes=[srcres], cost=ch)
        S.op(e, lambda: V.tensor_tensor(out=oa, in0=xa, in1=ta, op=ALU.subtract), reads=[srcres, sqres], writes=[dstres], cost=ch)
        S.op(e, lambda: V.tensor_tensor(out=ob, in0=xb_, in1=tb_, op=ALU.add), reads=[srcres, sqres], writes=[dstres], cost=ch)

    def XR(xres):
        return [(xres, s) for s in range(4)]

    def tok_rows(ap2d):
        return ap2d.rearrange("(s p) d -> p s d", p=128)

    dbg = {}

    apos[0] = 0
    hid = carve([128, NJ, T], BF16)
    dn = [carve([128, NJ, 512], BF16), carve([128, NJ, 512], BF16)]
    kvw = carve([128, 8, 256], BF16)
    kvs = carve([128, 2, 4, 128], F32)
    ksq = carve([128, 4, 128], F32)
    tkc = carve([128, 4, 2, 32], F32)
    tks = carve([128, 4, 2, 32], F32)
    krb = carve([128, 4, 128], BF16)
    kTb = [carve([128, T], BF16), carve([128, T], BF16)]
    vsb = [carve([128, 4, 130], BF16), carve([128, 4, 130], BF16)]
    hb2 = carve([128, 4, D], BF16)
    hT2 = carve([128, 8, T], BF16)
    ALT = (hb2, "hb2", hT2, "hT2")

    S.op(SP, lambda: nc.sync.dma_start(out=kvw, in_=s_kv), reads=[("scr", "kv")], writes=["kvw"], dma_key="kvw")

    for t in range(NCT if stage >= 1 else 0):
        b = t % 2
        X, xres = Xb[b], "X%d" % b
        def prep1(tt):
            bb = tt % 2
            S.op(SP, lambda bb=bb, tt=tt: nc.sync.dma_start(out=Xb[bb][:, :, :], in_=tok_rows(xc[tt * T:(tt + 1) * T, :])),
                 writes=XR("X%d" % bb), dma_key="xl%d" % bb, cost=13600)
            rmsnorm_T(Xb[bb], "X%d" % bb, 0)
        if t == 0:
            prep1(0)
        ffn(0, X, xres, hid, dn, mid=(lambda t=t: prep1(t + 1)) if t + 1 < NCT else None)
        if t < NOT:
            S.op(SP, lambda b=b, t=t: nc.sync.dma_start(out=tok_rows(x1s[t * T:(t + 1) * T, :]), in_=Xb[b][:, :, :]),
                 reads=XR(xres), writes=[("x1s", t, s) for s in range(4)], dma_key="xs%d" % b, cost=13600)
        if P1 < 3:
            continue
        rmsnorm_T(X, xres, 1, ALT)
        if P1 < 4:
            continue
        rope_tables(t, 2, tkc, tks, "tk")
        if P1 < 5:
            continue
        for s in range(4):
            gb_ = s % 2
            for kc in range(8):
                S.op(PE, lambda kc=kc, s=s, gb_=gb_: nc.tensor.matmul(G[gb_][:, 0:256], lhsT=hT2[:, kc, s * 128:(s + 1) * 128], rhs=kvw[:, kc, :],
                                                                       start=(kc == 0), stop=(kc == 7)),
                     reads=[("hT2", kc), "kvw"], writes=[GN[gb_]], cost=200)
            S.op(ACT, lambda s=s, gb_=gb_: nc.scalar.copy(out=kvs[:, :, s, :], in_=G[gb_][:, 0:256].rearrange("p (a d) -> p a d", a=2)), reads=[GN[gb_]], writes=["kvs"])
        if P1 < 6:
            continue
        vb, vres = vsb[b], "vsb%d" % b
        kmt = km[:, t * 4:(t + 1) * 4]
        vb4 = vb.rearrange("p s (h e) -> p s h e", h=2)
        S.op(POOL, lambda vb4=vb4, kmt=kmt: nc.gpsimd.tensor_tensor(
            out=vb4[:, :, :, 0:64], in0=kvs[:, 1, :, :].rearrange("p s (h d) -> p s h d", h=2),
            in1=kmt.unsqueeze(2).unsqueeze(3).to_broadcast([128, 4, 2, 64]), op=ALU.mult),
            reads=["kvs", "km"], writes=[vres])
        S.op(POOL, lambda vb4=vb4, kmt=kmt: nc.gpsimd.tensor_copy(out=vb4[:, :, :, 64], in_=kmt.unsqueeze(2).to_broadcast([128, 4, 2])),
             reads=["km"], writes=[vres])
        S.op(SP, lambda vb=vb, t=t: nc.sync.dma_start(out=vss[t * T:(t + 1) * T, :].rearrange("(s p) e -> p s e", p=128), in_=vb),
             reads=[vres], writes=[("vss", t)], dma_key="vst%d" % b)
        if P1 < 7:
            continue
        head_norm_rope(POOL, kvs[:, 0, :, :], "kvs", 2, gk_bc, "gk", tkc, tks, "tk", ksq, "ksq", krb, "krb")
        for s in range(4):
            S.op(PE, lambda s=s: nc.tensor.transpose(out=tp[:, s * 128:(s + 1) * 128], in_=krb[:, s, :], identity=ident[:]),
                 reads=["krb", "ident"], writes=[("tpb", 0)], cost=118)
        kb_, kres = kTb[b], "kTb%d" % b
        S.op(DVE, lambda kb_=kb_: nc.vector.tensor_copy(out=kb_, in_=tp[:, 0:512]), reads=[("tpb", 0)], writes=[kres])
        S.op(SP, lambda kb_=kb_, t=t: nc.sync.dma_start(out=kts[:, t * T:(t + 1) * T], in_=kb_),
             reads=[kres], writes=[("kts", t)], dma_key="kst%d" % b)

    S.barrier()
    apos[0] = 0
    kT = carve([128, NCT * T], BF16)
    vAf = carve([128, NCH * 130 + 64], BF16)
    vA = vAf[:, 0:NCH * 130].rearrange("p (ch e) -> p ch e", e=130)
    qf = carve([128, 4, 512], F32)
    qsq = hb[:, :, :].rearrange("p s d -> p (s d)").bitcast(F32).rearrange("p (s d) -> p s d", s=4)
    tqc = carve([128, 4, 8, 32], F32)
    tqs = carve([128, 4, 8, 32], F32)
    qrb = carve([128, 4, 512], BF16)
    qTp = Xb[1][:, 0:2, :].rearrange("p a d -> p (a d)").bitcast(BF16).rearrange("p (h t) -> p h t", h=8)
    ub = carve([128, 4, 512], BF16)
    vnb = tqc.rearrange("p s h f -> p (s h f)").bitcast(BF16)[:, 0:2048].rearrange("p (s d) -> p s d", s=4)
    sgb = tqs.rearrange("p s h f -> p (s h f)").bitcast(BF16)[:, 0:2048].rearrange("p (s d) -> p s d", s=4)
    sgT = carve([128, 4, T], BF16)
    aT = carve([128, 8, T], BF16)
    mT = carve([128, 8, T], BF16)
    PT = [carve([128, 1024], BF16), carve([128, 1024], BF16), carve([128, 1024], BF16)]
    rden = carve([128, T], F32)
    ones1 = carve([128, 64], F32)
    onT = carve([128, T], F32)
    ggv_bc = carve([128, 512], F32)
    S.op(POOL, lambda: nc.gpsimd.memset(ones1, 1.0), writes=["ones1"])
    S.op(POOL, lambda: nc.gpsimd.memset(qTp, 0.0), writes=[("qT", c) for c in range(4)])
    S.op(POOL, lambda: nc.gpsimd.memset(vAf[:, NCH * 130:NCH * 130 + 64], 0.0), writes=["vApad"])
    small_load(ggv_bc, bc_row(g_gv), "ggv")

    for c in range(0, NCT if stage >= 2 else 0, 8):
        n = min(8, NCT - c)
        S.op(SP, lambda c=c, n=n: nc.sync.dma_start(out=kT[:, c * T:(c + n) * T], in_=kts[:, c * T:(c + n) * T]),
             reads=[("kts", t) for t in range(c, c + n)], writes=["kT"], dma_key="kTl", cost=9000)
        S.op(SP, lambda c=c, n=n: nc.sync.dma_start(out=vA[:, c * 4:(c + n) * 4, :],
                                                    in_=vss[c * T:(c + n) * T, :].rearrange("(ch p) e -> p ch e", p=128)),
             reads=[("vss", t) for t in range(c, c + n)], writes=["vA"], dma_key="vAl", cost=15000)

    def attention():
        heads = [(c, g) for c in range(4) for g in range(2)]
        seq = [(hi, i) for hi in range(8) for i in range(NPAIR)]
        Sv = [S0[:, :], S1[:, :], tpf]
        Sres = [[GN[0], GN[1]], [GN[2], GN[3]], [("tpb", 0), ("tpb", 1)]]

        def qk(n):
            hi, i = seq[n]
            c, g = heads[hi]
            sb_ = n % 3
            Sx = Sv[sb_]
            for u in range(2):
                ch = 2 * i + u
                S.op(PE, lambda u=u, ch=ch, c=c, g=g, Sx=Sx: nc.tensor.matmul(
                    Sx[:, u * 512:(u + 1) * 512], lhsT=kT[:, ch * 128:(ch + 1) * 128],
                    rhs=qTp[:, c * 2 + g, :], start=True, stop=True),
                    reads=["kT", ("qT", c)], writes=[Sres[sb_][u]])

        qk(0)
        qk(1)
        for n in range(len(seq)):
            hi, i = seq[n]
            c, g = heads[hi]
            sb_ = n % 3
            Sx = Sv[sb_]
            ob = hi % 2
            Ox = (O0, O1)[ob]
            if n + 2 < len(seq):
                qk(n + 2)
            S.op(ACT, lambda Sx=Sx, sb_=sb_: nc.scalar.activation(out=PT[sb_], in_=Sx, func=AF.Exp, scale=0.125),
                 reads=Sres[sb_], writes=["PT%d" % sb_], cost=1023)
            for u in range(2):
                ch = 2 * i + u
                S.op(PE, lambda u=u, ch=ch, g=g, Ox=Ox, sb_=sb_, i=i: nc.tensor.matmul(
                    Ox[:, :], lhsT=vAf[:, ch * 130 + g * 65:ch * 130 + g * 65 + 128], rhs=PT[sb_][:, u * 512:(u + 1) * 512],
                    start=(i == 0 and u == 0), stop=(i == NPAIR - 1 and u == 1)),
                    reads=["vA", "vApad", "PT%d" % sb_], writes=[GN[4 + ob]])
            if i == NPAIR - 1:
                h_true = g * 4 + c
                S.op(DVE, lambda Ox=Ox: nc.vector.reciprocal(out=rden[64:65, :], in_=Ox[64:65, :]), reads=[GN[4 + ob]], writes=["rden"], cost=2472)
                S.op(PE, lambda Sx=Sx: nc.tensor.matmul(Sx[0:64, 0:512], lhsT=ones1[64:65, 0:64], rhs=rden[64:65, :], start=True, stop=True),
                     reads=["rden", "ones1"], writes=[Sres[sb_][0]], cost=970)
                S.op(ACT, lambda Sx=Sx: nc.scalar.copy(out=onT[0:64, :], in_=Sx[0:64, 0:512]), reads=[Sres[sb_][0]], writes=["onT"])
                S.op(DVE, lambda Ox=Ox, h_true=h_true: nc.vector.tensor_tensor(out=aT[0:64, h_true, :], in0=Ox[0:64, :], in1=onT[0:64, :], op=ALU.mult),
                     reads=[GN[4 + ob], "onT"], writes=[("aT", h_true)])

    for t in range(NOT if stage >= 2 else 0):
        b = 0
        X, xres = Xb[b], "X%d" % b
        for s in range(4):
            S.op(SP, lambda s=s, t=t: nc.sync.dma_start(out=Xb[0][:, s, :], in_=x1s[t * T + s * 128:t * T + (s + 1) * 128, :]),
                 reads=[("x1s", t, s)], writes=[(xres, s)], dma_key="xa%d" % s, cost=5000)
        rmsnorm_T(X, xres, 1)
        rope_tables(t, 8, tqc, tqs, "tq")
        for pi in range(3):
            slot = ring_load("qgg%d" % pi, s_qgg[pi].rearrange("p kc c -> p (kc c)"), 4096)
            rv = ring[:, slot, :].rearrange("p (kc c) -> p kc c", kc=8)
            for s in range(4):
                gb_ = s % 2
                for kc in range(8):
                    S.op(PE, lambda kc=kc, s=s, gb_=gb_, rv=rv: nc.tensor.matmul(G[gb_], lhsT=hT[:, kc, s * 128:(s + 1) * 128], rhs=rv[:, kc, :],
                                                                                  start=(kc == 0), stop=(kc == 7)),
                         reads=[("hT", kc), ("ring", slot)], writes=[GN[gb_]])
                if pi == 0:
                    S.op(ACT, lambda s=s, gb_=gb_: nc.scalar.copy(out=qf[:, s, :], in_=G[gb_]), reads=[GN[gb_]], writes=["qf"])
                elif pi == 1:
                    S.op(ACT, lambda s=s, gb_=gb_: nc.scalar.activation(out=ub[:, s, :], in_=G[gb_], func=AF.Gelu), reads=[GN[gb_]], writes=["ub"])
                else:
                    S.op(ACT, lambda s=s, gb_=gb_: nc.scalar.activation(out=qf[:, s, :], in_=G[gb_], func=AF.Gelu), reads=[GN[gb_]], writes=["qf"])
            if pi == 0:
                head_norm_rope(DVE, qf, "qf", 8, gq_bc, "gq", tqc, tqs, "tq", qsq, "hb", qrb, "qrb")
                for c0 in range(0, 4, 2):
                    bank = (c0 // 2) % 2
                    for kk in range(2):
                        c = c0 + kk
                        for s in range(4):
                            col = bank * 1024 + kk * 512 + s * 128
                            S.op(PE, lambda c=c, s=s, col=col: nc.tensor.transpose(out=tp[:, col:col + 128], in_=qrb[:, s, c * 128:(c + 1) * 128],
                                                                                    identity=ident[:]),
                                 reads=["qrb", "ident"], writes=[("tpb", bank)], cost=118)
                    for kk in range(2):
                        c = c0 + kk
                        for g in range(2):
                            src_ps = tp[g * 64:(g + 1) * 64, bank * 1024 + kk * 512: bank * 1024 + (kk + 1) * 512]
                            dst = qTp[g * 64:(g + 1) * 64, c * 2 + g, :]
                            if bank == 0:
                                S.op(ACT, lambda src_ps=src_ps, dst=dst: nc.scalar.copy(out=dst, in_=src_ps), reads=[("tpb", bank)], writes=[("qT", c)])
                            else:
                                S.op(DVE, lambda src_ps=src_ps, dst=dst: nc.vector.tensor_copy(out=dst, in_=src_ps), reads=[("tpb", bank)], writes=[("qT", c)])
            if pi == 2:
                row_rstd(qf, "qf", 512, 512)
                for s in range(4):
                    S.op(DVE, lambda s=s: nc.vector.scalar_tensor_tensor(out=vnb[:, s, :], in0=qf[:, s, :], scalar=rstd[:, s:s + 1], in1=ggv_bc,
                                                                         op0=ALU.mult, op1=ALU.mult),
                         reads=["qf", ("rstd", 0), "ggv"], writes=["tqc"])
                for s in range(4):
                    gb_ = 2 + (s % 2)
                    for g in range(8):
                        S.op(PE, lambda s=s, g=g, gb_=gb_: nc.tensor.matmul(G[gb_][:, g * 64:(g + 1) * 64], lhsT=wsT[:, g, :], rhs=vnb[:, s, g * 64:(g + 1) * 64],
                                                                             start=True, stop=True),
                             reads=["tqc", "wsT"], writes=[GN[gb_]], cost=70)
                    tt, tn = (tmpA, "tmpA") if s % 2 == 0 else (tmpB, "tmpB")
                    S.op(DVE, lambda gb_=gb_, tt=tt: nc.vector.tensor_tensor(out=tt.rearrange("p (g c) -> p g c", g=8),
                                                                             in0=G[gb_].rearrange("p (g c) -> p g c", g=8),
                                                                             in1=bspT[:, :].unsqueeze(2).to_broadcast([128, 8, 64]), op=ALU.add),
                         reads=[GN[gb_], "bsp"], writes=[tn])
                    S.op(POOL, lambda s=s, tt=tt: nc.gpsimd.tensor_tensor(out=sgb[:, s, :], in0=tt, in1=ub[:, s, :], op=ALU.mult),
                         reads=[tn, "ub"], writes=["tqs"])
                transpose_T(sgb, "tqs", 4, sgT, "sgT")
        attention()
        for oc in range(8):
            slot = ring_load("mg%d" % oc, s_mg[oc], 3584)
            rg = ring[:, slot, 0:2048].rearrange("p (kc c) -> p kc c", kc=8)
            g0, g1 = (0, 1) if oc % 2 == 0 else (2, 3)
            for kc in range(8):
                S.op(PE, lambda kc=kc, rg=rg, g0=g0: nc.tensor.matmul(G[g0], lhsT=rg[:, kc, 0:128], rhs=hT[:, kc, :], start=(kc == 0), stop=(kc == 7)),
                     reads=[("ring", slot), ("hT", kc)], writes=[GN[g0]])
            for kc in range(8):
                S.op(PE, lambda kc=kc, rg=rg, g1=g1: nc.tensor.matmul(G[g1], lhsT=rg[:, kc, 128:256], rhs=hT[:, kc, :], start=(kc == 0), stop=(kc == 7)),
                     reads=[("ring", slot), ("hT", kc)], writes=[GN[g1]])
            for h in range(8):
                S.op(PE, lambda h=h, slot=slot: nc.tensor.matmul(G[4], lhsT=ring[0:64, slot, 2560 + h * 128:2560 + (h + 1) * 128], rhs=aT[0:64, h, :],
                                                                 start=(h == 0), stop=(h == 7)),
                     reads=[("ring", slot), ("aT", h)], writes=[GN[4]])
            for kc in range(4):
                S.op(PE, lambda kc=kc, slot=slot: nc.tensor.matmul(G[5], lhsT=ring[:, slot, 2048 + kc * 128:2048 + (kc + 1) * 128], rhs=sgT[:, kc, :],
                                                                   start=(kc == 0), stop=(kc == 3)),
                     reads=[("ring", slot), ("sgT", kc)], writes=[GN[5]])
            S.op(ACT, lambda g0=g0: nc.scalar.activation(out=tmpA, in_=G[g0], func=AF.Sigmoid), reads=[GN[g0]], writes=["tmpA"])
            S.op(ACT, lambda g1=g1: nc.scalar.activation(out=tmpB, in_=G[g1], func=AF.Sigmoid), reads=[GN[g1]], writes=["tmpB"])
            S.op(DVE, lambda: nc.vector.tensor_tensor(out=tmpA, in0=tmpA, in1=G[4], op=ALU.mult), reads=["tmpA", GN[4]], writes=["tmpA"])
            S.op(DVE, lambda: nc.vector.tensor_tensor(out=tmpB, in0=tmpB, in1=G[5], op=ALU.mult), reads=["tmpB", GN[5]], writes=["tmpB"])
            S.op(POOL, lambda oc=oc: nc.gpsimd.tensor_tensor(out=mT[:, oc, :], in0=tmpA, in1=tmpB, op=ALU.add),
                 reads=["tmpA", "tmpB"], writes=[("mT", oc)])
        wo_slots = [ring_load("wo%d" % h, s_wo[h].rearrange("p kc c -> p (kc c)"), 4096) for h in range(2)]
        for s in range(4):
            for h in range(2):
                slot = wo_slots[h]
                rv = ring[:, slot, :].rearrange("p (kc c) -> p kc c", kc=8)
                gb_ = h
                for kc in range(8):
                    S.op(PE, lambda kc=kc, s=s, gb_=gb_, rv=rv: nc.tensor.matmul(G[gb_], lhsT=mT[:, kc, s * 128:(s + 1) * 128], rhs=rv[:, kc, :],
                                                                                  start=(kc == 0), stop=(kc == 7)),
                         reads=[("mT", kc), ("ring", slot)], writes=[GN[gb_]])
                S.op(DVE, lambda s=s, h=h, gb_=gb_, X=X: nc.vector.tensor_tensor(out=X[:, s, h * 512:(h + 1) * 512], in0=G[gb_],
                                                                                  in1=X[:, s, h * 512:(h + 1) * 512], op=ALU.add),
                     reads=[GN[gb_], (xres, s)], writes=[(xres, s)])
            S.op(SP, lambda s=s, t=t: nc.sync.dma_start(out=x1s[t * T + s * 128:t * T + (s + 1) * 128, :], in_=Xb[0][:, s, :]),
                 reads=[(xres, s)], writes=[("x1s", t, s)], dma_key="xb%d" % s, cost=5000)

    S.barrier()
    apos[0] = 0
    hid = carve([128, NJ, T], BF16)
    dn = [carve([128, NJ, 512], BF16), carve([128, NJ, 512], BF16)]
    pf = carve([128, 4, PLE], F32)
    pbf = carve([128, 4, PLE], BF16)
    pT = carve([128, 2, T], BF16)
    gfin_bc = carve([128, D], F32)
    hb3 = carve([128, 4, D], BF16)
    hT3 = carve([128, 8, T], BF16)
    ALT = (hb3, "hb3", hT3, "hT3")
    small_load(gfin_bc, bc_row(g_fin), "gfin")
    out_ops = []
    for t in range(NOT if stage >= 3 else 0):
        b = t % 2
        X, xres = Xb[b], "X%d" % b
        def prep2(tt):
            bb = tt % 2
            S.op(SP, lambda bb=bb, tt=tt: nc.sync.dma_start(out=Xb[bb][:, :, :], in_=tok_rows(x1s[tt * T:(tt + 1) * T, :])),
                 reads=[("x1s", tt, s) for s in range(4)], writes=XR("X%d" % bb), dma_key="xl%d" % bb, cost=13600)
            rmsnorm_T(Xb[bb], "X%d" % bb, 2)
        if t == 0:
            prep2(0)
        ffn(1, X, xres, hid, dn, mid=(lambda t=t: prep2(t + 1)) if t + 1 < NOT else None)
        rmsnorm_T(X, xres, 3, ALT)
        S.op(SP, lambda t=t: nc.sync.dma_start(out=pf, in_=tok_rows(pin[t * T:(t + 1) * T, :])), writes=["pf"], dma_key="pfl")
        S.op(POOL, lambda: nc.gpsimd.tensor_copy(out=pbf, in_=pf), reads=["pf"], writes=["pbf"])
        transpose_T(pbf, "pbf", 2, pT, "pT")
        slotp = ring_load("pl", s_pl.rearrange("p kc c -> p (kc c)"), 2048)
        rvp = ring[:, slotp, 0:2048].rearrange("p (kc c) -> p kc c", kc=2)
        for h in range(2):
            slot = ring_load("pg%d" % h, s_pg[h].rearrange("p kc c -> p (kc c)"), 4096)
            rv = ring[:, slot, :].rearrange("p (kc c) -> p kc c", kc=8)
            for s in range(4):
                g0, g1 = (0, 1) if s % 2 == 0 else (2, 3)
                for kc in range(8):
                    S.op(PE, lambda kc=kc, s=s, g0=g0, rv=rv: nc.tensor.matmul(G[g0], lhsT=hT3[:, kc, s * 128:(s + 1) * 128], rhs=rv[:, kc, :],
                                                                                start=(kc == 0), stop=(kc == 7)),
                         reads=[("hT3", kc), ("ring", slot)], writes=[GN[g0]])
                for k2 in range(2):
                    S.op(PE, lambda k2=k2, s=s, g1=g1, h=h, rvp=rvp: nc.tensor.matmul(G[g1], lhsT=pT[:, k2, s * 128:(s + 1) * 128],
                                                                                       rhs=rvp[:, k2, h * 512:(h + 1) * 512],
                                                                                       start=(k2 == 0), stop=(k2 == 1)),
                         reads=[("pT", k2), ("ring", slotp)], writes=[GN[g1]])
                tt, tn = (tmpA, "tmpA") if s % 2 == 0 else (tmpB, "tmpB")
                S.op(ACT, lambda g0=g0, tt=tt: nc.scalar.activation(out=tt, in_=G[g0], func=AF.Sigmoid), reads=[GN[g0]], writes=[tn])
                S.op(DVE, lambda g1=g1, tt=tt: nc.vector.tensor_tensor(out=tt, in0=tt, in1=G[g1], op=ALU.mult), reads=[tn, GN[g1]], writes=[tn])
                S.op(POOL, lambda s=s, h=h, tt=tt, X=X: nc.gpsimd.tensor_tensor(out=X[:, s, h * 512:(h + 1) * 512], in0=X[:, s, h * 512:(h + 1) * 512],
                                                                                 in1=tt, op=ALU.add),
                     reads=[tn, (xres, s)], writes=[(xres, s)])
        row_rstd(X, xres, D, D)
        for s in range(4):
            S.op(DVE, lambda s=s, X=X: nc.vector.scalar_tensor_tensor(out=X[:, s, :], in0=X[:, s, :], scalar=rstd[:, s:s + 1], in1=gfin_bc,
                                                                      op0=ALU.mult, op1=ALU.mult),
                 reads=[(xres, s), ("rstd", 0), "gfin"], writes=[(xres, s)])
        out_ops.append(S.op(SP, lambda b=b, t=t: nc.sync.dma_start(out=tok_rows(y[t * T:(t + 1) * T, :]), in_=Xb[b][:, :, :]),
                            reads=XR(xres), dma_key="ys%d" % b, cost=13600))
    if stage < 3 and stage >= 1:
        out_ops.append(S.op(SP, lambda: nc.sync.dma_start(out=y[:, :], in_=x1s[:, :]), reads=[("x1s", t, s) for t in range(NOT) for s in range(4)], dma_key="dbg"))
    S.op(SP, None, extra=out_ops)

    if SCHEDULE:
        S.schedule()
        build_program.last_est_ns = S.est_ns
    semkeys = S.finalize()
    by_eng = {e: [o for o in S.ops if o.eng == e] for e in ENGS}
    with ExitStack() as es:
        sems = {k: es.enter_context(nc.semaphore("s%d" % i)) for i, k in enumerate(semkeys)}
        block = es.enter_context(nc.Block())

        def emit(engname, eng):
            waited = {}
            for o in by_eng[engname]:
                need = {}
                for d in o.deps:
                    if not d.signal or d.fn is None:
                        continue
                    if d.eng == PE and o.eng == PE and d.dma_key is None and o.dma_key is None:
                        continue
                    if need.get(d.sem, 0) < d.val:
                        need[d.sem] = d.val
                for k, v in need.items():
                    if waited.get(k, 0) < v:
                        eng.wait_ge(sems[k], v)
                        waited[k] = v
                if o.fn is not None:
                    ins = o.fn()
                    if o.signal:
                        ins.then_inc(sems[o.sem], 16 if o.dma_key is not None else 1)
                else:
                    assert not o.signal

        @block.tensor
        def _(e):
            emit(PE, e)

        @block.scalar
        def _(e):
            emit(ACT, e)

        @block.vector
        def _(e):
            emit(DVE, e)

        @block.gpsimd
        def _(e):
            emit(POOL, e)

        @block.sync
        def _(e):
            emit(SP, e)
    return nc


NCT_FULL, NOT_FULL = 32, 8
WNAMES = ["g_ffn1", "w_ffn1_gu", "w_ffn1_down", "g_mix", "w_in", "g_q", "g_k", "g_gmlp_v", "w_spatial", "b_spatial",
          "w_branch_attn", "w_branch_gmlp", "w_out", "g_ffn2", "w_ffn2_gu", "w_ffn2_down", "g_ple", "w_ple_gate", "w_ple", "g_final"]


def _pos_table(tok_idx):
    tok_idx = np.asarray(tok_idx, np.int64)
    return np.stack([tok_idx // 64, tok_idx % 64], axis=1).astype(np.float32)


def kernel(**inputs):
    xp = np.asarray(inputs["x_prompt"], np.float32)
    xs = np.asarray(inputs["x_sample"], np.float32)
    pp = np.asarray(inputs["p_prompt"], np.float32)
    ps = np.asarray(inputs["p_sample"], np.float32)
    w = {k: np.ascontiguousarray(np.asarray(inputs[k], np.float32)) for k in WNAMES}
    w["g_final"] = w["g_final"].reshape(1, D)
    NTOK = NCT_FULL * T
    own = NOT_FULL * T
    in_maps = []
    for c in range(8):
        if c < 4:
            order = [c] + [(c + k) % 4 for k in range(1, 4)]
            xcx = np.concatenate([xp[o] for o in order], axis=0)
            posi = np.concatenate([np.arange(own)] * 4)
            msk = np.zeros(NTOK, np.float32); msk[:own] = 1.0
            pc = pp[0, c]
        else:
            q = c - 4
            order = [q] + [(q + k) % 4 for k in range(1, 4)]
            xcx = np.concatenate([xs[0, o * own:(o + 1) * own] for o in order], axis=0)
            posi = np.concatenate([np.arange(o * own, (o + 1) * own) for o in order])
            msk = np.ones(NTOK, np.float32)
            pc = ps[0, 0, q * own:(q + 1) * own]
        m = {"xc": np.ascontiguousarray(xcx), "pin": np.ascontiguousarray(pc), "pos": _pos_table(posi),
             "kmask": np.ascontiguousarray(msk.reshape(NTOK // 128, 128).T)}
        m.update(w)
        in_maps.append(m)
    nc = build_program(NCT_FULL, NOT_FULL)
    res = run_bass_kernel_spmd(nc, in_maps, core_ids=list(range(8)))
    ys = [np.asarray(res.results[c]["y"], np.float32) for c in range(8)]
    y_prompt = np.stack(ys[0:4], axis=0)
    y_sample = np.concatenate(ys[4:8], axis=0)[None]
    return (y_prompt, y_sample)
```

```python
import math
SCHEDULE = True
P0 = 255
P1 = 99
P1SUB = 99
P1T = 0
from contextlib import ExitStack
import numpy as np
import concourse.bass as bass
import concourse.mybir as mybir
from concourse.bass_utils import run_bass_kernel_spmd

F32 = mybir.dt.float32
BF16 = mybir.dt.bfloat16
I32 = mybir.dt.int32
AF = mybir.ActivationFunctionType
ALU = mybir.AluOpType
AX = mybir.AxisListType

D = 1024
DFF = 2816
NJ = DFF // 128
PLE = 256
EPS = 1e-6
T = 512
NSLOT = 3
SLOTW = 4096
PE, ACT, DVE, POOL, SP = "pe", "act", "dve", "pool", "sp"
ENGS = [PE, ACT, DVE, POOL, SP]


class Op:
    __slots__ = ("eng", "fn", "deps", "dma_key", "sem", "val", "signal", "idx", "cost", "phase", "t0", "t1")

    def __init__(self, eng, fn, deps, dma_key, cost):
        self.eng, self.fn, self.deps, self.dma_key, self.cost = eng, fn, deps, dma_key, cost
        self.sem = None
        self.val = 0
        self.signal = False


DEFAULT_COST = {PE: 216, ACT: 600, DVE: 650, POOL: 900, SP: 2500}


class Sched:
    def __init__(self):
        self.ops = []
        self.last_write = {}
        self.readers = {}
        self.phase = 0
        self.since_barrier = []
        self.cur_barrier = {}

    def op(self, eng, fn, reads=(), writes=(), dma_key=None, extra=(), cost=None):
        deps = set(extra)
        for r in reads:
            w = self.last_write.get(r)
            if w is not None:
                deps.add(w)
        for w_ in writes:
            w = self.last_write.get(w_)
            if w is not None:
                deps.add(w)
            for rd in self.readers.get(w_, ()):
                deps.add(rd)
        bar = self.cur_barrier.get(eng)
        if bar is not None:
            deps.add(bar)
        o = Op(eng, fn, deps, dma_key, DEFAULT_COST[eng] if cost is None else cost)
        o.idx = len(self.ops)
        o.phase = self.phase
        o.sem = (eng, self.phase) if dma_key is None else ("dma", dma_key)
        self.ops.append(o)
        for r in reads:
            self.readers.setdefault(r, []).append(o)
        for w_ in writes:
            self.last_write[w_] = o
            self.readers[w_] = []
        if fn is not None:
            self.since_barrier.append(o)
        return o

    def barrier(self):
        prev = list(self.since_barrier)
        self.since_barrier = []
        for e in ENGS:
            self.cur_barrier[e] = self.op(e, None, extra=prev, cost=0)
        self.phase += 1

    def schedule(self):
        import heapq
        n = len(self.ops)
        succ = [[] for _ in range(n)]
        indeg = [0] * n
        for o in self.ops:
            indeg[o.idx] = len(o.deps)
            for d in o.deps:
                succ[d.idx].append(o)
        pending = {e: [] for e in ENGS}
        avail = {e: [] for e in ENGS}
        free = {e: 0.0 for e in ENGS}
        ready_t = [0.0] * n
        for o in self.ops:
            if indeg[o.idx] == 0:
                heapq.heappush(pending[o.eng], (0.0, o.idx))
        order = []
        done = 0
        while done < n:
            best = None
            for e in ENGS:
                pe_, av = pending[e], avail[e]
                while pe_ and pe_[0][0] <= free[e]:
                    heapq.heappush(av, heapq.heappop(pe_)[1])
                if av:
                    cand = (free[e], av[0], e, True)
                elif pe_:
                    cand = (pe_[0][0], pe_[0][1], e, False)
                else:
                    continue
                if best is None or cand[:2] < best[:2]:
                    best = cand
            assert best is not None, "dependency cycle"
            start, idx, e, from_av = best
            if from_av:
                heapq.heappop(avail[e])
            else:
                heapq.heappop(pending[e])
            o = self.ops[idx]
            o.t0 = start
            if o.dma_key is not None:
                free[e] = start + (1100.0 if e == POOL else 350.0)
                o.t1 = start + o.cost
            else:
                o.t1 = start + o.cost
                free[e] = o.t1
            order.append(o)
            done += 1
            for sc in succ[idx]:
                if ready_t[sc.idx] < o.t1:
                    ready_t[sc.idx] = o.t1
                indeg[sc.idx] -= 1
                if indeg[sc.idx] == 0:
                    heapq.heappush(pending[sc.eng], (ready_t[sc.idx], sc.idx))
        self.ops = order
        self.est_ns = max(o.t1 for o in order)

    def finalize(self):
        for o in self.ops:
            for d in o.deps:
                if d.eng == PE and o.eng == PE and d.dma_key is None and o.dma_key is None:
                    continue
                if d.fn is None:
                    continue
                d.signal = True
        for o in self.ops:
            if o.dma_key is not None and o.fn is not None:
                o.signal = True
        counts = {}
        for o in self.ops:
            if o.signal:
                counts[o.sem] = counts.get(o.sem, 0) + (16 if o.dma_key is not None else 1)
                o.val = counts[o.sem]
        return sorted(counts.keys(), key=str)


def build_program(NCT, NOT, stage=3):
    NCH = NCT * 4
    NPAIR = NCH // 2
    nc = bass.Bass("TRN2", target_bir_lowering=False)

    def din(name, shape, dt=F32):
        return nc.dram_tensor(name, list(shape), dt, kind="ExternalInput").ap()

    xc = din("xc", [NCT * T, D])
    pin = din("pin", [NOT * T, PLE])
    pos = din("pos", [NCT * T, 2])
    kmask = din("kmask", [128, NCH])
    g_ffn1 = din("g_ffn1", [1, D]); w1gu = din("w_ffn1_gu", [1, D, 2 * DFF]); w1d = din("w_ffn1_down", [1, DFF, D])
    g_mix = din("g_mix", [1, D]); w_in = din("w_in", [1, D, 3840])
    g_q = din("g_q", [1, 64]); g_k = din("g_k", [1, 64]); g_gv = din("g_gmlp_v", [1, 512])
    w_sp = din("w_spatial", [1, 8, 128, 128]); b_sp = din("b_spatial", [1, 8, 128])
    w_ba = din("w_branch_attn", [1, 512, D]); w_bg = din("w_branch_gmlp", [1, 512, D]); w_out = din("w_out", [1, D, D])
    g_ffn2 = din("g_ffn2", [1, D]); w2gu = din("w_ffn2_gu", [1, D, 2 * DFF]); w2d = din("w_ffn2_down", [1, DFF, D])
    g_ple = din("g_ple", [1, D]); w_pg = din("w_ple_gate", [1, D, D]); w_pl = din("w_ple", [1, PLE, D])
    g_fin = din("g_final", [1, D])
    y = nc.dram_tensor("y", [NOT * T, D], F32, kind="ExternalOutput").ap()

    def dscr(name, shape, dt=BF16):
        return nc.dram_tensor(name, list(shape), dt).ap()

    s_gu = [dscr("s_gu1", [11, 128, 8, 512]), dscr("s_gu2", [11, 128, 8, 512])]
    s_dn = [dscr("s_dn1", [2, 128, NJ, 512]), dscr("s_dn2", [2, 128, NJ, 512])]
    s_kv = dscr("s_kv", [128, 8, 256])
    s_qgg = dscr("s_qgg", [3, 128, 8, 512])
    s_mg = dscr("s_mg", [8, 128, 3584])
    s_wo = dscr("s_wo", [2, 128, 8, 512])
    s_pg = dscr("s_pg", [2, 128, 8, 512])
    s_pl = dscr("s_pl", [128, 2, 1024])
    x1s = dscr("x1s", [NOT * T, D], F32)
    kts = dscr("kts", [128, NCT * T])
    vss = dscr("vss", [NCT * T, 130])

    S = Sched()

    def sb(name, shape, dt):
        return nc.alloc_sbuf_tensor(name, list(shape), dt)

    ident = sb("ident", [128, 128], BF16)
    gcol = sb("gcol", [128, 4, 8], F32)
    gq_bc = sb("gq_bc", [128, 64], F32)
    gk_bc = sb("gk_bc", [128, 64], F32)
    bspT = sb("bspT", [128, 8], F32)
    wsT = sb("wsT", [128, 8, 128], BF16)
    inv_bc = sb("inv_bc", [128, 16], F32)
    km = sb("km", [128, NCH], F32)
    negpi = sb("negpi", [128, 1], F32)
    epsb = sb("epsb", [128, 1], F32)
    ring = sb("ring", [128, NSLOT, SLOTW], BF16)
    Xb = [sb("X0", [128, 4, D], F32), sb("X1", [128, 4, D], F32)]
    hb = sb("hb", [128, 4, D], BF16)
    hT = sb("hT", [128, 8, T], BF16)
    ss = sb("ss", [128, 8], F32)
    rstd = sb("rstd", [128, 8], F32)
    tmpA = sb("tmpA", [128, T], F32)[:, :]
    tmpB = sb("tmpB", [128, T], F32)[:, :]
    posb = sb("posb", [128, 4, 2], F32)
    ang = sb("ang", [128, 4, 2, 16], F32)
    angm = sb("angm", [128, 4, 2, 16], F32)
    cs = sb("cs", [128, 4, 32], F32)
    sn = sb("sn", [128, 4, 32], F32)
    angki = sb("angki", [128, 4, 32], I32)
    angkf = sb("angkf", [128, 4, 32], F32)
    angr = sb("angr", [128, 4, 32], F32)
    hss = sb("hss", [128, 32], F32)
    hrs = sb("hrs", [128, 32], F32)
    ARENA_BYTES = 123 * 1024
    arena = sb("arena", [128, ARENA_BYTES // 4], F32)
    apos = [0]

    def carve(shape, dt):
        esz = 4 if dt in (F32, I32) else 2
        n = int(np.prod(shape[1:]))
        nbytes = (n * esz + 31) // 32 * 32
        off = apos[0]
        apos[0] += nbytes
        assert apos[0] <= ARENA_BYTES, (apos[0], ARENA_BYTES)
        v = arena[:, off // 4:(off + nbytes) // 4]
        if esz == 2:
            v = v.bitcast(BF16)
        v = v[:, 0:n]
        if len(shape) == 3:
            v = v.rearrange("p (a b) -> p a b", a=shape[1])
        elif len(shape) == 4:
            v = v.rearrange("p (a b c) -> p a b c", a=shape[1], b=shape[2])
        return v

    tp = nc.alloc_psum_tensor("tp", [128, 2048], BF16)
    S0 = nc.alloc_psum_tensor("S0", [128, 1024], F32)
    S1 = nc.alloc_psum_tensor("S1", [128, 1024], F32)
    O0 = nc.alloc_psum_tensor("O0", [128, 512], F32)
    O1 = nc.alloc_psum_tensor("O1", [128, 512], F32)
    G = [S0[:, 0:512], S0[:, 512:1024], S1[:, 0:512], S1[:, 512:1024], O0[:, :], O1[:, :]]
    GN = ["G0", "G1", "G2", "G3", "G4", "G5"]
    tpf = tp[:, :].bitcast(F32)

    def vec(e):
        return nc.vector if e == DVE else nc.gpsimd

    deferred_casts = []

    def cast(key, out_ap, in_ap):
        o_ = S.op(POOL, lambda o=out_ap, i=in_ap: nc.gpsimd.dma_start(out=o, in_=i),
                  writes=[("scr", key)], dma_key="c_" + key, cost=9000)
        if not (key.startswith("gu0") or key.startswith("dn0") or key == "kv"):
            deferred_casts.append(o_)
        return o_

    def small_load(out_ap, in_ap, res):
        def f():
            with nc.allow_non_contiguous_dma(reason="tiny constant layout load"):
                return nc.sync.dma_start(out=out_ap, in_=in_ap)
        return S.op(SP, f, writes=[res], dma_key="k_" + str(res).replace("'", "").replace(" ", ""))

    def bc_row(ap2d):
        return ap2d.partition_broadcast(128).rearrange("p o d -> p (o d)")

    gl = sb("gl", [40, 128], F32)
    identf = sb("identf", [40, 40], F32)
    for i, g in enumerate([g_ffn1, g_mix, g_ffn2, g_ple]):
        S.op(SP, lambda i=i, g=g: nc.sync.dma_start(out=gl[i * 8:(i + 1) * 8, :], in_=g.rearrange("o (kc p) -> (o kc) p", p=128)),
             writes=[("gl", i)], dma_key="k_gl%d" % i)
    S.op(SP, lambda: nc.sync.dma_start(out=gl[32:40, :], in_=b_sp[0]), writes=[("gl", 4)], dma_key="k_gl4")

    S.op(POOL, lambda: nc.gpsimd.memset(identf[:], 0.0), writes=["identf"])
    S.op(POOL, lambda: nc.gpsimd.affine_select(out=identf[:], in_=identf[:], pattern=[[-1, 40]], compare_op=ALU.not_equal,
                                               fill=1.0, base=0, channel_multiplier=1), reads=["identf"], writes=["identf"])
    S.op(PE, lambda: nc.tensor.matmul(G[0][:, 0:40], lhsT=gl[0:40, :], rhs=identf[0:40, 0:40], start=True, stop=True),
         reads=[("gl", i) for i in range(5)] + ["identf"], writes=[GN[0]], cost=400)
    S.op(DVE, lambda: nc.vector.tensor_copy(out=gcol[:, :, :].rearrange("p g k -> p (g k)"), in_=G[0][:, 0:32]),
         reads=[GN[0]], writes=[("gcol", i) for i in range(4)], cost=200)
    S.op(DVE, lambda: nc.vector.tensor_copy(out=bspT[:, :], in_=G[0][:, 32:40]), reads=[GN[0]], writes=["bsp"], cost=200)
    if P0 & 2:
        small_load(gq_bc[:, :], bc_row(g_q), "gq")
        small_load(gk_bc[:, :], bc_row(g_k), "gk")
    small_load(km[:, :], kmask[:, :], "km")

    S.op(POOL, lambda: nc.gpsimd.memset(ident[:], 0.0), writes=["ident"])
    S.op(POOL, lambda: nc.gpsimd.affine_select(out=ident[:], in_=ident[:], pattern=[[-1, 128]], compare_op=ALU.not_equal,
                                               fill=1.0, base=0, channel_multiplier=1), reads=["ident"], writes=["ident"])
    S.op(POOL, lambda: nc.gpsimd.memset(negpi[:], -math.pi), writes=["negpi"])
    S.op(POOL, lambda: nc.gpsimd.memset(epsb[:], EPS), writes=["epsb"])
    S.op(POOL, lambda: nc.gpsimd.iota(out=cs[:, 0, 0:16].bitcast(I32), pattern=[[1, 16]], base=0, channel_multiplier=0),
         writes=["cs"])
    S.op(POOL, lambda: nc.gpsimd.tensor_copy(out=sn[:, 0, 0:16], in_=cs[:, 0, 0:16].bitcast(I32)), reads=["cs"], writes=["sn"])
    S.op(ACT, lambda: nc.scalar.activation(out=inv_bc[:, :], in_=sn[:, 0, 0:16], func=AF.Exp, scale=-math.log(10000.0) / 16.0),
         reads=["sn"], writes=["inv"])

    S.op(SP, lambda: nc.sync.dma_start(out=Xb[0][:, 0, :].rearrange("p (g q) -> p g q", g=8),
                                       in_=w_sp[0].rearrange("g p q -> p g q")), writes=[("X0", 0)], dma_key="const")
    S.op(DVE, lambda: nc.vector.tensor_copy(out=hb[:, 0, :], in_=Xb[0][:, 0, :]), reads=[("X0", 0)], writes=["hb"])
    for g in range(8):
        S.op(PE, lambda g=g: nc.tensor.transpose(out=tp[:, g * 128:(g + 1) * 128], in_=hb[:, 0, g * 128:(g + 1) * 128], identity=ident[:]),
             reads=["hb", "ident"], writes=[("tpb", 0)], cost=118)
    S.op(DVE, lambda: nc.vector.tensor_copy(out=wsT[:, :, :].rearrange("q g p -> q (g p)"), in_=tp[:, 0:1024]),
         reads=[("tpb", 0)], writes=["wsT"])

    def kcp(ap2d):
        return ap2d.rearrange("(kc p) c -> p kc c", p=128)

    def cast_ffn(idx, wgu, wd):
        for i in range(11):
            cast("gu%d_%d" % (idx, i), s_gu[idx][i, :, :, 0:256], kcp(wgu[0, :, 256 * i:256 * i + 256]))
            cast("gu%d_%d" % (idx, i), s_gu[idx][i, :, :, 256:512], kcp(wgu[0, :, DFF + 256 * i:DFF + 256 * i + 256]))
        for h in range(2):
            cast("dn%d_%d" % (idx, h), s_dn[idx][h], kcp(wd[0, :, 512 * h:512 * h + 512]))

    cast_ffn(0, w1gu, w1d)
    cast("kv", s_kv, kcp(w_in[0, :, 512:768]))
    for g in range(2):
        for c in range(4):
            cast("qgg0", s_qgg[0, :, :, c * 128 + g * 64:c * 128 + g * 64 + 64], kcp(w_in[0, :, (g * 4 + c) * 64:(g * 4 + c) * 64 + 64]))
    for i, c0 in ((1, 768), (2, 1280)):
        cast("qgg%d" % i, s_qgg[i], kcp(w_in[0, :, c0:c0 + 512]))
    for oc in range(8):
        k = "mg%d" % oc
        gts = s_mg[oc, :, 0:2048].rearrange("p (kc c) -> p kc c", kc=8)
        cast(k, gts[:, :, 0:128], kcp(w_in[0, :, 1792 + oc * 128:1792 + oc * 128 + 128]))
        cast(k, gts[:, :, 128:256], kcp(w_in[0, :, 2816 + oc * 128:2816 + oc * 128 + 128]))
        cast(k, s_mg[oc, :, 2048:2560].rearrange("p (kc c) -> p kc c", kc=4), kcp(w_bg[0, :, oc * 128:oc * 128 + 128]))
        cast(k, s_mg[oc, 0:64, 2560:3584].rearrange("p (h c) -> p h c", h=8),
             w_ba[0, :, oc * 128:oc * 128 + 128].rearrange("(h p) c -> p h c", p=64))
    for h in range(2):
        cast("wo%d" % h, s_wo[h], kcp(w_out[0, :, 512 * h:512 * h + 512]))
    cast_ffn(1, w2gu, w2d)
    for h in range(2):
        cast("pg%d" % h, s_pg[h], kcp(w_pg[0, :, 512 * h:512 * h + 512]))
    cast("pl", s_pl, kcp(w_pl[0, :, :]))

    ring_n = [0]

    def ring_load(key, src_ap, width):
        slot = ring_n[0] % NSLOT
        ring_n[0] += 1
        S.op(SP, lambda s=slot, a=src_ap, w=width: nc.sync.dma_start(out=ring[:, s, 0:w], in_=a),
             reads=[("scr", key)], writes=[("ring", slot)], dma_key="ring%d" % slot, cost=2000 + width * 256 // 180)
        return slot

    def transpose_T(src, srcres, nkc, outT, outres, gi=None):
        for k0 in range(0, nkc, 2):
            bank = (k0 // 2) % 2
            for kk in range(2):
                kc = k0 + kk
                for s in range(4):
                    col = bank * 1024 + kk * 512 + s * 128
                    S.op(PE, lambda kc=kc, s=s, col=col: nc.tensor.transpose(out=tp[:, col:col + 128],
                                                                            in_=src[:, s, kc * 128:(kc + 1) * 128], identity=ident[:]),
                         reads=[srcres, "ident"], writes=[("tpb", bank)], cost=118)
            for kk in range(2):
                kc = k0 + kk
                e = ACT if bank == 0 else DVE
                src_ps = tp[:, bank * 1024 + kk * 512: bank * 1024 + (kk + 1) * 512]
                if gi is None:
                    if e == ACT:
                        f = lambda kc=kc, src_ps=src_ps: nc.scalar.copy(out=outT[:, kc, :], in_=src_ps)
                    else:
                        f = lambda kc=kc, src_ps=src_ps: nc.vector.tensor_copy(out=outT[:, kc, :], in_=src_ps)
                    rd = [("tpb", bank)]
                else:
                    if e == ACT:
                        f = lambda kc=kc, src_ps=src_ps: nc.scalar.activation(out=outT[:, kc, :], in_=src_ps, func=AF.Copy,
                                                                               scale=gcol[:, gi, kc:kc + 1])
                    else:
                        f = lambda kc=kc, src_ps=src_ps: nc.vector.tensor_scalar(out=outT[:, kc, :], in0=src_ps,
                                                                                  scalar1=gcol[:, gi, kc:kc + 1], scalar2=None, op0=ALU.mult)
                    rd = [("tpb", bank), ("gcol", gi)]
                S.op(e, f, reads=rd, writes=[(outres, kc)])

    def row_rstd(X, xres, width, nrm, hbuf=None, hbres="hb", c0=0):
        hbuf = hb if hbuf is None else hbuf
        for s in range(4):
            S.op(ACT, lambda s=s: nc.scalar.activation(out=hbuf[:, s, 0:width], in_=X[:, s, 0:width], func=AF.Square, accum_out=ss[:, c0 + s:c0 + s + 1]),
                 reads=[(xres, s)], writes=[hbres, ("ss", c0 + s)], cost=1056 if width > 512 else 843)
        S.op(ACT, lambda: nc.scalar.activation(out=rstd[:, c0:c0 + 4], in_=ss[:, c0:c0 + 4], func=AF.Sqrt, scale=1.0 / nrm, bias=epsb[:, 0:1]),
             reads=[("ss", c0 + s) for s in range(4)] + ["epsb"], writes=[("rstd", c0)], cost=300)
        S.op(DVE, lambda: nc.vector.reciprocal(out=rstd[:, c0:c0 + 4], in_=rstd[:, c0:c0 + 4]), reads=[("rstd", c0)], writes=[("rstd", c0)], cost=190)

    def rmsnorm_T(X, xres, gi, alt=None):
        hbuf, hbres, outT, outres = (hb, "hb", hT, "hT") if alt is None else alt
        c0 = 0 if alt is None else 4
        row_rstd(X, xres, D, D, hbuf, hbres, c0)
        for s in range(4):
            S.op(DVE, lambda s=s: nc.vector.tensor_scalar(out=hbuf[:, s, :], in0=X[:, s, :], scalar1=rstd[:, c0 + s:c0 + s + 1], scalar2=None, op0=ALU.mult),
                 reads=[(xres, s), ("rstd", c0)], writes=[hbres])
        transpose_T(hbuf, hbres, 8, outT, outres, gi)

    def ffn(idx, X, xres, hid, dn, mid=None):
        for i in range(11):
            slot = ring_load("gu%d_%d" % (idx, i), s_gu[idx][i].rearrange("p kc c -> p (kc c)"), 4096)
            rv = ring[:, slot, :].rearrange("p (kc c) -> p kc c", kc=8)
            for jj in range(2):
                j = 2 * i + jj
                ga, gb = (0, 1) if j % 2 == 0 else (2, 3)
                for kc in range(8):
                    S.op(PE, lambda kc=kc, jj=jj, ga=ga, rv=rv: nc.tensor.matmul(G[ga], lhsT=rv[:, kc, jj * 128:(jj + 1) * 128], rhs=hT[:, kc, :],
                                                                                  start=(kc == 0), stop=(kc == 7)),
                         reads=[("ring", slot), ("hT", kc)], writes=[GN[ga]])
                for kc in range(8):
                    S.op(PE, lambda kc=kc, jj=jj, gb=gb, rv=rv: nc.tensor.matmul(G[gb], lhsT=rv[:, kc, 256 + jj * 128:256 + (jj + 1) * 128], rhs=hT[:, kc, :],
                                                                                  start=(kc == 0), stop=(kc == 7)),
                         reads=[("ring", slot), ("hT", kc)], writes=[GN[gb]])
                tt, tn = (tmpA, "tmpA") if j % 2 == 0 else (tmpB, "tmpB")
                S.op(ACT, lambda ga=ga, tt=tt: nc.scalar.activation(out=tt[:, :], in_=G[ga], func=AF.Silu), reads=[GN[ga]], writes=[tn])
                S.op(DVE, lambda j=j, gb=gb, tt=tt: nc.vector.tensor_tensor(out=hid[:, j, :], in0=tt[:, :], in1=G[gb], op=ALU.mult),
                     reads=[tn, GN[gb]], writes=[("hid", j)])
        if mid is not None:
            mid()
        for h in range(2):
            S.op(SP, lambda h=h: nc.sync.dma_start(out=dn[h], in_=s_dn[idx][h]),
                 reads=[("scr", "dn%d_%d" % (idx, h))], writes=[("dn", h)], dma_key="dn%d" % h, cost=18000)
            for s in range(4):
                b = 4 + (s % 2)
                for j in range(NJ):
                    S.op(PE, lambda j=j, s=s, h=h, b=b: nc.tensor.matmul(G[b], lhsT=hid[:, j, s * 128:(s + 1) * 128], rhs=dn[h][:, j, :],
                                                                         start=(j == 0), stop=(j == NJ - 1)),
                         reads=[("hid", j), ("dn", h)], writes=[GN[b]])
                S.op(DVE, lambda s=s, h=h, b=b: nc.vector.scalar_tensor_tensor(out=X[:, s, h * 512:(h + 1) * 512], in0=G[b], scalar=0.5,
                                                                               in1=X[:, s, h * 512:(h + 1) * 512], op0=ALU.mult, op1=ALU.add),
                     reads=[GN[b], (xres, s)], writes=[(xres, s)])

    def rope_tables(t, nh, tabc, tabs, tabres):
        S.op(SP, lambda: nc.sync.dma_start(out=posb[:, :, :], in_=pos[t * T:(t + 1) * T, :].rearrange("(s p) a -> p s a", p=128)),
             writes=["posb"], dma_key="posb")
        for a in range(2):
            S.op(DVE, lambda a=a: nc.vector.tensor_tensor(out=ang[:, :, a, :], in0=posb[:, :, a:a + 1].to_broadcast([128, 4, 16]),
                                                           in1=inv_bc[:, :].unsqueeze(1).to_broadcast([128, 4, 16]), op=ALU.mult),
                 reads=["posb", "inv"], writes=["ang"], cost=200)
        angf = ang[:, :, :, :].rearrange("p s a f -> p s (a f)")
        angmf = angm[:, :, :, :].rearrange("p s a f -> p s (a f)")
        TWO_PI = 2.0 * math.pi

        def sin_of(dst, shift):
            S.op(DVE, lambda: nc.vector.tensor_scalar(out=angmf, in0=angf, scalar1=shift, scalar2=1.0 / TWO_PI, op0=ALU.add, op1=ALU.mult),
                 reads=["ang"], writes=["angm"])
            S.op(DVE, lambda: nc.vector.tensor_copy(out=angki[:, :, :], in_=angmf), reads=["angm"], writes=["angki"])
            S.op(DVE, lambda: nc.vector.tensor_copy(out=angkf[:, :, :], in_=angki[:, :, :]), reads=["angki"], writes=["angkf"])
            S.op(DVE, lambda: nc.vector.tensor_scalar(out=angmf, in0=angf, scalar1=shift, scalar2=None, op0=ALU.add),
                 reads=["ang", "angki"], writes=["angm"])
            S.op(DVE, lambda: nc.vector.scalar_tensor_tensor(out=angr[:, :, :], in0=angkf[:, :, :], scalar=-TWO_PI, in1=angmf, op0=ALU.mult, op1=ALU.add),
                 reads=["angkf", "angm"], writes=["angr"])
            S.op(DVE, lambda: nc.vector.tensor_scalar(out=angmf, in0=angr[:, :, :], scalar1=math.pi, scalar2=TWO_PI, op0=ALU.is_gt, op1=ALU.mult),
                 reads=["angr"], writes=["angm"])
            S.op(DVE, lambda: nc.vector.tensor_tensor(out=angr[:, :, :], in0=angr[:, :, :], in1=angmf, op=ALU.subtract),
                 reads=["angr", "angm"], writes=["angr"])
            S.op(ACT, lambda: nc.scalar.activation(out=dst[:, :, :], in_=angr[:, :, :], func=AF.Sin), reads=["angr"], writes=[("cs" if dst is cs else "sn")])

        sin_of(sn, 0.0)
        sin_of(cs, 0.5 * math.pi)
        S.op(POOL, lambda: nc.gpsimd.tensor_copy(out=tabc, in_=cs[:, :, :].unsqueeze(2).to_broadcast([128, 4, nh, 32])),
             reads=["cs"], writes=[tabres + "c"])
        S.op(POOL, lambda: nc.gpsimd.tensor_copy(out=tabs, in_=sn[:, :, :].unsqueeze(2).to_broadcast([128, 4, nh, 32])),
             reads=["sn"], writes=[tabres + "s"])

    def head_norm_rope(e, src, srcres, nh, gbc, gres, tabc, tabs, tabres, sq, sqres, dst, dstres):
        V = vec(e)
        SH = 4 * nh
        cb = 250 + SH * 64 * (0.9 if e == POOL else 0.55)
        ch = 250 + SH * 32 * (0.9 if e == POOL else 0.55)
        x3 = src.rearrange("p s (h d) -> p (s h) d", h=nh)
        sq3 = sq.rearrange("p s (h d) -> p (s h) d", h=nh)
        S.op(e, lambda: V.tensor_tensor(out=sq3, in0=x3, in1=x3, op=ALU.mult), reads=[srcres], writes=[sqres], cost=cb)
        S.op(DVE, lambda: nc.vector.tensor_reduce(out=hss[:, 0:SH], in_=sq3, axis=AX.X, op=ALU.add), reads=[sqres], writes=["hss"], cost=250 + SH * 64 * 0.55)
        S.op(ACT, lambda: nc.scalar.activation(out=hrs[:, 0:SH], in_=hss[:, 0:SH], func=AF.Sqrt, scale=1.0 / 64, bias=epsb[:, 0:1]),
             reads=["hss", "epsb"], writes=["hrs"], cost=300)
        S.op(DVE, lambda: nc.vector.reciprocal(out=hrs[:, 0:SH], in_=hrs[:, 0:SH]), reads=["hrs"], writes=["hrs"], cost=190)
        S.op(e, lambda: V.tensor_tensor(out=x3, in0=x3, in1=hrs[:, 0:SH].unsqueeze(2).to_broadcast([128, SH, 64]), op=ALU.mult),
             reads=[srcres, "hrs"], writes=[srcres], cost=cb)
        S.op(e, lambda: V.tensor_tensor(out=x3, in0=x3, in1=gbc[:, :].unsqueeze(1).to_broadcast([128, SH, 64]), op=ALU.mult),
             reads=[srcres, gres], writes=[srcres], cost=cb)
        pat = "p s (h a r f) -> p (s h) a r f"
        x5 = src.rearrange(pat, h=nh, a=2, r=2)
        q5 = sq.rearrange(pat, h=nh, a=2, r=2)
        d5 = dst.rearrange(pat, h=nh, a=2, r=2)
        xa, xb_ = x5[:, :, :, 0, :], x5[:, :, :, 1, :]
        ta, tb_ = q5[:, :, :, 0, :], q5[:, :, :, 1, :]
        c4 = tabc.rearrange("p s h (a f) -> p (s h) a f", a=2)
        s4 = tabs.rearrange("p s h (a f) -> p (s h) a f", a=2)
        oa, ob = d5[:, :, :, 0, :], d5[:, :, :, 1, :]
        S.op(e, lambda: V.tensor_tensor(out=ta, in0=xb_, in1=s4, op=ALU.mult), reads=[srcres, tabres + "s"], writes=[sqres], cost=ch)
        S.op(e, lambda: V.tensor_tensor(out=tb_, in0=xa, in1=s4, op=ALU.mult), reads=[srcres, tabres + "s"], writes=[sqres], cost=ch)
        S.op(e, lambda: V.tensor_tensor(out=xa, in0=xa, in1=c4, op=ALU.mult), reads=[srcres, sqres, tabres + "c"], writes=[srcres], cost=ch)
        S.op(e, lambda: V.tensor_tensor(out=xb_, in0=xb_, in1=c4, op=ALU.mult), reads=[srcres, sqres, tabres + "c"], writes=[srcres], cost=ch)
        S.op(e, lambda: V.tensor_tensor(out=oa, in0=xa, in1=ta, op=ALU.subtract), reads=[srcres, sqres], writes=[dstres], cost=ch)
        S.op(e, lambda: V.tensor_tensor(out=ob, in0=xb_, in1=tb_, op=ALU.add), reads=[srcres, sqres], writes=[dstres], cost=ch)

    def XR(xres):
        return [(xres, s) for s in range(4)]

    def tok_rows(ap2d):
        return ap2d.rearrange("(s p) d -> p s d", p=128)

    dbg = {}

    apos[0] = 0
    hid = carve([128, NJ, T], BF16)
    dn = [carve([128, NJ, 512], BF16), carve([128, NJ, 512], BF16)]
    kvw = carve([128, 8, 256], BF16)
    kvs = carve([128, 2, 4, 128], F32)
    ksq = carve([128, 4, 128], F32)
    tkc = carve([128, 4, 2, 32], F32)
    tks = carve([128, 4, 2, 32], F32)
    krb = carve([128, 4, 128], BF16)
    kTb = [carve([128, T], BF16), carve([128, T], BF16)]
    vsb = [carve([128, 4, 130], BF16), carve([128, 4, 130], BF16)]
    hb2 = carve([128, 4, D], BF16)
    hT2 = carve([128, 8, T], BF16)
    ALT = (hb2, "hb2", hT2, "hT2")

    S.op(SP, lambda: nc.sync.dma_start(out=kvw, in_=s_kv), reads=[("scr", "kv")], writes=["kvw"], dma_key="kvw")

    for t in range(NCT if stage >= 1 else 0):
        b = t % 2
        X, xres = Xb[b], "X%d" % b
        def prep1(tt):
            bb = tt % 2
            ld = S.op(SP, lambda bb=bb, tt=tt: nc.sync.dma_start(out=Xb[bb][:, :, :], in_=tok_rows(xc[tt * T:(tt + 1) * T, :])),
                      writes=XR("X%d" % bb), dma_key="xl%d" % bb, cost=13600)
            if tt == min(2, NCT - 1):
                for dc in deferred_casts:
                    dc.deps.add(ld)
            rmsnorm_T(Xb[bb], "X%d" % bb, 0)
        if t == 0:
            prep1(0)
        ffn(0, X, xres, hid, dn, mid=(lambda t=t: prep1(t + 1)) if t + 1 < NCT else None)
        if t < NOT:
            S.op(SP, lambda b=b, t=t: nc.sync.dma_start(out=tok_rows(x1s[t * T:(t + 1) * T, :]), in_=Xb[b][:, :, :]),
                 reads=XR(xres), writes=[("x1s", t, s) for s in range(4)], dma_key="xs%d" % b, cost=13600)
        if P1 < 3:
            continue
        rmsnorm_T(X, xres, 1, ALT)
        if P1 < 4:
            continue
        rope_tables(t, 2, tkc, tks, "tk")
        if P1 < 5:
            continue
        for s in range(4):
            gb_ = s % 2
            for kc in range(8):
                S.op(PE, lambda kc=kc, s=s, gb_=gb_: nc.tensor.matmul(G[gb_][:, 0:256], lhsT=hT2[:, kc, s * 128:(s + 1) * 128], rhs=kvw[:, kc, :],
                                                                       start=(kc == 0), stop=(kc == 7)),
                     reads=[("hT2", kc), "kvw"], writes=[GN[gb_]], cost=200)
            S.op(ACT, lambda s=s, gb_=gb_: nc.scalar.copy(out=kvs[:, :, s, :], in_=G[gb_][:, 0:256].rearrange("p (a d) -> p a d", a=2)), reads=[GN[gb_]], writes=["kvs"])
        if P1 < 6:
            continue
        vb, vres = vsb[b], "vsb%d" % b
        kmt = km[:, t * 4:(t + 1) * 4]
        vb4 = vb.rearrange("p s (h e) -> p s h e", h=2)
        S.op(POOL, lambda vb4=vb4, kmt=kmt: nc.gpsimd.tensor_tensor(
            out=vb4[:, :, :, 0:64], in0=kvs[:, 1, :, :].rearrange("p s (h d) -> p s h d", h=2),
            in1=kmt.unsqueeze(2).unsqueeze(3).to_broadcast([128, 4, 2, 64]), op=ALU.mult),
            reads=["kvs", "km"], writes=[vres])
        S.op(POOL, lambda vb4=vb4, kmt=kmt: nc.gpsimd.tensor_copy(out=vb4[:, :, :, 64], in_=kmt.unsqueeze(2).to_broadcast([128, 4, 2])),
             reads=["km"], writes=[vres])
        S.op(SP, lambda vb=vb, t=t: nc.sync.dma_start(out=vss[t * T:(t + 1) * T, :].rearrange("(s p) e -> p s e", p=128), in_=vb),
             reads=[vres], writes=[("vss", t)], dma_key="vst%d" % b)
        if P1 < 7:
            continue
        head_norm_rope(POOL, kvs[:, 0, :, :], "kvs", 2, gk_bc, "gk", tkc, tks, "tk", ksq, "ksq", krb, "krb")
        for s in range(4):
            S.op(PE, lambda s=s: nc.tensor.transpose(out=tp[:, s * 128:(s + 1) * 128], in_=krb[:, s, :], identity=ident[:]),
                 reads=["krb", "ident"], writes=[("tpb", 0)], cost=118)
        kb_, kres = kTb[b], "kTb%d" % b
        S.op(DVE, lambda kb_=kb_: nc.vector.tensor_copy(out=kb_, in_=tp[:, 0:512]), reads=[("tpb", 0)], writes=[kres])
        S.op(SP, lambda kb_=kb_, t=t: nc.sync.dma_start(out=kts[:, t * T:(t + 1) * T], in_=kb_),
             reads=[kres], writes=[("kts", t)], dma_key="kst%d" % b)

    S.barrier()
    apos[0] = 0
    kT = carve([128, NCT * T], BF16)
    vAf = carve([128, NCH * 130 + 64], BF16)
    vA = vAf[:, 0:NCH * 130].rearrange("p (ch e) -> p ch e", e=130)
    qf = carve([128, 4, 512], F32)
    qsq = hb[:, :, :].rearrange("p s d -> p (s d)").bitcast(F32).rearrange("p (s d) -> p s d", s=4)
    tqc = carve([128, 4, 8, 32], F32)
    tqs = carve([128, 4, 8, 32], F32)
    qrb = carve([128, 4, 512], BF16)
    qTp = Xb[1][:, 0:2, :].rearrange("p a d -> p (a d)").bitcast(BF16).rearrange("p (h t) -> p h t", h=8)
    ub = carve([128, 4, 512], BF16)
    vnb = tqc.rearrange("p s h f -> p (s h f)").bitcast(BF16)[:, 0:2048].rearrange("p (s d) -> p s d", s=4)
    sgb = tqs.rearrange("p s h f -> p (s h f)").bitcast(BF16)[:, 0:2048].rearrange("p (s d) -> p s d", s=4)
    sgT = carve([128, 4, T], BF16)
    aT = carve([128, 8, T], BF16)
    mT = carve([128, 8, T], BF16)
    PT = [carve([128, 1024], BF16), carve([128, 1024], BF16), carve([128, 1024], BF16)]
    rden = carve([128, T], F32)
    ones1 = carve([128, 64], F32)
    onT = carve([128, T], F32)
    ggv_bc = carve([128, 512], F32)
    S.op(POOL, lambda: nc.gpsimd.memset(ones1, 1.0), writes=["ones1"])
    S.op(POOL, lambda: nc.gpsimd.memset(qTp, 0.0), writes=[("qT", c) for c in range(4)])
    S.op(POOL, lambda: nc.gpsimd.memset(vAf[:, NCH * 130:NCH * 130 + 64], 0.0), writes=["vApad"])
    small_load(ggv_bc, bc_row(g_gv), "ggv")

    for c in range(0, NCT if stage >= 2 else 0, 8):
        n = min(8, NCT - c)
        S.op(SP, lambda c=c, n=n: nc.sync.dma_start(out=kT[:, c * T:(c + n) * T], in_=kts[:, c * T:(c + n) * T]),
             reads=[("kts", t) for t in range(c, c + n)], writes=["kT"], dma_key="kTl", cost=9000)
        S.op(SP, lambda c=c, n=n: nc.sync.dma_start(out=vA[:, c * 4:(c + n) * 4, :],
                                                    in_=vss[c * T:(c + n) * T, :].rearrange("(ch p) e -> p ch e", p=128)),
             reads=[("vss", t) for t in range(c, c + n)], writes=["vA"], dma_key="vAl", cost=15000)

    def attention():
        heads = [(c, g) for c in range(4) for g in range(2)]
        seq = [(hi, i) for hi in range(8) for i in range(NPAIR)]
        Sv = [S0[:, :], S1[:, :], tpf]
        Sres = [[GN[0], GN[1]], [GN[2], GN[3]], [("tpb", 0), ("tpb", 1)]]

        def qk(n):
            hi, i = seq[n]
            c, g = heads[hi]
            sb_ = n % 3
            Sx = Sv[sb_]
            for u in range(2):
                ch = 2 * i + u
                S.op(PE, lambda u=u, ch=ch, c=c, g=g, Sx=Sx: nc.tensor.matmul(
                    Sx[:, u * 512:(u + 1) * 512], lhsT=kT[:, ch * 128:(ch + 1) * 128],
                    rhs=qTp[:, c * 2 + g, :], start=True, stop=True),
                    reads=["kT", ("qT", c)], writes=[Sres[sb_][u]])

        qk(0)
        qk(1)
        for n in range(len(seq)):
            hi, i = seq[n]
            c, g = heads[hi]
            sb_ = n % 3
            Sx = Sv[sb_]
            ob = hi % 2
            Ox = (O0, O1)[ob]
            if n + 2 < len(seq):
                qk(n + 2)
            S.op(ACT, lambda Sx=Sx, sb_=sb_: nc.scalar.activation(out=PT[sb_], in_=Sx, func=AF.Exp, scale=0.125),
                 reads=Sres[sb_], writes=["PT%d" % sb_], cost=1023)
            for u in range(2):
                ch = 2 * i + u
                S.op(PE, lambda u=u, ch=ch, g=g, Ox=Ox, sb_=sb_, i=i: nc.tensor.matmul(
                    Ox[:, :], lhsT=vAf[:, ch * 130 + g * 65:ch * 130 + g * 65 + 128], rhs=PT[sb_][:, u * 512:(u + 1) * 512],
                    start=(i == 0 and u == 0), stop=(i == NPAIR - 1 and u == 1)),
                    reads=["vA", "vApad", "PT%d" % sb_], writes=[GN[4 + ob]])
            if i == NPAIR - 1:
                h_true = g * 4 + c
                S.op(DVE, lambda Ox=Ox: nc.vector.reciprocal(out=rden[64:65, :], in_=Ox[64:65, :]), reads=[GN[4 + ob]], writes=["rden"], cost=2472)
                S.op(PE, lambda Sx=Sx: nc.tensor.matmul(Sx[0:64, 0:512], lhsT=ones1[64:65, 0:64], rhs=rden[64:65, :], start=True, stop=True),
                     reads=["rden", "ones1"], writes=[Sres[sb_][0]], cost=970)
                S.op(ACT, lambda Sx=Sx: nc.scalar.copy(out=onT[0:64, :], in_=Sx[0:64, 0:512]), reads=[Sres[sb_][0]], writes=["onT"])
                S.op(DVE, lambda Ox=Ox, h_true=h_true: nc.vector.tensor_tensor(out=aT[0:64, h_true, :], in0=Ox[0:64, :], in1=onT[0:64, :], op=ALU.mult),
                     reads=[GN[4 + ob], "onT"], writes=[("aT", h_true)])

    for t in range(NOT if stage >= 2 else 0):
        b = 0
        X, xres = Xb[b], "X%d" % b
        for s in range(4):
            S.op(SP, lambda s=s, t=t: nc.sync.dma_start(out=Xb[0][:, s, :], in_=x1s[t * T + s * 128:t * T + (s + 1) * 128, :]),
                 reads=[("x1s", t, s)], writes=[(xres, s)], dma_key="xa%d" % s, cost=5000)
        rmsnorm_T(X, xres, 1)
        rope_tables(t, 8, tqc, tqs, "tq")
        for pi in range(3):
            slot = ring_load("qgg%d" % pi, s_qgg[pi].rearrange("p kc c -> p (kc c)"), 4096)
            rv = ring[:, slot, :].rearrange("p (kc c) -> p kc c", kc=8)
            for s in range(4):
                gb_ = s % 2
                for kc in range(8):
                    S.op(PE, lambda kc=kc, s=s, gb_=gb_, rv=rv: nc.tensor.matmul(G[gb_], lhsT=hT[:, kc, s * 128:(s + 1) * 128], rhs=rv[:, kc, :],
                                                                                  start=(kc == 0), stop=(kc == 7)),
                         reads=[("hT", kc), ("ring", slot)], writes=[GN[gb_]])
                if pi == 0:
                    S.op(ACT, lambda s=s, gb_=gb_: nc.scalar.copy(out=qf[:, s, :], in_=G[gb_]), reads=[GN[gb_]], writes=["qf"])
                elif pi == 1:
                    S.op(ACT, lambda s=s, gb_=gb_: nc.scalar.activation(out=ub[:, s, :], in_=G[gb_], func=AF.Gelu), reads=[GN[gb_]], writes=["ub"])
                else:
                    S.op(ACT, lambda s=s, gb_=gb_: nc.scalar.activation(out=qf[:, s, :], in_=G[gb_], func=AF.Gelu), reads=[GN[gb_]], writes=["qf"])
            if pi == 0:
                head_norm_rope(DVE, qf, "qf", 8, gq_bc, "gq", tqc, tqs, "tq", qsq, "hb", qrb, "qrb")
                for c0 in range(0, 4, 2):
                    bank = (c0 // 2) % 2
                    for kk in range(2):
                        c = c0 + kk
                        for s in range(4):
                            col = bank * 1024 + kk * 512 + s * 128
                            S.op(PE, lambda c=c, s=s, col=col: nc.tensor.transpose(out=tp[:, col:col + 128], in_=qrb[:, s, c * 128:(c + 1) * 128],
                                                                                    identity=ident[:]),
                                 reads=["qrb", "ident"], writes=[("tpb", bank)], cost=118)
                    for kk in range(2):
                        c = c0 + kk
                        for g in range(2):
                            src_ps = tp[g * 64:(g + 1) * 64, bank * 1024 + kk * 512: bank * 1024 + (kk + 1) * 512]
                            dst = qTp[g * 64:(g + 1) * 64, c * 2 + g, :]
                            if bank == 0:
                                S.op(ACT, lambda src_ps=src_ps, dst=dst: nc.scalar.copy(out=dst, in_=src_ps), reads=[("tpb", bank)], writes=[("qT", c)])
                            else:
                                S.op(DVE, lambda src_ps=src_ps, dst=dst: nc.vector.tensor_copy(out=dst, in_=src_ps), reads=[("tpb", bank)], writes=[("qT", c)])
            if pi == 2:
                row_rstd(qf, "qf", 512, 512)
                for s in range(4):
                    S.op(DVE, lambda s=s: nc.vector.scalar_tensor_tensor(out=vnb[:, s, :], in0=qf[:, s, :], scalar=rstd[:, s:s + 1], in1=ggv_bc,
                                                                         op0=ALU.mult, op1=ALU.mult),
                         reads=["qf", ("rstd", 0), "ggv"], writes=["tqc"])
                for s in range(4):
                    gb_ = 2 + (s % 2)
                    for g in range(8):
                        S.op(PE, lambda s=s, g=g, gb_=gb_: nc.tensor.matmul(G[gb_][:, g * 64:(g + 1) * 64], lhsT=wsT[:, g, :], rhs=vnb[:, s, g * 64:(g + 1) * 64],
                                                                             start=True, stop=True),
                             reads=["tqc", "wsT"], writes=[GN[gb_]], cost=70)
                    tt, tn = (tmpA, "tmpA") if s % 2 == 0 else (tmpB, "tmpB")
                    S.op(DVE, lambda gb_=gb_, tt=tt: nc.vector.tensor_tensor(out=tt.rearrange("p (g c) -> p g c", g=8),
                                                                             in0=G[gb_].rearrange("p (g c) -> p g c", g=8),
                                                                             in1=bspT[:, :].unsqueeze(2).to_broadcast([128, 8, 64]), op=ALU.add),
                         reads=[GN[gb_], "bsp"], writes=[tn])
                    S.op(POOL, lambda s=s, tt=tt: nc.gpsimd.tensor_tensor(out=sgb[:, s, :], in0=tt, in1=ub[:, s, :], op=ALU.mult),
                         reads=[tn, "ub"], writes=["tqs"])
                transpose_T(sgb, "tqs", 4, sgT, "sgT")
        attention()
        for oc in range(8):
            slot = ring_load("mg%d" % oc, s_mg[oc], 3584)
            rg = ring[:, slot, 0:2048].rearrange("p (kc c) -> p kc c", kc=8)
            g0, g1 = (0, 1) if oc % 2 == 0 else (2, 3)
            for kc in range(8):
                S.op(PE, lambda kc=kc, rg=rg, g0=g0: nc.tensor.matmul(G[g0], lhsT=rg[:, kc, 0:128], rhs=hT[:, kc, :], start=(kc == 0), stop=(kc == 7)),
                     reads=[("ring", slot), ("hT", kc)], writes=[GN[g0]])
            for kc in range(8):
                S.op(PE, lambda kc=kc, rg=rg, g1=g1: nc.tensor.matmul(G[g1], lhsT=rg[:, kc, 128:256], rhs=hT[:, kc, :], start=(kc == 0), stop=(kc == 7)),
                     reads=[("ring", slot), ("hT", kc)], writes=[GN[g1]])
            for h in range(8):
                S.op(PE, lambda h=h, slot=slot: nc.tensor.matmul(G[4], lhsT=ring[0:64, slot, 2560 + h * 128:2560 + (h + 1) * 128], rhs=aT[0:64, h, :],
                                                                 start=(h == 0), stop=(h == 7)),
                     reads=[("ring", slot), ("aT", h)], writes=[GN[4]])
            for kc in range(4):
                S.op(PE, lambda kc=kc, slot=slot: nc.tensor.matmul(G[5], lhsT=ring[:, slot, 2048 + kc * 128:2048 + (kc + 1) * 128], rhs=sgT[:, kc, :],
                                                                   start=(kc == 0), stop=(kc == 3)),
                     reads=[("ring", slot), ("sgT", kc)], writes=[GN[5]])
            S.op(ACT, lambda g0=g0: nc.scalar.activation(out=tmpA, in_=G[g0], func=AF.Sigmoid), reads=[GN[g0]], writes=["tmpA"])
            S.op(ACT, lambda g1=g1: nc.scalar.activation(out=tmpB, in_=G[g1], func=AF.Sigmoid), reads=[GN[g1]], writes=["tmpB"])
            S.op(DVE, lambda: nc.vector.tensor_tensor(out=tmpA, in0=tmpA, in1=G[4], op=ALU.mult), reads=["tmpA", GN[4]], writes=["tmpA"])
            S.op(DVE, lambda: nc.vector.tensor_tensor(out=tmpB, in0=tmpB, in1=G[5], op=ALU.mult), reads=["tmpB", GN[5]], writes=["tmpB"])
            S.op(POOL, lambda oc=oc: nc.gpsimd.tensor_tensor(out=mT[:, oc, :], in0=tmpA, in1=tmpB, op=ALU.add),
                 reads=["tmpA", "tmpB"], writes=[("mT", oc)])
        wo_slots = [ring_load("wo%d" % h, s_wo[h].rearrange("p kc c -> p (kc c)"), 4096) for h in range(2)]
        for s in range(4):
            for h in range(2):
                slot = wo_slots[h]
                rv = ring[:, slot, :].rearrange("p (kc c) -> p kc c", kc=8)
                gb_ = h
                for kc in range(8):
                    S.op(PE, lambda kc=kc, s=s, gb_=gb_, rv=rv: nc.tensor.matmul(G[gb_], lhsT=mT[:, kc, s * 128:(s + 1) * 128], rhs=rv[:, kc, :],
                                                                                  start=(kc == 0), stop=(kc == 7)),
                         reads=[("mT", kc), ("ring", slot)], writes=[GN[gb_]])
                S.op(DVE, lambda s=s, h=h, gb_=gb_, X=X: nc.vector.tensor_tensor(out=X[:, s, h * 512:(h + 1) * 512], in0=G[gb_],
                                                                                  in1=X[:, s, h * 512:(h + 1) * 512], op=ALU.add),
                     reads=[GN[gb_], (xres, s)], writes=[(xres, s)])
            S.op(SP, lambda s=s, t=t: nc.sync.dma_start(out=x1s[t * T + s * 128:t * T + (s + 1) * 128, :], in_=Xb[0][:, s, :]),
                 reads=[(xres, s)], writes=[("x1s", t, s)], dma_key="xb%d" % s, cost=5000)

    S.barrier()
    apos[0] = 0
    hid = carve([128, NJ, T], BF16)
    dn = [carve([128, NJ, 512], BF16), carve([128, NJ, 512], BF16)]
    pf = carve([128, 4, PLE], F32)
    pbf = carve([128, 4, PLE], BF16)
    pT = carve([128, 2, T], BF16)
    gfin_bc = carve([128, D], F32)
    hb3 = carve([128, 4, D], BF16)
    hT3 = carve([128, 8, T], BF16)
    ALT = (hb3, "hb3", hT3, "hT3")
    small_load(gfin_bc, bc_row(g_fin), "gfin")
    out_ops = []
    for t in range(NOT if stage >= 3 else 0):
        b = t % 2
        X, xres = Xb[b], "X%d" % b
        def prep2(tt):
            bb = tt % 2
            S.op(SP, lambda bb=bb, tt=tt: nc.sync.dma_start(out=Xb[bb][:, :, :], in_=tok_rows(x1s[tt * T:(tt + 1) * T, :])),
                 reads=[("x1s", tt, s) for s in range(4)], writes=XR("X%d" % bb), dma_key="xl%d" % bb, cost=13600)
            rmsnorm_T(Xb[bb], "X%d" % bb, 2)
        if t == 0:
            prep2(0)
        ffn(1, X, xres, hid, dn, mid=(lambda t=t: prep2(t + 1)) if t + 1 < NOT else None)
        rmsnorm_T(X, xres, 3, ALT)
        S.op(SP, lambda t=t: nc.sync.dma_start(out=pf, in_=tok_rows(pin[t * T:(t + 1) * T, :])), writes=["pf"], dma_key="pfl")
        S.op(POOL, lambda: nc.gpsimd.tensor_copy(out=pbf, in_=pf), reads=["pf"], writes=["pbf"])
        transpose_T(pbf, "pbf", 2, pT, "pT")
        slotp = ring_load("pl", s_pl.rearrange("p kc c -> p (kc c)"), 2048)
        rvp = ring[:, slotp, 0:2048].rearrange("p (kc c) -> p kc c", kc=2)
        for h in range(2):
            slot = ring_load("pg%d" % h, s_pg[h].rearrange("p kc c -> p (kc c)"), 4096)
            rv = ring[:, slot, :].rearrange("p (kc c) -> p kc c", kc=8)
            for s in range(4):
                g0, g1 = (0, 1) if s % 2 == 0 else (2, 3)
                for kc in range(8):
                    S.op(PE, lambda kc=kc, s=s, g0=g0, rv=rv: nc.tensor.matmul(G[g0], lhsT=hT3[:, kc, s * 128:(s + 1) * 128], rhs=rv[:, kc, :],
                                                                                start=(kc == 0), stop=(kc == 7)),
                         reads=[("hT3", kc), ("ring", slot)], writes=[GN[g0]])
                for k2 in range(2):
                    S.op(PE, lambda k2=k2, s=s, g1=g1, h=h, rvp=rvp: nc.tensor.matmul(G[g1], lhsT=pT[:, k2, s * 128:(s + 1) * 128],
                                                                                       rhs=rvp[:, k2, h * 512:(h + 1) * 512],
                                                                                       start=(k2 == 0), stop=(k2 == 1)),
                         reads=[("pT", k2), ("ring", slotp)], writes=[GN[g1]])
                tt, tn = (tmpA, "tmpA") if s % 2 == 0 else (tmpB, "tmpB")
                S.op(ACT, lambda g0=g0, tt=tt: nc.scalar.activation(out=tt, in_=G[g0], func=AF.Sigmoid), reads=[GN[g0]], writes=[tn])
                S.op(DVE, lambda g1=g1, tt=tt: nc.vector.tensor_tensor(out=tt, in0=tt, in1=G[g1], op=ALU.mult), reads=[tn, GN[g1]], writes=[tn])
                S.op(POOL, lambda s=s, h=h, tt=tt, X=X: nc.gpsimd.tensor_tensor(out=X[:, s, h * 512:(h + 1) * 512], in0=X[:, s, h * 512:(h + 1) * 512],
                                                                                 in1=tt, op=ALU.add),
                     reads=[tn, (xres, s)], writes=[(xres, s)])
        row_rstd(X, xres, D, D)
        for s in range(4):
            S.op(DVE, lambda s=s, X=X: nc.vector.scalar_tensor_tensor(out=X[:, s, :], in0=X[:, s, :], scalar=rstd[:, s:s + 1], in1=gfin_bc,
                                                                      op0=ALU.mult, op1=ALU.mult),
                 reads=[(xres, s), ("rstd", 0), "gfin"], writes=[(xres, s)])
        out_ops.append(S.op(SP, lambda b=b, t=t: nc.sync.dma_start(out=tok_rows(y[t * T:(t + 1) * T, :]), in_=Xb[b][:, :, :]),
                            reads=XR(xres), dma_key="ys%d" % b, cost=13600))
    if stage < 3 and stage >= 1:
        out_ops.append(S.op(SP, lambda: nc.sync.dma_start(out=y[:, :], in_=x1s[:, :]), reads=[("x1s", t, s) for t in range(NOT) for s in range(4)], dma_key="dbg"))
    S.op(SP, None, extra=out_ops)

    if SCHEDULE:
        S.schedule()
        build_program.last_est_ns = S.est_ns
    semkeys = S.finalize()
    by_eng = {e: [o for o in S.ops if o.eng == e] for e in ENGS}
    with ExitStack() as es:
        sems = {k: es.enter_context(nc.semaphore("s%d" % i)) for i, k in enumerate(semkeys)}
        block = es.enter_context(nc.Block())

        def emit(engname, eng):
            waited = {}
            for o in by_eng[engname]:
                need = {}
                for d in o.deps:
                    if not d.signal or d.fn is None:
                        continue
                    if d.eng == PE and o.eng == PE and d.dma_key is None and o.dma_key is None:
                        continue
                    if need.get(d.sem, 0) < d.val:
                        need[d.sem] = d.val
                for k, v in need.items():
                    if waited.get(k, 0) < v:
                        eng.wait_ge(sems[k], v)
                        waited[k] = v
                if o.fn is not None:
                    ins = o.fn()
                    if o.signal:
                        ins.then_inc(sems[o.sem], 16 if o.dma_key is not None else 1)
                else:
                    assert not o.signal

        @block.tensor
        def _(e):
            emit(PE, e)

        @block.scalar
        def _(e):
            emit(ACT, e)

        @block.vector
        def _(e):
            emit(DVE, e)

        @block.gpsimd
        def _(e):
            emit(POOL, e)

        @block.sync
        def _(e):
            emit(SP, e)
    return nc


NCT_FULL, NOT_FULL = 32, 8
WNAMES = ["g_ffn1", "w_ffn1_gu", "w_ffn1_down", "g_mix", "w_in", "g_q", "g_k", "g_gmlp_v", "w_spatial", "b_spatial",
          "w_branch_attn", "w_branch_gmlp", "w_out", "g_ffn2", "w_ffn2_gu", "w_ffn2_down", "g_ple", "w_ple_gate", "w_ple", "g_final"]


def _pos_table(tok_idx):
    tok_idx = np.asarray(tok_idx, np.int64)
    return np.stack([tok_idx // 64, tok_idx % 64], axis=1).astype(np.float32)


def kernel(**inputs):
    xp = np.asarray(inputs["x_prompt"], np.float32)
    xs = np.asarray(inputs["x_sample"], np.float32)
    pp = np.asarray(inputs["p_prompt"], np.float32)
    ps = np.asarray(inputs["p_sample"], np.float32)
    w = {k: np.ascontiguousarray(np.asarray(inputs[k], np.float32)) for k in WNAMES}
    w["g_final"] = w["g_final"].reshape(1, D)
    NTOK = NCT_FULL * T
    own = NOT_FULL * T
    in_maps = []
    for c in range(8):
        if c < 4:
            order = [c] + [(c + k) % 4 for k in range(1, 4)]
            xcx = np.concatenate([xp[o] for o in order], axis=0)
            posi = np.concatenate([np.arange(own)] * 4)
            msk = np.zeros(NTOK, np.float32); msk[:own] = 1.0
            pc = pp[0, c]
        else:
            q = c - 4
            order = [q] + [(q + k) % 4 for k in range(1, 4)]
            xcx = np.concatenate([xs[0, o * own:(o + 1) * own] for o in order], axis=0)
            posi = np.concatenate([np.arange(o * own, (o + 1) * own) for o in order])
            msk = np.ones(NTOK, np.float32)
            pc = ps[0, 0, q * own:(q + 1) * own]
        m = {"xc": np.ascontiguousarray(xcx), "pin": np.ascontiguousarray(pc), "pos": _pos_table(posi),
             "kmask": np.ascontiguousarray(msk.reshape(NTOK // 128, 128).T)}
        m.update(w)
        in_maps.append(m)
    nc = build_program(NCT_FULL, NOT_FULL)
    res = run_bass_kernel_spmd(nc, in_maps, core_ids=list(range(8)))
    ys = [np.asarray(res.results[c]["y"], np.float32) for c in range(8)]
    y_prompt = np.stack(ys[0:4], axis=0)
    y_sample = np.concatenate(ys[4:8], axis=0)[None]
    return (y_prompt, y_sample)
```

```python
import math
SCHEDULE = True
P0 = 255
P1 = 99
P1SUB = 99
P1T = 0
from contextlib import ExitStack
import numpy as np
import concourse.bass as bass
import concourse.mybir as mybir
from concourse.bass_utils import run_bass_kernel_spmd

F32 = mybir.dt.float32
BF16 = mybir.dt.bfloat16
I32 = mybir.dt.int32
AF = mybir.ActivationFunctionType
ALU = mybir.AluOpType
AX = mybir.AxisListType

D = 1024
DFF = 2816
NJ = DFF // 128
PLE = 256
EPS = 1e-6
T = 512
NSLOT = 3
SLOTW = 4096
PE, ACT, DVE, POOL, SP = "pe", "act", "dve", "pool", "sp"
ENGS = [PE, ACT, DVE, POOL, SP]


class Op:
    __slots__ = ("eng", "fn", "deps", "dma_key", "sem", "val", "signal", "idx", "cost", "phase", "t0", "t1")

    def __init__(self, eng, fn, deps, dma_key, cost):
        self.eng, self.fn, self.deps, self.dma_key, self.cost = eng, fn, deps, dma_key, cost
        self.sem = None
        self.val = 0
        self.signal = False


DEFAULT_COST = {PE: 216, ACT: 600, DVE: 650, POOL: 900, SP: 2500}


class Sched:
    def __init__(self):
        self.ops = []
        self.last_write = {}
        self.readers = {}
        self.phase = 0
        self.since_barrier = []
        self.cur_barrier = {}

    def op(self, eng, fn, reads=(), writes=(), dma_key=None, extra=(), cost=None):
        deps = set(extra)
        for r in reads:
            w = self.last_write.get(r)
            if w is not None:
                deps.add(w)
        for w_ in writes:
            w = self.last_write.get(w_)
            if w is not None:
                deps.add(w)
            for rd in self.readers.get(w_, ()):
                deps.add(rd)
        bar = self.cur_barrier.get(eng)
        if bar is not None:
            deps.add(bar)
        o = Op(eng, fn, deps, dma_key, DEFAULT_COST[eng] if cost is None else cost)
        o.idx = len(self.ops)
        o.phase = self.phase
        o.sem = (eng, self.phase) if dma_key is None else ("dma", dma_key)
        self.ops.append(o)
        for r in reads:
            self.readers.setdefault(r, []).append(o)
        for w_ in writes:
            self.last_write[w_] = o
            self.readers[w_] = []
        if fn is not None:
            self.since_barrier.append(o)
        return o

    def barrier(self):
        prev = list(self.since_barrier)
        self.since_barrier = []
        for e in ENGS:
            self.cur_barrier[e] = self.op(e, None, extra=prev, cost=0)
        self.phase += 1

    def schedule(self):
        import heapq
        n = len(self.ops)
        succ = [[] for _ in range(n)]
        indeg = [0] * n
        for o in self.ops:
            indeg[o.idx] = len(o.deps)
            for d in o.deps:
                succ[d.idx].append(o)
        pending = {e: [] for e in ENGS}
        avail = {e: [] for e in ENGS}
        free = {e: 0.0 for e in ENGS}
        ready_t = [0.0] * n
        for o in self.ops:
            if indeg[o.idx] == 0:
                heapq.heappush(pending[o.eng], (0.0, o.idx))
        order = []
        done = 0
        while done < n:
            best = None
            for e in ENGS:
                pe_, av = pending[e], avail[e]
                while pe_ and pe_[0][0] <= free[e]:
                    heapq.heappush(av, heapq.heappop(pe_)[1])
                if av:
                    cand = (free[e], av[0], e, True)
                elif pe_:
                    cand = (pe_[0][0], pe_[0][1], e, False)
                else:
                    continue
                if best is None or cand[:2] < best[:2]:
                    best = cand
            assert best is not None, "dependency cycle"
            start, idx, e, from_av = best
            if from_av:
                heapq.heappop(avail[e])
            else:
                heapq.heappop(pending[e])
            o = self.ops[idx]
            o.t0 = start
            if o.dma_key is not None:
                free[e] = start + (1100.0 if e == POOL else 350.0)
                o.t1 = start + o.cost
            else:
                o.t1 = start + o.cost
                free[e] = o.t1
            order.append(o)
            done += 1
            for sc in succ[idx]:
                if ready_t[sc.idx] < o.t1:
                    ready_t[sc.idx] = o.t1
                indeg[sc.idx] -= 1
                if indeg[sc.idx] == 0:
                    heapq.heappush(pending[sc.eng], (ready_t[sc.idx], sc.idx))
        self.ops = order
        self.est_ns = max(o.t1 for o in order)

    def finalize(self):
        for o in self.ops:
            for d in o.deps:
                if d.eng == PE and o.eng == PE and d.dma_key is None and o.dma_key is None:
                    continue
                if d.fn is None:
                    continue
                d.signal = True
        for o in self.ops:
            if o.dma_key is not None and o.fn is not None:
                o.signal = True
        counts = {}
        for o in self.ops:
            if o.signal:
                counts[o.sem] = counts.get(o.sem, 0) + (16 if o.dma_key is not None else 1)
                o.val = counts[o.sem]
        return sorted(counts.keys(), key=str)


def build_program(NCT, NOT, stage=3):
    NCH = NCT * 4
    NPAIR = NCH // 2
    nc = bass.Bass("TRN2", target_bir_lowering=False)

    def din(name, shape, dt=F32):
        return nc.dram_tensor(name, list(shape), dt, kind="ExternalInput").ap()

    xc = din("xc", [NCT * T, D])
    pin = din("pin", [NOT * T, PLE])
    pos = din("pos", [NCT * T, 2])
    kmask = din("kmask", [128, NCH])
    g_ffn1 = din("g_ffn1", [1, D]); w1gu = din("w_ffn1_gu", [1, D, 2 * DFF]); w1d = din("w_ffn1_down", [1, DFF, D])
    g_mix = din("g_mix", [1, D]); w_in = din("w_in", [1, D, 3840])
    g_q = din("g_q", [1, 64]); g_k = din("g_k", [1, 64]); g_gv = din("g_gmlp_v", [1, 512])
    w_sp = din("w_spatial", [1, 8, 128, 128]); b_sp = din("b_spatial", [1, 8, 128])
    w_ba = din("w_branch_attn", [1, 512, D]); w_bg = din("w_branch_gmlp", [1, 512, D]); w_out = din("w_out", [1, D, D])
    g_ffn2 = din("g_ffn2", [1, D]); w2gu = din("w_ffn2_gu", [1, D, 2 * DFF]); w2d = din("w_ffn2_down", [1, DFF, D])
    g_ple = din("g_ple", [1, D]); w_pg = din("w_ple_gate", [1, D, D]); w_pl = din("w_ple", [1, PLE, D])
    g_fin = din("g_final", [1, D])
    y = nc.dram_tensor("y", [NOT * T, D], F32, kind="ExternalOutput").ap()

    def dscr(name, shape, dt=BF16):
        return nc.dram_tensor(name, list(shape), dt).ap()

    s_gu = [dscr("s_gu1", [11, 128, 8, 512]), dscr("s_gu2", [11, 128, 8, 512])]
    s_dn = [dscr("s_dn1", [2, 128, NJ, 512]), dscr("s_dn2", [2, 128, NJ, 512])]
    s_kv = dscr("s_kv", [128, 8, 256])
    s_qgg = dscr("s_qgg", [3, 128, 8, 512])
    s_mg = dscr("s_mg", [8, 128, 3584])
    s_wo = dscr("s_wo", [2, 128, 8, 512])
    s_pg = dscr("s_pg", [2, 128, 8, 512])
    s_pl = dscr("s_pl", [128, 2, 1024])
    x1s = dscr("x1s", [NOT * T, D], F32)
    kts = dscr("kts", [128, NCT * T])
    vss = dscr("vss", [NCT * T, 130])

    S = Sched()

    def sb(name, shape, dt):
        return nc.alloc_sbuf_tensor(name, list(shape), dt)

    ident = sb("ident", [128, 128], BF16)
    gcol = sb("gcol", [128, 4, 8], F32)
    gq_bc = sb("gq_bc", [128, 64], F32)
    gk_bc = sb("gk_bc", [128, 64], F32)
    bspT = sb("bspT", [128, 8], F32)
    wsT = sb("wsT", [128, 8, 128], BF16)
    inv_bc = sb("inv_bc", [128, 16], F32)
    km = sb("km", [128, NCH], F32)
    negpi = sb("negpi", [128, 1], F32)
    epsb = sb("epsb", [128, 1], F32)
    ring = sb("ring", [128, NSLOT, SLOTW], BF16)
    Xb = [sb("X0", [128, 4, D], F32), sb("X1", [128, 4, D], F32)]
    hb = sb("hb", [128, 4, D], BF16)
    hT = sb("hT", [128, 8, T], BF16)
    ss = sb("ss", [128, 8], F32)
    rstd = sb("rstd", [128, 8], F32)
    tmpA = sb("tmpA", [128, T], F32)[:, :]
    tmpB = sb("tmpB", [128, T], F32)[:, :]
    posb = sb("posb", [128, 4, 2], F32)
    ang = sb("ang", [128, 4, 2, 16], F32)
    angm = sb("angm", [128, 4, 2, 16], F32)
    cs = sb("cs", [128, 4, 32], F32)
    sn = sb("sn", [128, 4, 32], F32)
    angki = sb("angki", [128, 4, 32], I32)
    angkf = sb("angkf", [128, 4, 32], F32)
    angr = sb("angr", [128, 4, 32], F32)
    hss = sb("hss", [128, 32], F32)
    hrs = sb("hrs", [128, 32], F32)
    ARENA_BYTES = 123 * 1024
    arena = sb("arena", [128, ARENA_BYTES // 4], F32)
    apos = [0]

    def carve(shape, dt):
        esz = 4 if dt in (F32, I32) else 2
        n = int(np.prod(shape[1:]))
        nbytes = (n * esz + 31) // 32 * 32
        off = apos[0]
        apos[0] += nbytes
        assert apos[0] <= ARENA_BYTES, (apos[0], ARENA_BYTES)
        v = arena[:, off // 4:(off + nbytes) // 4]
        if esz == 2:
            v = v.bitcast(BF16)
        v = v[:, 0:n]
        if len(shape) == 3:
            v = v.rearrange("p (a b) -> p a b", a=shape[1])
        elif len(shape) == 4:
            v = v.rearrange("p (a b c) -> p a b c", a=shape[1], b=shape[2])
        return v

    tp = nc.alloc_psum_tensor("tp", [128, 2048], BF16)
    S0 = nc.alloc_psum_tensor("S0", [128, 1024], F32)
    S1 = nc.alloc_psum_tensor("S1", [128, 1024], F32)
    O0 = nc.alloc_psum_tensor("O0", [128, 512], F32)
    O1 = nc.alloc_psum_tensor("O1", [128, 512], F32)
    G = [S0[:, 0:512], S0[:, 512:1024], S1[:, 0:512], S1[:, 512:1024], O0[:, :], O1[:, :]]
    GN = ["G0", "G1", "G2", "G3", "G4", "G5"]
    tpf = tp[:, :].bitcast(F32)

    def vec(e):
        return nc.vector if e == DVE else nc.gpsimd

    deferred_casts = []

    def cast(key, out_ap, in_ap):
        o_ = S.op(POOL, lambda o=out_ap, i=in_ap: nc.gpsimd.dma_start(out=o, in_=i),
                  writes=[("scr", key)], dma_key="c_" + key, cost=9000)
        if not (key.startswith("gu0") or key.startswith("dn0") or key == "kv"):
            deferred_casts.append(o_)
        return o_

    def small_load(out_ap, in_ap, res):
        def f():
            with nc.allow_non_contiguous_dma(reason="tiny constant layout load"):
                return nc.sync.dma_start(out=out_ap, in_=in_ap)
        return S.op(SP, f, writes=[res], dma_key="k_" + str(res).replace("'", "").replace(" ", ""))

    def bc_row(ap2d):
        return ap2d.partition_broadcast(128).rearrange("p o d -> p (o d)")

    gl = sb("gl", [40, 128], F32)
    identf = sb("identf", [40, 40], F32)
    for i, g in enumerate([g_ffn1, g_mix, g_ffn2, g_ple]):
        S.op(SP, lambda i=i, g=g: nc.sync.dma_start(out=gl[i * 8:(i + 1) * 8, :], in_=g.rearrange("o (kc p) -> (o kc) p", p=128)),
             writes=[("gl", i)], dma_key="k_gl%d" % i)
    S.op(SP, lambda: nc.sync.dma_start(out=gl[32:40, :], in_=b_sp[0]), writes=[("gl", 4)], dma_key="k_gl4")

    S.op(POOL, lambda: nc.gpsimd.memset(identf[:], 0.0), writes=["identf"])
    S.op(POOL, lambda: nc.gpsimd.affine_select(out=identf[:], in_=identf[:], pattern=[[-1, 40]], compare_op=ALU.not_equal,
                                               fill=1.0, base=0, channel_multiplier=1), reads=["identf"], writes=["identf"])
    S.op(PE, lambda: nc.tensor.matmul(G[0][:, 0:40], lhsT=gl[0:40, :], rhs=identf[0:40, 0:40], start=True, stop=True),
         reads=[("gl", i) for i in range(5)] + ["identf"], writes=[GN[0]], cost=400)
    S.op(DVE, lambda: nc.vector.tensor_copy(out=gcol[:, :, :].rearrange("p g k -> p (g k)"), in_=G[0][:, 0:32]),
         reads=[GN[0]], writes=[("gcol", i) for i in range(4)], cost=200)
    S.op(DVE, lambda: nc.vector.tensor_copy(out=bspT[:, :], in_=G[0][:, 32:40]), reads=[GN[0]], writes=["bsp"], cost=200)
    if P0 & 2:
        small_load(gq_bc[:, :], bc_row(g_q), "gq")
        small_load(gk_bc[:, :], bc_row(g_k), "gk")
    small_load(km[:, :], kmask[:, :], "km")

    S.op(POOL, lambda: nc.gpsimd.memset(ident[:], 0.0), writes=["ident"])
    S.op(POOL, lambda: nc.gpsimd.affine_select(out=ident[:], in_=ident[:], pattern=[[-1, 128]], compare_op=ALU.not_equal,
                                               fill=1.0, base=0, channel_multiplier=1), reads=["ident"], writes=["ident"])
    S.op(POOL, lambda: nc.gpsimd.memset(negpi[:], -math.pi), writes=["negpi"])
    S.op(POOL, lambda: nc.gpsimd.memset(epsb[:], EPS), writes=["epsb"])
    S.op(POOL, lambda: nc.gpsimd.iota(out=cs[:, 0, 0:16].bitcast(I32), pattern=[[1, 16]], base=0, channel_multiplier=0),
         writes=["cs"])
    S.op(POOL, lambda: nc.gpsimd.tensor_copy(out=sn[:, 0, 0:16], in_=cs[:, 0, 0:16].bitcast(I32)), reads=["cs"], writes=["sn"])
    S.op(ACT, lambda: nc.scalar.activation(out=inv_bc[:, :], in_=sn[:, 0, 0:16], func=AF.Exp, scale=-math.log(10000.0) / 16.0),
         reads=["sn"], writes=["inv"])

    S.op(SP, lambda: nc.sync.dma_start(out=Xb[0][:, 0, :].rearrange("p (g q) -> p g q", g=8),
                                       in_=w_sp[0].rearrange("g p q -> p g q")), writes=[("X0", 0)], dma_key="const")
    S.op(DVE, lambda: nc.vector.tensor_copy(out=hb[:, 0, :], in_=Xb[0][:, 0, :]), reads=[("X0", 0)], writes=["hb"])
    for g in range(8):
        S.op(PE, lambda g=g: nc.tensor.transpose(out=tp[:, g * 128:(g + 1) * 128], in_=hb[:, 0, g * 128:(g + 1) * 128], identity=ident[:]),
             reads=["hb", "ident"], writes=[("tpb", 0)], cost=118)
    S.op(DVE, lambda: nc.vector.tensor_copy(out=wsT[:, :, :].rearrange("q g p -> q (g p)"), in_=tp[:, 0:1024]),
         reads=[("tpb", 0)], writes=["wsT"])

    def kcp(ap2d):
        return ap2d.rearrange("(kc p) c -> p kc c", p=128)

    def cast_ffn(idx, wgu, wd):
        for i in range(11):
            cast("gu%d_%d" % (idx, i), s_gu[idx][i, :, :, 0:256], kcp(wgu[0, :, 256 * i:256 * i + 256]))
            cast("gu%d_%d" % (idx, i), s_gu[idx][i, :, :, 256:512], kcp(wgu[0, :, DFF + 256 * i:DFF + 256 * i + 256]))
        for h in range(2):
            cast("dn%d_%d" % (idx, h), s_dn[idx][h], kcp(wd[0, :, 512 * h:512 * h + 512]))

    cast_ffn(0, w1gu, w1d)
    cast("kv", s_kv, kcp(w_in[0, :, 512:768]))
    for g in range(2):
        for c in range(4):
            cast("qgg0", s_qgg[0, :, :, c * 128 + g * 64:c * 128 + g * 64 + 64], kcp(w_in[0, :, (g * 4 + c) * 64:(g * 4 + c) * 64 + 64]))
    for i, c0 in ((1, 768), (2, 1280)):
        cast("qgg%d" % i, s_qgg[i], kcp(w_in[0, :, c0:c0 + 512]))
    for oc in range(8):
        k = "mg%d" % oc
        gts = s_mg[oc, :, 0:2048].rearrange("p (kc c) -> p kc c", kc=8)
        cast(k, gts[:, :, 0:128], kcp(w_in[0, :, 1792 + oc * 128:1792 + oc * 128 + 128]))
        cast(k, gts[:, :, 128:256], kcp(w_in[0, :, 2816 + oc * 128:2816 + oc * 128 + 128]))
        cast(k, s_mg[oc, :, 2048:2560].rearrange("p (kc c) -> p kc c", kc=4), kcp(w_bg[0, :, oc * 128:oc * 128 + 128]))
        cast(k, s_mg[oc, 0:64, 2560:3584].rearrange("p (h c) -> p h c", h=8),
             w_ba[0, :, oc * 128:oc * 128 + 128].rearrange("(h p) c -> p h c", p=64))
    for h in range(2):
        cast("wo%d" % h, s_wo[h], kcp(w_out[0, :, 512 * h:512 * h + 512]))
    cast_ffn(1, w2gu, w2d)
    for h in range(2):
        cast("pg%d" % h, s_pg[h], kcp(w_pg[0, :, 512 * h:512 * h + 512]))
    cast("pl", s_pl, kcp(w_pl[0, :, :]))

    ring_n = [0]

    def ring_load(key, src_ap, width):
        slot = ring_n[0] % NSLOT
        ring_n[0] += 1
        S.op(SP, lambda s=slot, a=src_ap, w=width: nc.sync.dma_start(out=ring[:, s, 0:w], in_=a),
             reads=[("scr", key)], writes=[("ring", slot), ("ringb", slot)], dma_key="ring%d" % slot, cost=2000 + width * 256 // 180)
        return slot

    def ring_load_mg(oc):
        slot = ring_n[0] % NSLOT
        ring_n[0] += 1
        prior = list(S.readers.get(("ring", slot), []))
        if S.last_write.get(("ring", slot)) is not None:
            prior.append(S.last_write[("ring", slot)])
        S.op(SP, lambda s=slot: nc.sync.dma_start(out=ring[:, s, 0:2560], in_=s_mg[oc, :, 0:2560]),
             reads=[("scr", "mg%d" % oc)], writes=[("ring", slot)], dma_key="ring%d" % slot, cost=5600)
        S.op(SP, lambda s=slot: nc.sync.dma_start(out=ring[0:64, s, 2560:3584], in_=s_mg[oc, 0:64, 2560:3584]),
             reads=[("scr", "mg%d" % oc)], writes=[("ringb", slot)], dma_key="ringb%d" % slot, cost=3500, extra=prior)
        return slot

    def transpose_T(src, srcres, nkc, outT, outres, gi=None):
        for k0 in range(0, nkc, 2):
            bank = (k0 // 2) % 2
            for kk in range(2):
                kc = k0 + kk
                for s in range(4):
                    col = bank * 1024 + kk * 512 + s * 128
                    S.op(PE, lambda kc=kc, s=s, col=col: nc.tensor.transpose(out=tp[:, col:col + 128],
                                                                            in_=src[:, s, kc * 128:(kc + 1) * 128], identity=ident[:]),
                         reads=[srcres, "ident"], writes=[("tpb", bank)], cost=118)
            for kk in range(2):
                kc = k0 + kk
                e = ACT if bank == 0 else DVE
                src_ps = tp[:, bank * 1024 + kk * 512: bank * 1024 + (kk + 1) * 512]
                if gi is None:
                    if e == ACT:
                        f = lambda kc=kc, src_ps=src_ps: nc.scalar.copy(out=outT[:, kc, :], in_=src_ps)
                    else:
                        f = lambda kc=kc, src_ps=src_ps: nc.vector.tensor_copy(out=outT[:, kc, :], in_=src_ps)
                    rd = [("tpb", bank)]
                else:
                    if e == ACT:
                        f = lambda kc=kc, src_ps=src_ps: nc.scalar.activation(out=outT[:, kc, :], in_=src_ps, func=AF.Copy,
                                                                               scale=gcol[:, gi, kc:kc + 1])
                    else:
                        f = lambda kc=kc, src_ps=src_ps: nc.vector.tensor_scalar(out=outT[:, kc, :], in0=src_ps,
                                                                                  scalar1=gcol[:, gi, kc:kc + 1], scalar2=None, op0=ALU.mult)
                    rd = [("tpb", bank), ("gcol", gi)]
                S.op(e, f, reads=rd, writes=[(outres, kc)])

    def row_rstd(X, xres, width, nrm, hbuf=None, hbres="hb", c0=0):
        hbuf = hb if hbuf is None else hbuf
        for s in range(4):
            S.op(ACT, lambda s=s: nc.scalar.activation(out=hbuf[:, s, 0:width], in_=X[:, s, 0:width], func=AF.Square, accum_out=ss[:, c0 + s:c0 + s + 1]),
                 reads=[(xres, s)], writes=[hbres, ("ss", c0 + s)], cost=1056 if width > 512 else 843)
        S.op(ACT, lambda: nc.scalar.activation(out=rstd[:, c0:c0 + 4], in_=ss[:, c0:c0 + 4], func=AF.Sqrt, scale=1.0 / nrm, bias=epsb[:, 0:1]),
             reads=[("ss", c0 + s) for s in range(4)] + ["epsb"], writes=[("rstd", c0)], cost=300)
        S.op(DVE, lambda: nc.vector.reciprocal(out=rstd[:, c0:c0 + 4], in_=rstd[:, c0:c0 + 4]), reads=[("rstd", c0)], writes=[("rstd", c0)], cost=190)

    def rmsnorm_T(X, xres, gi, alt=None):
        hbuf, hbres, outT, outres = (hb, "hb", hT, "hT") if alt is None else alt
        c0 = 0 if alt is None else 4
        row_rstd(X, xres, D, D, hbuf, hbres, c0)
        for s in range(4):
            S.op(DVE, lambda s=s: nc.vector.tensor_scalar(out=hbuf[:, s, :], in0=X[:, s, :], scalar1=rstd[:, c0 + s:c0 + s + 1], scalar2=None, op0=ALU.mult),
                 reads=[(xres, s), ("rstd", c0)], writes=[hbres])
        transpose_T(hbuf, hbres, 8, outT, outres, gi)

    def ffn(idx, X, xres, hid, dn, mid=None):
        for i in range(11):
            slot = ring_load("gu%d_%d" % (idx, i), s_gu[idx][i].rearrange("p kc c -> p (kc c)"), 4096)
            rv = ring[:, slot, :].rearrange("p (kc c) -> p kc c", kc=8)
            for jj in range(2):
                j = 2 * i + jj
                ga, gb = (0, 1) if j % 2 == 0 else (2, 3)
                for kc in range(8):
                    S.op(PE, lambda kc=kc, jj=jj, ga=ga, rv=rv: nc.tensor.matmul(G[ga], lhsT=rv[:, kc, jj * 128:(jj + 1) * 128], rhs=hT[:, kc, :],
                                                                                  start=(kc == 0), stop=(kc == 7)),
                         reads=[("ring", slot), ("hT", kc)], writes=[GN[ga]])
                for kc in range(8):
                    S.op(PE, lambda kc=kc, jj=jj, gb=gb, rv=rv: nc.tensor.matmul(G[gb], lhsT=rv[:, kc, 256 + jj * 128:256 + (jj + 1) * 128], rhs=hT[:, kc, :],
                                                                                  start=(kc == 0), stop=(kc == 7)),
                         reads=[("ring", slot), ("hT", kc)], writes=[GN[gb]])
                tt, tn = (tmpA, "tmpA") if j % 2 == 0 else (tmpB, "tmpB")
                S.op(ACT, lambda ga=ga, tt=tt: nc.scalar.activation(out=tt[:, :], in_=G[ga], func=AF.Silu), reads=[GN[ga]], writes=[tn])
                S.op(DVE, lambda j=j, gb=gb, tt=tt: nc.vector.tensor_tensor(out=hid[:, j, :], in0=tt[:, :], in1=G[gb], op=ALU.mult),
                     reads=[tn, GN[gb]], writes=[("hid", j)])
        if mid is not None:
            mid()
        for h in range(2):
            S.op(SP, lambda h=h: nc.sync.dma_start(out=dn[h], in_=s_dn[idx][h]),
                 reads=[("scr", "dn%d_%d" % (idx, h))], writes=[("dn", h)], dma_key="dn%d" % h, cost=18000)
            for s in range(4):
                b = 4 + (s % 2)
                for j in range(NJ):
                    S.op(PE, lambda j=j, s=s, h=h, b=b: nc.tensor.matmul(G[b], lhsT=hid[:, j, s * 128:(s + 1) * 128], rhs=dn[h][:, j, :],
                                                                         start=(j == 0), stop=(j == NJ - 1)),
                         reads=[("hid", j), ("dn", h)], writes=[GN[b]])
                S.op(DVE, lambda s=s, h=h, b=b: nc.vector.scalar_tensor_tensor(out=X[:, s, h * 512:(h + 1) * 512], in0=G[b], scalar=0.5,
                                                                               in1=X[:, s, h * 512:(h + 1) * 512], op0=ALU.mult, op1=ALU.add),
                     reads=[GN[b], (xres, s)], writes=[(xres, s)])

    def rope_tables(t, nh, tabc, tabs, tabres):
        S.op(SP, lambda: nc.sync.dma_start(out=posb[:, :, :], in_=pos[t * T:(t + 1) * T, :].rearrange("(s p) a -> p s a", p=128)),
             writes=["posb"], dma_key="posb")
        for a in range(2):
            S.op(DVE, lambda a=a: nc.vector.tensor_tensor(out=ang[:, :, a, :], in0=posb[:, :, a:a + 1].to_broadcast([128, 4, 16]),
                                                           in1=inv_bc[:, :].unsqueeze(1).to_broadcast([128, 4, 16]), op=ALU.mult),
                 reads=["posb", "inv"], writes=["ang"], cost=200)
        angf = ang[:, :, :, :].rearrange("p s a f -> p s (a f)")
        angmf = angm[:, :, :, :].rearrange("p s a f -> p s (a f)")
        TWO_PI = 2.0 * math.pi

        def sin_of(dst, shift):
            S.op(DVE, lambda: nc.vector.tensor_scalar(out=angmf, in0=angf, scalar1=shift, scalar2=1.0 / TWO_PI, op0=ALU.add, op1=ALU.mult),
                 reads=["ang"], writes=["angm"])
            S.op(DVE, lambda: nc.vector.tensor_copy(out=angki[:, :, :], in_=angmf), reads=["angm"], writes=["angki"])
            S.op(DVE, lambda: nc.vector.tensor_copy(out=angkf[:, :, :], in_=angki[:, :, :]), reads=["angki"], writes=["angkf"])
            S.op(DVE, lambda: nc.vector.tensor_scalar(out=angmf, in0=angf, scalar1=shift, scalar2=None, op0=ALU.add),
                 reads=["ang", "angki"], writes=["angm"])
            S.op(DVE, lambda: nc.vector.scalar_tensor_tensor(out=angr[:, :, :], in0=angkf[:, :, :], scalar=-TWO_PI, in1=angmf, op0=ALU.mult, op1=ALU.add),
                 reads=["angkf", "angm"], writes=["angr"])
            S.op(DVE, lambda: nc.vector.tensor_scalar(out=angmf, in0=angr[:, :, :], scalar1=math.pi, scalar2=TWO_PI, op0=ALU.is_gt, op1=ALU.mult),
                 reads=["angr"], writes=["angm"])
            S.op(DVE, lambda: nc.vector.tensor_tensor(out=angr[:, :, :], in0=angr[:, :, :], in1=angmf, op=ALU.subtract),
                 reads=["angr", "angm"], writes=["angr"])
            S.op(ACT, lambda: nc.scalar.activation(out=dst[:, :, :], in_=angr[:, :, :], func=AF.Sin), reads=["angr"], writes=[("cs" if dst is cs else "sn")])

        sin_of(sn, 0.0)
        sin_of(cs, 0.5 * math.pi)
        S.op(POOL, lambda: nc.gpsimd.tensor_copy(out=tabc, in_=cs[:, :, :].unsqueeze(2).to_broadcast([128, 4, nh, 32])),
             reads=["cs"], writes=[tabres + "c"])
        S.op(POOL, lambda: nc.gpsimd.tensor_copy(out=tabs, in_=sn[:, :, :].unsqueeze(2).to_broadcast([128, 4, nh, 32])),
             reads=["sn"], writes=[tabres + "s"])

    def head_norm_rope(e, src, srcres, nh, gbc, gres, tabc, tabs, tabres, sq, sqres, dst, dstres):
        V = vec(e)
        SH = 4 * nh
        cb = 250 + SH * 64 * (0.9 if e == POOL else 0.55)
        ch = 250 + SH * 32 * (0.9 if e == POOL else 0.55)
        x3 = src.rearrange("p s (h d) -> p (s h) d", h=nh)
        sq3 = sq.rearrange("p s (h d) -> p (s h) d", h=nh)
        S.op(e, lambda: V.tensor_tensor(out=sq3, in0=x3, in1=x3, op=ALU.mult), reads=[srcres], writes=[sqres], cost=cb)
        S.op(DVE, lambda: nc.vector.tensor_reduce(out=hss[:, 0:SH], in_=sq3, axis=AX.X, op=ALU.add), reads=[sqres], writes=["hss"], cost=250 + SH * 64 * 0.55)
        S.op(ACT, lambda: nc.scalar.activation(out=hrs[:, 0:SH], in_=hss[:, 0:SH], func=AF.Sqrt, scale=1.0 / 64, bias=epsb[:, 0:1]),
             reads=["hss", "epsb"], writes=["hrs"], cost=300)
        S.op(DVE, lambda: nc.vector.reciprocal(out=hrs[:, 0:SH], in_=hrs[:, 0:SH]), reads=["hrs"], writes=["hrs"], cost=190)
        S.op(e, lambda: V.tensor_tensor(out=x3, in0=x3, in1=hrs[:, 0:SH].unsqueeze(2).to_broadcast([128, SH, 64]), op=ALU.mult),
             reads=[srcres, "hrs"], writes=[srcres], cost=cb)
        S.op(e, lambda: V.tensor_tensor(out=x3, in0=x3, in1=gbc[:, :].unsqueeze(1).to_broadcast([128, SH, 64]), op=ALU.mult),
             reads=[srcres, gres], writes=[srcres], cost=cb)
        pat = "p s (h a r f) -> p (s h) a r f"
        x5 = src.rearrange(pat, h=nh, a=2, r=2)
        q5 = sq.rearrange(pat, h=nh, a=2, r=2)
        d5 = dst.rearrange(pat, h=nh, a=2, r=2)
        xa, xb_ = x5[:, :, :, 0, :], x5[:, :, :, 1, :]
        ta, tb_ = q5[:, :, :, 0, :], q5[:, :, :, 1, :]
        c4 = tabc.rearrange("p s h (a f) -> p (s h) a f", a=2)
        s4 = tabs.rearrange("p s h (a f) -> p (s h) a f", a=2)
        oa, ob = d5[:, :, :, 0, :], d5[:, :, :, 1, :]
        S.op(e, lambda: V.tensor_tensor(out=ta, in0=xb_, in1=s4, op=ALU.mult), reads=[srcres, tabres + "s"], writes=[sqres], cost=ch)
        S.op(e, lambda: V.tensor_tensor(out=tb_, in0=xa, in1=s4, op=ALU.mult), reads=[srcres, tabres + "s"], writes=[sqres], cost=ch)
        S.op(e, lambda: V.tensor_tensor(out=xa, in0=xa, in1=c4, op=ALU.mult), reads=[srcres, sqres, tabres + "c"], writes=[srcres], cost=ch)
        S.op(e, lambda: V.tensor_tensor(out=xb_, in0=xb_, in1=c4, op=ALU.mult), reads=[srcres, sqres, tabres + "c"], writes=[srcres], cost=ch)
        S.op(e, lambda: V.tensor_tensor(out=oa, in0=xa, in1=ta, op=ALU.subtract), reads=[srcres, sqres], writes=[dstres], cost=ch)
        S.op(e, lambda: V.tensor_tensor(out=ob, in0=xb_, in1=tb_, op=ALU.add), reads=[srcres, sqres], writes=[dstres], cost=ch)

    def XR(xres):
        return [(xres, s) for s in range(4)]

    def tok_rows(ap2d):
        return ap2d.rearrange("(s p) d -> p s d", p=128)

    dbg = {}

    apos[0] = 0
    hid = carve([128, NJ, T], BF16)
    dn = [carve([128, NJ, 512], BF16), carve([128, NJ, 512], BF16)]
    kvw = carve([128, 8, 256], BF16)
    kvs = carve([128, 2, 4, 128], F32)
    ksq = carve([128, 4, 128], F32)
    tkc = carve([128, 4, 2, 32], F32)
    tks = carve([128, 4, 2, 32], F32)
    krb = carve([128, 4, 128], BF16)
    kTb = [carve([128, T], BF16), carve([128, T], BF16)]
    vsb = [carve([128, 4, 130], BF16), carve([128, 4, 130], BF16)]
    hb2 = carve([128, 4, D], BF16)
    hT2 = carve([128, 8, T], BF16)
    ALT = (hb2, "hb2", hT2, "hT2")

    S.op(SP, lambda: nc.sync.dma_start(out=kvw, in_=s_kv), reads=[("scr", "kv")], writes=["kvw"], dma_key="kvw")

    for t in range(NCT if stage >= 1 else 0):
        b = t % 2
        X, xres = Xb[b], "X%d" % b
        def prep1(tt):
            bb = tt % 2
            ld = S.op(SP, lambda bb=bb, tt=tt: nc.sync.dma_start(out=Xb[bb][:, :, :], in_=tok_rows(xc[tt * T:(tt + 1) * T, :])),
                      writes=XR("X%d" % bb), dma_key="xl%d" % bb, cost=13600)
            if tt == min(2, NCT - 1):
                for dc in deferred_casts:
                    dc.deps.add(ld)
            rmsnorm_T(Xb[bb], "X%d" % bb, 0)
        if t == 0:
            prep1(0)
        ffn(0, X, xres, hid, dn, mid=(lambda t=t: prep1(t + 1)) if t + 1 < NCT else None)
        if t < NOT:
            S.op(SP, lambda b=b, t=t: nc.sync.dma_start(out=tok_rows(x1s[t * T:(t + 1) * T, :]), in_=Xb[b][:, :, :]),
                 reads=XR(xres), writes=[("x1s", t, s) for s in range(4)], dma_key="xs%d" % b, cost=13600)
        if P1 < 3:
            continue
        rmsnorm_T(X, xres, 1, ALT)
        if P1 < 4:
            continue
        rope_tables(t, 2, tkc, tks, "tk")
        if P1 < 5:
            continue
        for s in range(4):
            gb_ = s % 2
            for kc in range(8):
                S.op(PE, lambda kc=kc, s=s, gb_=gb_: nc.tensor.matmul(G[gb_][:, 0:256], lhsT=hT2[:, kc, s * 128:(s + 1) * 128], rhs=kvw[:, kc, :],
                                                                       start=(kc == 0), stop=(kc == 7)),
                     reads=[("hT2", kc), "kvw"], writes=[GN[gb_]], cost=200)
            S.op(ACT, lambda s=s, gb_=gb_: nc.scalar.copy(out=kvs[:, :, s, :], in_=G[gb_][:, 0:256].rearrange("p (a d) -> p a d", a=2)), reads=[GN[gb_]], writes=["kvs"])
        if P1 < 6:
            continue
        vb, vres = vsb[b], "vsb%d" % b
        kmt = km[:, t * 4:(t + 1) * 4]
        vb4 = vb.rearrange("p s (h e) -> p s h e", h=2)
        S.op(POOL, lambda vb4=vb4, kmt=kmt: nc.gpsimd.tensor_tensor(
            out=vb4[:, :, :, 0:64], in0=kvs[:, 1, :, :].rearrange("p s (h d) -> p s h d", h=2),
            in1=kmt.unsqueeze(2).unsqueeze(3).to_broadcast([128, 4, 2, 64]), op=ALU.mult),
            reads=["kvs", "km"], writes=[vres])
        S.op(POOL, lambda vb4=vb4, kmt=kmt: nc.gpsimd.tensor_copy(out=vb4[:, :, :, 64], in_=kmt.unsqueeze(2).to_broadcast([128, 4, 2])),
             reads=["km"], writes=[vres])
        S.op(SP, lambda vb=vb, t=t: nc.sync.dma_start(out=vss[t * T:(t + 1) * T, :].rearrange("(s p) e -> p s e", p=128), in_=vb),
             reads=[vres], writes=[("vss", t)], dma_key="vst%d" % b)
        if P1 < 7:
            continue
        head_norm_rope(POOL, kvs[:, 0, :, :], "kvs", 2, gk_bc, "gk", tkc, tks, "tk", ksq, "ksq", krb, "krb")
        for s in range(4):
            S.op(PE, lambda s=s: nc.tensor.transpose(out=tp[:, s * 128:(s + 1) * 128], in_=krb[:, s, :], identity=ident[:]),
                 reads=["krb", "ident"], writes=[("tpb", 0)], cost=118)
        kb_, kres = kTb[b], "kTb%d" % b
        S.op(DVE, lambda kb_=kb_: nc.vector.tensor_copy(out=kb_, in_=tp[:, 0:512]), reads=[("tpb", 0)], writes=[kres])
        S.op(SP, lambda kb_=kb_, t=t: nc.sync.dma_start(out=kts[:, t * T:(t + 1) * T], in_=kb_),
             reads=[kres], writes=[("kts", t)], dma_key="kst%d" % b)

    S.barrier()
    apos[0] = 0
    kT = carve([128, NCT * T], BF16)
    vAf = carve([128, NCH * 130 + 64], BF16)
    vA = vAf[:, 0:NCH * 130].rearrange("p (ch e) -> p ch e", e=130)
    qf = carve([128, 4, 512], F32)
    qsq = hb[:, :, :].rearrange("p s d -> p (s d)").bitcast(F32).rearrange("p (s d) -> p s d", s=4)
    tqc = carve([128, 4, 8, 32], F32)
    tqs = carve([128, 4, 8, 32], F32)
    qrb = carve([128, 4, 512], BF16)
    qTp = Xb[1][:, 0:2, :].rearrange("p a d -> p (a d)").bitcast(BF16).rearrange("p (h t) -> p h t", h=8)
    ub = carve([128, 4, 512], BF16)
    vnb = tqc.rearrange("p s h f -> p (s h f)").bitcast(BF16)[:, 0:2048].rearrange("p (s d) -> p s d", s=4)
    sgb = tqs.rearrange("p s h f -> p (s h f)").bitcast(BF16)[:, 0:2048].rearrange("p (s d) -> p s d", s=4)
    sgT = carve([128, 4, T], BF16)
    aT = carve([128, 8, T], BF16)
    mT = carve([128, 8, T], BF16)
    PT = [carve([128, 1024], BF16), carve([128, 1024], BF16), carve([128, 1024], BF16)]
    rden = carve([128, T], F32)
    ones1 = carve([128, 64], F32)
    onT = carve([128, T], F32)
    ggv_bc = carve([128, 512], F32)
    S.op(POOL, lambda: nc.gpsimd.memset(ones1, 1.0), writes=["ones1"])
    S.op(POOL, lambda: nc.gpsimd.memset(qTp, 0.0), writes=[("qT", c) for c in range(4)])
    S.op(POOL, lambda: nc.gpsimd.memset(vAf[:, NCH * 130:NCH * 130 + 64], 0.0), writes=["vApad"])
    small_load(ggv_bc, bc_row(g_gv), "ggv")

    for c in range(0, NCT if stage >= 2 else 0, 8):
        n = min(8, NCT - c)
        S.op(SP, lambda c=c, n=n: nc.sync.dma_start(out=kT[:, c * T:(c + n) * T], in_=kts[:, c * T:(c + n) * T]),
             reads=[("kts", t) for t in range(c, c + n)], writes=["kT"], dma_key="kTl", cost=9000)
        S.op(SP, lambda c=c, n=n: nc.sync.dma_start(out=vA[:, c * 4:(c + n) * 4, :],
                                                    in_=vss[c * T:(c + n) * T, :].rearrange("(ch p) e -> p ch e", p=128)),
             reads=[("vss", t) for t in range(c, c + n)], writes=["vA"], dma_key="vAl", cost=15000)

    def attention():
        heads = [(c, g) for c in range(4) for g in range(2)]
        seq = [(hi, i) for hi in range(8) for i in range(NPAIR)]
        Sv = [S0[:, :], S1[:, :], tpf]
        Sres = [[GN[0], GN[1]], [GN[2], GN[3]], [("tpb", 0), ("tpb", 1)]]

        def qk(n):
            hi, i = seq[n]
            c, g = heads[hi]
            sb_ = n % 3
            Sx = Sv[sb_]
            for u in range(2):
                ch = 2 * i + u
                S.op(PE, lambda u=u, ch=ch, c=c, g=g, Sx=Sx: nc.tensor.matmul(
                    Sx[:, u * 512:(u + 1) * 512], lhsT=kT[:, ch * 128:(ch + 1) * 128],
                    rhs=qTp[:, c * 2 + g, :], start=True, stop=True),
                    reads=["kT", ("qT", c)], writes=[Sres[sb_][u]])

        qk(0)
        qk(1)
        for n in range(len(seq)):
            hi, i = seq[n]
            c, g = heads[hi]
            sb_ = n % 3
            Sx = Sv[sb_]
            ob = hi % 2
            Ox = (O0, O1)[ob]
            if n + 2 < len(seq):
                qk(n + 2)
            S.op(ACT, lambda Sx=Sx, sb_=sb_: nc.scalar.activation(out=PT[sb_], in_=Sx, func=AF.Exp, scale=0.125),
                 reads=Sres[sb_], writes=["PT%d" % sb_], cost=1023)
            for u in range(2):
                ch = 2 * i + u
                S.op(PE, lambda u=u, ch=ch, g=g, Ox=Ox, sb_=sb_, i=i: nc.tensor.matmul(
                    Ox[:, :], lhsT=vAf[:, ch * 130 + g * 65:ch * 130 + g * 65 + 128], rhs=PT[sb_][:, u * 512:(u + 1) * 512],
                    start=(i == 0 and u == 0), stop=(i == NPAIR - 1 and u == 1)),
                    reads=["vA", "vApad", "PT%d" % sb_], writes=[GN[4 + ob]])
            if i == NPAIR - 1:
                h_true = g * 4 + c
                S.op(DVE, lambda Ox=Ox: nc.vector.reciprocal(out=rden[64:65, :], in_=Ox[64:65, :]), reads=[GN[4 + ob]], writes=["rden"], cost=2472)
                S.op(PE, lambda Sx=Sx: nc.tensor.matmul(Sx[0:64, 0:512], lhsT=ones1[64:65, 0:64], rhs=rden[64:65, :], start=True, stop=True),
                     reads=["rden", "ones1"], writes=[Sres[sb_][0]], cost=970)
                S.op(ACT, lambda Sx=Sx: nc.scalar.copy(out=onT[0:64, :], in_=Sx[0:64, 0:512]), reads=[Sres[sb_][0]], writes=["onT"])
                S.op(DVE, lambda Ox=Ox, h_true=h_true: nc.vector.tensor_tensor(out=aT[0:64, h_true, :], in0=Ox[0:64, :], in1=onT[0:64, :], op=ALU.mult),
                     reads=[GN[4 + ob], "onT"], writes=[("aT", h_true)])

    for t in range(NOT if stage >= 2 else 0):
        b = 0
        X, xres = Xb[b], "X%d" % b
        for s in range(4):
            S.op(SP, lambda s=s, t=t: nc.sync.dma_start(out=Xb[0][:, s, :], in_=x1s[t * T + s * 128:t * T + (s + 1) * 128, :]),
                 reads=[("x1s", t, s)], writes=[(xres, s)], dma_key="xa%d" % s, cost=5000)
        rmsnorm_T(X, xres, 1)
        rope_tables(t, 8, tqc, tqs, "tq")
        for pi in range(3):
            slot = ring_load("qgg%d" % pi, s_qgg[pi].rearrange("p kc c -> p (kc c)"), 4096)
            rv = ring[:, slot, :].rearrange("p (kc c) -> p kc c", kc=8)
            for s in range(4):
                gb_ = s % 2
                for kc in range(8):
                    S.op(PE, lambda kc=kc, s=s, gb_=gb_, rv=rv: nc.tensor.matmul(G[gb_], lhsT=hT[:, kc, s * 128:(s + 1) * 128], rhs=rv[:, kc, :],
                                                                                  start=(kc == 0), stop=(kc == 7)),
                         reads=[("hT", kc), ("ring", slot)], writes=[GN[gb_]])
                if pi == 0:
                    S.op(ACT, lambda s=s, gb_=gb_: nc.scalar.copy(out=qf[:, s, :], in_=G[gb_]), reads=[GN[gb_]], writes=["qf"])
                elif pi == 1:
                    S.op(ACT, lambda s=s, gb_=gb_: nc.scalar.activation(out=ub[:, s, :], in_=G[gb_], func=AF.Gelu), reads=[GN[gb_]], writes=["ub"])
                else:
                    S.op(ACT, lambda s=s, gb_=gb_: nc.scalar.activation(out=qf[:, s, :], in_=G[gb_], func=AF.Gelu), reads=[GN[gb_]], writes=["qf"])
            if pi == 0:
                head_norm_rope(DVE, qf, "qf", 8, gq_bc, "gq", tqc, tqs, "tq", qsq, "hb", qrb, "qrb")
                for c0 in range(0, 4, 2):
                    bank = (c0 // 2) % 2
                    for kk in range(2):
                        c = c0 + kk
                        for s in range(4):
                            col = bank * 1024 + kk * 512 + s * 128
                            S.op(PE, lambda c=c, s=s, col=col: nc.tensor.transpose(out=tp[:, col:col + 128], in_=qrb[:, s, c * 128:(c + 1) * 128],
                                                                                    identity=ident[:]),
                                 reads=["qrb", "ident"], writes=[("tpb", bank)], cost=118)
                    for kk in range(2):
                        c = c0 + kk
                        for g in range(2):
                            src_ps = tp[g * 64:(g + 1) * 64, bank * 1024 + kk * 512: bank * 1024 + (kk + 1) * 512]
                            dst = qTp[g * 64:(g + 1) * 64, c * 2 + g, :]
                            if bank == 0:
                                S.op(ACT, lambda src_ps=src_ps, dst=dst: nc.scalar.copy(out=dst, in_=src_ps), reads=[("tpb", bank)], writes=[("qT", c)])
                            else:
                                S.op(DVE, lambda src_ps=src_ps, dst=dst: nc.vector.tensor_copy(out=dst, in_=src_ps), reads=[("tpb", bank)], writes=[("qT", c)])
            if pi == 2:
                row_rstd(qf, "qf", 512, 512)
                for s in range(4):
                    S.op(DVE, lambda s=s: nc.vector.scalar_tensor_tensor(out=vnb[:, s, :], in0=qf[:, s, :], scalar=rstd[:, s:s + 1], in1=ggv_bc,
                                                                         op0=ALU.mult, op1=ALU.mult),
                         reads=["qf", ("rstd", 0), "ggv"], writes=["tqc"])
                for s in range(4):
                    gb_ = 2 + (s % 2)
                    for g in range(8):
                        S.op(PE, lambda s=s, g=g, gb_=gb_: nc.tensor.matmul(G[gb_][:, g * 64:(g + 1) * 64], lhsT=wsT[:, g, :], rhs=vnb[:, s, g * 64:(g + 1) * 64],
                                                                             start=True, stop=True),
                             reads=["tqc", "wsT"], writes=[GN[gb_]], cost=70)
                    tt, tn = (tmpA, "tmpA") if s % 2 == 0 else (tmpB, "tmpB")
                    S.op(DVE, lambda gb_=gb_, tt=tt: nc.vector.tensor_tensor(out=tt.rearrange("p (g c) -> p g c", g=8),
                                                                             in0=G[gb_].rearrange("p (g c) -> p g c", g=8),
                                                                             in1=bspT[:, :].unsqueeze(2).to_broadcast([128, 8, 64]), op=ALU.add),
                         reads=[GN[gb_], "bsp"], writes=[tn])
                    S.op(POOL, lambda s=s, tt=tt: nc.gpsimd.tensor_tensor(out=sgb[:, s, :], in0=tt, in1=ub[:, s, :], op=ALU.mult),
                         reads=[tn, "ub"], writes=["tqs"])
                transpose_T(sgb, "tqs", 4, sgT, "sgT")
        attention()
        for oc in range(8):
            slot = ring_load_mg(oc)
            rg = ring[:, slot, 0:2048].rearrange("p (kc c) -> p kc c", kc=8)
            g0, g1 = (0, 1) if oc % 2 == 0 else (2, 3)
            for kc in range(8):
                S.op(PE, lambda kc=kc, rg=rg, g0=g0: nc.tensor.matmul(G[g0], lhsT=rg[:, kc, 0:128], rhs=hT[:, kc, :], start=(kc == 0), stop=(kc == 7)),
                     reads=[("ring", slot), ("hT", kc)], writes=[GN[g0]])
            for kc in range(8):
                S.op(PE, lambda kc=kc, rg=rg, g1=g1: nc.tensor.matmul(G[g1], lhsT=rg[:, kc, 128:256], rhs=hT[:, kc, :], start=(kc == 0), stop=(kc == 7)),
                     reads=[("ring", slot), ("hT", kc)], writes=[GN[g1]])
            for h in range(8):
                S.op(PE, lambda h=h, slot=slot: nc.tensor.matmul(G[4], lhsT=ring[0:64, slot, 2560 + h * 128:2560 + (h + 1) * 128], rhs=aT[0:64, h, :],
                                                                 start=(h == 0), stop=(h == 7)),
                     reads=[("ringb", slot), ("aT", h)], writes=[GN[4]])
            for kc in range(4):
                S.op(PE, lambda kc=kc, slot=slot: nc.tensor.matmul(G[5], lhsT=ring[:, slot, 2048 + kc * 128:2048 + (kc + 1) * 128], rhs=sgT[:, kc, :],
                                                                   start=(kc == 0), stop=(kc == 3)),
                     reads=[("ring", slot), ("sgT", kc)], writes=[GN[5]])
            S.op(ACT, lambda g0=g0: nc.scalar.activation(out=tmpA, in_=G[g0], func=AF.Sigmoid), reads=[GN[g0]], writes=["tmpA"])
            S.op(ACT, lambda g1=g1: nc.scalar.activation(out=tmpB, in_=G[g1], func=AF.Sigmoid), reads=[GN[g1]], writes=["tmpB"])
            S.op(DVE, lambda: nc.vector.tensor_tensor(out=tmpA, in0=tmpA, in1=G[4], op=ALU.mult), reads=["tmpA", GN[4]], writes=["tmpA"])
            S.op(DVE, lambda: nc.vector.tensor_tensor(out=tmpB, in0=tmpB, in1=G[5], op=ALU.mult), reads=["tmpB", GN[5]], writes=["tmpB"])
            S.op(POOL, lambda oc=oc: nc.gpsimd.tensor_tensor(out=mT[:, oc, :], in0=tmpA, in1=tmpB, op=ALU.add),
                 reads=["tmpA", "tmpB"], writes=[("mT", oc)])
        wo_slots = [ring_load("wo%d" % h, s_wo[h].rearrange("p kc c -> p (kc c)"), 4096) for h in range(2)]
        for s in range(4):
            for h in range(2):
                slot = wo_slots[h]
                rv = ring[:, slot, :].rearrange("p (kc c) -> p kc c", kc=8)
                gb_ = h
                for kc in range(8):
                    S.op(PE, lambda kc=kc, s=s, gb_=gb_, rv=rv: nc.tensor.matmul(G[gb_], lhsT=mT[:, kc, s * 128:(s + 1) * 128], rhs=rv[:, kc, :],
                                                                                  start=(kc == 0), stop=(kc == 7)),
                         reads=[("mT", kc), ("ring", slot)], writes=[GN[gb_]])
                S.op(DVE, lambda s=s, h=h, gb_=gb_, X=X: nc.vector.tensor_tensor(out=X[:, s, h * 512:(h + 1) * 512], in0=G[gb_],
                                                                                  in1=X[:, s, h * 512:(h + 1) * 512], op=ALU.add),
                     reads=[GN[gb_], (xres, s)], writes=[(xres, s)])
            S.op(SP, lambda s=s, t=t: nc.sync.dma_start(out=x1s[t * T + s * 128:t * T + (s + 1) * 128, :], in_=Xb[0][:, s, :]),
                 reads=[(xres, s)], writes=[("x1s", t, s)], dma_key="xb%d" % s, cost=5000)

    S.barrier()
    apos[0] = 0
    hid = carve([128, NJ, T], BF16)
    dn = [carve([128, NJ, 512], BF16), carve([128, NJ, 512], BF16)]
    pf = carve([128, 4, PLE], F32)
    pbf = carve([128, 4, PLE], BF16)
    pT = carve([128, 2, T], BF16)
    gfin_bc = carve([128, D], F32)
    hb3 = carve([128, 4, D], BF16)
    hT3 = carve([128, 8, T], BF16)
    ALT = (hb3, "hb3", hT3, "hT3")
    small_load(gfin_bc, bc_row(g_fin), "gfin")
    out_ops = []
    for t in range(NOT if stage >= 3 else 0):
        b = t % 2
        X, xres = Xb[b], "X%d" % b
        def prep2(tt):
            bb = tt % 2
            S.op(SP, lambda bb=bb, tt=tt: nc.sync.dma_start(out=Xb[bb][:, :, :], in_=tok_rows(x1s[tt * T:(tt + 1) * T, :])),
                 reads=[("x1s", tt, s) for s in range(4)], writes=XR("X%d" % bb), dma_key="xl%d" % bb, cost=13600)
            rmsnorm_T(Xb[bb], "X%d" % bb, 2)
        if t == 0:
            prep2(0)
        ffn(1, X, xres, hid, dn, mid=(lambda t=t: prep2(t + 1)) if t + 1 < NOT else None)
        rmsnorm_T(X, xres, 3, ALT)
        S.op(SP, lambda t=t: nc.sync.dma_start(out=pf, in_=tok_rows(pin[t * T:(t + 1) * T, :])), writes=["pf"], dma_key="pfl")
        S.op(POOL, lambda: nc.gpsimd.tensor_copy(out=pbf, in_=pf), reads=["pf"], writes=["pbf"])
        transpose_T(pbf, "pbf", 2, pT, "pT")
        slotp = ring_load("pl", s_pl.rearrange("p kc c -> p (kc c)"), 2048)
        rvp = ring[:, slotp, 0:2048].rearrange("p (kc c) -> p kc c", kc=2)
        for h in range(2):
            slot = ring_load("pg%d" % h, s_pg[h].rearrange("p kc c -> p (kc c)"), 4096)
            rv = ring[:, slot, :].rearrange("p (kc c) -> p kc c", kc=8)
            for s in range(4):
                g0, g1 = (0, 1) if s % 2 == 0 else (2, 3)
                for kc in range(8):
                    S.op(PE, lambda kc=kc, s=s, g0=g0, rv=rv: nc.tensor.matmul(G[g0], lhsT=hT3[:, kc, s * 128:(s + 1) * 128], rhs=rv[:, kc, :],
                                                                                start=(kc == 0), stop=(kc == 7)),
                         reads=[("hT3", kc), ("ring", slot)], writes=[GN[g0]])
                for k2 in range(2):
                    S.op(PE, lambda k2=k2, s=s, g1=g1, h=h, rvp=rvp: nc.tensor.matmul(G[g1], lhsT=pT[:, k2, s * 128:(s + 1) * 128],
                                                                                       rhs=rvp[:, k2, h * 512:(h + 1) * 512],
                                                                                       start=(k2 == 0), stop=(k2 == 1)),
                         reads=[("pT", k2), ("ring", slotp)], writes=[GN[g1]])
                tt, tn = (tmpA, "tmpA") if s % 2 == 0 else (tmpB, "tmpB")
                S.op(ACT, lambda g0=g0, tt=tt: nc.scalar.activation(out=tt, in_=G[g0], func=AF.Sigmoid), reads=[GN[g0]], writes=[tn])
                S.op(DVE, lambda g1=g1, tt=tt: nc.vector.tensor_tensor(out=tt, in0=tt, in1=G[g1], op=ALU.mult), reads=[tn, GN[g1]], writes=[tn])
                S.op(POOL, lambda s=s, h=h, tt=tt, X=X: nc.gpsimd.tensor_tensor(out=X[:, s, h * 512:(h + 1) * 512], in0=X[:, s, h * 512:(h + 1) * 512],
                                                                                 in1=tt, op=ALU.add),
                     reads=[tn, (xres, s)], writes=[(xres, s)])
        row_rstd(X, xres, D, D)
        for s in range(4):
            S.op(DVE, lambda s=s, X=X: nc.vector.scalar_tensor_tensor(out=X[:, s, :], in0=X[:, s, :], scalar=rstd[:, s:s + 1], in1=gfin_bc,
                                                                      op0=ALU.mult, op1=ALU.mult),
                 reads=[(xres, s), ("rstd", 0), "gfin"], writes=[(xres, s)])
        out_ops.append(S.op(SP, lambda b=b, t=t: nc.sync.dma_start(out=tok_rows(y[t * T:(t + 1) * T, :]), in_=Xb[b][:, :, :]),
                            reads=XR(xres), dma_key="ys%d" % b, cost=13600))
    if stage < 3 and stage >= 1:
        out_ops.append(S.op(SP, lambda: nc.sync.dma_start(out=y[:, :], in_=x1s[:, :]), reads=[("x1s", t, s) for t in range(NOT) for s in range(4)], dma_key="dbg"))
    S.op(SP, None, extra=out_ops)

    if SCHEDULE:
        S.schedule()
        build_program.last_est_ns = S.est_ns
    semkeys = S.finalize()
    by_eng = {e: [o for o in S.ops if o.eng == e] for e in ENGS}
    with ExitStack() as es:
        sems = {k: es.enter_context(nc.semaphore("s%d" % i)) for i, k in enumerate(semkeys)}
        block = es.enter_context(nc.Block())

        def emit(engname, eng):
            waited = {}
            for o in by_eng[engname]:
                need = {}
                for d in o.deps:
                    if not d.signal or d.fn is None:
                        continue
                    if d.eng == PE and o.eng == PE and d.dma_key is None and o.dma_key is None:
                        continue
                    if need.get(d.sem, 0) < d.val:
                        need[d.sem] = d.val
                for k, v in need.items():
                    if waited.get(k, 0) < v:
                        eng.wait_ge(sems[k], v)
                        waited[k] = v
                if o.fn is not None:
                    ins = o.fn()
                    if o.signal:
                        ins.then_inc(sems[o.sem], 16 if o.dma_key is not None else 1)
                else:
                    assert not o.signal

        @block.tensor
        def _(e):
            emit(PE, e)

        @block.scalar
        def _(e):
            emit(ACT, e)

        @block.vector
        def _(e):
            emit(DVE, e)

        @block.gpsimd
        def _(e):
            emit(POOL, e)

        @block.sync
        def _(e):
            emit(SP, e)
    return nc


NCT_FULL, NOT_FULL = 32, 8
WNAMES = ["g_ffn1", "w_ffn1_gu", "w_ffn1_down", "g_mix", "w_in", "g_q", "g_k", "g_gmlp_v", "w_spatial", "b_spatial",
          "w_branch_attn", "w_branch_gmlp", "w_out", "g_ffn2", "w_ffn2_gu", "w_ffn2_down", "g_ple", "w_ple_gate", "w_ple", "g_final"]


def _pos_table(tok_idx):
    tok_idx = np.asarray(tok_idx, np.int64)
    return np.stack([tok_idx // 64, tok_idx % 64], axis=1).astype(np.float32)


def kernel(**inputs):
    xp = np.asarray(inputs["x_prompt"], np.float32)
    xs = np.asarray(inputs["x_sample"], np.float32)
    pp = np.asarray(inputs["p_prompt"], np.float32)
    ps = np.asarray(inputs["p_sample"], np.float32)
    w = {k: np.ascontiguousarray(np.asarray(inputs[k], np.float32)) for k in WNAMES}
    w["g_final"] = w["g_final"].reshape(1, D)
    NTOK = NCT_FULL * T
    own = NOT_FULL * T
    in_maps = []
    for c in range(8):
        if c < 4:
            order = [c] + [(c + k) % 4 for k in range(1, 4)]
            xcx = np.concatenate([xp[o] for o in order], axis=0)
            posi = np.concatenate([np.arange(own)] * 4)
            msk = np.zeros(NTOK, np.float32); msk[:own] = 1.0
            pc = pp[0, c]
        else:
            q = c - 4
            order = [q] + [(q + k) % 4 for k in range(1, 4)]
            xcx = np.concatenate([xs[0, o * own:(o + 1) * own] for o in order], axis=0)
            posi = np.concatenate([np.arange(o * own, (o + 1) * own) for o in order])
            msk = np.ones(NTOK, np.float32)
            pc = ps[0, 0, q * own:(q + 1) * own]
        m = {"xc": np.ascontiguousarray(xcx), "pin": np.ascontiguousarray(pc), "pos": _pos_table(posi),
             "kmask": np.ascontiguousarray(msk.reshape(NTOK // 128, 128).T)}
        m.update(w)
        in_maps.append(m)
    nc = build_program(NCT_FULL, NOT_FULL)
    res = run_bass_kernel_spmd(nc, in_maps, core_ids=list(range(8)))
    ys = [np.asarray(res.results[c]["y"], np.float32) for c in range(8)]
    y_prompt = np.stack(ys[0:4], axis=0)
    y_sample = np.concatenate(ys[4:8], axis=0)[None]
    return (y_prompt, y_sample)
```
